# Optimizing a Trainium2 kernel written in Bass

```python
import jax, jax.numpy as jnp
from jax import lax
import numpy as np

D_MODEL = 1024
BATCH = 8
SEQ = 4096
DEPTH = 1

SSM_EXPAND = 2
SSM_D_INNER = SSM_EXPAND * D_MODEL
SSM_HEAD_DIM = 64
SSM_N_HEADS = SSM_D_INNER // SSM_HEAD_DIM
SSM_N_GROUPS = 8
SSM_D_STATE = 128
SSM_CHUNK = 128
SSM_CONV_DIM = SSM_D_INNER + 2 * SSM_N_GROUPS * SSM_D_STATE
GDN_HEAD_K = 128
GDN_HEAD_V = 128
GDN_N_QK_HEADS = D_MODEL // GDN_HEAD_K
GDN_N_V_HEADS = 2 * GDN_N_QK_HEADS
GDN_KEY_DIM = GDN_N_QK_HEADS * GDN_HEAD_K
GDN_VAL_DIM = GDN_N_V_HEADS * GDN_HEAD_V
GDN_CHUNK = 64
GDN_CONV_DIM = 2 * GDN_KEY_DIM + GDN_VAL_DIM
CONV_K = 4
MLP_HIDDEN = 4 * D_MODEL
EPS = 1e-6
IN_SPLIT_SIZES = (SSM_D_INNER, SSM_CONV_DIM, SSM_N_HEADS, GDN_CONV_DIM, GDN_VAL_DIM,
                  GDN_N_V_HEADS, GDN_N_V_HEADS, D_MODEL, D_MODEL)
IN_PROJ_DIM = sum(IN_SPLIT_SIZES)

kernel_name = "hybrid_ssd_gdn_sandwich_adaln_block"


def rmsnorm(x, w):
    xf = x.astype(jnp.float32)
    y = xf * lax.rsqrt(jnp.mean(xf * xf, axis=-1, keepdims=True) + EPS)
    return (y * w.astype(jnp.float32)).astype(x.dtype)


def l2norm(x):
    return x * lax.rsqrt(jnp.sum(x * x, axis=-1, keepdims=True) + EPS)


def causal_depthwise_conv(x, w):
    return lax.conv_general_dilated(
        x, w[:, None, :].astype(x.dtype), window_strides=(1,), padding=[(CONV_K - 1, 0)],
        dimension_numbers=('NWC', 'WIO', 'NWC'), feature_group_count=x.shape[-1])


def ssd_chunked_scan(xh, dt, A, Bm, Cm):
    Bsz, S, H, P = xh.shape
    G, N = Bm.shape[-2:]
    hg = H // G
    L = SSM_CHUNK
    nc = S // L
    xdt = jnp.moveaxis((xh * dt[..., None]).reshape(Bsz, nc, L, G, hg, P), 1, 0)
    a = jnp.moveaxis((dt * A).reshape(Bsz, nc, L, G, hg), 1, 0)
    Bc = jnp.moveaxis(Bm.reshape(Bsz, nc, L, G, N), 1, 0)
    Cc = jnp.moveaxis(Cm.reshape(Bsz, nc, L, G, N), 1, 0)
    causal = jnp.tril(jnp.ones((L, L), dtype=bool))[None, :, :, None, None]

    def step(state, inp):
        xc, ac, bc, cc = inp
        acum = jnp.cumsum(ac, axis=1)
        seg = acum[:, :, None] - acum[:, None, :]
        decay = jnp.exp(jnp.where(causal, seg, -jnp.inf))
        cb = jnp.einsum('blgn,bsgn->blsg', cc, bc)
        y_diag = jnp.einsum('blsg,blsgh,bsghp->blghp', cb, decay, xc)
        y_off = jnp.einsum('blgn,bghpn->blghp', cc, state) * jnp.exp(acum)[..., None]
        a_last = acum[:, -1]
        w_s = jnp.exp(a_last[:, None] - acum)
        new_state = state * jnp.exp(a_last)[..., None, None] + jnp.einsum(
            'bsgn,bsgh,bsghp->bghpn', bc, w_s, xc)
        return new_state, y_diag + y_off

    state0 = jnp.zeros((Bsz, G, hg, P, N), dtype=jnp.float32)
    _, y = lax.scan(step, state0, (xdt, a, Bc, Cc))
    return jnp.moveaxis(y, 0, 1).reshape(Bsz, S, H, P)


def gated_delta_rule_chunked(q, k, v, g, beta):
    Bsz, S, H, dk = q.shape
    dv = v.shape[-1]
    L = GDN_CHUNK
    nc = S // L
    q = q * (dk ** -0.5)

    def chunks(t):
        t = t.reshape((Bsz, nc, L, H) + t.shape[3:])
        return jnp.moveaxis(t, (1, 3), (0, 2))

    causal = jnp.tril(jnp.ones((L, L), dtype=bool))
    strict = jnp.tril(jnp.ones((L, L), dtype=bool), -1)
    eye = jnp.eye(L, dtype=jnp.float32)

    def step(state, inp):
        qc, kc, vc, gc, bc = inp
        gcum = jnp.cumsum(gc, axis=-1)
        dmat = jnp.exp(jnp.where(causal, gcum[..., :, None] - gcum[..., None, :], -jnp.inf))
        kb = kc * bc[..., None]
        a_low = jnp.where(strict, jnp.einsum('bhid,bhjd->bhij', kb, kc) * dmat, 0.0)
        rhs = jnp.concatenate([vc * bc[..., None], kb * jnp.exp(gcum)[..., None]], axis=-1)
        sol = lax.linalg.triangular_solve(eye + a_low, rhs, left_side=True, lower=True,
                                          unit_diagonal=True)
        u, w = sol[..., :dv], sol[..., dv:]
        attn = jnp.einsum('bhid,bhjd->bhij', qc, kc) * dmat
        v_new = u - jnp.einsum('bhlk,bhkv->bhlv', w, state)
        o = jnp.einsum('bhlk,bhkv->bhlv', qc * jnp.exp(gcum)[..., None], state) + jnp.einsum(
            'bhij,bhjv->bhiv', attn, v_new)
        g_last = gcum[..., -1]
        k_dec = kc * jnp.exp(g_last[..., None] - gcum)[..., None]
        new_state = state * jnp.exp(g_last)[..., None, None] + jnp.einsum(
            'bhlk,bhlv->bhkv', k_dec, v_new)
        return new_state, o

    state0 = jnp.zeros((Bsz, H, dk, dv), dtype=jnp.float32)
    _, o = lax.scan(step, state0, (chunks(q), chunks(k), chunks(v), chunks(g), chunks(beta)))
    return jnp.moveaxis(o, (0, 2), (1, 3)).reshape(Bsz, S, H, dv)


def mamba2_branch(z, xbc, dt_raw, conv_w, conv_b, dt_bias, A_log, d_skip, norm_w):
    f32 = jnp.float32
    Bsz, S, _ = xbc.shape
    xbc = jax.nn.silu(causal_depthwise_conv(xbc, conv_w) + conv_b)
    xs, Bm, Cm = jnp.split(xbc, [SSM_D_INNER, SSM_D_INNER + SSM_N_GROUPS * SSM_D_STATE], axis=-1)
    xh = xs.reshape(Bsz, S, SSM_N_HEADS, SSM_HEAD_DIM).astype(f32)
    dt = jax.nn.softplus(dt_raw.astype(f32) + dt_bias.astype(f32))
    A = -jnp.exp(A_log.astype(f32))
    y = ssd_chunked_scan(xh, dt, A,
                         Bm.reshape(Bsz, S, SSM_N_GROUPS, SSM_D_STATE).astype(f32),
                         Cm.reshape(Bsz, S, SSM_N_GROUPS, SSM_D_STATE).astype(f32))
    y = y + d_skip.astype(f32)[:, None] * xh
    y = y.reshape(Bsz, S, SSM_D_INNER) * jax.nn.silu(z.astype(f32))
    y = rmsnorm(y.reshape(Bsz, S, SSM_N_GROUPS, -1), norm_w.reshape(SSM_N_GROUPS, -1))
    return y.reshape(Bsz, S, SSM_D_INNER).astype(z.dtype)


def gated_deltanet_branch(qkv, z, b, a, conv_w, dt_bias, A_log, norm_w):
    f32 = jnp.float32
    Bsz, S, _ = qkv.shape
    qkv = jax.nn.silu(causal_depthwise_conv(qkv, conv_w))
    q, k, v = jnp.split(qkv, [GDN_KEY_DIM, 2 * GDN_KEY_DIM], axis=-1)
    rep = GDN_N_V_HEADS // GDN_N_QK_HEADS
    q = jnp.repeat(l2norm(q.reshape(Bsz, S, GDN_N_QK_HEADS, GDN_HEAD_K).astype(f32)), rep, axis=2)
    k = jnp.repeat(l2norm(k.reshape(Bsz, S, GDN_N_QK_HEADS, GDN_HEAD_K).astype(f32)), rep, axis=2)
    v = v.reshape(Bsz, S, GDN_N_V_HEADS, GDN_HEAD_V).astype(f32)
    beta = jax.nn.sigmoid(b.astype(f32))
    g = -jnp.exp(A_log.astype(f32)) * jax.nn.softplus(a.astype(f32) + dt_bias.astype(f32))
    o = gated_delta_rule_chunked(q, k, v, g, beta)
    o = rmsnorm(o, norm_w) * jax.nn.silu(z.reshape(Bsz, S, GDN_N_V_HEADS, GDN_HEAD_V).astype(f32))
    return o.reshape(Bsz, S, GDN_VAL_DIM).astype(z.dtype)


def setup_inputs(seed: int = 0) -> dict:
    key = jax.random.key(seed)
    ks = jax.random.split(key, 26)
    f32 = jnp.float32
    nrm = lambda k, shape, scale: jax.random.normal(k, shape, f32) * scale
    gain = lambda k, shape: 1.0 + 0.02 * jax.random.normal(k, shape, f32)

    def inv_softplus_dt(k, shape):
        dt = jnp.exp(jax.random.uniform(k, shape, f32, np.log(1e-3), np.log(1e-1)))
        return dt + jnp.log(-jnp.expm1(-dt))

    Dm = D_MODEL
    return {
        'x': jax.random.normal(ks[0], (BATCH, SEQ, Dm), f32),
        'c': jax.random.normal(ks[1], (BATCH, Dm), f32),
        'w_ada': nrm(ks[2], (DEPTH, Dm, 6 * Dm), 0.5 * Dm ** -0.5),
        'b_ada': nrm(ks[3], (DEPTH, 6 * Dm), 0.02),
        'norm_mix_pre': gain(ks[4], (DEPTH, Dm)),
        'norm_mix_post': gain(ks[5], (DEPTH, Dm)),
        'w_in': nrm(ks[6], (DEPTH, Dm, IN_PROJ_DIM), Dm ** -0.5),
        'ssm_conv_w': nrm(ks[7], (DEPTH, CONV_K, SSM_CONV_DIM), CONV_K ** -0.5),
        'ssm_conv_b': nrm(ks[8], (DEPTH, SSM_CONV_DIM), 0.02),
        'ssm_dt_bias': inv_softplus_dt(ks[9], (DEPTH, SSM_N_HEADS)),
        'ssm_A_log': jnp.log(jax.random.uniform(ks[10], (DEPTH, SSM_N_HEADS), f32, 1.0, 16.0)),
        'ssm_D': gain(ks[11], (DEPTH, SSM_N_HEADS)),
        'ssm_norm_w': gain(ks[12], (DEPTH, SSM_D_INNER)),
        'gdn_conv_w': nrm(ks[13], (DEPTH, CONV_K, GDN_CONV_DIM), CONV_K ** -0.5),
        'gdn_dt_bias': inv_softplus_dt(ks[14], (DEPTH, GDN_N_V_HEADS)),
        'gdn_A_log': jnp.log(jax.random.uniform(ks[15], (DEPTH, GDN_N_V_HEADS), f32, 1.0, 16.0)),
        'gdn_norm_w': gain(ks[16], (DEPTH, GDN_HEAD_V)),
        'w_ssm_up': nrm(ks[17], (DEPTH, SSM_D_INNER, Dm), SSM_D_INNER ** -0.5),
        'w_gdn_up': nrm(ks[18], (DEPTH, GDN_VAL_DIM, Dm), GDN_VAL_DIM ** -0.5),
        'w_out': nrm(ks[19], (DEPTH, Dm, Dm), Dm ** -0.5),
        'norm_mlp_pre': gain(ks[20], (DEPTH, Dm)),
        'norm_mlp_post': gain(ks[21], (DEPTH, Dm)),
        'w_mlp_up': nrm(ks[22], (DEPTH, Dm, MLP_HIDDEN), Dm ** -0.5),
        'w_mlp_down': nrm(ks[23], (DEPTH, MLP_HIDDEN, Dm), MLP_HIDDEN ** -0.5),
    }


def reference(x, c, w_ada, b_ada, norm_mix_pre, norm_mix_post, w_in, ssm_conv_w, ssm_conv_b,
              ssm_dt_bias, ssm_A_log, ssm_D, ssm_norm_w, gdn_conv_w, gdn_dt_bias, gdn_A_log,
              gdn_norm_w, w_ssm_up, w_gdn_up, w_out, norm_mlp_pre, norm_mlp_post, w_mlp_up,
              w_mlp_down):
    offsets = [int(o) for o in np.cumsum(IN_SPLIT_SIZES)[:-1]]
    c_act = jax.nn.silu(c)
    for l in range(DEPTH):
        mod = c_act @ w_ada[l] + b_ada[l]
        sh1, sc1, g1, sh2, sc2, g2 = [m[:, None, :] for m in jnp.split(mod, 6, axis=-1)]

        h = rmsnorm(x, norm_mix_pre[l]) * (1.0 + sc1) + sh1
        proj = h @ w_in[l]
        (z_ssm, xbc, dt_raw, qkv, z_gdn, b_gdn, a_gdn,
         gate_ssm, gate_gdn) = jnp.split(proj, offsets, axis=-1)
        y_ssm = mamba2_branch(z_ssm, xbc, dt_raw, ssm_conv_w[l], ssm_conv_b[l], ssm_dt_bias[l],
                              ssm_A_log[l], ssm_D[l], ssm_norm_w[l]) @ w_ssm_up[l]
        y_gdn = gated_deltanet_branch(qkv, z_gdn, b_gdn, a_gdn, gdn_conv_w[l], gdn_dt_bias[l],
                                      gdn_A_log[l], gdn_norm_w[l]) @ w_gdn_up[l]
        merged = jax.nn.sigmoid(gate_ssm) * y_ssm + jax.nn.sigmoid(gate_gdn) * y_gdn
        x = x + g1 * rmsnorm(merged @ w_out[l], norm_mix_post[l])

        h = rmsnorm(x, norm_mlp_pre[l]) * (1.0 + sc2) + sh2
        y = jnp.square(jax.nn.relu(h @ w_mlp_up[l])) @ w_mlp_down[l]
        x = x + g2 * rmsnorm(y, norm_mlp_post[l])
    return x
```

```python
import numpy as np
import ml_dtypes
from contextlib import ExitStack
import concourse.bass as bass
import concourse.mybir as mybir
from concourse.bass_utils import run_bass_kernel_spmd

F32 = mybir.dt.float32
BF16 = mybir.dt.bfloat16
AF = mybir.ActivationFunctionType
ALU = mybir.AluOpType

D = 1024
EPS = 1e-6
COMPUTE = ("pe", "act", "dve", "pool")
NSLOT = 8


STRICT = False


class Sched:
    def __init__(self, nc, st):
        self.nc = nc
        self.st = st
        self.streams = {e: [] for e in ("pe", "act", "dve", "pool", "sp")}
        self.sem = {}
        for e in COMPUTE:
            self.sem[e] = st.enter_context(nc.semaphore("c_" + e))
        for i in range(NSLOT):
            self.sem[("sp", i)] = st.enter_context(nc.semaphore("d_sp%d" % i))
        self.count = {k: 0 for k in self.sem}
        self.dma_idx = 0
        self.known = {e: {} for e in self.streams}
        self.clock = {}
        self.last_write = {}
        self.readers = {}
        self.ninstr = 0
        self.nwaits = 0

    def _need(self, eng, ev, waits):
        c, n = ev
        if self.known[eng].get(c, 0) >= n:
            return
        if waits.get(c, 0) < n:
            waits[c] = n

    def _deps(self, eng, reads, writes):
        waits = {}
        for k in reads:
            ev = self.last_write.get(k)
            if ev is not None:
                if ev[0] == eng and eng == "pe":
                    continue
                self._need(eng, ev, waits)
        for k in writes:
            ev = self.last_write.get(k)
            if ev is not None and (STRICT and eng != "pe" or not (ev[0] == eng and eng in COMPUTE)):
                self._need(eng, ev, waits)
            for rv in self.readers.get(k, ()):
                if rv[0] == eng and eng in COMPUTE and not STRICT:
                    continue
                self._need(eng, rv, waits)
        return waits

    def _apply(self, eng, waits):
        kn = self.known[eng]
        for c, n in waits.items():
            ck = self.clock.get((c, n))
            if ck:
                for cc, nn in ck.items():
                    if kn.get(cc, 0) < nn:
                        kn[cc] = nn
            if kn.get(c, 0) < n:
                kn[c] = n

    def _record(self, ev, eng, reads, writes):
        ck = dict(self.known[eng])
        ck[ev[0]] = ev[1]
        self.clock[ev] = ck
        for k in reads:
            self.readers.setdefault(k, []).append(ev)
        for k in writes:
            self.last_write[k] = ev
            self.readers[k] = []

    def op(self, eng, fn, reads=(), writes=()):
        waits = self._deps(eng, reads, writes)
        self._apply(eng, waits)
        self.count[eng] += 1
        ev = (eng, self.count[eng])
        self._record(ev, eng, reads, writes)
        self.streams[eng].append((list(waits.items()), fn, (eng, 1)))
        self.ninstr += 1
        self.nwaits += len(waits)
        return ev

    def dma(self, out, in_, reads=(), writes=()):
        q = "sp"
        slot = (q, self.dma_idx % NSLOT)
        self.dma_idx += 1
        waits = self._deps(q, reads, writes)
        if self.count[slot] > 0:
            self._need(q, (slot, self.count[slot]), waits)
        self._apply(q, waits)
        self.count[slot] += 1
        ev = (slot, self.count[slot])
        self._record(ev, q, reads, writes)
        fn = lambda e, out=out, in_=in_: e.dma_start(out=out, in_=in_)
        self.streams[q].append((list(waits.items()), fn, (slot, 16)))
        self.ninstr += 1
        self.nwaits += len(waits)
        return ev

    def barrier(self):
        for eng in self.streams:
            waits = {}
            for c, n in self.count.items():
                if n > 0 and c != eng:
                    self._need(eng, (c, n), waits)
            self._apply(eng, waits)
            self.streams[eng].append((list(waits.items()), None, None))
        self.last_write = {}
        self.readers = {}

    def emit(self):
        nc = self.nc
        block = self.st.enter_context(nc.Block())
        sem = self.sem

        def run(stream):
            def body(e):
                for waits, fn, inc in stream:
                    for c, n in waits:
                        e.wait_ge(sem[c], n * (1 if c in COMPUTE else 16))
                    if fn is not None:
                        fn(e).then_inc(sem[inc[0]], inc[1])
            return body

        block.tensor(run(self.streams["pe"]))
        block.scalar(run(self.streams["act"]))
        block.vector(run(self.streams["dve"]))
        block.gpsimd(run(self.streams["pool"]))
        block.sync(run(self.streams["sp"]))


OFF_Z1 = 0
OFF_XBC = 2048
OFF_DT = 6144
OFF_QKV = 6176
OFF_Z2 = 10272
OFF_B = 12320
OFF_A = 12336
OFF_GS = 12352
OFF_GG = 13376

C_ID, C_U, C_GT, C_SU, C_ONE, C_ONED, C_EPS, C_BD8, C_CMT, NCONST = 0, 1, 2, 3, 4, 5, 6, 7, 8, 12
R_DTB1, R_AL1, R_D1, R_DTB2, R_AL2, R_NW1, R_NW2, RLEN = 0, 32, 64, 96, 112, 128, 2176, 2304


def make_consts():
    k = np.arange(128)[:, None]
    l = np.arange(128)[None, :]
    c = np.zeros((128, NCONST, 128), np.float32)
    c[:, C_ID] = (k == l)
    c[:, C_U] = (k <= l)
    c[:, C_GT] = (k > l)
    c[:, C_SU] = (l > k)
    c[:, C_ONE] = 1.0
    c[:, C_ONED] = 1.0 / D
    c[:, C_EPS] = EPS
    c[:, C_BD8] = (k // 8 == l // 8)
    for n, b in enumerate((8, 16, 32, 64)):
        cm = ((k // (2 * b) == l // (2 * b)) & ((k // b) % 2 == 0) & ((l // b) % 2 == 1))
        c[:, C_CMT + n] = cm.T
    return c


def rsqrt_to(S, cst, dst, dstk, src, srck, scale=1.0):
    S.op("act", lambda e: e.activation(out=dst, in_=src, func=AF.Sqrt, bias=cst[:, C_EPS, 0:1], scale=scale),
         reads=[srck, "cst"], writes=[dstk])
    S.op("dve", lambda e: e.reciprocal(out=dst, in_=dst), reads=[dstk], writes=[dstk])


def build(T, phases=(0, 1, 2, 3, 4, 5), debug=False):
    NT = T // 128
    NB = T // 512
    nc = bass.Bass("TRN2", target_bir_lowering=False)

    def din(name, shape, dt=F32):
        return nc.dram_tensor(name, shape, dt, kind="ExternalInput").ap()

    def dscr(name, shape, dt):
        return nc.dram_tensor(name, shape, dt, kind="ExternalOutput").ap()

    x = din("x", [T, D])
    c_col = din("c_col", [128, 8])
    w_ada = din("w_ada", [D, 6 * D])
    b_ada_col = din("b_ada_col", [128, 48])
    nw_col = din("nw_col", [128, 4, 8])
    w_in = din("w_in", [D, 14400])
    cw_ssm = din("cw_ssm", [128, 32, 5])
    cw_gdn = din("cw_gdn", [128, 32, 4])
    rowv = din("rowv", [1, RLEN])
    w_su = din("w_su", [2048, D])
    w_gu = din("w_gu", [2048, D])
    w_out = din("w_out", [D, D])
    w_up = din("w_up", [D, 4096])
    w_dn = din("w_dn", [4096, D])
    consts = din("consts", [128, NCONST, 128])
    out = nc.dram_tensor("out", [T, D], F32, kind="ExternalOutput").ap()

    XT = dscr("s_xt", [D, T], F32)
    XBC_T = dscr("s_xbct", [4096, T], BF16)
    QKV_T = dscr("s_qkvt", [4096, T], BF16)
    G_T = dscr("s_gt", [2048, T], BF16)
    Z = dscr("s_z", [T, 4096], BF16)
    SM = dscr("s_sm", [T, 96], F32)
    if debug:
        Y = dscr("s_y", [T, 4096], BF16)
        X2T = dscr("s_x2t", [D, T], F32)
        MOD = dscr("s_mod", [128, 48], F32)
    else:
        Y = Z
        X2T = XT
        MOD = None

    with ExitStack() as st:
        S = Sched(nc, st)
        sb = lambda name, shape, dt=F32: st.enter_context(nc.sbuf_tensor(name, shape, dt))
        cst = sb("cst", [128, NCONST, 128])
        cstb = sb("cstb", [128, 1, 128], BF16)
        pv = sb("pv", [128, 6, 8])

        S.dma(cst[:], consts, writes=["cst"])
        S.op("pool", lambda e: e.tensor_copy(out=cstb[:], in_=cst[:, 0:1, :]), reads=["cst"], writes=["cstb"])
        ident = cst[:, C_ID, :]
        identb = cstb[:, C_ID, :]
        Umat = cst[:, C_U, :]
        GTm = cst[:, C_GT, :]
        SUm = cst[:, C_SU, :]
        ones = cst[:, C_ONE, :]
        onesD = cst[:, C_ONED, :]
        if 0 in phases:
            with ExitStack() as ps:
                lsb = lambda name, shape, dt=F32: ps.enter_context(nc.sbuf_tensor(name, shape, dt))
                cact = lsb("cact", [128, 8])
                csig = lsb("csig", [128, 8])
                wa = [lsb("wa%d" % i, [128, 8, 512]) for i in range(2)]
                modsb = lsb("modsb", [128, 48])
                bada = lsb("bada", [128, 48])
                nwc = lsb("nwc", [128, 4, 8])
                modps = ps.enter_context(nc.psum_tensor("modps", [128, 512], F32))
                S.dma(cact[:], c_col, writes=["cact"])
                S.dma(bada[:], b_ada_col, writes=["bada"])
                S.dma(nwc[:], nw_col, writes=["nwc"])
                S.op("act", lambda e: e.activation(out=csig[:], in_=cact[:], func=AF.Sigmoid), reads=["cact"], writes=["csig"])
                S.op("dve", lambda e: e.tensor_tensor(out=cact[:], in0=cact[:], in1=csig[:], op=ALU.mult),
                     reads=["cact", "csig"], writes=["cact"])
                wav = w_ada.rearrange("(k p) f -> p k f", p=128)
                for fb in range(12):
                    w = wa[fb % 2]
                    wk = "wa%d" % (fb % 2)
                    S.dma(w[:], wav[:, :, fb * 512:(fb + 1) * 512], writes=[wk])
                    for j in range(4):
                        col = fb * 4 + j
                        for k in range(8):
                            S.op("pe", lambda e, w=w, j=j, k=k, col=col: e.matmul(
                                modps[:, col:col + 1], w[:, k, j * 128:(j + 1) * 128], cact[:, k:k + 1],
                                start=(k == 0), stop=(k == 7)), reads=[wk, "cact"], writes=["modps"])
                S.op("dve", lambda e: e.tensor_tensor(out=modsb[:], in0=modps[:, 0:48], in1=bada[:], op=ALU.add),
                     reads=["modps", "bada"], writes=["modsb"])
                S.op("dve", lambda e: e.scalar_tensor_tensor(out=pv[:, 0, :], in0=modsb[:, 8:16], scalar=1.0, in1=nwc[:, 0, :],
                                                             op0=ALU.add, op1=ALU.mult), reads=["modsb", "nwc"], writes=["pv"])
                S.op("dve", lambda e: e.tensor_copy(out=pv[:, 1, :], in_=modsb[:, 0:8]), reads=["modsb"], writes=["pv"])
                S.op("dve", lambda e: e.tensor_tensor(out=pv[:, 2, :], in0=modsb[:, 16:24], in1=nwc[:, 1, :], op=ALU.mult),
                     reads=["modsb", "nwc"], writes=["pv"])
                S.op("dve", lambda e: e.scalar_tensor_tensor(out=pv[:, 3, :], in0=modsb[:, 32:40], scalar=1.0, in1=nwc[:, 2, :],
                                                             op0=ALU.add, op1=ALU.mult), reads=["modsb", "nwc"], writes=["pv"])
                S.op("dve", lambda e: e.tensor_copy(out=pv[:, 4, :], in_=modsb[:, 24:32]), reads=["modsb"], writes=["pv"])
                S.op("dve", lambda e: e.tensor_tensor(out=pv[:, 5, :], in0=modsb[:, 40:48], in1=nwc[:, 3, :], op=ALU.mult),
                     reads=["modsb", "nwc"], writes=["pv"])
                if debug:
                    S.dma(MOD, modsb[:], reads=["modsb"], writes=["MOD"])
                S.barrier()

        def normT(xT, xk, sq, sqk, rstd, rstdk, ssps, sspsk, tmp, tmpk, hdst, hk, ia, ish, W=512):
            S.op("act", lambda e: e.activation(out=sq[:], in_=xT[:], func=AF.Square), reads=[xk], writes=[sqk])
            for k in range(8):
                S.op("pe", lambda e, k=k: e.matmul(ssps[:, 0:W], onesD, sq[:, k, :], start=(k == 0), stop=(k == 7)),
                     reads=[sqk, "cst"], writes=[sspsk])
            rsqrt_to(S, cst, rstd[:], rstdk, ssps[:, 0:W], sspsk)
            for k in range(8):
                t = tmp[k % 2]
                tk = tmpk[k % 2]
                S.op("dve", lambda e, k=k, t=t: e.tensor_tensor(out=t[:], in0=xT[:, k, :], in1=rstd[:], op=ALU.mult),
                     reads=[xk, rstdk], writes=[tk])
                S.op("act", lambda e, k=k, t=t: e.activation(out=hdst(k), in_=t[:], func=AF.Identity,
                                                           bias=pv[:, ish, k:k + 1], scale=pv[:, ia, k:k + 1]),
                     reads=[tk, "pv"], writes=[hk])

        hT_cm = None
        hst = ExitStack()
        if 1 in phases or 2 in phases:
            hT_cm = hst.enter_context(nc.sbuf_tensor("hT", [128, 8, T], BF16))

        if 1 in phases:
            with ExitStack() as ps:
                lsb = lambda name, shape, dt=F32: ps.enter_context(nc.sbuf_tensor(name, shape, dt))
                xtok = [lsb("xtok%d" % i, [128, 4, D]) for i in range(2)]
                xTb = [lsb("xTb%d" % i, [128, 8, 512]) for i in range(2)]
                sq = lsb("sq1", [128, 8, 512])
                rstd = lsb("rstd1", [128, 512])
                tmp = [lsb("tmp1_%d" % i, [128, 512]) for i in range(2)]
                tps = [ps.enter_context(nc.psum_tensor("tps%d" % i, [128, 512], F32)) for i in range(4)]
                ssps = ps.enter_context(nc.psum_tensor("ssps1", [128, 512], F32))
                XTv = XT.rearrange("(k p) t -> p k t", p=128)
                for nb in range(NB):
                    xt = xtok[nb % 2]
                    xtk = "xtok%d" % (nb % 2)
                    xT = xTb[nb % 2]
                    xTk = "xTb%d" % (nb % 2)
                    S.dma(xt[:], x[nb * 512:(nb + 1) * 512, :].rearrange("(a p) f -> p a f", p=128), writes=[xtk])
                    for k in range(8):
                        tp = tps[k % 4]
                        tpk = "tps%d" % (k % 4)
                        for a in range(4):
                            S.op("pe", lambda e, tp=tp, a=a, k=k, xt=xt: e.transpose(
                                tp[:, a * 128:(a + 1) * 128], xt[:, a, k * 128:(k + 1) * 128], ident),
                                reads=[xtk, "cst"], writes=[tpk])
                        if k % 2 == 0:
                            S.op("act", lambda e, tp=tp, k=k, xT=xT: e.activation(out=xT[:, k, :], in_=tp[:], func=AF.Identity),
                                 reads=[tpk], writes=[xTk])
                        else:
                            S.op("dve", lambda e, tp=tp, k=k, xT=xT: e.tensor_copy(out=xT[:, k, :], in_=tp[:]),
                                 reads=[tpk], writes=[xTk])
                    S.dma(XTv[:, :, nb * 512:(nb + 1) * 512], xT[:], reads=[xTk], writes=["XT"])
                    normT(xT, xTk, sq, "sq1", rstd, "rstd1", ssps, "ssps1", tmp, ["tmp1_0", "tmp1_1"],
                          lambda k, nb=nb: hT_cm[:, k, nb * 512:(nb + 1) * 512], ("hT", nb), 0, 1)
                S.barrier()

        if 2 in phases:
            hkeys = [("hT", nb) for nb in range(NB)]
            with ExitStack() as ps:
                lsb = lambda name, shape, dt=F32: ps.enter_context(nc.sbuf_tensor(name, shape, dt))
                wst = [lsb("wst%d" % i, [128, 8, 128]) for i in range(2)]
                wbf = [lsb("wbf%d" % i, [128, 8, 128], BF16) for i in range(2)]
                pc = [lsb("pc%d" % i, [128, T + 3]) for i in range(2)]
                acc = lsb("acc", [128, T])
                sq2 = lsb("sq2", [128, T])
                rs = lsb("rs", [128, T])
                ob = [lsb("ob%d" % i, [128, T], BF16) for i in range(2)]
                cws = lsb("cws", [128, 32, 5])
                cwg = lsb("cwg", [128, 32, 4])
                pps = [ps.enter_context(nc.psum_tensor("pps%d" % i, [128, 512], F32)) for i in range(4)]
                sps = [ps.enter_context(nc.psum_tensor("sps%d" % i, [128, 512], F32)) for i in range(2)]
                S.dma(cws[:], cw_ssm, writes=["cws"])
                S.dma(cwg[:], cw_gdn, writes=["cwg"])
                for i in range(2):
                    S.op("pool", lambda e, i=i: e.memset(pc[i][:, 0:3], 0.0), writes=["pc%d" % i])
                w_in_v = w_in.rearrange("(k p) f -> p k f", p=128)
                XBCv = XBC_T.rearrange("(b p) t -> b p t", p=128)
                QKVv = QKV_T.rearrange("(b p) t -> b p t", p=128)
                GTv = G_T.rearrange("(b p) t -> b p t", p=128)
                blocks = []
                for cb in range(32):
                    blocks.append(("xbc", cb, OFF_XBC + cb * 128))
                for cb in range(32):
                    blocks.append(("qkv", cb, OFF_QKV + cb * 128))
                for cb in range(8):
                    blocks.append(("gate", cb, OFF_GS + cb * 128))
                for cb in range(8):
                    blocks.append(("gate", 8 + cb, OFF_GG + cb * 128))
                pcount = 0
                for bi, (kind, cb, off) in enumerate(blocks):
                    ws = wst[bi % 2]
                    wsk = "wst%d" % (bi % 2)
                    wb = wbf[bi % 2]
                    wbk = "wbf%d" % (bi % 2)
                    if bi == 0:
                        S.dma(ws[:], w_in_v[:, :, off:off + 128], writes=[wsk])
                    if bi + 1 < len(blocks):
                        noff = blocks[bi + 1][2]
                        S.dma(wst[(bi + 1) % 2][:], w_in_v[:, :, noff:noff + 128], writes=["wst%d" % ((bi + 1) % 2)])
                    S.op("pool", lambda e, ws=ws, wb=wb: e.tensor_copy(out=wb[:], in_=ws[:]), reads=[wsk], writes=[wbk])
                    p_ = pc[bi % 2]
                    pk = "pc%d" % (bi % 2)
                    o_ = ob[bi % 2]
                    ok = "ob%d" % (bi % 2)
                    for tb in range(NB):
                        pp = pps[pcount % 4]
                        ppk = "pps%d" % (pcount % 4)
                        pcount += 1
                        for k in range(8):
                            S.op("pe", lambda e, pp=pp, wb=wb, k=k, tb=tb: e.matmul(
                                pp[:], wb[:, k, :], hT_cm[:, k, tb * 512:(tb + 1) * 512], start=(k == 0), stop=(k == 7)),
                                reads=[wbk, hkeys[tb]], writes=[ppk])
                        if kind == "gate":
                            S.op("act", lambda e, pp=pp, o_=o_, tb=tb: e.activation(
                                out=o_[:, tb * 512:(tb + 1) * 512], in_=pp[:], func=AF.Sigmoid), reads=[ppk], writes=[ok])
                        else:
                            S.op("act", lambda e, pp=pp, p_=p_, tb=tb: e.activation(
                                out=p_[:, 3 + tb * 512:3 + (tb + 1) * 512], in_=pp[:], func=AF.Identity), reads=[ppk], writes=[pk])
                            if kind == "xbc":
                                S.op("act", lambda e, pp=pp, tb=tb, cb=cb: e.activation(
                                    out=acc[:, tb * 512:(tb + 1) * 512], in_=pp[:], func=AF.Identity,
                                    bias=cws[:, cb, 4:5], scale=cws[:, cb, 3:4]), reads=[ppk, "cws"], writes=["acc"])
                            else:
                                S.op("act", lambda e, pp=pp, tb=tb, cb=cb: e.activation(
                                    out=acc[:, tb * 512:(tb + 1) * 512], in_=pp[:], func=AF.Identity,
                                    scale=cwg[:, cb, 3:4]), reads=[ppk, "cwg"], writes=["acc"])
                    if kind == "gate":
                        S.dma(GTv[cb], o_[:], reads=[ok], writes=["G_T"])
                        continue
                    cwt = cws if kind == "xbc" else cwg
                    cwk = "cws" if kind == "xbc" else "cwg"
                    for j in range(1, 4):
                        S.op("dve", lambda e, p_=p_, cb=cb, j=j, cwt=cwt: e.scalar_tensor_tensor(
                            out=acc[:], in0=p_[:, 3 - j:3 - j + T], scalar=cwt[:, cb, 3 - j:4 - j], in1=acc[:],
                            op0=ALU.mult, op1=ALU.add), reads=[pk, cwk, "acc"], writes=["acc"])
                    if kind == "qkv" and cb < 16:
                        S.op("act", lambda e: e.activation(out=acc[:], in_=acc[:], func=AF.Silu), reads=["acc"], writes=["acc"])
                        S.op("act", lambda e: e.activation(out=sq2[:], in_=acc[:], func=AF.Square), reads=["acc"], writes=["sq2"])
                        for tb in range(NB):
                            sp = sps[tb % 2]
                            spk = "sps%d" % (tb % 2)
                            S.op("pe", lambda e, sp=sp, tb=tb: e.matmul(sp[:], ones, sq2[:, tb * 512:(tb + 1) * 512],
                                                                        start=True, stop=True),
                                 reads=["sq2", "cst"], writes=[spk])
                            rsqrt_to(S, cst, rs[:, tb * 512:(tb + 1) * 512], "rs", sp[:], spk)
                        qs = (128.0 ** -0.5) if cb < 8 else 1.0
                        S.op("dve", lambda e, o_=o_, qs=qs: e.scalar_tensor_tensor(
                            out=o_[:], in0=acc[:], scalar=qs, in1=rs[:], op0=ALU.mult, op1=ALU.mult),
                            reads=["acc", "rs"], writes=[ok])
                    else:
                        S.op("act", lambda e, o_=o_: e.activation(out=o_[:], in_=acc[:], func=AF.Silu), reads=["acc"], writes=[ok])
                    dst = XBCv[cb] if kind == "xbc" else QKVv[cb]
                    S.dma(dst, o_[:], reads=[ok], writes=[kind + "_T"])
                S.barrier()

            with ExitStack() as ps:
                lsb = lambda name, shape, dt=F32: ps.enter_context(nc.sbuf_tensor(name, shape, dt))
                wzs = [lsb("wzs%d" % i, [128, 8, 512]) for i in range(2)]
                wzb = [lsb("wzb%d" % i, [128, 8, 512], BF16) for i in range(2)]
                zb = [lsb("zb%d" % i, [128, 512], BF16) for i in range(3)]
                wss = lsb("wss", [128, 8, 64])
                wsb = lsb("wsb", [128, 8, 64], BF16)
                smt = [lsb("smt%d" % i, [128, 96]) for i in range(2)]
                t1 = [lsb("t1_%d" % i, [128, 48]) for i in range(2)]
                rows = lsb("rows2", [128, RLEN])
                arow = lsb("arow", [128, 48])
                S.dma(rows[:], rowv.partition_broadcast(128), writes=["rows"])
                S.op("act", lambda e: e.activation(out=arow[:, 0:32], in_=rows[:, R_AL1:R_AL1 + 32], func=AF.Exp),
                     reads=["rows"], writes=["arow"])
                S.op("act", lambda e: e.activation(out=arow[:, 32:48], in_=rows[:, R_AL2:R_AL2 + 16], func=AF.Exp),
                     reads=["rows"], writes=["arow"])
                S.op("dve", lambda e: e.tensor_scalar(out=arow[:], in0=arow[:], scalar1=-1.0, scalar2=None, op0=ALU.mult),
                     reads=["arow"], writes=["arow"])
                pps = [ps.enter_context(nc.psum_tensor("zps%d" % i, [128, 512], F32)) for i in range(4)]
                sps = [ps.enter_context(nc.psum_tensor("smps%d" % i, [128, 512], F32)) for i in range(2)]
                w_in_v = w_in.rearrange("(k p) f -> p k f", p=128)
                pcount = 0
                for blk in range(8):
                    off = (OFF_Z1 + blk * 512) if blk < 4 else (OFF_Z2 + (blk - 4) * 512)
                    ws = wzs[blk % 2]
                    wsk = "wzs%d" % (blk % 2)
                    wb = wzb[blk % 2]
                    wbk = "wzb%d" % (blk % 2)
                    if blk == 0:
                        S.dma(ws[:], w_in_v[:, :, off:off + 512], writes=[wsk])
                    if blk + 1 < 8:
                        noff = (OFF_Z1 + (blk + 1) * 512) if blk + 1 < 4 else (OFF_Z2 + (blk + 1 - 4) * 512)
                        S.dma(wzs[(blk + 1) % 2][:], w_in_v[:, :, noff:noff + 512], writes=["wzs%d" % ((blk + 1) % 2)])
                    S.op("pool", lambda e, ws=ws, wb=wb: e.tensor_copy(out=wb[:], in_=ws[:]), reads=[wsk], writes=[wbk])
                    for tt in range(NT):
                        pp = pps[pcount % 4]
                        ppk = "zps%d" % (pcount % 4)
                        z_ = zb[pcount % 3]
                        zk = "zb%d" % (pcount % 3)
                        pcount += 1
                        for k in range(8):
                            S.op("pe", lambda e, pp=pp, wb=wb, k=k, tt=tt: e.matmul(
                                pp[:], hT_cm[:, k, tt * 128:(tt + 1) * 128], wb[:, k, :], start=(k == 0), stop=(k == 7)),
                                reads=[wbk, hkeys[tt // 4]], writes=[ppk])
                        S.op("act", lambda e, pp=pp, z_=z_: e.activation(out=z_[:], in_=pp[:], func=AF.Silu), reads=[ppk], writes=[zk])
                        S.dma(Z[tt * 128:(tt + 1) * 128, blk * 512:(blk + 1) * 512], z_[:], reads=[zk], writes=["Z"])
                S.dma(wss[:, :, 0:32], w_in_v[:, :, OFF_DT:OFF_DT + 32], writes=["wss"])
                S.dma(wss[:, :, 32:64], w_in_v[:, :, OFF_B:OFF_B + 32], writes=["wss"])
                S.op("pool", lambda e: e.tensor_copy(out=wsb[:], in_=wss[:]), reads=["wss"], writes=["wsb"])
                for tt in range(NT):
                    sp = sps[tt % 2]
                    spk = "smps%d" % (tt % 2)
                    sm_ = smt[tt % 2]
                    smk = "smt%d" % (tt % 2)
                    t_ = t1[tt % 2]
                    tk = "t1_%d" % (tt % 2)
                    for k in range(8):
                        S.op("pe", lambda e, sp=sp, k=k, tt=tt: e.matmul(
                            sp[:, 0:64], hT_cm[:, k, tt * 128:(tt + 1) * 128], wsb[:, k, :], start=(k == 0), stop=(k == 7)),
                            reads=["wsb", hkeys[tt // 4]], writes=[spk])
                    S.op("dve", lambda e, sp=sp, t_=t_: e.tensor_tensor(out=t_[:, 0:32], in0=sp[:, 0:32],
                                                                        in1=rows[:, R_DTB1:R_DTB1 + 32], op=ALU.add),
                         reads=["rows"], writes=[spk, tk])
                    S.op("dve", lambda e, sp=sp, t_=t_: e.tensor_tensor(out=t_[:, 32:48], in0=sp[:, 48:64],
                                                                        in1=rows[:, R_DTB2:R_DTB2 + 16], op=ALU.add),
                         reads=["rows"], writes=[spk, tk])
                    S.op("act", lambda e, t_=t_: e.activation(out=t_[:], in_=t_[:], func=AF.Exp), reads=[tk], writes=[tk])
                    S.op("dve", lambda e, t_=t_: e.tensor_scalar(out=t_[:], in0=t_[:], scalar1=1.0, scalar2=None, op0=ALU.add),
                         reads=[tk], writes=[tk])
                    S.op("act", lambda e, t_=t_: e.activation(out=t_[:], in_=t_[:], func=AF.Ln), reads=[tk], writes=[tk])
                    S.op("act", lambda e, sp=sp, sm_=sm_: e.activation(out=sm_[:, 32:48], in_=sp[:, 32:48], func=AF.Sigmoid),
                         writes=[spk, smk])
                    S.op("dve", lambda e, t_=t_, sm_=sm_: e.tensor_copy(out=sm_[:, 0:32], in_=t_[:, 0:32]), reads=[tk], writes=[smk])
                    S.op("dve", lambda e, t_=t_, sm_=sm_: e.tensor_tensor(out=sm_[:, 48:96], in0=t_[:], in1=arow[:], op=ALU.mult),
                         reads=[tk, "arow"], writes=[smk])
                    S.dma(SM[tt * 128:(tt + 1) * 128, :], sm_[:], reads=[smk], writes=["SM"])
                S.barrier()

        hst.close()
        if 3 in phases:
            phase3(nc, S, st, T, XBC_T, QKV_T, Z, SM, Y, cst, cstb, rowv)
        if 4 in phases:
            phase4a(nc, S, T, Y, G_T, XT, X2T, w_su, w_gu, w_out, cst, cstb, pv)
        if 5 in phases:
            phase4b(nc, S, T, X2T, out, w_up, w_dn, cst, pv, normT)
        else:
            pass
        S.barrier()
        S.emit()
    return nc, S


def phase3(nc, S, st_outer, T, XBC_T, QKV_T, Z, SM, Y, cst, cstb, rowv):
    NT = T // 128
    ident = cst[:, C_ID, :]
    identb = cstb[:, C_ID, :]
    Umat = cst[:, C_U, :]
    GTm = cst[:, C_GT, :]
    SUm = cst[:, C_SU, :]
    ones = cst[:, C_ONE, :]
    with ExitStack() as ps:
        lsb = lambda name, shape, dt=F32: ps.enter_context(nc.sbuf_tensor(name, shape, dt))
        smt = [lsb("p3sm%d" % i, [128, 96]) for i in range(2)]
        rows = lsb("rows3", [128, RLEN])
        S.dma(rows[:], rowv.partition_broadcast(128), writes=["rows"])
        xbct = [lsb("p3xbc%d" % i, [128, 32, 128], BF16) for i in range(2)]
        qkvt = [lsb("p3qkv%d" % i, [128, 32, 128], BF16) for i in range(2)]
        zt = [lsb("p3z%d" % i, [128, 4096], BF16) for i in range(1)]
        ytile = lsb("p3y", [128, 4096], BF16)
        xs_tok = lsb("xs_tok", [128, 2048], BF16)
        b_tok = lsb("b_tok", [128, 1024], BF16)
        k_tok = lsb("k_tok", [128, 1024], BF16)
        v_tok = lsb("v_tok", [128, 2048], BF16)
        c_sb = lsb("c_sb", [128, 48])
        e_sb = lsb("e_sb", [128, 48])
        f_sb = lsb("f_sb", [128, 48])
        dA_sb = lsb("dA_sb", [128, 48])
        nbeta = lsb("nbeta", [128, 16])
        ST = lsb("ST", [128, 8, 256])
        STb = lsb("STb", [128, 8, 256], BF16)
        GS = lsb("GS", [128, 16, 128])
        GSb = lsb("GSb", [128, 16, 128], BF16)
        IB = [[lsb("ib%d_%d" % (s_, n_), [128, 4, 128]) for n_ in range(6)] + [lsb("ibm%d" % s_, [128, 5, 4, 128])] for s_ in range(2)]
        attnT = lsb("attnT", [128, 16, 128], BF16)
        T2T = lsb("T2T", [128, 16, 128], BF16)
        ke = lsb("ke", [128, 16, 128], BF16)
        kf = lsb("kf", [128, 16, 128], BF16)
        nwT = lsb("nwT", [128, 16, 128], BF16)
        KKm = lsb("KKm", [128, 8, 128])
        QKm = lsb("QKm", [128, 8, 128])
        Am = [lsb("Am%d" % i, [128, 4, 128]) for i in range(2)]
        DT = [lsb("DT%d" % i, [128, 4, 128]) for i in range(4)]
        MT = [lsb("MT%d" % i, [128, 4, 128], BF16) for i in range(2)]
        CBTm = [lsb("CBTm%d" % i, [128, 128]) for i in range(2)]
        xdt = [lsb("xdt%d" % i, [128, 256], BF16) for i in range(2)]
        xw = [lsb("xw%d" % i, [128, 256], BF16) for i in range(2)]
        xsD = [lsb("xsD%d" % i, [128, 256]) for i in range(2)]
        yacc = [lsb("yacc%d" % i, [128, 256]) for i in range(2)]
        yz = [lsb("yz%d" % i, [128, 256]) for i in range(2)]
        ssq = [lsb("ssq%d" % i, [128, 4]) for i in range(2)]
        vnew = [lsb("vnew%d" % i, [128, 4, 128], BF16) for i in range(2)]
        osb = [lsb("osb%d" % i, [128, 4, 128]) for i in range(1)] * 2
        on = [lsb("on%d" % i, [128, 4, 128]) for i in range(1)] * 2
        banks = [ps.enter_context(nc.psum_tensor("pb%d" % i, [128, 512], F32)) for i in range(8)]
        bctr = [0]

        def bank():
            i = bctr[0] % 8
            bctr[0] += 1
            return banks[i], "pb%d" % i
        rr = {"am": 0, "ama": 0, "g": 0, "v": 0}

        S.op("pool", lambda e: e.memset(ST[:], 0.0), writes=["ST"])
        S.op("pool", lambda e: e.memset(STb[:], 0.0), writes=["STb"])
        S.op("pool", lambda e: e.memset(GS[:], 0.0), writes=["GS"])
        S.op("pool", lambda e: e.memset(GSb[:], 0.0), writes=["GSb"])

        XBCv = XBC_T.rearrange("(b p) t -> p b t", p=128)
        QKVv = QKV_T.rearrange("(b p) t -> p b t", p=128)

        def loads(tt):
            i = tt % 2
            S.dma(smt[i][:], SM[tt * 128:(tt + 1) * 128, :], reads=["SM"], writes=["p3sm%d" % i])
            S.dma(xbct[i][:], XBCv[:, :, tt * 128:(tt + 1) * 128], reads=["xbc_T"], writes=["p3xbc%d" % i])
            S.dma(qkvt[i][:], QKVv[:, :, tt * 128:(tt + 1) * 128], reads=["qkv_T"], writes=["p3qkv%d" % i])

        def load_z(tt):
            S.dma(zt[0][:], Z[tt * 128:(tt + 1) * 128, :], reads=["Z"], writes=["p3z0"])

        def bc(ap2, n, w):
            return ap2.unsqueeze(2).to_broadcast([128, n, w])

        def v3(ap2, h=4):
            return ap2.rearrange("p (h w) -> p h w", h=h)

        loads(0)
        load_z(0)
        for tt in range(NT):
            i = tt % 2
            sm, smk = smt[i], "p3sm%d" % i
            xbc, xbk = xbct[i], "p3xbc%d" % i
            qkv, qkk = qkvt[i], "p3qkv%d" % i
            z, zk = zt[0], "p3z0"
            y, yk = ytile, "p3y"
            if tt + 1 < NT:
                loads(tt + 1)
            bD, bDk = bank()
            S.op("pe", lambda e, bD=bD, sm=sm: e.matmul(bD[:, 0:48], Umat, sm[:, 48:96], start=True, stop=True),
                 reads=[smk, "cst"], writes=[bDk])
            S.op("pe", lambda e, bD=bD, sm=sm: e.matmul(bD[:, 64:112], ones, sm[:, 48:96], start=True, stop=True),
                 reads=[smk, "cst"], writes=[bDk])
            S.op("act", lambda e, bD=bD: e.activation(out=c_sb[:], in_=bD[:, 0:48], func=AF.Identity), writes=[bDk, "c_sb"])
            S.op("act", lambda e, bD=bD: e.activation(out=e_sb[:], in_=bD[:, 0:48], func=AF.Exp), writes=[bDk, "e_sb"])
            S.op("act", lambda e, bD=bD: e.activation(out=dA_sb[:], in_=bD[:, 64:112], func=AF.Exp), writes=[bDk, "dA_sb"])
            S.op("act", lambda e, bD=bD: e.activation(out=f_sb[:], in_=bD[:, 64:112], func=AF.Identity), writes=[bDk, "f_sb"])
            S.op("dve", lambda e: e.tensor_tensor(out=f_sb[:], in0=f_sb[:], in1=c_sb[:], op=ALU.subtract),
                 reads=["c_sb", "f_sb"], writes=["f_sb"])
            S.op("act", lambda e: e.activation(out=f_sb[:], in_=f_sb[:], func=AF.Exp), reads=["f_sb"], writes=["f_sb"])
            S.op("pool", lambda e, sm=sm: e.tensor_scalar(out=nbeta[:], in0=sm[:, 32:48], scalar1=-1.0, scalar2=None, op0=ALU.mult),
                 reads=[smk], writes=["nbeta"])
            jobs = [(xbc, xbk, 0, xs_tok, "xs_tok", 2), (xbc, xbk, 16, b_tok, "b_tok", 1),
                    (qkv, qkk, 8, k_tok, "k_tok", 1), (qkv, qkk, 16, v_tok, "v_tok", 2)]
            nev = 0
            for (src, srck, b0, dst, dstk, nq) in jobs:
                for q8 in range(nq):
                    bk_, bkk = bank()
                    bb = bk_[:].bitcast(BF16)
                    for a in range(8):
                        blk = b0 + q8 * 8 + a
                        S.op("pe", lambda e, bb=bb, a=a, src=src, blk=blk: e.transpose(
                            bb[:, a * 128:(a + 1) * 128], src[:, blk, :], identb), reads=[srck, "cstb"], writes=[bkk])
                    if nev % 2 == 0:
                        S.op("act", lambda e, bb=bb, dst=dst, q8=q8: e.activation(
                            out=dst[:, q8 * 1024:(q8 + 1) * 1024], in_=bb, func=AF.Identity), writes=[bkk, dstk])
                    else:
                        S.op("dve", lambda e, bb=bb, dst=dst, q8=q8: e.tensor_copy(
                            out=dst[:, q8 * 1024:(q8 + 1) * 1024], in_=bb), writes=[bkk, dstk])
                    nev += 1

            for half in range(2):
                bk_, bkk = bank()
                for j in range(4):
                    hq = half * 4 + j
                    S.op("pe", lambda e, bk_=bk_, j=j, hq=hq, qkv=qkv: e.matmul(
                        bk_[:, j * 128:(j + 1) * 128], qkv[:, 8 + hq, :], qkv[:, 8 + hq, :], start=True, stop=True),
                        reads=[qkk], writes=[bkk])
                S.op("dve", lambda e, bk_=bk_, half=half: e.tensor_tensor(
                    out=KKm[:, half * 4:(half + 1) * 4, :], in0=v3(bk_[:]), in1=SUm.unsqueeze(1).to_broadcast([128, 4, 128]), op=ALU.mult),
                    reads=["cst"], writes=[bkk, "KKm"])
                bq_, bqk = bank()
                for j in range(4):
                    hq = half * 4 + j
                    S.op("pe", lambda e, bq_=bq_, j=j, hq=hq, qkv=qkv: e.matmul(
                        bq_[:, j * 128:(j + 1) * 128], qkv[:, 8 + hq, :], qkv[:, hq, :], start=True, stop=True),
                        reads=[qkk], writes=[bqk])
                S.op("dve", lambda e, bq_=bq_, half=half: e.tensor_tensor(
                    out=QKm[:, half * 4:(half + 1) * 4, :], in0=v3(bq_[:]), in1=Umat.unsqueeze(1).to_broadcast([128, 4, 128]), op=ALU.mult),
                    reads=["cst"], writes=[bqk, "QKm"])
            def fl(t):
                return t[:].rearrange("p h w -> p (h w)")

            def bcm(plane):
                return cst[:, plane, :].unsqueeze(1).to_broadcast([128, 4, 128])

            def build_DT(cols, r, sm, smk):
                ra = rr["ama"] % 2
                rr["ama"] += 1
                S.op("dve", lambda e: e.tensor_tensor(out=Am[ra][:], in0=bcm(C_GT), in1=bc(sm[:, cols:cols + 4], 4, 128), op=ALU.mult),
                     reads=[smk, "cst"], writes=["Am%d" % ra])
                sg, sgk = bank()
                for j in range(4):
                    S.op("pe", lambda e, sg=sg, j=j: e.matmul(sg[:, j * 128:(j + 1) * 128], Am[ra][:, j, :], Umat, start=True, stop=True),
                         reads=["Am%d" % ra, "cst"], writes=[sgk])
                S.op("act", lambda e, sg=sg: e.activation(out=fl(DT[r]), in_=sg[:], func=AF.Exp), writes=[sgk, "DT%d" % r])

            def gdn_quad(q, s_, sm=sm, smk=smk, qkv=qkv, qkk=qkk, z=z, zk=zk, y=y, yk=yk):
                X, XT, Dv, DvT, E, F, XM = IB[s_]
                kX, kXT, kDv, kDvT, kE, kF, kXM = [("ib", s_, n_) for n_ in range(7)]
                qs = slice(q * 4, (q + 1) * 4)
                r = rr["am"] % 4
                rr["am"] += 1
                build_DT(80 + q * 4, r, sm, smk)
                for j in range(4):
                    hv = q * 4 + j
                    hq = hv // 2
                    S.op("dve", lambda e, j=j, hv=hv, hq=hq: e.scalar_tensor_tensor(
                        out=X[:, j, :], in0=KKm[:, hq, :], scalar=nbeta[:, hv:hv + 1], in1=DT[r][:, j, :], op0=ALU.mult, op1=ALU.mult),
                        reads=["KKm", "nbeta", "DT%d" % r], writes=[kX])
                S.op("dve", lambda e: e.tensor_tensor(
                    out=attnT[:, qs, :].rearrange("p (a b) w -> p a b w", a=2),
                    in0=QKm[:, 2 * q:2 * q + 2, :].unsqueeze(2).to_broadcast([128, 2, 2, 128]),
                    in1=DT[r][:].rearrange("p (a b) w -> p a b w", a=2), op=ALU.mult),
                    reads=["QKm", "DT%d" % r], writes=[("attnT", q)])
                yield

                def mm4(lhs, lk, rhs, rk):
                    b_, bk = bank()
                    for j in range(4):
                        S.op("pe", lambda e, b_=b_, j=j: e.matmul(b_[:, j * 128:(j + 1) * 128], lhs[:, j, :], rhs[:, j, :], start=True, stop=True),
                             reads=[lk, rk], writes=[bk])
                    return b_, bk

                def tr4(src, sk):
                    b_, bk = bank()
                    for j in range(4):
                        S.op("pe", lambda e, b_=b_, j=j: e.transpose(b_[:, j * 128:(j + 1) * 128], src[:, j, :], ident),
                             reads=[sk, "cst"], writes=[bk])
                    return b_, bk

                def ev_act(b_, bk, dst, dk):
                    S.op("act", lambda e: e.activation(out=fl(dst), in_=b_[:], func=AF.Identity), writes=[bk, dk])

                def ev_dve(b_, bk, dst, dk):
                    S.op("dve", lambda e: e.tensor_copy(out=fl(dst), in_=b_[:]), writes=[bk, dk])

                def acc_dve(b_, bk, dst, dk, out=None, ok=None):
                    o_ = fl(dst) if out is None else out
                    S.op("dve", lambda e: e.tensor_tensor(out=o_, in0=fl(dst), in1=b_[:], op=ALU.add),
                         reads=[dk], writes=[bk, dk if ok is None else ok])

                b_, bk = tr4(X, kX)
                ev_act(b_, bk, XT, kXT)
                S.op("dve", lambda e: e.tensor_tensor(out=E[:], in0=X[:], in1=bcm(C_BD8), op=ALU.mult), reads=[kX, "cst"], writes=[kE])
                S.op("dve", lambda e: e.tensor_tensor(out=Dv[:], in0=E[:], in1=bcm(C_ID), op=ALU.add), reads=[kE, "cst"], writes=[kDv])
                yield
                S.op("dve", lambda e: e.tensor_tensor(
                    out=XM[:], in0=XT[:].unsqueeze(1).to_broadcast([128, 5, 4, 128]),
                    in1=cst[:, C_BD8:C_BD8 + 5, :].unsqueeze(2).to_broadcast([128, 5, 4, 128]), op=ALU.mult),
                    reads=[kXT, "cst"], writes=[kXM])
                yield
                X0T = XM[:, 0, :, :]
                S.op("dve", lambda e: e.tensor_tensor(out=DvT[:], in0=X0T, in1=bcm(C_ID), op=ALU.add), reads=[kXM, "cst"], writes=[kDvT])
                b1, b1k = mm4(X0T, kXM, E, kE)
                b2, b2k = mm4(E, kE, X0T, kXM)
                ev_dve(b1, b1k, F, kF)
                ev_act(b2, b2k, X, kX)
                yield
                b1, b1k = mm4(X, kX, Dv, kDv)
                b2, b2k = mm4(Dv, kDv, X, kX)
                acc_dve(b1, b1k, Dv, kDv)
                acc_dve(b2, b2k, DvT, kDvT)
                b3_, b3k_ = mm4(F, kF, X, kX)
                ev_act(b3_, b3k_, E, kE)
                yield
                b1, b1k = mm4(E, kE, Dv, kDv)
                b2, b2k = mm4(Dv, kDv, E, kE)
                acc_dve(b1, b1k, Dv, kDv)
                acc_dve(b2, b2k, DvT, kDvT)
                yield
                for n, b in enumerate((8, 16, 32, 64)):
                    b_, bk = mm4(XM[:, 1 + n, :, :], kXM, Dv, kDv)
                    ev_act(b_, bk, E, kE)
                    yield
                    b1, b1k = mm4(DvT, kDvT, E, kE)
                    if b != 64:
                        b2, b2k = mm4(E, kE, DvT, kDvT)
                        acc_dve(b1, b1k, Dv, kDv)
                        acc_dve(b2, b2k, DvT, kDvT)
                        yield
                    else:
                        acc_dve(b1, b1k, Dv, kDv, out=T2T[:, qs, :].rearrange("p h w -> p (h w)"), ok=("T2T", q))
                        yield
                kq = k_tok[:, 2 * q * 128:(2 * q + 2) * 128].rearrange("p (a w) -> p a w", a=2).unsqueeze(2).to_broadcast([128, 2, 2, 128])
                S.op("pool", lambda e: e.tensor_tensor(
                    out=ke[:, qs, :].rearrange("p (a b) w -> p a b w", a=2), in0=kq,
                    in1=e_sb[:, 32 + q * 4:36 + q * 4].rearrange("p (a b) -> p a b", a=2).unsqueeze(3).to_broadcast([128, 2, 2, 128]),
                    op=ALU.mult), reads=["k_tok", "e_sb"], writes=[("ke", q)])
                S.op("pool", lambda e: e.tensor_tensor(
                    out=kf[:, qs, :].rearrange("p (a b) w -> p a b w", a=2), in0=kq,
                    in1=f_sb[:, 32 + q * 4:36 + q * 4].rearrange("p (a b) -> p a b", a=2).unsqueeze(3).to_broadcast([128, 2, 2, 128]),
                    op=ALU.mult), reads=["k_tok", "f_sb"], writes=[("kf", q)])
                wp, wpk = bank()
                for j in range(4):
                    hv = q * 4 + j
                    S.op("pe", lambda e, wp=wp, j=j, hv=hv: e.matmul(wp[:, j * 128:(j + 1) * 128], ke[:, hv, :], T2T[:, hv, :],
                                                                     start=True, stop=True),
                         reads=[("ke", q), ("T2T", q)], writes=[wpk])
                S.op("act", lambda e, wp=wp: e.activation(out=nwT[:, qs, :].rearrange("p h w -> p (h w)"), in_=wp[:],
                                                        func=AF.Identity, scale=-1.0), writes=[wpk, ("nwT", q)])
                yield
                qs = slice(q * 4, (q + 1) * 4)
                vi = rr["v"] % 2
                rr["v"] += 1
                vp, vpk = bank()
                for j in range(4):
                    hv = q * 4 + j
                    S.op("pe", lambda e, vp=vp, j=j, hv=hv: e.matmul(vp[:, j * 128:(j + 1) * 128], T2T[:, hv, :],
                                                                     v_tok[:, hv * 128:(hv + 1) * 128], start=True, stop=False),
                         reads=[("T2T", q), "v_tok"], writes=[vpk])
                    S.op("pe", lambda e, vp=vp, j=j, hv=hv: e.matmul(vp[:, j * 128:(j + 1) * 128], nwT[:, hv, :], GSb[:, hv, :],
                                                                     start=False, stop=True),
                         reads=[("nwT", q), ("GSb", q)], writes=[vpk])
                for j in range(4):
                    hv = q * 4 + j
                    S.op("act", lambda e, vp=vp, j=j, hv=hv, vi=vi: e.activation(
                        out=vnew[vi][:, j, :], in_=vp[:, j * 128:(j + 1) * 128], func=AF.Identity, scale=sm[:, 32 + hv:33 + hv]),
                        reads=[smk], writes=[vpk, "vnew%d" % vi])
                yield
                oi, oik = bank()
                for j in range(4):
                    hv = q * 4 + j
                    hq = hv // 2
                    S.op("pe", lambda e, oi=oi, j=j, hv=hv, hq=hq: e.matmul(oi[:, j * 128:(j + 1) * 128], qkv[:, hq, :], GSb[:, hv, :],
                                                                               start=True, stop=True),
                         reads=[qkk, ("GSb", q)], writes=[oik])
                oa, oak = bank()
                for j in range(4):
                    hv = q * 4 + j
                    S.op("pe", lambda e, oa=oa, j=j, hv=hv, vi=vi: e.matmul(oa[:, j * 128:(j + 1) * 128], attnT[:, hv, :], vnew[vi][:, j, :],
                                                                           start=True, stop=True),
                         reads=[("attnT", q), "vnew%d" % vi], writes=[oak])
                sn, snk = bank()
                for j in range(4):
                    hv = q * 4 + j
                    S.op("pe", lambda e, sn=sn, j=j, hv=hv, vi=vi: e.matmul(sn[:, j * 128:(j + 1) * 128], kf[:, hv, :], vnew[vi][:, j, :],
                                                                           start=True, stop=True),
                         reads=[("kf", q), "vnew%d" % vi], writes=[snk])
                for j in range(4):
                    hv = q * 4 + j
                    S.op("act", lambda e, oi=oi, j=j, hv=hv, vi=vi: e.activation(
                        out=osb[vi][:, j, :], in_=oi[:, j * 128:(j + 1) * 128], func=AF.Identity, scale=e_sb[:, 32 + hv:33 + hv]),
                        reads=["e_sb"], writes=[oik, "osb0"])
                S.op("dve", lambda e, oa=oa, vi=vi: e.tensor_tensor(
                    out=osb[vi][:].rearrange("p h w -> p (h w)"), in0=osb[vi][:].rearrange("p h w -> p (h w)"), in1=oa[:], op=ALU.add),
                    reads=["osb0"], writes=[oak, "osb0"])
                for j in range(4):
                    S.op("act", lambda e, j=j, vi=vi: e.activation(out=on[vi][:, j, :], in_=osb[vi][:, j, :], func=AF.Square,
                                                                 accum_out=ssq[vi][:, j:j + 1]),
                         reads=["osb0"], writes=["on0", "ssq%d" % vi])
                rsqrt_to(S, cst, ssq[vi][:], "ssq%d" % vi, ssq[vi][:], "ssq%d" % vi, 1.0 / 128)
                for j in range(4):
                    hv = q * 4 + j
                    S.op("dve", lambda e, j=j, vi=vi: e.scalar_tensor_tensor(
                        out=on[vi][:, j, :], in0=osb[vi][:, j, :], scalar=ssq[vi][:, j:j + 1], in1=rows[:, R_NW2:R_NW2 + 128],
                        op0=ALU.mult, op1=ALU.mult), reads=["osb0", "ssq%d" % vi, "rows"], writes=["on0"])
                S.op("pool", lambda e, vi=vi: e.tensor_tensor(
                    out=y[:, 2048 + q * 512:2048 + (q + 1) * 512], in0=on[vi][:].rearrange("p h w -> p (h w)"),
                    in1=z[:, 2048 + q * 512:2048 + (q + 1) * 512], op=ALU.mult), reads=["on0", zk], writes=[yk])
                for j in range(4):
                    hv = q * 4 + j
                    S.op("dve", lambda e, sn=sn, j=j, hv=hv: e.scalar_tensor_tensor(
                        out=GS[:, hv, :], in0=GS[:, hv, :], scalar=dA_sb[:, 32 + hv:33 + hv], in1=sn[:, j * 128:(j + 1) * 128],
                        op0=ALU.mult, op1=ALU.add), reads=[("GS", q), "dA_sb"], writes=[snk, ("GS", q)])
                S.op("act", lambda e, qs=qs: e.activation(out=GSb[:, qs, :], in_=GS[:, qs, :], func=AF.Identity),
                     reads=[("GS", q)], writes=[("GSb", q)])
                yield

            def ssd_group(g, sm=sm, smk=smk, xbc=xbc, xbk=xbk, z=z, zk=zk, y=y, yk=yk):
                gi = rr["g"] % 2
                rr["g"] += 1
                r = rr["am"] % 4
                rr["am"] += 1
                b2, b2k = bank()
                S.op("pe", lambda e: e.matmul(b2[:, 0:128], xbc[:, 16 + g, :], xbc[:, 24 + g, :], start=True, stop=True),
                     reads=[xbk], writes=[b2k])
                S.op("dve", lambda e: e.tensor_tensor(out=CBTm[gi][:], in0=b2[:, 0:128], in1=Umat, op=ALU.mult),
                     reads=["cst"], writes=[b2k, "CBTm%d" % gi])
                xs3 = v3(xs_tok[:, g * 256:(g + 1) * 256])
                S.op("dve", lambda e: e.tensor_tensor(
                    out=v3(xdt[gi][:]), in0=xs3, in1=bc(sm[:, g * 4:(g + 1) * 4], 4, 64), op=ALU.mult),
                    reads=["xs_tok", smk], writes=["xdt%d" % gi])
                S.op("dve", lambda e: e.tensor_tensor(
                    out=v3(xw[gi][:]), in0=v3(xdt[gi][:]), in1=bc(f_sb[:, g * 4:(g + 1) * 4], 4, 64), op=ALU.mult),
                    reads=["xdt%d" % gi, "f_sb"], writes=["xw%d" % gi])
                S.op("pool", lambda e: e.tensor_tensor(
                    out=v3(xsD[gi][:]), in0=xs3, in1=bc(rows[:, R_D1 + g * 4:R_D1 + (g + 1) * 4], 4, 64), op=ALU.mult),
                    reads=["xs_tok", "rows"], writes=["xsD%d" % gi])
                build_DT(48 + g * 4, r, sm, smk)
                yield
                S.op("dve", lambda e: e.tensor_tensor(out=MT[gi][:], in0=DT[r][:],
                                                      in1=CBTm[gi][:].unsqueeze(1).to_broadcast([128, 4, 128]), op=ALU.mult),
                     reads=["CBTm%d" % gi, "DT%d" % r], writes=["MT%d" % gi])
                b3, b3k = bank()
                for h in range(4):
                    S.op("pe", lambda e, h=h: e.matmul(b3[:, h * 64:(h + 1) * 64], MT[gi][:, h, :],
                                                       xdt[gi][:, h * 64:(h + 1) * 64], start=True, stop=True),
                         reads=["MT%d" % gi, "xdt%d" % gi], writes=[b3k])
                S.op("pe", lambda e: e.matmul(b3[:, 256:512], xbc[:, 24 + g, :], STb[:, g, :], start=True, stop=True),
                     reads=[xbk, ("STb", g)], writes=[b3k])
                b4, b4k = bank()
                S.op("pe", lambda e: e.matmul(b4[:, 256:512], b_tok[:, g * 128:(g + 1) * 128], xw[gi][:], start=True, stop=True),
                     reads=["b_tok", "xw%d" % gi], writes=[b4k])
                S.op("dve", lambda e: e.tensor_tensor(
                    out=v3(yacc[gi][:]), in0=v3(b3[:, 256:512]), in1=bc(e_sb[:, g * 4:(g + 1) * 4], 4, 64), op=ALU.mult),
                    reads=["e_sb"], writes=[b3k, "yacc%d" % gi])
                S.op("dve", lambda e: e.tensor_tensor(out=yacc[gi][:], in0=yacc[gi][:], in1=b3[:, 0:256], op=ALU.add),
                     reads=["yacc%d" % gi], writes=[b3k, "yacc%d" % gi])
                S.op("pool", lambda e: e.tensor_tensor(out=yacc[gi][:], in0=yacc[gi][:], in1=xsD[gi][:], op=ALU.add),
                     reads=["yacc%d" % gi, "xsD%d" % gi], writes=["yacc%d" % gi])
                S.op("dve", lambda e: e.tensor_tensor(out=yz[gi][:], in0=yacc[gi][:], in1=z[:, g * 256:(g + 1) * 256], op=ALU.mult),
                     reads=["yacc%d" % gi, zk], writes=["yz%d" % gi])
                S.op("act", lambda e: e.activation(out=xsD[gi][:], in_=yz[gi][:], func=AF.Square, accum_out=ssq[gi][:, 0:1]),
                     reads=["yz%d" % gi], writes=["xsD%d" % gi, "ssq%d" % gi])
                rsqrt_to(S, cst, ssq[gi][:, 0:1], "ssq%d" % gi, ssq[gi][:, 0:1], "ssq%d" % gi, 1.0 / 256)
                S.op("dve", lambda e: e.scalar_tensor_tensor(
                    out=y[:, g * 256:(g + 1) * 256], in0=yz[gi][:], scalar=ssq[gi][:, 0:1],
                    in1=rows[:, R_NW1 + g * 256:R_NW1 + (g + 1) * 256], op0=ALU.mult, op1=ALU.mult),
                    reads=["yz%d" % gi, "ssq%d" % gi, "rows"], writes=[yk])
                S.op("pool", lambda e: e.tensor_tensor(
                    out=v3(ST[:, g, :]), in0=v3(ST[:, g, :]), in1=bc(dA_sb[:, g * 4:(g + 1) * 4], 4, 64), op=ALU.mult),
                    reads=[("ST", g), "dA_sb"], writes=[("ST", g)])
                S.op("dve", lambda e: e.tensor_tensor(out=ST[:, g, :], in0=ST[:, g, :], in1=b4[:, 256:512], op=ALU.add),
                     reads=[("ST", g)], writes=[b4k, ("ST", g)])
                S.op("act", lambda e: e.activation(out=STb[:, g, :], in_=ST[:, g, :], func=AF.Identity),
                     reads=[("ST", g)], writes=[("STb", g)])
                yield

            pend_g = [gdn_quad(q, q % 2) for q in range(4)]
            pend_s = [ssd_group(g) for g in range(8)]
            live = []
            slots = {"g": 0, "s": 0}

            def refill():
                while slots["g"] < 2 and pend_g:
                    live.append(("g", pend_g.pop(0)))
                    slots["g"] += 1
                while slots["s"] < 2 and pend_s:
                    live.append(("s", pend_s.pop(0)))
                    slots["s"] += 1
            refill()
            while live:
                for item in list(live):
                    kind, g_ = item
                    try:
                        next(g_)
                    except StopIteration:
                        live.remove(item)
                        slots[kind] -= 1
                refill()

            S.dma(Y[tt * 128:(tt + 1) * 128, :], y[:], reads=[yk], writes=["Y"])
            if tt + 1 < NT:
                load_z(tt + 1)
        S.barrier()


def load_weight_bf16(nc, S, dst, dstk, src_v, nk, ncols, stg, stgk):
    cnt = 0
    kc = 2
    for k0 in range(0, nk, kc):
        for c0 in range(0, ncols, 512):
            s_ = stg[cnt % 2]
            sk = stgk[cnt % 2]
            cnt += 1
            S.dma(s_[:], src_v[:, k0:k0 + kc, c0:c0 + 512], writes=[sk])
            eng = "pool" if cnt % 2 == 0 else "dve"
            S.op(eng, lambda e, s_=s_, k0=k0, c0=c0: e.tensor_copy(out=dst[:, k0:k0 + kc, c0:c0 + 512], in_=s_[:]),
                 reads=[sk], writes=[dstk])


def phase4a(nc, S, T, Y, G_T, XT, X2T, w_su, w_gu, w_out, cst, cstb, pv):
    BW = 256
    NBW = T // BW
    NA = BW // 128
    identb = cstb[:, C_ID, :]
    onesD = cst[:, C_ONED, :]
    with ExitStack() as ps:
        lsb = lambda name, shape, dt=F32: ps.enter_context(nc.sbuf_tensor(name, shape, dt))
        Wsu = lsb("Wsu", [128, 16, D], BF16)
        Wgu = lsb("Wgu", [128, 16, D], BF16)
        Wo = lsb("Wo", [128, 8, D], BF16)
        stg = [lsb("stg%d" % i, [128, 2, 512]) for i in range(2)]
        ytoks = [lsb("ytok%d" % i, [128, NA, 4096], BF16) for i in range(2)]
        yT = lsb("yT", [128, 32, BW], BF16)
        gTs = [lsb("gT%d" % i, [128, 16, BW], BF16) for i in range(2)]
        xTs = [lsb("xT4_%d" % i, [128, 8, BW]) for i in range(2)]
        t1 = [lsb("m_t1_%d" % i, [128, BW]) for i in range(2)]
        t2 = [lsb("m_t2_%d" % i, [128, BW]) for i in range(2)]
        mT = lsb("mT", [128, 8, BW], BF16)
        m2s = lsb("m2s", [128, 8, BW])
        sqm = lsb("sqm", [128, 8, BW])
        rstd = lsb("rstd4", [128, BW])
        pu = [ps.enter_context(nc.psum_tensor("pu%d" % i, [128, 512], F32)) for i in range(4)]
        pt = [ps.enter_context(nc.psum_tensor("ptb%d" % i, [128, 1024], BF16)) for i in range(2)]
        pss = ps.enter_context(nc.psum_tensor("pss4", [128, 512], F32))
        load_weight_bf16(nc, S, Wsu, "Wsu", w_su.rearrange("(k p) f -> p k f", p=128), 16, D, stg, ["stg0", "stg1"])
        load_weight_bf16(nc, S, Wgu, "Wgu", w_gu.rearrange("(k p) f -> p k f", p=128), 16, D, stg, ["stg0", "stg1"])
        load_weight_bf16(nc, S, Wo, "Wo", w_out.rearrange("(k p) f -> p k f", p=128), 8, D, stg, ["stg0", "stg1"])
        GTv = G_T.rearrange("(b p) t -> p b t", p=128)
        XTv = XT.rearrange("(k p) t -> p k t", p=128)
        X2Tv = X2T.rearrange("(k p) t -> p k t", p=128)
        pc = 0
        def loads4(nb):
            t0 = nb * BW
            i = nb % 2
            S.dma(ytoks[i][:], Y[t0:t0 + BW, :].rearrange("(a p) f -> p a f", p=128), reads=["Y"], writes=["ytok%d" % i])
            S.dma(gTs[i][:], GTv[:, :, t0:t0 + BW], reads=["G_T"], writes=["gT%d" % i])
            S.dma(xTs[i][:], XTv[:, :, t0:t0 + BW], reads=["XT"], writes=["xT4_%d" % i])
        loads4(0)
        for nb in range(NBW):
            t0 = nb * BW
            ytok, gT, xT = ytoks[nb % 2], gTs[nb % 2], xTs[nb % 2]
            ytk, gTk, xTk = "ytok%d" % (nb % 2), "gT%d" % (nb % 2), "xT4_%d" % (nb % 2)
            if nb + 1 < NBW:
                loads4(nb + 1)
            for cb in range(32):
                p_ = pt[cb % 2]
                pk = "ptb%d" % (cb % 2)
                for a in range(NA):
                    S.op("pe", lambda e, p_=p_, a=a, cb=cb, ytok=ytok: e.transpose(p_[:, a * 128:(a + 1) * 128],
                                                                      ytok[:, a, cb * 128:(cb + 1) * 128], identb),
                         reads=[ytk, "cstb"], writes=[pk])
                if cb % 2 == 0:
                    S.op("act", lambda e, p_=p_, cb=cb: e.activation(out=yT[:, cb, :], in_=p_[:, 0:BW], func=AF.Identity),
                         reads=[pk], writes=["yT"])
                else:
                    S.op("dve", lambda e, p_=p_, cb=cb: e.tensor_copy(out=yT[:, cb, :], in_=p_[:, 0:BW]), reads=[pk], writes=["yT"])
            for blk in range(8):
                p1 = pu[pc % 4]
                p1k = "pu%d" % (pc % 4)
                pc += 1
                p2 = pu[pc % 4]
                p2k = "pu%d" % (pc % 4)
                pc += 1
                for k in range(16):
                    S.op("pe", lambda e, p1=p1, k=k, blk=blk: e.matmul(p1[:, 0:BW], Wsu[:, k, blk * 128:(blk + 1) * 128], yT[:, k, :],
                                                                       start=(k == 0), stop=(k == 15)),
                         reads=["Wsu", "yT"], writes=[p1k])
                for k in range(16):
                    S.op("pe", lambda e, p2=p2, k=k, blk=blk: e.matmul(p2[:, 0:BW], Wgu[:, k, blk * 128:(blk + 1) * 128], yT[:, 16 + k, :],
                                                                       start=(k == 0), stop=(k == 15)),
                         reads=["Wgu", "yT"], writes=[p2k])
                a1 = t1[blk % 2]
                a1k = "m_t1_%d" % (blk % 2)
                a2 = t2[blk % 2]
                a2k = "m_t2_%d" % (blk % 2)
                S.op("dve", lambda e, p1=p1, a1=a1, blk=blk, gT=gT: e.tensor_tensor(out=a1[:], in0=p1[:, 0:BW], in1=gT[:, blk, :], op=ALU.mult),
                     reads=[p1k, gTk], writes=[a1k])
                S.op("dve", lambda e, p2=p2, a2=a2, blk=blk, gT=gT: e.tensor_tensor(out=a2[:], in0=p2[:, 0:BW], in1=gT[:, 8 + blk, :], op=ALU.mult),
                     reads=[p2k, gTk], writes=[a2k])
                S.op("pool", lambda e, a1=a1, a2=a2, blk=blk: e.tensor_tensor(out=mT[:, blk, :], in0=a1[:], in1=a2[:], op=ALU.add),
                     reads=[a1k, a2k], writes=["mT"])
            for blk in range(8):
                p1 = pu[pc % 4]
                p1k = "pu%d" % (pc % 4)
                pc += 1
                for k in range(8):
                    S.op("pe", lambda e, p1=p1, k=k, blk=blk: e.matmul(p1[:, 0:BW], Wo[:, k, blk * 128:(blk + 1) * 128], mT[:, k, :],
                                                                       start=(k == 0), stop=(k == 7)),
                         reads=["Wo", "mT"], writes=[p1k])
                S.op("act", lambda e, p1=p1, blk=blk: e.activation(out=m2s[:, blk, :], in_=p1[:, 0:BW], func=AF.Identity),
                     reads=[p1k], writes=["m2s"])
                S.op("act", lambda e, p1=p1, blk=blk: e.activation(out=sqm[:, blk, :], in_=p1[:, 0:BW], func=AF.Square),
                     reads=[p1k], writes=["sqm"])
            for k in range(8):
                S.op("pe", lambda e, k=k: e.matmul(pss[:, 0:BW], onesD, sqm[:, k, :], start=(k == 0), stop=(k == 7)),
                     reads=["sqm", "cst"], writes=["pss4"])
            rsqrt_to(S, cst, rstd[:], "rstd4", pss[:, 0:BW], "pss4")
            for blk in range(8):
                a1 = t1[blk % 2]
                a1k = "m_t1_%d" % (blk % 2)
                S.op("pool", lambda e, a1=a1, blk=blk: e.tensor_tensor(out=a1[:], in0=m2s[:, blk, :], in1=rstd[:], op=ALU.mult),
                     reads=["m2s", "rstd4"], writes=[a1k])
                S.op("dve", lambda e, a1=a1, blk=blk, xT=xT: e.scalar_tensor_tensor(
                    out=xT[:, blk, :], in0=a1[:], scalar=pv[:, 2, blk:blk + 1], in1=xT[:, blk, :], op0=ALU.mult, op1=ALU.add),
                    reads=[a1k, "pv", xTk], writes=[xTk])
            S.dma(X2Tv[:, :, t0:t0 + BW], xT[:], reads=[xTk], writes=["X2T"])
        S.barrier()


def phase4b(nc, S, T, X2T, out, w_up, w_dn, cst, pv, normT):
    BW = 256
    NBW = T // BW
    NA = BW // 128
    ident = cst[:, C_ID, :]
    onesD = cst[:, C_ONED, :]
    with ExitStack() as ps:
        lsb = lambda name, shape, dt=F32: ps.enter_context(nc.sbuf_tensor(name, shape, dt))
        Wup = lsb("Wup", [128, 8, 4096], BF16)
        Wdn = lsb("Wdn", [128, 32, D], BF16)
        stg = [lsb("stgb%d" % i, [128, 2, 512]) for i in range(2)]
        xTs = [lsb("x5T%d" % i, [128, 8, BW]) for i in range(2)]
        sq = lsb("sq5", [128, 8, BW])
        rstd = lsb("rstd5", [128, BW])
        tmp = [lsb("tmp5_%d" % i, [128, BW]) for i in range(2)]
        h2T = lsb("h2T", [128, 8, BW], BF16)
        rl = [lsb("rl%d" % i, [128, BW]) for i in range(2)]
        actT = lsb("actT", [128, 32, BW], BF16)
        dns = lsb("dns", [128, 8, BW])
        otok = [lsb("otok%d" % i, [128, 512]) for i in range(2)]
        pu = [ps.enter_context(nc.psum_tensor("p5u%d" % i, [128, 512], F32)) for i in range(4)]
        pss = ps.enter_context(nc.psum_tensor("pss5", [128, 512], F32))
        po = [ps.enter_context(nc.psum_tensor("p5o%d" % i, [128, 512], F32)) for i in range(2)]
        load_weight_bf16(nc, S, Wup, "Wup", w_up.rearrange("(k p) f -> p k f", p=128), 8, 4096, stg, ["stgb0", "stgb1"])
        load_weight_bf16(nc, S, Wdn, "Wdn", w_dn.rearrange("(k p) f -> p k f", p=128), 32, D, stg, ["stgb0", "stgb1"])
        X2Tv = X2T.rearrange("(k p) t -> p k t", p=128)
        pc = 0
        oc = 0
        S.dma(xTs[0][:], X2Tv[:, :, 0:BW], reads=["X2T"], writes=["x5T0"])
        for nb in range(NBW):
            t0 = nb * BW
            xT = xTs[nb % 2]
            xk = "x5T%d" % (nb % 2)
            if nb + 1 < NBW:
                S.dma(xTs[(nb + 1) % 2][:], X2Tv[:, :, t0 + BW:t0 + 2 * BW], reads=["X2T"], writes=["x5T%d" % ((nb + 1) % 2)])
            normT(xT, xk, sq, "sq5", rstd, "rstd5", pss, "pss5", tmp, ["tmp5_0", "tmp5_1"],
                  lambda k: h2T[:, k, :], "h2T", 3, 4, BW)
            for blk in range(32):
                p1 = pu[pc % 4]
                p1k = "p5u%d" % (pc % 4)
                pc += 1
                for k in range(8):
                    S.op("pe", lambda e, p1=p1, k=k, blk=blk: e.matmul(p1[:, 0:BW], Wup[:, k, blk * 128:(blk + 1) * 128], h2T[:, k, :],
                                                                       start=(k == 0), stop=(k == 7)),
                         reads=["Wup", "h2T"], writes=[p1k])
                r_ = rl[blk % 2]
                rk = "rl%d" % (blk % 2)
                S.op("act", lambda e, p1=p1, r_=r_: e.activation(out=r_[:], in_=p1[:, 0:BW], func=AF.Relu), reads=[p1k], writes=[rk])
                eng = "pool" if blk % 2 == 0 else "dve"
                S.op(eng, lambda e, r_=r_, blk=blk: e.tensor_tensor(out=actT[:, blk, :], in0=r_[:], in1=r_[:], op=ALU.mult),
                     reads=[rk], writes=["actT"])
            for blk in range(8):
                p1 = pu[pc % 4]
                p1k = "p5u%d" % (pc % 4)
                pc += 1
                for k in range(32):
                    S.op("pe", lambda e, p1=p1, k=k, blk=blk: e.matmul(p1[:, 0:BW], Wdn[:, k, blk * 128:(blk + 1) * 128], actT[:, k, :],
                                                                       start=(k == 0), stop=(k == 31)),
                         reads=["Wdn", "actT"], writes=[p1k])
                S.op("act", lambda e, p1=p1, blk=blk: e.activation(out=dns[:, blk, :], in_=p1[:, 0:BW], func=AF.Identity),
                     reads=[p1k], writes=["dns"])
                S.op("act", lambda e, p1=p1, blk=blk: e.activation(out=sq[:, blk, :], in_=p1[:, 0:BW], func=AF.Square),
                     reads=[p1k], writes=["sq5"])
            for k in range(8):
                S.op("pe", lambda e, k=k: e.matmul(pss[:, 0:BW], onesD, sq[:, k, :], start=(k == 0), stop=(k == 7)),
                     reads=["sq5", "cst"], writes=["pss5"])
            rsqrt_to(S, cst, rstd[:], "rstd5", pss[:, 0:BW], "pss5")
            for blk in range(8):
                t_ = tmp[blk % 2]
                tk = "tmp5_%d" % (blk % 2)
                S.op("pool", lambda e, t_=t_, blk=blk: e.tensor_tensor(out=t_[:], in0=dns[:, blk, :], in1=rstd[:], op=ALU.mult),
                     reads=["dns", "rstd5"], writes=[tk])
                S.op("dve", lambda e, t_=t_, blk=blk, xT=xT: e.scalar_tensor_tensor(
                    out=dns[:, blk, :], in0=t_[:], scalar=pv[:, 5, blk:blk + 1], in1=xT[:, blk, :], op0=ALU.mult, op1=ALU.add),
                    reads=[tk, "pv", xk, "dns"], writes=["dns"])
            for a in range(NA):
                for half in range(2):
                    ot = otok[oc % 2]
                    otk = "otok%d" % (oc % 2)
                    oc += 1
                    p_ = po[half]
                    pk = "p5o%d" % half
                    for b4 in range(4):
                        blk = half * 4 + b4
                        S.op("pe", lambda e, p_=p_, b4=b4, blk=blk, a=a: e.transpose(
                            p_[:, b4 * 128:(b4 + 1) * 128], dns[:, blk, a * 128:(a + 1) * 128], ident),
                            reads=["dns", "cst"], writes=[pk])
                    if half == 0:
                        S.op("act", lambda e, p_=p_, ot=ot: e.activation(out=ot[:], in_=p_[:], func=AF.Identity),
                             reads=[pk], writes=[otk])
                    else:
                        S.op("dve", lambda e, p_=p_, ot=ot: e.tensor_copy(out=ot[:], in_=p_[:]), reads=[pk], writes=[otk])
                    S.dma(out[t0 + a * 128:t0 + (a + 1) * 128, half * 512:(half + 1) * 512], ot[:], reads=[otk], writes=["out"])
        S.barrier()


def host_inputs(inputs, b, T):
    f = lambda a: np.ascontiguousarray(np.asarray(a, dtype=np.float32))
    col = lambda v: f(np.asarray(v).reshape(-1, 128).T)
    nw = np.stack([col(inputs["norm_mix_pre"][0]), col(inputs["norm_mix_post"][0]),
                   col(inputs["norm_mlp_pre"][0]), col(inputs["norm_mlp_post"][0])], axis=1)
    cws = np.concatenate([np.asarray(inputs["ssm_conv_w"][0]), np.asarray(inputs["ssm_conv_b"])], axis=0)
    cws = cws.reshape(5, 32, 128).transpose(2, 1, 0)
    cwg = np.asarray(inputs["gdn_conv_w"][0]).reshape(4, 32, 128).transpose(2, 1, 0)
    rowv = np.concatenate([np.asarray(inputs["ssm_dt_bias"][0]), np.asarray(inputs["ssm_A_log"][0]), np.asarray(inputs["ssm_D"][0]),
                           np.asarray(inputs["gdn_dt_bias"][0]), np.asarray(inputs["gdn_A_log"][0]),
                           np.asarray(inputs["ssm_norm_w"][0]), np.asarray(inputs["gdn_norm_w"][0])])[None, :]
    return {
        "x": f(np.asarray(inputs["x"])[b, :T]),
        "c_col": col(np.asarray(inputs["c"])[b]),
        "w_ada": f(inputs["w_ada"][0]),
        "b_ada_col": col(inputs["b_ada"][0]),
        "nw_col": f(nw),
        "w_in": f(inputs["w_in"][0]),
        "cw_ssm": f(cws),
        "cw_gdn": f(cwg),
        "rowv": f(rowv),
        "w_su": f(inputs["w_ssm_up"][0]),
        "w_gu": f(inputs["w_gdn_up"][0]),
        "w_out": f(inputs["w_out"][0]),
        "w_up": f(inputs["w_mlp_up"][0]),
        "w_dn": f(inputs["w_mlp_down"][0]),
        "consts": make_consts(),
    }


def kernel(**inputs):
    T = 4096
    nc, S = build(T)
    shared = None
    in_maps = []
    for b in range(8):
        m = host_inputs(inputs, b, T)
        if shared is None:
            shared = m
        else:
            for k in m:
                if k not in ("x", "c_col"):
                    m[k] = shared[k]
        in_maps.append(m)
    res = run_bass_kernel_spmd(nc, in_maps, core_ids=list(range(8)))
    return np.stack([np.asarray(r["out"], dtype=np.float32) for r in res.results], axis=0)
```

```python
import numpy as np
import ml_dtypes
from contextlib import ExitStack
import concourse.bass as bass
import concourse.mybir as mybir
from concourse.bass_utils import run_bass_kernel_spmd

F32 = mybir.dt.float32
BF16 = mybir.dt.bfloat16
AF = mybir.ActivationFunctionType
ALU = mybir.AluOpType

D = 1024
EPS = 1e-6
COMPUTE = ("pe", "act", "dve", "pool")
NSLOT = 8


STRICT = False


class Sched:
    def __init__(self, nc, st):
        self.nc = nc
        self.st = st
        self.streams = {e: [] for e in ("pe", "act", "dve", "pool", "sp")}
        self.sem = {}
        for e in COMPUTE:
            self.sem[e] = st.enter_context(nc.semaphore("c_" + e))
        for i in range(NSLOT):
            self.sem[("sp", i)] = st.enter_context(nc.semaphore("d_sp%d" % i))
        self.count = {k: 0 for k in self.sem}
        self.dma_idx = 0
        self.known = {e: {} for e in self.streams}
        self.clock = {}
        self.last_write = {}
        self.readers = {}
        self.ninstr = 0
        self.nwaits = 0

    def _need(self, eng, ev, waits):
        c, n = ev
        if self.known[eng].get(c, 0) >= n:
            return
        if waits.get(c, 0) < n:
            waits[c] = n

    def _deps(self, eng, reads, writes):
        waits = {}
        for k in reads:
            ev = self.last_write.get(k)
            if ev is not None:
                if ev[0] == eng and eng == "pe":
                    continue
                self._need(eng, ev, waits)
        for k in writes:
            ev = self.last_write.get(k)
            if ev is not None and (STRICT and eng != "pe" or not (ev[0] == eng and eng in COMPUTE)):
                self._need(eng, ev, waits)
            for rv in self.readers.get(k, ()):
                if rv[0] == eng and eng in COMPUTE and not STRICT:
                    continue
                self._need(eng, rv, waits)
        return waits

    def _apply(self, eng, waits):
        kn = self.known[eng]
        for c, n in waits.items():
            ck = self.clock.get((c, n))
            if ck:
                for cc, nn in ck.items():
                    if kn.get(cc, 0) < nn:
                        kn[cc] = nn
            if kn.get(c, 0) < n:
                kn[c] = n

    def _record(self, ev, eng, reads, writes):
        ck = dict(self.known[eng])
        ck[ev[0]] = ev[1]
        self.clock[ev] = ck
        for k in reads:
            self.readers.setdefault(k, []).append(ev)
        for k in writes:
            self.last_write[k] = ev
            self.readers[k] = []

    def op(self, eng, fn, reads=(), writes=()):
        waits = self._deps(eng, reads, writes)
        self._apply(eng, waits)
        self.count[eng] += 1
        ev = (eng, self.count[eng])
        self._record(ev, eng, reads, writes)
        self.streams[eng].append((list(waits.items()), fn, (eng, 1)))
        self.ninstr += 1
        self.nwaits += len(waits)
        return ev

    def dma(self, out, in_, reads=(), writes=()):
        q = "sp"
        slot = (q, self.dma_idx % NSLOT)
        self.dma_idx += 1
        waits = self._deps(q, reads, writes)
        if self.count[slot] > 0:
            self._need(q, (slot, self.count[slot]), waits)
        self._apply(q, waits)
        self.count[slot] += 1
        ev = (slot, self.count[slot])
        self._record(ev, q, reads, writes)
        fn = lambda e, out=out, in_=in_: e.dma_start(out=out, in_=in_)
        self.streams[q].append((list(waits.items()), fn, (slot, 16)))
        self.ninstr += 1
        self.nwaits += len(waits)
        return ev

    def barrier(self):
        for eng in self.streams:
            waits = {}
            for c, n in self.count.items():
                if n > 0 and c != eng:
                    self._need(eng, (c, n), waits)
            self._apply(eng, waits)
            self.streams[eng].append((list(waits.items()), None, None))
        self.last_write = {}
        self.readers = {}

    def emit(self):
        nc = self.nc
        block = self.st.enter_context(nc.Block())
        sem = self.sem

        def run(stream):
            def body(e):
                for waits, fn, inc in stream:
                    for c, n in waits:
                        e.wait_ge(sem[c], n * (1 if c in COMPUTE else 16))
                    if fn is not None:
                        fn(e).then_inc(sem[inc[0]], inc[1])
            return body

        block.tensor(run(self.streams["pe"]))
        block.scalar(run(self.streams["act"]))
        block.vector(run(self.streams["dve"]))
        block.gpsimd(run(self.streams["pool"]))
        block.sync(run(self.streams["sp"]))


OFF_Z1 = 0
OFF_XBC = 2048
OFF_DT = 6144
OFF_QKV = 6176
OFF_Z2 = 10272
OFF_B = 12320
OFF_A = 12336
OFF_GS = 12352
OFF_GG = 13376

C_ID, C_U, C_GT, C_SU, C_ONE, C_ONED, C_EPS, C_BD8, C_CMT, NCONST = 0, 1, 2, 3, 4, 5, 6, 7, 8, 12
R_DTB1, R_AL1, R_D1, R_DTB2, R_AL2, R_NW1, R_NW2, RLEN = 0, 32, 64, 96, 112, 128, 2176, 2304


def make_consts():
    k = np.arange(128)[:, None]
    l = np.arange(128)[None, :]
    c = np.zeros((128, NCONST, 128), np.float32)
    c[:, C_ID] = (k == l)
    c[:, C_U] = (k <= l)
    c[:, C_GT] = (k > l)
    c[:, C_SU] = (l > k)
    c[:, C_ONE] = 1.0
    c[:, C_ONED] = 1.0 / D
    c[:, C_EPS] = EPS
    c[:, C_BD8] = (k // 8 == l // 8)
    for n, b in enumerate((8, 16, 32, 64)):
        cm = ((k // (2 * b) == l // (2 * b)) & ((k // b) % 2 == 0) & ((l // b) % 2 == 1))
        c[:, C_CMT + n] = cm.T
    return c


def rsqrt_to(S, cst, dst, dstk, src, srck, scale=1.0):
    S.op("act", lambda e: e.activation(out=dst, in_=src, func=AF.Sqrt, bias=cst[:, C_EPS, 0:1], scale=scale),
         reads=[srck, "cst"], writes=[dstk])
    S.op("dve", lambda e: e.reciprocal(out=dst, in_=dst), reads=[dstk], writes=[dstk])


def build(T, phases=(0, 1, 2, 3, 4, 5), debug=False):
    NT = T // 128
    NB = T // 512
    nc = bass.Bass("TRN2", target_bir_lowering=False)

    def din(name, shape, dt=F32):
        return nc.dram_tensor(name, shape, dt, kind="ExternalInput").ap()

    def dscr(name, shape, dt):
        return nc.dram_tensor(name, shape, dt, kind="ExternalOutput").ap()

    x = din("x", [T, D])
    c_col = din("c_col", [128, 8])
    w_ada = din("w_ada", [D, 6 * D])
    b_ada_col = din("b_ada_col", [128, 48])
    nw_col = din("nw_col", [128, 4, 8])
    w_in = din("w_in", [D, 14400])
    cw_ssm = din("cw_ssm", [128, 32, 5])
    cw_gdn = din("cw_gdn", [128, 32, 4])
    rowv = din("rowv", [1, RLEN])
    w_su = din("w_su", [2048, D])
    w_gu = din("w_gu", [2048, D])
    w_out = din("w_out", [D, D])
    w_up = din("w_up", [D, 4096])
    w_dn = din("w_dn", [4096, D])
    consts = din("consts", [128, NCONST, 128])
    out = nc.dram_tensor("out", [T, D], F32, kind="ExternalOutput").ap()

    XT = dscr("s_xt", [D, T], F32)
    XBC_T = dscr("s_xbct", [4096, T], BF16)
    QKV_T = dscr("s_qkvt", [4096, T], BF16)
    G_T = dscr("s_gt", [2048, T], BF16)
    Z = dscr("s_z", [T, 4096], BF16)
    SM = dscr("s_sm", [T, 96], F32)
    if debug:
        Y = dscr("s_y", [T, 4096], BF16)
        X2T = dscr("s_x2t", [D, T], F32)
        MOD = dscr("s_mod", [128, 48], F32)
    else:
        Y = Z
        X2T = XT
        MOD = None

    with ExitStack() as st:
        S = Sched(nc, st)
        sb = lambda name, shape, dt=F32: st.enter_context(nc.sbuf_tensor(name, shape, dt))
        cst = sb("cst", [128, NCONST, 128])
        cstb = sb("cstb", [128, 1, 128], BF16)
        pv = sb("pv", [128, 6, 8])

        S.dma(cst[:], consts, writes=["cst"])
        S.op("pool", lambda e: e.tensor_copy(out=cstb[:], in_=cst[:, 0:1, :]), reads=["cst"], writes=["cstb"])
        ident = cst[:, C_ID, :]
        identb = cstb[:, C_ID, :]
        Umat = cst[:, C_U, :]
        GTm = cst[:, C_GT, :]
        SUm = cst[:, C_SU, :]
        ones = cst[:, C_ONE, :]
        onesD = cst[:, C_ONED, :]
        if 0 in phases:
            with ExitStack() as ps:
                lsb = lambda name, shape, dt=F32: ps.enter_context(nc.sbuf_tensor(name, shape, dt))
                cact = lsb("cact", [128, 8])
                csig = lsb("csig", [128, 8])
                wa = [lsb("wa%d" % i, [128, 8, 512]) for i in range(2)]
                modsb = lsb("modsb", [128, 48])
                bada = lsb("bada", [128, 48])
                nwc = lsb("nwc", [128, 4, 8])
                modps = ps.enter_context(nc.psum_tensor("modps", [128, 512], F32))
                S.dma(cact[:], c_col, writes=["cact"])
                S.dma(bada[:], b_ada_col, writes=["bada"])
                S.dma(nwc[:], nw_col, writes=["nwc"])
                S.op("act", lambda e: e.activation(out=csig[:], in_=cact[:], func=AF.Sigmoid), reads=["cact"], writes=["csig"])
                S.op("dve", lambda e: e.tensor_tensor(out=cact[:], in0=cact[:], in1=csig[:], op=ALU.mult),
                     reads=["cact", "csig"], writes=["cact"])
                wav = w_ada.rearrange("(k p) f -> p k f", p=128)
                for fb in range(12):
                    w = wa[fb % 2]
                    wk = "wa%d" % (fb % 2)
                    S.dma(w[:], wav[:, :, fb * 512:(fb + 1) * 512], writes=[wk])
                    for j in range(4):
                        col = fb * 4 + j
                        for k in range(8):
                            S.op("pe", lambda e, w=w, j=j, k=k, col=col: e.matmul(
                                modps[:, col:col + 1], w[:, k, j * 128:(j + 1) * 128], cact[:, k:k + 1],
                                start=(k == 0), stop=(k == 7)), reads=[wk, "cact"], writes=["modps"])
                S.op("dve", lambda e: e.tensor_tensor(out=modsb[:], in0=modps[:, 0:48], in1=bada[:], op=ALU.add),
                     reads=["modps", "bada"], writes=["modsb"])
                S.op("dve", lambda e: e.scalar_tensor_tensor(out=pv[:, 0, :], in0=modsb[:, 8:16], scalar=1.0, in1=nwc[:, 0, :],
                                                             op0=ALU.add, op1=ALU.mult), reads=["modsb", "nwc"], writes=["pv"])
                S.op("dve", lambda e: e.tensor_copy(out=pv[:, 1, :], in_=modsb[:, 0:8]), reads=["modsb"], writes=["pv"])
                S.op("dve", lambda e: e.tensor_tensor(out=pv[:, 2, :], in0=modsb[:, 16:24], in1=nwc[:, 1, :], op=ALU.mult),
                     reads=["modsb", "nwc"], writes=["pv"])
                S.op("dve", lambda e: e.scalar_tensor_tensor(out=pv[:, 3, :], in0=modsb[:, 32:40], scalar=1.0, in1=nwc[:, 2, :],
                                                             op0=ALU.add, op1=ALU.mult), reads=["modsb", "nwc"], writes=["pv"])
                S.op("dve", lambda e: e.tensor_copy(out=pv[:, 4, :], in_=modsb[:, 24:32]), reads=["modsb"], writes=["pv"])
                S.op("dve", lambda e: e.tensor_tensor(out=pv[:, 5, :], in0=modsb[:, 40:48], in1=nwc[:, 3, :], op=ALU.mult),
                     reads=["modsb", "nwc"], writes=["pv"])
                if debug:
                    S.dma(MOD, modsb[:], reads=["modsb"], writes=["MOD"])
                S.barrier()

        def normT(xT, xk, sq, sqk, rstd, rstdk, ssps, sspsk, tmp, tmpk, hdst, hk, ia, ish, W=512):
            S.op("act", lambda e: e.activation(out=sq[:], in_=xT[:], func=AF.Square), reads=[xk], writes=[sqk])
            for k in range(8):
                S.op("pe", lambda e, k=k: e.matmul(ssps[:, 0:W], onesD, sq[:, k, :], start=(k == 0), stop=(k == 7)),
                     reads=[sqk, "cst"], writes=[sspsk])
            rsqrt_to(S, cst, rstd[:], rstdk, ssps[:, 0:W], sspsk)
            for k in range(8):
                t = tmp[k % 2]
                tk = tmpk[k % 2]
                S.op("dve", lambda e, k=k, t=t: e.tensor_tensor(out=t[:], in0=xT[:, k, :], in1=rstd[:], op=ALU.mult),
                     reads=[xk, rstdk], writes=[tk])
                S.op("act", lambda e, k=k, t=t: e.activation(out=hdst(k), in_=t[:], func=AF.Identity,
                                                           bias=pv[:, ish, k:k + 1], scale=pv[:, ia, k:k + 1]),
                     reads=[tk, "pv"], writes=[hk])

        hT_cm = None
        hst = ExitStack()
        if 1 in phases or 2 in phases:
            hT_cm = hst.enter_context(nc.sbuf_tensor("hT", [128, 8, T], BF16))

        if 1 in phases:
            with ExitStack() as ps:
                lsb = lambda name, shape, dt=F32: ps.enter_context(nc.sbuf_tensor(name, shape, dt))
                xtok = [lsb("xtok%d" % i, [128, 4, D]) for i in range(2)]
                xTb = [lsb("xTb%d" % i, [128, 8, 512]) for i in range(2)]
                sq = lsb("sq1", [128, 8, 512])
                rstd = lsb("rstd1", [128, 512])
                tmp = [lsb("tmp1_%d" % i, [128, 512]) for i in range(2)]
                tps = [ps.enter_context(nc.psum_tensor("tps%d" % i, [128, 512], F32)) for i in range(4)]
                ssps = ps.enter_context(nc.psum_tensor("ssps1", [128, 512], F32))
                XTv = XT.rearrange("(k p) t -> p k t", p=128)
                for nb in range(NB):
                    xt = xtok[nb % 2]
                    xtk = "xtok%d" % (nb % 2)
                    xT = xTb[nb % 2]
                    xTk = "xTb%d" % (nb % 2)
                    S.dma(xt[:], x[nb * 512:(nb + 1) * 512, :].rearrange("(a p) f -> p a f", p=128), writes=[xtk])
                    for k in range(8):
                        tp = tps[k % 4]
                        tpk = "tps%d" % (k % 4)
                        for a in range(4):
                            S.op("pe", lambda e, tp=tp, a=a, k=k, xt=xt: e.transpose(
                                tp[:, a * 128:(a + 1) * 128], xt[:, a, k * 128:(k + 1) * 128], ident),
                                reads=[xtk, "cst"], writes=[tpk])
                        if k % 2 == 0:
                            S.op("act", lambda e, tp=tp, k=k, xT=xT: e.activation(out=xT[:, k, :], in_=tp[:], func=AF.Identity),
                                 reads=[tpk], writes=[xTk])
                        else:
                            S.op("dve", lambda e, tp=tp, k=k, xT=xT: e.tensor_copy(out=xT[:, k, :], in_=tp[:]),
                                 reads=[tpk], writes=[xTk])
                    S.dma(XTv[:, :, nb * 512:(nb + 1) * 512], xT[:], reads=[xTk], writes=["XT"])
                    normT(xT, xTk, sq, "sq1", rstd, "rstd1", ssps, "ssps1", tmp, ["tmp1_0", "tmp1_1"],
                          lambda k, nb=nb: hT_cm[:, k, nb * 512:(nb + 1) * 512], ("hT", nb), 0, 1)
                S.barrier()

        if 2 in phases:
            hkeys = [("hT", nb) for nb in range(NB)]
            with ExitStack() as ps:
                lsb = lambda name, shape, dt=F32: ps.enter_context(nc.sbuf_tensor(name, shape, dt))
                wst = [lsb("wst%d" % i, [128, 8, 128]) for i in range(2)]
                wbf = [lsb("wbf%d" % i, [128, 8, 128], BF16) for i in range(2)]
                pc = [lsb("pc%d" % i, [128, T + 3]) for i in range(2)]
                accs = [lsb("acc%d" % i, [128, T]) for i in range(2)]
                sq2 = lsb("sq2", [128, T])
                rs = lsb("rs", [128, T])
                ob = [lsb("ob%d" % i, [128, T], BF16) for i in range(2)]
                cws = lsb("cws", [128, 32, 5])
                cwg = lsb("cwg", [128, 32, 4])
                pps = [ps.enter_context(nc.psum_tensor("pps%d" % i, [128, 512], F32)) for i in range(4)]
                sps = [ps.enter_context(nc.psum_tensor("sps%d" % i, [128, 512], F32)) for i in range(2)]
                S.dma(cws[:], cw_ssm, writes=["cws"])
                S.dma(cwg[:], cw_gdn, writes=["cwg"])
                for i in range(2):
                    S.op("pool", lambda e, i=i: e.memset(pc[i][:, 0:3], 0.0), writes=["pc%d" % i])
                w_in_v = w_in.rearrange("(k p) f -> p k f", p=128)
                XBCv = XBC_T.rearrange("(b p) t -> b p t", p=128)
                QKVv = QKV_T.rearrange("(b p) t -> b p t", p=128)
                GTv = G_T.rearrange("(b p) t -> b p t", p=128)
                blocks = []
                for cb in range(32):
                    blocks.append(("xbc", cb, OFF_XBC + cb * 128))
                for cb in range(32):
                    blocks.append(("qkv", cb, OFF_QKV + cb * 128))
                for cb in range(8):
                    blocks.append(("gate", cb, OFF_GS + cb * 128))
                for cb in range(8):
                    blocks.append(("gate", 8 + cb, OFF_GG + cb * 128))
                pcount = 0
                for bi, (kind, cb, off) in enumerate(blocks):
                    acc = accs[bi % 2]
                    acck = "acc%d" % (bi % 2)
                    ws = wst[bi % 2]
                    wsk = "wst%d" % (bi % 2)
                    wb = wbf[bi % 2]
                    wbk = "wbf%d" % (bi % 2)
                    if bi == 0:
                        S.dma(ws[:], w_in_v[:, :, off:off + 128], writes=[wsk])
                    if bi + 1 < len(blocks):
                        noff = blocks[bi + 1][2]
                        S.dma(wst[(bi + 1) % 2][:], w_in_v[:, :, noff:noff + 128], writes=["wst%d" % ((bi + 1) % 2)])
                    S.op("pool", lambda e, ws=ws, wb=wb: e.tensor_copy(out=wb[:], in_=ws[:]), reads=[wsk], writes=[wbk])
                    p_ = pc[bi % 2]
                    pk = "pc%d" % (bi % 2)
                    o_ = ob[bi % 2]
                    ok = "ob%d" % (bi % 2)
                    for tb in range(NB):
                        pp = pps[pcount % 4]
                        ppk = "pps%d" % (pcount % 4)
                        pcount += 1
                        for k in range(8):
                            S.op("pe", lambda e, pp=pp, wb=wb, k=k, tb=tb: e.matmul(
                                pp[:], wb[:, k, :], hT_cm[:, k, tb * 512:(tb + 1) * 512], start=(k == 0), stop=(k == 7)),
                                reads=[wbk, hkeys[tb]], writes=[ppk])
                        if kind == "gate":
                            S.op("act", lambda e, pp=pp, o_=o_, tb=tb: e.activation(
                                out=o_[:, tb * 512:(tb + 1) * 512], in_=pp[:], func=AF.Sigmoid), reads=[ppk], writes=[ok])
                        else:
                            S.op("act", lambda e, pp=pp, p_=p_, tb=tb: e.activation(
                                out=p_[:, 3 + tb * 512:3 + (tb + 1) * 512], in_=pp[:], func=AF.Identity), reads=[ppk], writes=[pk])
                            if kind == "xbc":
                                S.op("act", lambda e, pp=pp, tb=tb, cb=cb, acc=acc: e.activation(
                                    out=acc[:, tb * 512:(tb + 1) * 512], in_=pp[:], func=AF.Identity,
                                    bias=cws[:, cb, 4:5], scale=cws[:, cb, 3:4]), reads=[ppk, "cws"], writes=[acck])
                            else:
                                S.op("act", lambda e, pp=pp, tb=tb, cb=cb, acc=acc: e.activation(
                                    out=acc[:, tb * 512:(tb + 1) * 512], in_=pp[:], func=AF.Identity,
                                    scale=cwg[:, cb, 3:4]), reads=[ppk, "cwg"], writes=[acck])
                    if kind == "gate":
                        S.dma(GTv[cb], o_[:], reads=[ok], writes=["G_T"])
                        continue
                    cwt = cws if kind == "xbc" else cwg
                    cwk = "cws" if kind == "xbc" else "cwg"
                    for j in range(1, 4):
                        S.op("dve", lambda e, p_=p_, cb=cb, j=j, cwt=cwt, acc=acc: e.scalar_tensor_tensor(
                            out=acc[:], in0=p_[:, 3 - j:3 - j + T], scalar=cwt[:, cb, 3 - j:4 - j], in1=acc[:],
                            op0=ALU.mult, op1=ALU.add), reads=[pk, cwk, acck], writes=[acck])
                    if kind == "qkv" and cb < 16:
                        S.op("act", lambda e, acc=acc: e.activation(out=acc[:], in_=acc[:], func=AF.Silu), reads=[acck], writes=[acck])
                        S.op("act", lambda e, acc=acc: e.activation(out=sq2[:], in_=acc[:], func=AF.Square), reads=[acck], writes=["sq2"])
                        for tb in range(NB):
                            sp = sps[tb % 2]
                            spk = "sps%d" % (tb % 2)
                            S.op("pe", lambda e, sp=sp, tb=tb: e.matmul(sp[:], ones, sq2[:, tb * 512:(tb + 1) * 512],
                                                                        start=True, stop=True),
                                 reads=["sq2", "cst"], writes=[spk])
                            rsqrt_to(S, cst, rs[:, tb * 512:(tb + 1) * 512], "rs", sp[:], spk)
                        qs = (128.0 ** -0.5) if cb < 8 else 1.0
                        S.op("dve", lambda e, o_=o_, qs=qs, acc=acc: e.scalar_tensor_tensor(
                            out=o_[:], in0=acc[:], scalar=qs, in1=rs[:], op0=ALU.mult, op1=ALU.mult),
                            reads=[acck, "rs"], writes=[ok])
                    else:
                        S.op("act", lambda e, o_=o_, acc=acc: e.activation(out=o_[:], in_=acc[:], func=AF.Silu), reads=[acck], writes=[ok])
                    dst = XBCv[cb] if kind == "xbc" else QKVv[cb]
                    S.dma(dst, o_[:], reads=[ok], writes=[kind + "_T"])
                S.barrier()

            with ExitStack() as ps:
                lsb = lambda name, shape, dt=F32: ps.enter_context(nc.sbuf_tensor(name, shape, dt))
                wzs = [lsb("wzs%d" % i, [128, 8, 512]) for i in range(2)]
                wzb = [lsb("wzb%d" % i, [128, 8, 512], BF16) for i in range(2)]
                zb = [lsb("zb%d" % i, [128, 512], BF16) for i in range(3)]
                wss = lsb("wss", [128, 8, 64])
                wsb = lsb("wsb", [128, 8, 64], BF16)
                smt = [lsb("smt%d" % i, [128, 96]) for i in range(2)]
                t1 = [lsb("t1_%d" % i, [128, 48]) for i in range(2)]
                rows = lsb("rows2", [128, RLEN])
                arow = lsb("arow", [128, 48])
                S.dma(rows[:], rowv.partition_broadcast(128), writes=["rows"])
                S.op("act", lambda e: e.activation(out=arow[:, 0:32], in_=rows[:, R_AL1:R_AL1 + 32], func=AF.Exp),
                     reads=["rows"], writes=["arow"])
                S.op("act", lambda e: e.activation(out=arow[:, 32:48], in_=rows[:, R_AL2:R_AL2 + 16], func=AF.Exp),
                     reads=["rows"], writes=["arow"])
                S.op("dve", lambda e: e.tensor_scalar(out=arow[:], in0=arow[:], scalar1=-1.0, scalar2=None, op0=ALU.mult),
                     reads=["arow"], writes=["arow"])
                pps = [ps.enter_context(nc.psum_tensor("zps%d" % i, [128, 512], F32)) for i in range(4)]
                sps = [ps.enter_context(nc.psum_tensor("smps%d" % i, [128, 512], F32)) for i in range(2)]
                w_in_v = w_in.rearrange("(k p) f -> p k f", p=128)
                pcount = 0
                for blk in range(8):
                    off = (OFF_Z1 + blk * 512) if blk < 4 else (OFF_Z2 + (blk - 4) * 512)
                    ws = wzs[blk % 2]
                    wsk = "wzs%d" % (blk % 2)
                    wb = wzb[blk % 2]
                    wbk = "wzb%d" % (blk % 2)
                    if blk == 0:
                        S.dma(ws[:], w_in_v[:, :, off:off + 512], writes=[wsk])
                    if blk + 1 < 8:
                        noff = (OFF_Z1 + (blk + 1) * 512) if blk + 1 < 4 else (OFF_Z2 + (blk + 1 - 4) * 512)
                        S.dma(wzs[(blk + 1) % 2][:], w_in_v[:, :, noff:noff + 512], writes=["wzs%d" % ((blk + 1) % 2)])
                    S.op("act", lambda e, ws=ws, wb=wb: e.activation(out=wb[:], in_=ws[:], func=AF.Identity), reads=[wsk], writes=[wbk])
                    for tt in range(NT):
                        pp = pps[pcount % 4]
                        ppk = "zps%d" % (pcount % 4)
                        z_ = zb[pcount % 3]
                        zk = "zb%d" % (pcount % 3)
                        pcount += 1
                        for k in range(8):
                            S.op("pe", lambda e, pp=pp, wb=wb, k=k, tt=tt: e.matmul(
                                pp[:], hT_cm[:, k, tt * 128:(tt + 1) * 128], wb[:, k, :], start=(k == 0), stop=(k == 7)),
                                reads=[wbk, hkeys[tt // 4]], writes=[ppk])
                        S.op("act", lambda e, pp=pp, z_=z_: e.activation(out=z_[:], in_=pp[:], func=AF.Silu), reads=[ppk], writes=[zk])
                        S.dma(Z[tt * 128:(tt + 1) * 128, blk * 512:(blk + 1) * 512], z_[:], reads=[zk], writes=["Z"])
                S.dma(wss[:, :, 0:32], w_in_v[:, :, OFF_DT:OFF_DT + 32], writes=["wss"])
                S.dma(wss[:, :, 32:64], w_in_v[:, :, OFF_B:OFF_B + 32], writes=["wss"])
                S.op("pool", lambda e: e.tensor_copy(out=wsb[:], in_=wss[:]), reads=["wss"], writes=["wsb"])
                for tt in range(NT):
                    sp = sps[tt % 2]
                    spk = "smps%d" % (tt % 2)
                    sm_ = smt[tt % 2]
                    smk = "smt%d" % (tt % 2)
                    t_ = t1[tt % 2]
                    tk = "t1_%d" % (tt % 2)
                    for k in range(8):
                        S.op("pe", lambda e, sp=sp, k=k, tt=tt: e.matmul(
                            sp[:, 0:64], hT_cm[:, k, tt * 128:(tt + 1) * 128], wsb[:, k, :], start=(k == 0), stop=(k == 7)),
                            reads=["wsb", hkeys[tt // 4]], writes=[spk])
                    S.op("dve", lambda e, sp=sp, t_=t_: e.tensor_tensor(out=t_[:, 0:32], in0=sp[:, 0:32],
                                                                        in1=rows[:, R_DTB1:R_DTB1 + 32], op=ALU.add),
                         reads=["rows"], writes=[spk, tk])
                    S.op("dve", lambda e, sp=sp, t_=t_: e.tensor_tensor(out=t_[:, 32:48], in0=sp[:, 48:64],
                                                                        in1=rows[:, R_DTB2:R_DTB2 + 16], op=ALU.add),
                         reads=["rows"], writes=[spk, tk])
                    S.op("act", lambda e, t_=t_: e.activation(out=t_[:], in_=t_[:], func=AF.Exp), reads=[tk], writes=[tk])
                    S.op("dve", lambda e, t_=t_: e.tensor_scalar(out=t_[:], in0=t_[:], scalar1=1.0, scalar2=None, op0=ALU.add),
                         reads=[tk], writes=[tk])
                    S.op("act", lambda e, t_=t_: e.activation(out=t_[:], in_=t_[:], func=AF.Ln), reads=[tk], writes=[tk])
                    S.op("act", lambda e, sp=sp, sm_=sm_: e.activation(out=sm_[:, 32:48], in_=sp[:, 32:48], func=AF.Sigmoid),
                         writes=[spk, smk])
                    S.op("dve", lambda e, t_=t_, sm_=sm_: e.tensor_copy(out=sm_[:, 0:32], in_=t_[:, 0:32]), reads=[tk], writes=[smk])
                    S.op("dve", lambda e, t_=t_, sm_=sm_: e.tensor_tensor(out=sm_[:, 48:96], in0=t_[:], in1=arow[:], op=ALU.mult),
                         reads=[tk, "arow"], writes=[smk])
                    S.dma(SM[tt * 128:(tt + 1) * 128, :], sm_[:], reads=[smk], writes=["SM"])
                S.barrier()

        hst.close()
        if 3 in phases:
            phase3(nc, S, st, T, XBC_T, QKV_T, Z, SM, Y, cst, cstb, rowv)
        if 4 in phases:
            phase4a(nc, S, T, Y, G_T, XT, X2T, w_su, w_gu, w_out, cst, cstb, pv)
        if 5 in phases:
            phase4b(nc, S, T, X2T, out, w_up, w_dn, cst, pv, normT)
        else:
            pass
        S.barrier()
        S.emit()
    return nc, S


def phase3(nc, S, st_outer, T, XBC_T, QKV_T, Z, SM, Y, cst, cstb, rowv):
    NT = T // 128
    ident = cst[:, C_ID, :]
    identb = cstb[:, C_ID, :]
    Umat = cst[:, C_U, :]
    GTm = cst[:, C_GT, :]
    SUm = cst[:, C_SU, :]
    ones = cst[:, C_ONE, :]
    with ExitStack() as ps:
        lsb = lambda name, shape, dt=F32: ps.enter_context(nc.sbuf_tensor(name, shape, dt))
        smt = [lsb("p3sm%d" % i, [128, 96]) for i in range(2)]
        rows = lsb("rows3", [128, RLEN])
        S.dma(rows[:], rowv.partition_broadcast(128), writes=["rows"])
        xbct = [lsb("p3xbc%d" % i, [128, 32, 128], BF16) for i in range(2)]
        qkvt = [lsb("p3qkv%d" % i, [128, 32, 128], BF16) for i in range(2)]
        zt = [lsb("p3z%d" % i, [128, 4096], BF16) for i in range(1)]
        ytile = lsb("p3y", [128, 4096], BF16)
        xs_tok = lsb("xs_tok", [128, 2048], BF16)
        b_tok = lsb("b_tok", [128, 1024], BF16)
        k_tok = lsb("k_tok", [128, 1024], BF16)
        v_tok = lsb("v_tok", [128, 2048], BF16)
        c_sb = lsb("c_sb", [128, 48])
        e_sb = lsb("e_sb", [128, 48])
        f_sb = lsb("f_sb", [128, 48])
        dA_sb = lsb("dA_sb", [128, 48])
        nbeta = lsb("nbeta", [128, 16])
        ST = lsb("ST", [128, 8, 256])
        STb = lsb("STb", [128, 8, 256], BF16)
        GS = lsb("GS", [128, 16, 128])
        GSb = lsb("GSb", [128, 16, 128], BF16)
        IB = [[lsb("ib%d_%d" % (s_, n_), [128, 4, 128]) for n_ in range(6)] + [lsb("ibm%d" % s_, [128, 5, 4, 128])] for s_ in range(2)]
        attnT = lsb("attnT", [128, 16, 128], BF16)
        T2T = lsb("T2T", [128, 16, 128], BF16)
        ke = lsb("ke", [128, 16, 128], BF16)
        kf = lsb("kf", [128, 16, 128], BF16)
        nwT = lsb("nwT", [128, 16, 128], BF16)
        KKm = lsb("KKm", [128, 8, 128])
        QKm = lsb("QKm", [128, 8, 128])
        Am = [lsb("Am%d" % i, [128, 4, 128]) for i in range(2)]
        DT = [lsb("DT%d" % i, [128, 4, 128]) for i in range(4)]
        MT = [lsb("MT%d" % i, [128, 4, 128], BF16) for i in range(2)]
        CBTm = [lsb("CBTm%d" % i, [128, 128]) for i in range(2)]
        xdt = [lsb("xdt%d" % i, [128, 256], BF16) for i in range(2)]
        xw = [lsb("xw%d" % i, [128, 256], BF16) for i in range(2)]
        xsD = [lsb("xsD%d" % i, [128, 256]) for i in range(2)]
        yacc = [lsb("yacc%d" % i, [128, 256]) for i in range(2)]
        yz = [lsb("yz%d" % i, [128, 256]) for i in range(2)]
        ssq = [lsb("ssq%d" % i, [128, 4]) for i in range(2)]
        vnew = [lsb("vnew%d" % i, [128, 4, 128], BF16) for i in range(2)]
        osb = [lsb("osb%d" % i, [128, 4, 128]) for i in range(1)] * 2
        on = [lsb("on%d" % i, [128, 4, 128]) for i in range(1)] * 2
        banks = [ps.enter_context(nc.psum_tensor("pb%d" % i, [128, 512], F32)) for i in range(8)]
        bctr = [0]

        def bank():
            i = bctr[0] % 8
            bctr[0] += 1
            return banks[i], "pb%d" % i
        rr = {"am": 0, "ama": 0, "g": 0, "v": 0}

        S.op("pool", lambda e: e.memset(ST[:], 0.0), writes=["ST"])
        S.op("pool", lambda e: e.memset(STb[:], 0.0), writes=["STb"])
        S.op("pool", lambda e: e.memset(GS[:], 0.0), writes=["GS"])
        S.op("pool", lambda e: e.memset(GSb[:], 0.0), writes=["GSb"])

        XBCv = XBC_T.rearrange("(b p) t -> p b t", p=128)
        QKVv = QKV_T.rearrange("(b p) t -> p b t", p=128)

        def loads(tt):
            i = tt % 2
            S.dma(smt[i][:], SM[tt * 128:(tt + 1) * 128, :], reads=["SM"], writes=["p3sm%d" % i])
            S.dma(xbct[i][:], XBCv[:, :, tt * 128:(tt + 1) * 128], reads=["xbc_T"], writes=["p3xbc%d" % i])
            S.dma(qkvt[i][:], QKVv[:, :, tt * 128:(tt + 1) * 128], reads=["qkv_T"], writes=["p3qkv%d" % i])

        def load_z(tt):
            S.dma(zt[0][:], Z[tt * 128:(tt + 1) * 128, :], reads=["Z"], writes=["p3z0"])

        def bc(ap2, n, w):
            return ap2.unsqueeze(2).to_broadcast([128, n, w])

        def v3(ap2, h=4):
            return ap2.rearrange("p (h w) -> p h w", h=h)

        loads(0)
        load_z(0)
        for tt in range(NT):
            i = tt % 2
            sm, smk = smt[i], "p3sm%d" % i
            xbc, xbk = xbct[i], "p3xbc%d" % i
            qkv, qkk = qkvt[i], "p3qkv%d" % i
            z, zk = zt[0], "p3z0"
            y, yk = ytile, "p3y"
            if tt + 1 < NT:
                loads(tt + 1)
            bD, bDk = bank()
            S.op("pe", lambda e, bD=bD, sm=sm: e.matmul(bD[:, 0:48], Umat, sm[:, 48:96], start=True, stop=True),
                 reads=[smk, "cst"], writes=[bDk])
            S.op("pe", lambda e, bD=bD, sm=sm: e.matmul(bD[:, 64:112], ones, sm[:, 48:96], start=True, stop=True),
                 reads=[smk, "cst"], writes=[bDk])
            S.op("act", lambda e, bD=bD: e.activation(out=c_sb[:], in_=bD[:, 0:48], func=AF.Identity), writes=[bDk, "c_sb"])
            S.op("act", lambda e, bD=bD: e.activation(out=e_sb[:], in_=bD[:, 0:48], func=AF.Exp), writes=[bDk, "e_sb"])
            S.op("act", lambda e, bD=bD: e.activation(out=dA_sb[:], in_=bD[:, 64:112], func=AF.Exp), writes=[bDk, "dA_sb"])
            S.op("act", lambda e, bD=bD: e.activation(out=f_sb[:], in_=bD[:, 64:112], func=AF.Identity), writes=[bDk, "f_sb"])
            S.op("dve", lambda e: e.tensor_tensor(out=f_sb[:], in0=f_sb[:], in1=c_sb[:], op=ALU.subtract),
                 reads=["c_sb", "f_sb"], writes=["f_sb"])
            S.op("act", lambda e: e.activation(out=f_sb[:], in_=f_sb[:], func=AF.Exp), reads=["f_sb"], writes=["f_sb"])
            S.op("pool", lambda e, sm=sm: e.tensor_scalar(out=nbeta[:], in0=sm[:, 32:48], scalar1=-1.0, scalar2=None, op0=ALU.mult),
                 reads=[smk], writes=["nbeta"])
            jobs = [(xbc, xbk, 0, xs_tok, "xs_tok", 2), (xbc, xbk, 16, b_tok, "b_tok", 1),
                    (qkv, qkk, 8, k_tok, "k_tok", 1), (qkv, qkk, 16, v_tok, "v_tok", 2)]
            nev = 0
            for (src, srck, b0, dst, dstk, nq) in jobs:
                for q8 in range(nq):
                    bk_, bkk = bank()
                    bb = bk_[:].bitcast(BF16)
                    for a in range(8):
                        blk = b0 + q8 * 8 + a
                        S.op("pe", lambda e, bb=bb, a=a, src=src, blk=blk: e.transpose(
                            bb[:, a * 128:(a + 1) * 128], src[:, blk, :], identb), reads=[srck, "cstb"], writes=[bkk])
                    if nev % 2 == 0:
                        S.op("act", lambda e, bb=bb, dst=dst, q8=q8: e.activation(
                            out=dst[:, q8 * 1024:(q8 + 1) * 1024], in_=bb, func=AF.Identity), writes=[bkk, dstk])
                    else:
                        S.op("dve", lambda e, bb=bb, dst=dst, q8=q8: e.tensor_copy(
                            out=dst[:, q8 * 1024:(q8 + 1) * 1024], in_=bb), writes=[bkk, dstk])
                    nev += 1

            for half in range(2):
                bk_, bkk = bank()
                for j in range(4):
                    hq = half * 4 + j
                    S.op("pe", lambda e, bk_=bk_, j=j, hq=hq, qkv=qkv: e.matmul(
                        bk_[:, j * 128:(j + 1) * 128], qkv[:, 8 + hq, :], qkv[:, 8 + hq, :], start=True, stop=True),
                        reads=[qkk], writes=[bkk])
                S.op("dve", lambda e, bk_=bk_, half=half: e.tensor_tensor(
                    out=KKm[:, half * 4:(half + 1) * 4, :], in0=v3(bk_[:]), in1=SUm.unsqueeze(1).to_broadcast([128, 4, 128]), op=ALU.mult),
                    reads=["cst"], writes=[bkk, "KKm"])
                bq_, bqk = bank()
                for j in range(4):
                    hq = half * 4 + j
                    S.op("pe", lambda e, bq_=bq_, j=j, hq=hq, qkv=qkv: e.matmul(
                        bq_[:, j * 128:(j + 1) * 128], qkv[:, 8 + hq, :], qkv[:, hq, :], start=True, stop=True),
                        reads=[qkk], writes=[bqk])
                S.op("dve", lambda e, bq_=bq_, half=half: e.tensor_tensor(
                    out=QKm[:, half * 4:(half + 1) * 4, :], in0=v3(bq_[:]), in1=Umat.unsqueeze(1).to_broadcast([128, 4, 128]), op=ALU.mult),
                    reads=["cst"], writes=[bqk, "QKm"])
            def fl(t):
                return t[:].rearrange("p h w -> p (h w)")

            def bcm(plane):
                return cst[:, plane, :].unsqueeze(1).to_broadcast([128, 4, 128])

            def build_DT(cols, r, sm, smk):
                ra = rr["ama"] % 2
                rr["ama"] += 1
                S.op("dve", lambda e: e.tensor_tensor(out=Am[ra][:], in0=bcm(C_GT), in1=bc(sm[:, cols:cols + 4], 4, 128), op=ALU.mult),
                     reads=[smk, "cst"], writes=["Am%d" % ra])
                sg, sgk = bank()
                for j in range(4):
                    S.op("pe", lambda e, sg=sg, j=j: e.matmul(sg[:, j * 128:(j + 1) * 128], Am[ra][:, j, :], Umat, start=True, stop=True),
                         reads=["Am%d" % ra, "cst"], writes=[sgk])
                S.op("act", lambda e, sg=sg: e.activation(out=fl(DT[r]), in_=sg[:], func=AF.Exp), writes=[sgk, "DT%d" % r])

            def gdn_quad(q, s_, sm=sm, smk=smk, qkv=qkv, qkk=qkk, z=z, zk=zk, y=y, yk=yk):
                X, XT, Dv, DvT, E, F, XM = IB[s_]
                kX, kXT, kDv, kDvT, kE, kF, kXM = [("ib", s_, n_) for n_ in range(7)]
                qs = slice(q * 4, (q + 1) * 4)
                r = rr["am"] % 4
                rr["am"] += 1
                build_DT(80 + q * 4, r, sm, smk)
                for j in range(4):
                    hv = q * 4 + j
                    hq = hv // 2
                    S.op("dve", lambda e, j=j, hv=hv, hq=hq: e.scalar_tensor_tensor(
                        out=X[:, j, :], in0=KKm[:, hq, :], scalar=nbeta[:, hv:hv + 1], in1=DT[r][:, j, :], op0=ALU.mult, op1=ALU.mult),
                        reads=["KKm", "nbeta", "DT%d" % r], writes=[kX])
                S.op("dve", lambda e: e.tensor_tensor(
                    out=attnT[:, qs, :].rearrange("p (a b) w -> p a b w", a=2),
                    in0=QKm[:, 2 * q:2 * q + 2, :].unsqueeze(2).to_broadcast([128, 2, 2, 128]),
                    in1=DT[r][:].rearrange("p (a b) w -> p a b w", a=2), op=ALU.mult),
                    reads=["QKm", "DT%d" % r], writes=[("attnT", q)])
                yield

                def mm4(lhs, lk, rhs, rk):
                    b_, bk = bank()
                    for j in range(4):
                        S.op("pe", lambda e, b_=b_, j=j: e.matmul(b_[:, j * 128:(j + 1) * 128], lhs[:, j, :], rhs[:, j, :], start=True, stop=True),
                             reads=[lk, rk], writes=[bk])
                    return b_, bk

                def tr4(src, sk):
                    b_, bk = bank()
                    for j in range(4):
                        S.op("pe", lambda e, b_=b_, j=j: e.transpose(b_[:, j * 128:(j + 1) * 128], src[:, j, :], ident),
                             reads=[sk, "cst"], writes=[bk])
                    return b_, bk

                def ev_act(b_, bk, dst, dk):
                    S.op("act", lambda e: e.activation(out=fl(dst), in_=b_[:], func=AF.Identity), writes=[bk, dk])

                def ev_dve(b_, bk, dst, dk):
                    S.op("dve", lambda e: e.tensor_copy(out=fl(dst), in_=b_[:]), writes=[bk, dk])

                def acc_dve(b_, bk, dst, dk, out=None, ok=None):
                    o_ = fl(dst) if out is None else out
                    S.op("dve", lambda e: e.tensor_tensor(out=o_, in0=fl(dst), in1=b_[:], op=ALU.add),
                         reads=[dk], writes=[bk, dk if ok is None else ok])

                b_, bk = tr4(X, kX)
                ev_act(b_, bk, XT, kXT)
                S.op("dve", lambda e: e.tensor_tensor(out=E[:], in0=X[:], in1=bcm(C_BD8), op=ALU.mult), reads=[kX, "cst"], writes=[kE])
                S.op("dve", lambda e: e.tensor_tensor(out=Dv[:], in0=E[:], in1=bcm(C_ID), op=ALU.add), reads=[kE, "cst"], writes=[kDv])
                yield
                S.op("dve", lambda e: e.tensor_tensor(
                    out=XM[:], in0=XT[:].unsqueeze(1).to_broadcast([128, 5, 4, 128]),
                    in1=cst[:, C_BD8:C_BD8 + 5, :].unsqueeze(2).to_broadcast([128, 5, 4, 128]), op=ALU.mult),
                    reads=[kXT, "cst"], writes=[kXM])
                yield
                X0T = XM[:, 0, :, :]
                S.op("dve", lambda e: e.tensor_tensor(out=DvT[:], in0=X0T, in1=bcm(C_ID), op=ALU.add), reads=[kXM, "cst"], writes=[kDvT])
                b1, b1k = mm4(X0T, kXM, E, kE)
                b2, b2k = mm4(E, kE, X0T, kXM)
                ev_dve(b1, b1k, F, kF)
                ev_act(b2, b2k, X, kX)
                yield
                b1, b1k = mm4(X, kX, Dv, kDv)
                b2, b2k = mm4(Dv, kDv, X, kX)
                acc_dve(b1, b1k, Dv, kDv)
                acc_dve(b2, b2k, DvT, kDvT)
                b3_, b3k_ = mm4(F, kF, X, kX)
                ev_act(b3_, b3k_, E, kE)
                yield
                b1, b1k = mm4(E, kE, Dv, kDv)
                b2, b2k = mm4(Dv, kDv, E, kE)
                acc_dve(b1, b1k, Dv, kDv)
                acc_dve(b2, b2k, DvT, kDvT)
                yield
                for n, b in enumerate((8, 16, 32, 64)):
                    b_, bk = mm4(XM[:, 1 + n, :, :], kXM, Dv, kDv)
                    ev_act(b_, bk, E, kE)
                    yield
                    b1, b1k = mm4(DvT, kDvT, E, kE)
                    if b != 64:
                        b2, b2k = mm4(E, kE, DvT, kDvT)
                        acc_dve(b1, b1k, Dv, kDv)
                        acc_dve(b2, b2k, DvT, kDvT)
                        yield
                    else:
                        acc_dve(b1, b1k, Dv, kDv, out=T2T[:, qs, :].rearrange("p h w -> p (h w)"), ok=("T2T", q))
                        yield
                kq = k_tok[:, 2 * q * 128:(2 * q + 2) * 128].rearrange("p (a w) -> p a w", a=2).unsqueeze(2).to_broadcast([128, 2, 2, 128])
                S.op("pool", lambda e: e.tensor_tensor(
                    out=ke[:, qs, :].rearrange("p (a b) w -> p a b w", a=2), in0=kq,
                    in1=e_sb[:, 32 + q * 4:36 + q * 4].rearrange("p (a b) -> p a b", a=2).unsqueeze(3).to_broadcast([128, 2, 2, 128]),
                    op=ALU.mult), reads=["k_tok", "e_sb"], writes=[("ke", q)])
                S.op("pool", lambda e: e.tensor_tensor(
                    out=kf[:, qs, :].rearrange("p (a b) w -> p a b w", a=2), in0=kq,
                    in1=f_sb[:, 32 + q * 4:36 + q * 4].rearrange("p (a b) -> p a b", a=2).unsqueeze(3).to_broadcast([128, 2, 2, 128]),
                    op=ALU.mult), reads=["k_tok", "f_sb"], writes=[("kf", q)])
                wp, wpk = bank()
                for j in range(4):
                    hv = q * 4 + j
                    S.op("pe", lambda e, wp=wp, j=j, hv=hv: e.matmul(wp[:, j * 128:(j + 1) * 128], ke[:, hv, :], T2T[:, hv, :],
                                                                     start=True, stop=True),
                         reads=[("ke", q), ("T2T", q)], writes=[wpk])
                S.op("act", lambda e, wp=wp: e.activation(out=nwT[:, qs, :].rearrange("p h w -> p (h w)"), in_=wp[:],
                                                        func=AF.Identity, scale=-1.0), writes=[wpk, ("nwT", q)])
                yield
                qs = slice(q * 4, (q + 1) * 4)
                vi = rr["v"] % 2
                rr["v"] += 1
                vp, vpk = bank()
                for j in range(4):
                    hv = q * 4 + j
                    S.op("pe", lambda e, vp=vp, j=j, hv=hv: e.matmul(vp[:, j * 128:(j + 1) * 128], T2T[:, hv, :],
                                                                     v_tok[:, hv * 128:(hv + 1) * 128], start=True, stop=False),
                         reads=[("T2T", q), "v_tok"], writes=[vpk])
                    S.op("pe", lambda e, vp=vp, j=j, hv=hv: e.matmul(vp[:, j * 128:(j + 1) * 128], nwT[:, hv, :], GSb[:, hv, :],
                                                                     start=False, stop=True),
                         reads=[("nwT", q), ("GSb", q)], writes=[vpk])
                for j in range(4):
                    hv = q * 4 + j
                    S.op("act", lambda e, vp=vp, j=j, hv=hv, vi=vi: e.activation(
                        out=vnew[vi][:, j, :], in_=vp[:, j * 128:(j + 1) * 128], func=AF.Identity, scale=sm[:, 32 + hv:33 + hv]),
                        reads=[smk], writes=[vpk, "vnew%d" % vi])
                yield
                oi, oik = bank()
                for j in range(4):
                    hv = q * 4 + j
                    hq = hv // 2
                    S.op("pe", lambda e, oi=oi, j=j, hv=hv, hq=hq: e.matmul(oi[:, j * 128:(j + 1) * 128], qkv[:, hq, :], GSb[:, hv, :],
                                                                               start=True, stop=True),
                         reads=[qkk, ("GSb", q)], writes=[oik])
                oa, oak = bank()
                for j in range(4):
                    hv = q * 4 + j
                    S.op("pe", lambda e, oa=oa, j=j, hv=hv, vi=vi: e.matmul(oa[:, j * 128:(j + 1) * 128], attnT[:, hv, :], vnew[vi][:, j, :],
                                                                           start=True, stop=True),
                         reads=[("attnT", q), "vnew%d" % vi], writes=[oak])
                sn, snk = bank()
                for j in range(4):
                    hv = q * 4 + j
                    S.op("pe", lambda e, sn=sn, j=j, hv=hv, vi=vi: e.matmul(sn[:, j * 128:(j + 1) * 128], kf[:, hv, :], vnew[vi][:, j, :],
                                                                           start=True, stop=True),
                         reads=[("kf", q), "vnew%d" % vi], writes=[snk])
                for j in range(4):
                    hv = q * 4 + j
                    S.op("act", lambda e, oi=oi, j=j, hv=hv, vi=vi: e.activation(
                        out=osb[vi][:, j, :], in_=oi[:, j * 128:(j + 1) * 128], func=AF.Identity, scale=e_sb[:, 32 + hv:33 + hv]),
                        reads=["e_sb"], writes=[oik, "osb0"])
                S.op("dve", lambda e, oa=oa, vi=vi: e.tensor_tensor(
                    out=osb[vi][:].rearrange("p h w -> p (h w)"), in0=osb[vi][:].rearrange("p h w -> p (h w)"), in1=oa[:], op=ALU.add),
                    reads=["osb0"], writes=[oak, "osb0"])
                for j in range(4):
                    S.op("act", lambda e, j=j, vi=vi: e.activation(out=on[vi][:, j, :], in_=osb[vi][:, j, :], func=AF.Square,
                                                                 accum_out=ssq[vi][:, j:j + 1]),
                         reads=["osb0"], writes=["on0", "ssq%d" % vi])
                rsqrt_to(S, cst, ssq[vi][:], "ssq%d" % vi, ssq[vi][:], "ssq%d" % vi, 1.0 / 128)
                for j in range(4):
                    hv = q * 4 + j
                    S.op("dve", lambda e, j=j, vi=vi: e.scalar_tensor_tensor(
                        out=on[vi][:, j, :], in0=osb[vi][:, j, :], scalar=ssq[vi][:, j:j + 1], in1=rows[:, R_NW2:R_NW2 + 128],
                        op0=ALU.mult, op1=ALU.mult), reads=["osb0", "ssq%d" % vi, "rows"], writes=["on0"])
                S.op("pool", lambda e, vi=vi: e.tensor_tensor(
                    out=y[:, 2048 + q * 512:2048 + (q + 1) * 512], in0=on[vi][:].rearrange("p h w -> p (h w)"),
                    in1=z[:, 2048 + q * 512:2048 + (q + 1) * 512], op=ALU.mult), reads=["on0", zk], writes=[yk])
                for j in range(4):
                    hv = q * 4 + j
                    S.op("dve", lambda e, sn=sn, j=j, hv=hv: e.scalar_tensor_tensor(
                        out=GS[:, hv, :], in0=GS[:, hv, :], scalar=dA_sb[:, 32 + hv:33 + hv], in1=sn[:, j * 128:(j + 1) * 128],
                        op0=ALU.mult, op1=ALU.add), reads=[("GS", q), "dA_sb"], writes=[snk, ("GS", q)])
                S.op("act", lambda e, qs=qs: e.activation(out=GSb[:, qs, :], in_=GS[:, qs, :], func=AF.Identity),
                     reads=[("GS", q)], writes=[("GSb", q)])
                yield

            def ssd_group(g, sm=sm, smk=smk, xbc=xbc, xbk=xbk, z=z, zk=zk, y=y, yk=yk):
                gi = rr["g"] % 2
                rr["g"] += 1
                r = rr["am"] % 4
                rr["am"] += 1
                b2, b2k = bank()
                S.op("pe", lambda e: e.matmul(b2[:, 0:128], xbc[:, 16 + g, :], xbc[:, 24 + g, :], start=True, stop=True),
                     reads=[xbk], writes=[b2k])
                S.op("dve", lambda e: e.tensor_tensor(out=CBTm[gi][:], in0=b2[:, 0:128], in1=Umat, op=ALU.mult),
                     reads=["cst"], writes=[b2k, "CBTm%d" % gi])
                xs3 = v3(xs_tok[:, g * 256:(g + 1) * 256])
                S.op("dve", lambda e: e.tensor_tensor(
                    out=v3(xdt[gi][:]), in0=xs3, in1=bc(sm[:, g * 4:(g + 1) * 4], 4, 64), op=ALU.mult),
                    reads=["xs_tok", smk], writes=["xdt%d" % gi])
                S.op("dve", lambda e: e.tensor_tensor(
                    out=v3(xw[gi][:]), in0=v3(xdt[gi][:]), in1=bc(f_sb[:, g * 4:(g + 1) * 4], 4, 64), op=ALU.mult),
                    reads=["xdt%d" % gi, "f_sb"], writes=["xw%d" % gi])
                S.op("pool", lambda e: e.tensor_tensor(
                    out=v3(xsD[gi][:]), in0=xs3, in1=bc(rows[:, R_D1 + g * 4:R_D1 + (g + 1) * 4], 4, 64), op=ALU.mult),
                    reads=["xs_tok", "rows"], writes=["xsD%d" % gi])
                build_DT(48 + g * 4, r, sm, smk)
                yield
                S.op("dve", lambda e: e.tensor_tensor(out=MT[gi][:], in0=DT[r][:],
                                                      in1=CBTm[gi][:].unsqueeze(1).to_broadcast([128, 4, 128]), op=ALU.mult),
                     reads=["CBTm%d" % gi, "DT%d" % r], writes=["MT%d" % gi])
                b3, b3k = bank()
                for h in range(4):
                    S.op("pe", lambda e, h=h: e.matmul(b3[:, h * 64:(h + 1) * 64], MT[gi][:, h, :],
                                                       xdt[gi][:, h * 64:(h + 1) * 64], start=True, stop=True),
                         reads=["MT%d" % gi, "xdt%d" % gi], writes=[b3k])
                S.op("pe", lambda e: e.matmul(b3[:, 256:512], xbc[:, 24 + g, :], STb[:, g, :], start=True, stop=True),
                     reads=[xbk, ("STb", g)], writes=[b3k])
                b4, b4k = bank()
                S.op("pe", lambda e: e.matmul(b4[:, 256:512], b_tok[:, g * 128:(g + 1) * 128], xw[gi][:], start=True, stop=True),
                     reads=["b_tok", "xw%d" % gi], writes=[b4k])
                S.op("dve", lambda e: e.tensor_tensor(
                    out=v3(yacc[gi][:]), in0=v3(b3[:, 256:512]), in1=bc(e_sb[:, g * 4:(g + 1) * 4], 4, 64), op=ALU.mult),
                    reads=["e_sb"], writes=[b3k, "yacc%d" % gi])
                S.op("dve", lambda e: e.tensor_tensor(out=yacc[gi][:], in0=yacc[gi][:], in1=b3[:, 0:256], op=ALU.add),
                     reads=["yacc%d" % gi], writes=[b3k, "yacc%d" % gi])
                S.op("pool", lambda e: e.tensor_tensor(out=yacc[gi][:], in0=yacc[gi][:], in1=xsD[gi][:], op=ALU.add),
                     reads=["yacc%d" % gi, "xsD%d" % gi], writes=["yacc%d" % gi])
                S.op("dve", lambda e: e.tensor_tensor(out=yz[gi][:], in0=yacc[gi][:], in1=z[:, g * 256:(g + 1) * 256], op=ALU.mult),
                     reads=["yacc%d" % gi, zk], writes=["yz%d" % gi])
                S.op("act", lambda e: e.activation(out=xsD[gi][:], in_=yz[gi][:], func=AF.Square, accum_out=ssq[gi][:, 0:1]),
                     reads=["yz%d" % gi], writes=["xsD%d" % gi, "ssq%d" % gi])
                rsqrt_to(S, cst, ssq[gi][:, 0:1], "ssq%d" % gi, ssq[gi][:, 0:1], "ssq%d" % gi, 1.0 / 256)
                S.op("dve", lambda e: e.scalar_tensor_tensor(
                    out=y[:, g * 256:(g + 1) * 256], in0=yz[gi][:], scalar=ssq[gi][:, 0:1],
                    in1=rows[:, R_NW1 + g * 256:R_NW1 + (g + 1) * 256], op0=ALU.mult, op1=ALU.mult),
                    reads=["yz%d" % gi, "ssq%d" % gi, "rows"], writes=[yk])
                S.op("pool", lambda e: e.tensor_tensor(
                    out=v3(ST[:, g, :]), in0=v3(ST[:, g, :]), in1=bc(dA_sb[:, g * 4:(g + 1) * 4], 4, 64), op=ALU.mult),
                    reads=[("ST", g), "dA_sb"], writes=[("ST", g)])
                S.op("dve", lambda e: e.tensor_tensor(out=ST[:, g, :], in0=ST[:, g, :], in1=b4[:, 256:512], op=ALU.add),
                     reads=[("ST", g)], writes=[b4k, ("ST", g)])
                S.op("act", lambda e: e.activation(out=STb[:, g, :], in_=ST[:, g, :], func=AF.Identity),
                     reads=[("ST", g)], writes=[("STb", g)])
                yield

            pend_g = [gdn_quad(q, q % 2) for q in range(4)]
            pend_s = [ssd_group(g) for g in range(8)]
            live = []
            slots = {"g": 0, "s": 0}

            def refill():
                while slots["g"] < 2 and pend_g:
                    live.append(("g", pend_g.pop(0)))
                    slots["g"] += 1
                while slots["s"] < 2 and pend_s:
                    live.append(("s", pend_s.pop(0)))
                    slots["s"] += 1
            refill()
            while live:
                for item in list(live):
                    kind, g_ = item
                    try:
                        next(g_)
                    except StopIteration:
                        live.remove(item)
                        slots[kind] -= 1
                refill()

            S.dma(Y[tt * 128:(tt + 1) * 128, :], y[:], reads=[yk], writes=["Y"])
            if tt + 1 < NT:
                load_z(tt + 1)
        S.barrier()


def load_weight_bf16(nc, S, dst, dstk, src_v, nk, ncols, stg, stgk):
    cnt = 0
    kc = 2
    for k0 in range(0, nk, kc):
        for c0 in range(0, ncols, 512):
            s_ = stg[cnt % 2]
            sk = stgk[cnt % 2]
            cnt += 1
            S.dma(s_[:], src_v[:, k0:k0 + kc, c0:c0 + 512], writes=[sk])
            if cnt % 2 == 0:
                S.op("act", lambda e, s_=s_, k0=k0, c0=c0: e.activation(out=dst[:, k0:k0 + kc, c0:c0 + 512], in_=s_[:], func=AF.Identity),
                     reads=[sk], writes=[dstk])
            else:
                S.op("dve", lambda e, s_=s_, k0=k0, c0=c0: e.tensor_copy(out=dst[:, k0:k0 + kc, c0:c0 + 512], in_=s_[:]),
                     reads=[sk], writes=[dstk])


def phase4a(nc, S, T, Y, G_T, XT, X2T, w_su, w_gu, w_out, cst, cstb, pv):
    BW = 256
    NBW = T // BW
    NA = BW // 128
    identb = cstb[:, C_ID, :]
    onesD = cst[:, C_ONED, :]
    with ExitStack() as ps:
        lsb = lambda name, shape, dt=F32: ps.enter_context(nc.sbuf_tensor(name, shape, dt))
        Wsu = lsb("Wsu", [128, 16, D], BF16)
        Wgu = lsb("Wgu", [128, 16, D], BF16)
        Wo = lsb("Wo", [128, 8, D], BF16)
        stg = [lsb("stg%d" % i, [128, 2, 512]) for i in range(2)]
        ytoks = [lsb("ytok%d" % i, [128, NA, 4096], BF16) for i in range(2)]
        yT = lsb("yT", [128, 32, BW], BF16)
        gTs = [lsb("gT%d" % i, [128, 16, BW], BF16) for i in range(2)]
        xTs = [lsb("xT4_%d" % i, [128, 8, BW]) for i in range(2)]
        t1 = [lsb("m_t1_%d" % i, [128, BW]) for i in range(2)]
        t2 = [lsb("m_t2_%d" % i, [128, BW]) for i in range(2)]
        mT = lsb("mT", [128, 8, BW], BF16)
        m2s = lsb("m2s", [128, 8, BW])
        sqm = lsb("sqm", [128, 8, BW])
        rstd = lsb("rstd4", [128, BW])
        pu = [ps.enter_context(nc.psum_tensor("pu%d" % i, [128, 512], F32)) for i in range(4)]
        pt = [ps.enter_context(nc.psum_tensor("ptb%d" % i, [128, 1024], BF16)) for i in range(2)]
        pss = ps.enter_context(nc.psum_tensor("pss4", [128, 512], F32))
        load_weight_bf16(nc, S, Wsu, "Wsu", w_su.rearrange("(k p) f -> p k f", p=128), 16, D, stg, ["stg0", "stg1"])
        load_weight_bf16(nc, S, Wgu, "Wgu", w_gu.rearrange("(k p) f -> p k f", p=128), 16, D, stg, ["stg0", "stg1"])
        load_weight_bf16(nc, S, Wo, "Wo", w_out.rearrange("(k p) f -> p k f", p=128), 8, D, stg, ["stg0", "stg1"])
        GTv = G_T.rearrange("(b p) t -> p b t", p=128)
        XTv = XT.rearrange("(k p) t -> p k t", p=128)
        X2Tv = X2T.rearrange("(k p) t -> p k t", p=128)
        pc = 0
        def loads4(nb):
            t0 = nb * BW
            i = nb % 2
            S.dma(ytoks[i][:], Y[t0:t0 + BW, :].rearrange("(a p) f -> p a f", p=128), reads=["Y"], writes=["ytok%d" % i])
            S.dma(gTs[i][:], GTv[:, :, t0:t0 + BW], reads=["G_T"], writes=["gT%d" % i])
            S.dma(xTs[i][:], XTv[:, :, t0:t0 + BW], reads=["XT"], writes=["xT4_%d" % i])
        loads4(0)
        for nb in range(NBW):
            t0 = nb * BW
            ytok, gT, xT = ytoks[nb % 2], gTs[nb % 2], xTs[nb % 2]
            ytk, gTk, xTk = "ytok%d" % (nb % 2), "gT%d" % (nb % 2), "xT4_%d" % (nb % 2)
            if nb + 1 < NBW:
                loads4(nb + 1)
            for cb in range(32):
                p_ = pt[cb % 2]
                pk = "ptb%d" % (cb % 2)
                for a in range(NA):
                    S.op("pe", lambda e, p_=p_, a=a, cb=cb, ytok=ytok: e.transpose(p_[:, a * 128:(a + 1) * 128],
                                                                      ytok[:, a, cb * 128:(cb + 1) * 128], identb),
                         reads=[ytk, "cstb"], writes=[pk])
                if cb % 2 == 0:
                    S.op("act", lambda e, p_=p_, cb=cb: e.activation(out=yT[:, cb, :], in_=p_[:, 0:BW], func=AF.Identity),
                         reads=[pk], writes=["yT"])
                else:
                    S.op("dve", lambda e, p_=p_, cb=cb: e.tensor_copy(out=yT[:, cb, :], in_=p_[:, 0:BW]), reads=[pk], writes=["yT"])
            for blk in range(8):
                p1 = pu[pc % 4]
                p1k = "pu%d" % (pc % 4)
                pc += 1
                p2 = pu[pc % 4]
                p2k = "pu%d" % (pc % 4)
                pc += 1
                for k in range(16):
                    S.op("pe", lambda e, p1=p1, k=k, blk=blk: e.matmul(p1[:, 0:BW], Wsu[:, k, blk * 128:(blk + 1) * 128], yT[:, k, :],
                                                                       start=(k == 0), stop=(k == 15)),
                         reads=["Wsu", "yT"], writes=[p1k])
                for k in range(16):
                    S.op("pe", lambda e, p2=p2, k=k, blk=blk: e.matmul(p2[:, 0:BW], Wgu[:, k, blk * 128:(blk + 1) * 128], yT[:, 16 + k, :],
                                                                       start=(k == 0), stop=(k == 15)),
                         reads=["Wgu", "yT"], writes=[p2k])
                a1 = t1[blk % 2]
                a1k = "m_t1_%d" % (blk % 2)
                a2 = t2[blk % 2]
                a2k = "m_t2_%d" % (blk % 2)
                S.op("dve", lambda e, p1=p1, a1=a1, blk=blk, gT=gT: e.tensor_tensor(out=a1[:], in0=p1[:, 0:BW], in1=gT[:, blk, :], op=ALU.mult),
                     reads=[p1k, gTk], writes=[a1k])
                S.op("dve", lambda e, p2=p2, a2=a2, blk=blk, gT=gT: e.tensor_tensor(out=a2[:], in0=p2[:, 0:BW], in1=gT[:, 8 + blk, :], op=ALU.mult),
                     reads=[p2k, gTk], writes=[a2k])
                S.op("pool", lambda e, a1=a1, a2=a2, blk=blk: e.tensor_tensor(out=mT[:, blk, :], in0=a1[:], in1=a2[:], op=ALU.add),
                     reads=[a1k, a2k], writes=["mT"])
            for blk in range(8):
                p1 = pu[pc % 4]
                p1k = "pu%d" % (pc % 4)
                pc += 1
                for k in range(8):
                    S.op("pe", lambda e, p1=p1, k=k, blk=blk: e.matmul(p1[:, 0:BW], Wo[:, k, blk * 128:(blk + 1) * 128], mT[:, k, :],
                                                                       start=(k == 0), stop=(k == 7)),
                         reads=["Wo", "mT"], writes=[p1k])
                S.op("act", lambda e, p1=p1, blk=blk: e.activation(out=m2s[:, blk, :], in_=p1[:, 0:BW], func=AF.Identity),
                     reads=[p1k], writes=["m2s"])
                S.op("act", lambda e, p1=p1, blk=blk: e.activation(out=sqm[:, blk, :], in_=p1[:, 0:BW], func=AF.Square),
                     reads=[p1k], writes=["sqm"])
            for k in range(8):
                S.op("pe", lambda e, k=k: e.matmul(pss[:, 0:BW], onesD, sqm[:, k, :], start=(k == 0), stop=(k == 7)),
                     reads=["sqm", "cst"], writes=["pss4"])
            rsqrt_to(S, cst, rstd[:], "rstd4", pss[:, 0:BW], "pss4")
            for blk in range(8):
                a1 = t1[blk % 2]
                a1k = "m_t1_%d" % (blk % 2)
                S.op("pool", lambda e, a1=a1, blk=blk: e.tensor_tensor(out=a1[:], in0=m2s[:, blk, :], in1=rstd[:], op=ALU.mult),
                     reads=["m2s", "rstd4"], writes=[a1k])
                S.op("dve", lambda e, a1=a1, blk=blk, xT=xT: e.scalar_tensor_tensor(
                    out=xT[:, blk, :], in0=a1[:], scalar=pv[:, 2, blk:blk + 1], in1=xT[:, blk, :], op0=ALU.mult, op1=ALU.add),
                    reads=[a1k, "pv", xTk], writes=[xTk])
            S.dma(X2Tv[:, :, t0:t0 + BW], xT[:], reads=[xTk], writes=["X2T"])
        S.barrier()


def phase4b(nc, S, T, X2T, out, w_up, w_dn, cst, pv, normT):
    BW = 256
    NBW = T // BW
    NA = BW // 128
    ident = cst[:, C_ID, :]
    onesD = cst[:, C_ONED, :]
    with ExitStack() as ps:
        lsb = lambda name, shape, dt=F32: ps.enter_context(nc.sbuf_tensor(name, shape, dt))
        Wup = lsb("Wup", [128, 8, 4096], BF16)
        Wdn = lsb("Wdn", [128, 32, D], BF16)
        stg = [lsb("stgb%d" % i, [128, 2, 512]) for i in range(2)]
        xTs = [lsb("x5T%d" % i, [128, 8, BW]) for i in range(2)]
        sq = lsb("sq5", [128, 8, BW])
        rstd = lsb("rstd5", [128, BW])
        tmp = [lsb("tmp5_%d" % i, [128, BW]) for i in range(2)]
        h2T = lsb("h2T", [128, 8, BW], BF16)
        rl = [lsb("rl%d" % i, [128, BW]) for i in range(2)]
        actT = lsb("actT", [128, 32, BW], BF16)
        dns = lsb("dns", [128, 8, BW])
        otok = [lsb("otok%d" % i, [128, 512]) for i in range(2)]
        pu = [ps.enter_context(nc.psum_tensor("p5u%d" % i, [128, 512], F32)) for i in range(4)]
        pss = ps.enter_context(nc.psum_tensor("pss5", [128, 512], F32))
        po = [ps.enter_context(nc.psum_tensor("p5o%d" % i, [128, 512], F32)) for i in range(2)]
        load_weight_bf16(nc, S, Wup, "Wup", w_up.rearrange("(k p) f -> p k f", p=128), 8, 4096, stg, ["stgb0", "stgb1"])
        load_weight_bf16(nc, S, Wdn, "Wdn", w_dn.rearrange("(k p) f -> p k f", p=128), 32, D, stg, ["stgb0", "stgb1"])
        X2Tv = X2T.rearrange("(k p) t -> p k t", p=128)
        pc = 0
        oc = 0
        S.dma(xTs[0][:], X2Tv[:, :, 0:BW], reads=["X2T"], writes=["x5T0"])
        for nb in range(NBW):
            t0 = nb * BW
            xT = xTs[nb % 2]
            xk = "x5T%d" % (nb % 2)
            if nb + 1 < NBW:
                S.dma(xTs[(nb + 1) % 2][:], X2Tv[:, :, t0 + BW:t0 + 2 * BW], reads=["X2T"], writes=["x5T%d" % ((nb + 1) % 2)])
            normT(xT, xk, sq, "sq5", rstd, "rstd5", pss, "pss5", tmp, ["tmp5_0", "tmp5_1"],
                  lambda k: h2T[:, k, :], "h2T", 3, 4, BW)
            for blk in range(32):
                p1 = pu[pc % 4]
                p1k = "p5u%d" % (pc % 4)
                pc += 1
                for k in range(8):
                    S.op("pe", lambda e, p1=p1, k=k, blk=blk: e.matmul(p1[:, 0:BW], Wup[:, k, blk * 128:(blk + 1) * 128], h2T[:, k, :],
                                                                       start=(k == 0), stop=(k == 7)),
                         reads=["Wup", "h2T"], writes=[p1k])
                r_ = rl[blk % 2]
                rk = "rl%d" % (blk % 2)
                S.op("act", lambda e, p1=p1, r_=r_: e.activation(out=r_[:], in_=p1[:, 0:BW], func=AF.Relu), reads=[p1k], writes=[rk])
                eng = "pool" if blk % 2 == 0 else "dve"
                S.op(eng, lambda e, r_=r_, blk=blk: e.tensor_tensor(out=actT[:, blk, :], in0=r_[:], in1=r_[:], op=ALU.mult),
                     reads=[rk], writes=["actT"])
            for blk in range(8):
                p1 = pu[pc % 4]
                p1k = "p5u%d" % (pc % 4)
                pc += 1
                for k in range(32):
                    S.op("pe", lambda e, p1=p1, k=k, blk=blk: e.matmul(p1[:, 0:BW], Wdn[:, k, blk * 128:(blk + 1) * 128], actT[:, k, :],
                                                                       start=(k == 0), stop=(k == 31)),
                         reads=["Wdn", "actT"], writes=[p1k])
                S.op("act", lambda e, p1=p1, blk=blk: e.activation(out=dns[:, blk, :], in_=p1[:, 0:BW], func=AF.Identity),
                     reads=[p1k], writes=["dns"])
                S.op("act", lambda e, p1=p1, blk=blk: e.activation(out=sq[:, blk, :], in_=p1[:, 0:BW], func=AF.Square),
                     reads=[p1k], writes=["sq5"])
            for k in range(8):
                S.op("pe", lambda e, k=k: e.matmul(pss[:, 0:BW], onesD, sq[:, k, :], start=(k == 0), stop=(k == 7)),
                     reads=["sq5", "cst"], writes=["pss5"])
            rsqrt_to(S, cst, rstd[:], "rstd5", pss[:, 0:BW], "pss5")
            for blk in range(8):
                t_ = tmp[blk % 2]
                tk = "tmp5_%d" % (blk % 2)
                S.op("pool", lambda e, t_=t_, blk=blk: e.tensor_tensor(out=t_[:], in0=dns[:, blk, :], in1=rstd[:], op=ALU.mult),
                     reads=["dns", "rstd5"], writes=[tk])
                S.op("dve", lambda e, t_=t_, blk=blk, xT=xT: e.scalar_tensor_tensor(
                    out=dns[:, blk, :], in0=t_[:], scalar=pv[:, 5, blk:blk + 1], in1=xT[:, blk, :], op0=ALU.mult, op1=ALU.add),
                    reads=[tk, "pv", xk, "dns"], writes=["dns"])
            for a in range(NA):
                for half in range(2):
                    ot = otok[oc % 2]
                    otk = "otok%d" % (oc % 2)
                    oc += 1
                    p_ = po[half]
                    pk = "p5o%d" % half
                    for b4 in range(4):
                        blk = half * 4 + b4
                        S.op("pe", lambda e, p_=p_, b4=b4, blk=blk, a=a: e.transpose(
                            p_[:, b4 * 128:(b4 + 1) * 128], dns[:, blk, a * 128:(a + 1) * 128], ident),
                            reads=["dns", "cst"], writes=[pk])
                    if half == 0:
                        S.op("act", lambda e, p_=p_, ot=ot: e.activation(out=ot[:], in_=p_[:], func=AF.Identity),
                             reads=[pk], writes=[otk])
                    else:
                        S.op("dve", lambda e, p_=p_, ot=ot: e.tensor_copy(out=ot[:], in_=p_[:]), reads=[pk], writes=[otk])
                    S.dma(out[t0 + a * 128:t0 + (a + 1) * 128, half * 512:(half + 1) * 512], ot[:], reads=[otk], writes=["out"])
        S.barrier()


def host_inputs(inputs, b, T):
    f = lambda a: np.ascontiguousarray(np.asarray(a, dtype=np.float32))
    col = lambda v: f(np.asarray(v).reshape(-1, 128).T)
    nw = np.stack([col(inputs["norm_mix_pre"][0]), col(inputs["norm_mix_post"][0]),
                   col(inputs["norm_mlp_pre"][0]), col(inputs["norm_mlp_post"][0])], axis=1)
    cws = np.concatenate([np.asarray(inputs["ssm_conv_w"][0]), np.asarray(inputs["ssm_conv_b"])], axis=0)
    cws = cws.reshape(5, 32, 128).transpose(2, 1, 0)
    cwg = np.asarray(inputs["gdn_conv_w"][0]).reshape(4, 32, 128).transpose(2, 1, 0)
    rowv = np.concatenate([np.asarray(inputs["ssm_dt_bias"][0]), np.asarray(inputs["ssm_A_log"][0]), np.asarray(inputs["ssm_D"][0]),
                           np.asarray(inputs["gdn_dt_bias"][0]), np.asarray(inputs["gdn_A_log"][0]),
                           np.asarray(inputs["ssm_norm_w"][0]), np.asarray(inputs["gdn_norm_w"][0])])[None, :]
    return {
        "x": f(np.asarray(inputs["x"])[b, :T]),
        "c_col": col(np.asarray(inputs["c"])[b]),
        "w_ada": f(inputs["w_ada"][0]),
        "b_ada_col": col(inputs["b_ada"][0]),
        "nw_col": f(nw),
        "w_in": f(inputs["w_in"][0]),
        "cw_ssm": f(cws),
        "cw_gdn": f(cwg),
        "rowv": f(rowv),
        "w_su": f(inputs["w_ssm_up"][0]),
        "w_gu": f(inputs["w_gdn_up"][0]),
        "w_out": f(inputs["w_out"][0]),
        "w_up": f(inputs["w_mlp_up"][0]),
        "w_dn": f(inputs["w_mlp_down"][0]),
        "consts": make_consts(),
    }


def kernel(**inputs):
    T = 4096
    nc, S = build(T)
    shared = None
    in_maps = []
    for b in range(8):
        m = host_inputs(inputs, b, T)
        if shared is None:
            shared = m
        else:
            for k in m:
                if k not in ("x", "c_col"):
                    m[k] = shared[k]
        in_maps.append(m)
    res = run_bass_kernel_spmd(nc, in_maps, core_ids=list(range(8)))
    return np.stack([np.asarray(r["out"], dtype=np.float32) for r in res.results], axis=0)
```

```python
import numpy as np
import ml_dtypes
from contextlib import ExitStack
import concourse.bass as bass
import concourse.mybir as mybir
from concourse.bass_utils import run_bass_kernel_spmd

F32 = mybir.dt.float32
BF16 = mybir.dt.bfloat16
AF = mybir.ActivationFunctionType
ALU = mybir.AluOpType

D = 1024
EPS = 1e-6
COMPUTE = ("pe", "act", "dve", "pool")
NSLOT = 8


STRICT = False


class Sched:
    def __init__(self, nc, st):
        self.nc = nc
        self.st = st
        self.streams = {e: [] for e in ("pe", "act", "dve", "pool", "sp")}
        self.sem = {}
        for e in COMPUTE:
            self.sem[e] = st.enter_context(nc.semaphore("c_" + e))
        for i in range(NSLOT):
            self.sem[("sp", i)] = st.enter_context(nc.semaphore("d_sp%d" % i))
        self.count = {k: 0 for k in self.sem}
        self.dma_idx = 0
        self.known = {e: {} for e in self.streams}
        self.clock = {}
        self.last_write = {}
        self.readers = {}
        self.ninstr = 0
        self.nwaits = 0

    def _need(self, eng, ev, waits):
        c, n = ev
        if self.known[eng].get(c, 0) >= n:
            return
        if waits.get(c, 0) < n:
            waits[c] = n

    def _deps(self, eng, reads, writes):
        waits = {}
        for k in reads:
            ev = self.last_write.get(k)
            if ev is not None:
                if ev[0] == eng and eng == "pe":
                    continue
                self._need(eng, ev, waits)
        for k in writes:
            ev = self.last_write.get(k)
            if ev is not None and (STRICT and eng != "pe" or not (ev[0] == eng and eng in COMPUTE)):
                self._need(eng, ev, waits)
            for rv in self.readers.get(k, ()):
                if rv[0] == eng and eng in COMPUTE and not STRICT:
                    continue
                self._need(eng, rv, waits)
        return waits

    def _apply(self, eng, waits):
        kn = self.known[eng]
        for c, n in waits.items():
            ck = self.clock.get((c, n))
            if ck:
                for cc, nn in ck.items():
                    if kn.get(cc, 0) < nn:
                        kn[cc] = nn
            if kn.get(c, 0) < n:
                kn[c] = n

    def _record(self, ev, eng, reads, writes):
        ck = dict(self.known[eng])
        ck[ev[0]] = ev[1]
        self.clock[ev] = ck
        for k in reads:
            self.readers.setdefault(k, []).append(ev)
        for k in writes:
            self.last_write[k] = ev
            self.readers[k] = []

    def op(self, eng, fn, reads=(), writes=()):
        waits = self._deps(eng, reads, writes)
        self._apply(eng, waits)
        self.count[eng] += 1
        ev = (eng, self.count[eng])
        self._record(ev, eng, reads, writes)
        self.streams[eng].append((list(waits.items()), fn, (eng, 1)))
        self.ninstr += 1
        self.nwaits += len(waits)
        return ev

    def dma(self, out, in_, reads=(), writes=()):
        q = "sp"
        slot = (q, self.dma_idx % NSLOT)
        self.dma_idx += 1
        waits = self._deps(q, reads, writes)
        if self.count[slot] > 0:
            self._need(q, (slot, self.count[slot]), waits)
        self._apply(q, waits)
        self.count[slot] += 1
        ev = (slot, self.count[slot])
        self._record(ev, q, reads, writes)
        fn = lambda e, out=out, in_=in_: e.dma_start(out=out, in_=in_)
        self.streams[q].append((list(waits.items()), fn, (slot, 16)))
        self.ninstr += 1
        self.nwaits += len(waits)
        return ev

    def barrier(self):
        for eng in self.streams:
            waits = {}
            for c, n in self.count.items():
                if n > 0 and c != eng:
                    self._need(eng, (c, n), waits)
            self._apply(eng, waits)
            self.streams[eng].append((list(waits.items()), None, None))
        self.last_write = {}
        self.readers = {}

    def emit(self):
        nc = self.nc
        block = self.st.enter_context(nc.Block())
        sem = self.sem

        def run(stream):
            def body(e):
                for waits, fn, inc in stream:
                    for c, n in waits:
                        e.wait_ge(sem[c], n * (1 if c in COMPUTE else 16))
                    if fn is not None:
                        fn(e).then_inc(sem[inc[0]], inc[1])
            return body

        block.tensor(run(self.streams["pe"]))
        block.scalar(run(self.streams["act"]))
        block.vector(run(self.streams["dve"]))
        block.gpsimd(run(self.streams["pool"]))
        block.sync(run(self.streams["sp"]))


OFF_Z1 = 0
OFF_XBC = 2048
OFF_DT = 6144
OFF_QKV = 6176
OFF_Z2 = 10272
OFF_B = 12320
OFF_A = 12336
OFF_GS = 12352
OFF_GG = 13376

C_ID, C_U, C_GT, C_SU, C_ONE, C_ONED, C_EPS, C_BD8, C_CMT, NCONST = 0, 1, 2, 3, 4, 5, 6, 7, 8, 12
R_DTB1, R_AL1, R_D1, R_DTB2, R_AL2, R_NW1, R_NW2, RLEN = 0, 32, 64, 96, 112, 128, 2176, 2304


def make_consts():
    k = np.arange(128)[:, None]
    l = np.arange(128)[None, :]
    c = np.zeros((128, NCONST, 128), np.float32)
    c[:, C_ID] = (k == l)
    c[:, C_U] = (k <= l)
    c[:, C_GT] = (k > l)
    c[:, C_SU] = (l > k)
    c[:, C_ONE] = 1.0
    c[:, C_ONED] = 1.0 / D
    c[:, C_EPS] = EPS
    c[:, C_BD8] = (k // 8 == l // 8)
    for n, b in enumerate((8, 16, 32, 64)):
        cm = ((k // (2 * b) == l // (2 * b)) & ((k // b) % 2 == 0) & ((l // b) % 2 == 1))
        c[:, C_CMT + n] = cm.T
    return c


def rsqrt_to(S, cst, dst, dstk, src, srck, scale=1.0):
    S.op("act", lambda e: e.activation(out=dst, in_=src, func=AF.Sqrt, bias=cst[:, C_EPS, 0:1], scale=scale),
         reads=[srck, "cst"], writes=[dstk])
    S.op("dve", lambda e: e.reciprocal(out=dst, in_=dst), reads=[dstk], writes=[dstk])


def build(T, phases=(0, 1, 2, 3, 4, 5), debug=False):
    NT = T // 128
    NB = T // 512
    nc = bass.Bass("TRN2", target_bir_lowering=False)

    def din(name, shape, dt=F32):
        return nc.dram_tensor(name, shape, dt, kind="ExternalInput").ap()

    def dscr(name, shape, dt):
        return nc.dram_tensor(name, shape, dt, kind="ExternalOutput").ap()

    x = din("x", [T, D])
    c_col = din("c_col", [128, 8])
    w_ada = din("w_ada", [D, 6 * D])
    b_ada_col = din("b_ada_col", [128, 48])
    nw_col = din("nw_col", [128, 4, 8])
    w_in = din("w_in", [D, 14400])
    cw_ssm = din("cw_ssm", [128, 32, 5])
    cw_gdn = din("cw_gdn", [128, 32, 4])
    rowv = din("rowv", [1, RLEN])
    w_su = din("w_su", [2048, D])
    w_gu = din("w_gu", [2048, D])
    w_out = din("w_out", [D, D])
    w_up = din("w_up", [D, 4096])
    w_dn = din("w_dn", [4096, D])
    consts = din("consts", [128, NCONST, 128])
    out = nc.dram_tensor("out", [T, D], F32, kind="ExternalOutput").ap()

    XT = dscr("s_xt", [D, T], F32)
    XBC_T = dscr("s_xbct", [4096, T], BF16)
    QKV_T = dscr("s_qkvt", [4096, T], BF16)
    G_T = dscr("s_gt", [2048, T], BF16)
    Z = dscr("s_z", [T, 4096], BF16)
    SM = dscr("s_sm", [T, 96], F32)
    if debug:
        Y = dscr("s_y", [T, 4096], BF16)
        X2T = dscr("s_x2t", [D, T], F32)
        MOD = dscr("s_mod", [128, 48], F32)
    else:
        Y = Z
        X2T = XT
        MOD = None

    with ExitStack() as st:
        S = Sched(nc, st)
        sb = lambda name, shape, dt=F32: st.enter_context(nc.sbuf_tensor(name, shape, dt))
        cst = sb("cst", [128, NCONST, 128])
        cstb = sb("cstb", [128, 1, 128], BF16)
        pv = sb("pv", [128, 6, 8])

        S.dma(cst[:], consts, writes=["cst"])
        S.op("pool", lambda e: e.tensor_copy(out=cstb[:], in_=cst[:, 0:1, :]), reads=["cst"], writes=["cstb"])
        ident = cst[:, C_ID, :]
        identb = cstb[:, C_ID, :]
        Umat = cst[:, C_U, :]
        GTm = cst[:, C_GT, :]
        SUm = cst[:, C_SU, :]
        ones = cst[:, C_ONE, :]
        onesD = cst[:, C_ONED, :]
        if 0 in phases:
            with ExitStack() as ps:
                lsb = lambda name, shape, dt=F32: ps.enter_context(nc.sbuf_tensor(name, shape, dt))
                cact = lsb("cact", [128, 8])
                csig = lsb("csig", [128, 8])
                wa = [lsb("wa%d" % i, [128, 8, 512]) for i in range(2)]
                modsb = lsb("modsb", [128, 48])
                bada = lsb("bada", [128, 48])
                nwc = lsb("nwc", [128, 4, 8])
                modps = ps.enter_context(nc.psum_tensor("modps", [128, 512], F32))
                S.dma(cact[:], c_col, writes=["cact"])
                S.dma(bada[:], b_ada_col, writes=["bada"])
                S.dma(nwc[:], nw_col, writes=["nwc"])
                S.op("act", lambda e: e.activation(out=csig[:], in_=cact[:], func=AF.Sigmoid), reads=["cact"], writes=["csig"])
                S.op("dve", lambda e: e.tensor_tensor(out=cact[:], in0=cact[:], in1=csig[:], op=ALU.mult),
                     reads=["cact", "csig"], writes=["cact"])
                wav = w_ada.rearrange("(k p) f -> p k f", p=128)
                for fb in range(12):
                    w = wa[fb % 2]
                    wk = "wa%d" % (fb % 2)
                    S.dma(w[:], wav[:, :, fb * 512:(fb + 1) * 512], writes=[wk])
                    for j in range(4):
                        col = fb * 4 + j
                        for k in range(8):
                            S.op("pe", lambda e, w=w, j=j, k=k, col=col: e.matmul(
                                modps[:, col:col + 1], w[:, k, j * 128:(j + 1) * 128], cact[:, k:k + 1],
                                start=(k == 0), stop=(k == 7)), reads=[wk, "cact"], writes=["modps"])
                S.op("dve", lambda e: e.tensor_tensor(out=modsb[:], in0=modps[:, 0:48], in1=bada[:], op=ALU.add),
                     reads=["modps", "bada"], writes=["modsb"])
                S.op("dve", lambda e: e.scalar_tensor_tensor(out=pv[:, 0, :], in0=modsb[:, 8:16], scalar=1.0, in1=nwc[:, 0, :],
                                                             op0=ALU.add, op1=ALU.mult), reads=["modsb", "nwc"], writes=["pv"])
                S.op("dve", lambda e: e.tensor_copy(out=pv[:, 1, :], in_=modsb[:, 0:8]), reads=["modsb"], writes=["pv"])
                S.op("dve", lambda e: e.tensor_tensor(out=pv[:, 2, :], in0=modsb[:, 16:24], in1=nwc[:, 1, :], op=ALU.mult),
                     reads=["modsb", "nwc"], writes=["pv"])
                S.op("dve", lambda e: e.scalar_tensor_tensor(out=pv[:, 3, :], in0=modsb[:, 32:40], scalar=1.0, in1=nwc[:, 2, :],
                                                             op0=ALU.add, op1=ALU.mult), reads=["modsb", "nwc"], writes=["pv"])
                S.op("dve", lambda e: e.tensor_copy(out=pv[:, 4, :], in_=modsb[:, 24:32]), reads=["modsb"], writes=["pv"])
                S.op("dve", lambda e: e.tensor_tensor(out=pv[:, 5, :], in0=modsb[:, 40:48], in1=nwc[:, 3, :], op=ALU.mult),
                     reads=["modsb", "nwc"], writes=["pv"])
                if debug:
                    S.dma(MOD, modsb[:], reads=["modsb"], writes=["MOD"])
                S.barrier()

        def normT(xT, xk, sq, sqk, rstd, rstdk, ssps, sspsk, tmp, tmpk, hdst, hk, ia, ish, W=512):
            S.op("act", lambda e: e.activation(out=sq[:], in_=xT[:], func=AF.Square), reads=[xk], writes=[sqk])
            for k in range(8):
                S.op("pe", lambda e, k=k: e.matmul(ssps[:, 0:W], onesD, sq[:, k, :], start=(k == 0), stop=(k == 7)),
                     reads=[sqk, "cst"], writes=[sspsk])
            rsqrt_to(S, cst, rstd[:], rstdk, ssps[:, 0:W], sspsk)
            for k in range(8):
                t = tmp[k % 2]
                tk = tmpk[k % 2]
                S.op("dve", lambda e, k=k, t=t: e.tensor_tensor(out=t[:], in0=xT[:, k, :], in1=rstd[:], op=ALU.mult),
                     reads=[xk, rstdk], writes=[tk])
                S.op("act", lambda e, k=k, t=t: e.activation(out=hdst(k), in_=t[:], func=AF.Identity,
                                                           bias=pv[:, ish, k:k + 1], scale=pv[:, ia, k:k + 1]),
                     reads=[tk, "pv"], writes=[hk])

        hT_cm = None
        hst = ExitStack()
        if 1 in phases or 2 in phases:
            hT_cm = hst.enter_context(nc.sbuf_tensor("hT", [128, 8, T], BF16))

        if 1 in phases:
            with ExitStack() as ps:
                lsb = lambda name, shape, dt=F32: ps.enter_context(nc.sbuf_tensor(name, shape, dt))
                xtok = [lsb("xtok%d" % i, [128, 4, D]) for i in range(2)]
                xTb = [lsb("xTb%d" % i, [128, 8, 512]) for i in range(2)]
                sq = lsb("sq1", [128, 8, 512])
                rstd = lsb("rstd1", [128, 512])
                tmp = [lsb("tmp1_%d" % i, [128, 512]) for i in range(2)]
                tps = [ps.enter_context(nc.psum_tensor("tps%d" % i, [128, 512], F32)) for i in range(4)]
                ssps = ps.enter_context(nc.psum_tensor("ssps1", [128, 512], F32))
                XTv = XT.rearrange("(k p) t -> p k t", p=128)
                for nb in range(NB):
                    xt = xtok[nb % 2]
                    xtk = "xtok%d" % (nb % 2)
                    xT = xTb[nb % 2]
                    xTk = "xTb%d" % (nb % 2)
                    S.dma(xt[:], x[nb * 512:(nb + 1) * 512, :].rearrange("(a p) f -> p a f", p=128), writes=[xtk])
                    for k in range(8):
                        tp = tps[k % 4]
                        tpk = "tps%d" % (k % 4)
                        for a in range(4):
                            S.op("pe", lambda e, tp=tp, a=a, k=k, xt=xt: e.transpose(
                                tp[:, a * 128:(a + 1) * 128], xt[:, a, k * 128:(k + 1) * 128], ident),
                                reads=[xtk, "cst"], writes=[tpk])
                        if k % 2 == 0:
                            S.op("act", lambda e, tp=tp, k=k, xT=xT: e.activation(out=xT[:, k, :], in_=tp[:], func=AF.Identity),
                                 reads=[tpk], writes=[xTk])
                        else:
                            S.op("dve", lambda e, tp=tp, k=k, xT=xT: e.tensor_copy(out=xT[:, k, :], in_=tp[:]),
                                 reads=[tpk], writes=[xTk])
                    S.dma(XTv[:, :, nb * 512:(nb + 1) * 512], xT[:], reads=[xTk], writes=["XT"])
                    normT(xT, xTk, sq, "sq1", rstd, "rstd1", ssps, "ssps1", tmp, ["tmp1_0", "tmp1_1"],
                          lambda k, nb=nb: hT_cm[:, k, nb * 512:(nb + 1) * 512], ("hT", nb), 0, 1)
                S.barrier()

        if 2 in phases:
            hkeys = [("hT", nb) for nb in range(NB)]
            with ExitStack() as ps:
                lsb = lambda name, shape, dt=F32: ps.enter_context(nc.sbuf_tensor(name, shape, dt))
                wst = [lsb("wst%d" % i, [128, 8, 128]) for i in range(2)]
                wbf = [lsb("wbf%d" % i, [128, 8, 128], BF16) for i in range(2)]
                pc = [lsb("pc%d" % i, [128, T + 3]) for i in range(2)]
                accs = [lsb("acc%d" % i, [128, T]) for i in range(2)]
                sq2 = lsb("sq2", [128, T])
                rs = lsb("rs", [128, T])
                ob = [lsb("ob%d" % i, [128, T], BF16) for i in range(2)]
                cws = lsb("cws", [128, 32, 5])
                cwg = lsb("cwg", [128, 32, 4])
                pps = [ps.enter_context(nc.psum_tensor("pps%d" % i, [128, 512], F32)) for i in range(4)]
                sps = [ps.enter_context(nc.psum_tensor("sps%d" % i, [128, 512], F32)) for i in range(2)]
                S.dma(cws[:], cw_ssm, writes=["cws"])
                S.dma(cwg[:], cw_gdn, writes=["cwg"])
                for i in range(2):
                    S.op("pool", lambda e, i=i: e.memset(pc[i][:, 0:3], 0.0), writes=["pc%d" % i])
                w_in_v = w_in.rearrange("(k p) f -> p k f", p=128)
                XBCv = XBC_T.rearrange("(b p) t -> b p t", p=128)
                QKVv = QKV_T.rearrange("(b p) t -> b p t", p=128)
                GTv = G_T.rearrange("(b p) t -> b p t", p=128)
                blocks = []
                for cb in range(32):
                    blocks.append(("xbc", cb, OFF_XBC + cb * 128))
                for cb in range(32):
                    blocks.append(("qkv", cb, OFF_QKV + cb * 128))
                for cb in range(8):
                    blocks.append(("gate", cb, OFF_GS + cb * 128))
                for cb in range(8):
                    blocks.append(("gate", 8 + cb, OFF_GG + cb * 128))
                pcount = 0
                pcnt = [0]

                def bufs(bi):
                    return (accs[bi % 2], "acc%d" % (bi % 2), pc[bi % 2], "pc%d" % (bi % 2), ob[bi % 2], "ob%d" % (bi % 2),
                            wbf[bi % 2], "wbf%d" % (bi % 2))

                def wload(bi):
                    off = blocks[bi][2]
                    ws, wsk, wb, wbk = wst[bi % 2], "wst%d" % (bi % 2), wbf[bi % 2], "wbf%d" % (bi % 2)
                    S.dma(ws[:], w_in_v[:, :, off:off + 128], writes=[wsk])
                    S.op("pool", lambda e: e.tensor_copy(out=wb[:], in_=ws[:]), reads=[wsk], writes=[wbk])

                def front(bi):
                    kind, cb, off = blocks[bi]
                    acc, acck, p_, pk, o_, ok, wb, wbk = bufs(bi)
                    if bi + 1 < len(blocks):
                        wload(bi + 1)
                    for tb in range(NB):
                        pp = pps[pcnt[0] % 4]
                        ppk = "pps%d" % (pcnt[0] % 4)
                        pcnt[0] += 1
                        for k in range(8):
                            S.op("pe", lambda e, pp=pp, k=k, tb=tb: e.matmul(
                                pp[:], wb[:, k, :], hT_cm[:, k, tb * 512:(tb + 1) * 512], start=(k == 0), stop=(k == 7)),
                                reads=[wbk, hkeys[tb]], writes=[ppk])
                        if kind == "gate":
                            S.op("act", lambda e, pp=pp, tb=tb: e.activation(
                                out=o_[:, tb * 512:(tb + 1) * 512], in_=pp[:], func=AF.Sigmoid), reads=[ppk], writes=[ok])
                        else:
                            S.op("act", lambda e, pp=pp, tb=tb: e.activation(
                                out=p_[:, 3 + tb * 512:3 + (tb + 1) * 512], in_=pp[:], func=AF.Identity), reads=[ppk], writes=[pk])
                            if kind == "xbc":
                                S.op("act", lambda e, pp=pp, tb=tb: e.activation(
                                    out=acc[:, tb * 512:(tb + 1) * 512], in_=pp[:], func=AF.Identity,
                                    bias=cws[:, cb, 4:5], scale=cws[:, cb, 3:4]), reads=[ppk, "cws"], writes=[acck])
                            else:
                                S.op("act", lambda e, pp=pp, tb=tb: e.activation(
                                    out=acc[:, tb * 512:(tb + 1) * 512], in_=pp[:], func=AF.Identity,
                                    scale=cwg[:, cb, 3:4]), reads=[ppk, "cwg"], writes=[acck])

                def conv(bi):
                    kind, cb, off = blocks[bi]
                    if kind == "gate":
                        return
                    acc, acck, p_, pk, o_, ok, wb, wbk = bufs(bi)
                    cwt = cws if kind == "xbc" else cwg
                    cwk = "cws" if kind == "xbc" else "cwg"
                    for j in range(1, 4):
                        S.op("dve", lambda e, j=j: e.scalar_tensor_tensor(
                            out=acc[:], in0=p_[:, 3 - j:3 - j + T], scalar=cwt[:, cb, 3 - j:4 - j], in1=acc[:],
                            op0=ALU.mult, op1=ALU.add), reads=[pk, cwk, acck], writes=[acck])

                def post(bi):
                    kind, cb, off = blocks[bi]
                    acc, acck, p_, pk, o_, ok, wb, wbk = bufs(bi)
                    if kind == "gate":
                        S.dma(GTv[cb], o_[:], reads=[ok], writes=["G_T"])
                        return
                    if kind == "qkv" and cb < 16:
                        S.op("act", lambda e: e.activation(out=acc[:], in_=acc[:], func=AF.Silu), reads=[acck], writes=[acck])
                        S.op("act", lambda e: e.activation(out=sq2[:], in_=acc[:], func=AF.Square), reads=[acck], writes=["sq2"])
                        for tb in range(NB):
                            sp = sps[tb % 2]
                            spk = "sps%d" % (tb % 2)
                            S.op("pe", lambda e, sp=sp, tb=tb: e.matmul(sp[:], ones, sq2[:, tb * 512:(tb + 1) * 512],
                                                                        start=True, stop=True),
                                 reads=["sq2", "cst"], writes=[spk])
                            rsqrt_to(S, cst, rs[:, tb * 512:(tb + 1) * 512], "rs", sp[:], spk)
                        qs = (128.0 ** -0.5) if cb < 8 else 1.0
                        S.op("dve", lambda e: e.scalar_tensor_tensor(
                            out=o_[:], in0=acc[:], scalar=qs, in1=rs[:], op0=ALU.mult, op1=ALU.mult),
                            reads=[acck, "rs"], writes=[ok])
                    else:
                        S.op("act", lambda e: e.activation(out=o_[:], in_=acc[:], func=AF.Silu), reads=[acck], writes=[ok])
                    dst = XBCv[cb] if kind == "xbc" else QKVv[cb]
                    S.dma(dst, o_[:], reads=[ok], writes=[kind + "_T"])

                wload(0)
                for bi in range(len(blocks)):
                    front(bi)
                    if bi > 0:
                        post(bi - 1)
                    conv(bi)
                post(len(blocks) - 1)
                S.barrier()

            with ExitStack() as ps:
                lsb = lambda name, shape, dt=F32: ps.enter_context(nc.sbuf_tensor(name, shape, dt))
                wzs = [lsb("wzs%d" % i, [128, 8, 512]) for i in range(2)]
                wzb = [lsb("wzb%d" % i, [128, 8, 512], BF16) for i in range(2)]
                zb = [lsb("zb%d" % i, [128, 512], BF16) for i in range(3)]
                wss = lsb("wss", [128, 8, 64])
                wsb = lsb("wsb", [128, 8, 64], BF16)
                smt = [lsb("smt%d" % i, [128, 96]) for i in range(2)]
                t1 = [lsb("t1_%d" % i, [128, 48]) for i in range(2)]
                rows = lsb("rows2", [128, RLEN])
                arow = lsb("arow", [128, 48])
                S.dma(rows[:], rowv.partition_broadcast(128), writes=["rows"])
                S.op("act", lambda e: e.activation(out=arow[:, 0:32], in_=rows[:, R_AL1:R_AL1 + 32], func=AF.Exp),
                     reads=["rows"], writes=["arow"])
                S.op("act", lambda e: e.activation(out=arow[:, 32:48], in_=rows[:, R_AL2:R_AL2 + 16], func=AF.Exp),
                     reads=["rows"], writes=["arow"])
                S.op("dve", lambda e: e.tensor_scalar(out=arow[:], in0=arow[:], scalar1=-1.0, scalar2=None, op0=ALU.mult),
                     reads=["arow"], writes=["arow"])
                pps = [ps.enter_context(nc.psum_tensor("zps%d" % i, [128, 512], F32)) for i in range(4)]
                sps = [ps.enter_context(nc.psum_tensor("smps%d" % i, [128, 512], F32)) for i in range(2)]
                w_in_v = w_in.rearrange("(k p) f -> p k f", p=128)
                pcount = 0
                for blk in range(8):
                    off = (OFF_Z1 + blk * 512) if blk < 4 else (OFF_Z2 + (blk - 4) * 512)
                    ws = wzs[blk % 2]
                    wsk = "wzs%d" % (blk % 2)
                    wb = wzb[blk % 2]
                    wbk = "wzb%d" % (blk % 2)
                    if blk == 0:
                        S.dma(ws[:], w_in_v[:, :, off:off + 512], writes=[wsk])
                    if blk + 1 < 8:
                        noff = (OFF_Z1 + (blk + 1) * 512) if blk + 1 < 4 else (OFF_Z2 + (blk + 1 - 4) * 512)
                        S.dma(wzs[(blk + 1) % 2][:], w_in_v[:, :, noff:noff + 512], writes=["wzs%d" % ((blk + 1) % 2)])
                    S.op("act", lambda e, ws=ws, wb=wb: e.activation(out=wb[:], in_=ws[:], func=AF.Identity), reads=[wsk], writes=[wbk])
                    for tt in range(NT):
                        pp = pps[pcount % 4]
                        ppk = "zps%d" % (pcount % 4)
                        z_ = zb[pcount % 3]
                        zk = "zb%d" % (pcount % 3)
                        pcount += 1
                        for k in range(8):
                            S.op("pe", lambda e, pp=pp, wb=wb, k=k, tt=tt: e.matmul(
                                pp[:], hT_cm[:, k, tt * 128:(tt + 1) * 128], wb[:, k, :], start=(k == 0), stop=(k == 7)),
                                reads=[wbk, hkeys[tt // 4]], writes=[ppk])
                        S.op("act", lambda e, pp=pp, z_=z_: e.activation(out=z_[:], in_=pp[:], func=AF.Silu), reads=[ppk], writes=[zk])
                        S.dma(Z[tt * 128:(tt + 1) * 128, blk * 512:(blk + 1) * 512], z_[:], reads=[zk], writes=["Z"])
                S.dma(wss[:, :, 0:32], w_in_v[:, :, OFF_DT:OFF_DT + 32], writes=["wss"])
                S.dma(wss[:, :, 32:64], w_in_v[:, :, OFF_B:OFF_B + 32], writes=["wss"])
                S.op("pool", lambda e: e.tensor_copy(out=wsb[:], in_=wss[:]), reads=["wss"], writes=["wsb"])
                for tt in range(NT):
                    sp = sps[tt % 2]
                    spk = "smps%d" % (tt % 2)
                    sm_ = smt[tt % 2]
                    smk = "smt%d" % (tt % 2)
                    t_ = t1[tt % 2]
                    tk = "t1_%d" % (tt % 2)
                    for k in range(8):
                        S.op("pe", lambda e, sp=sp, k=k, tt=tt: e.matmul(
                            sp[:, 0:64], hT_cm[:, k, tt * 128:(tt + 1) * 128], wsb[:, k, :], start=(k == 0), stop=(k == 7)),
                            reads=["wsb", hkeys[tt // 4]], writes=[spk])
                    S.op("dve", lambda e, sp=sp, t_=t_: e.tensor_tensor(out=t_[:, 0:32], in0=sp[:, 0:32],
                                                                        in1=rows[:, R_DTB1:R_DTB1 + 32], op=ALU.add),
                         reads=["rows"], writes=[spk, tk])
                    S.op("dve", lambda e, sp=sp, t_=t_: e.tensor_tensor(out=t_[:, 32:48], in0=sp[:, 48:64],
                                                                        in1=rows[:, R_DTB2:R_DTB2 + 16], op=ALU.add),
                         reads=["rows"], writes=[spk, tk])
                    S.op("act", lambda e, t_=t_: e.activation(out=t_[:], in_=t_[:], func=AF.Exp), reads=[tk], writes=[tk])
                    S.op("dve", lambda e, t_=t_: e.tensor_scalar(out=t_[:], in0=t_[:], scalar1=1.0, scalar2=None, op0=ALU.add),
                         reads=[tk], writes=[tk])
                    S.op("act", lambda e, t_=t_: e.activation(out=t_[:], in_=t_[:], func=AF.Ln), reads=[tk], writes=[tk])
                    S.op("act", lambda e, sp=sp, sm_=sm_: e.activation(out=sm_[:, 32:48], in_=sp[:, 32:48], func=AF.Sigmoid),
                         writes=[spk, smk])
                    S.op("dve", lambda e, t_=t_, sm_=sm_: e.tensor_copy(out=sm_[:, 0:32], in_=t_[:, 0:32]), reads=[tk], writes=[smk])
                    S.op("dve", lambda e, t_=t_, sm_=sm_: e.tensor_tensor(out=sm_[:, 48:96], in0=t_[:], in1=arow[:], op=ALU.mult),
                         reads=[tk, "arow"], writes=[smk])
                    S.dma(SM[tt * 128:(tt + 1) * 128, :], sm_[:], reads=[smk], writes=["SM"])
                S.barrier()

        hst.close()
        if 3 in phases:
            phase3(nc, S, st, T, XBC_T, QKV_T, Z, SM, Y, cst, cstb, rowv)
        if 4 in phases:
            phase4a(nc, S, T, Y, G_T, XT, X2T, w_su, w_gu, w_out, cst, cstb, pv)
        if 5 in phases:
            phase4b(nc, S, T, X2T, out, w_up, w_dn, cst, pv, normT)
        else:
            pass
        S.barrier()
        S.emit()
    return nc, S


def phase3(nc, S, st_outer, T, XBC_T, QKV_T, Z, SM, Y, cst, cstb, rowv):
    NT = T // 128
    ident = cst[:, C_ID, :]
    identb = cstb[:, C_ID, :]
    Umat = cst[:, C_U, :]
    GTm = cst[:, C_GT, :]
    SUm = cst[:, C_SU, :]
    ones = cst[:, C_ONE, :]
    with ExitStack() as ps:
        lsb = lambda name, shape, dt=F32: ps.enter_context(nc.sbuf_tensor(name, shape, dt))
        smt = [lsb("p3sm%d" % i, [128, 96]) for i in range(2)]
        rows = lsb("rows3", [128, RLEN])
        S.dma(rows[:], rowv.partition_broadcast(128), writes=["rows"])
        xbct = [lsb("p3xbc%d" % i, [128, 32, 128], BF16) for i in range(2)]
        qkvt = [lsb("p3qkv%d" % i, [128, 32, 128], BF16) for i in range(2)]
        zt = [lsb("p3z%d" % i, [128, 4096], BF16) for i in range(1)]
        ytile = lsb("p3y", [128, 4096], BF16)
        xs_tok = lsb("xs_tok", [128, 2048], BF16)
        b_tok = lsb("b_tok", [128, 1024], BF16)
        k_tok = lsb("k_tok", [128, 1024], BF16)
        v_tok = lsb("v_tok", [128, 2048], BF16)
        c_sb = lsb("c_sb", [128, 48])
        e_sb = lsb("e_sb", [128, 48])
        f_sb = lsb("f_sb", [128, 48])
        dA_sb = lsb("dA_sb", [128, 48])
        nbeta = lsb("nbeta", [128, 16])
        ST = lsb("ST", [128, 8, 256])
        STb = lsb("STb", [128, 8, 256], BF16)
        GS = lsb("GS", [128, 16, 128])
        GSb = lsb("GSb", [128, 16, 128], BF16)
        IB = [[lsb("ib%d_%d" % (s_, n_), [128, 4, 128]) for n_ in range(6)] + [lsb("ibm%d" % s_, [128, 5, 4, 128])] for s_ in range(2)]
        attnT = lsb("attnT", [128, 16, 128], BF16)
        T2T = lsb("T2T", [128, 16, 128], BF16)
        ke = lsb("ke", [128, 16, 128], BF16)
        kf = lsb("kf", [128, 16, 128], BF16)
        nwT = lsb("nwT", [128, 16, 128], BF16)
        KKm = lsb("KKm", [128, 8, 128])
        QKm = lsb("QKm", [128, 8, 128])
        Am = [lsb("Am%d" % i, [128, 4, 128]) for i in range(2)]
        DT = [lsb("DT%d" % i, [128, 4, 128]) for i in range(4)]
        MT = [lsb("MT%d" % i, [128, 4, 128], BF16) for i in range(2)]
        CBTm = [lsb("CBTm%d" % i, [128, 128]) for i in range(2)]
        xdt = [lsb("xdt%d" % i, [128, 256], BF16) for i in range(2)]
        xw = [lsb("xw%d" % i, [128, 256], BF16) for i in range(2)]
        xsD = [lsb("xsD%d" % i, [128, 256]) for i in range(2)]
        yacc = [lsb("yacc%d" % i, [128, 256]) for i in range(2)]
        yz = [lsb("yz%d" % i, [128, 256]) for i in range(2)]
        ssq = [lsb("ssq%d" % i, [128, 4]) for i in range(2)]
        vnew = [lsb("vnew%d" % i, [128, 4, 128], BF16) for i in range(2)]
        osb = [lsb("osb%d" % i, [128, 4, 128]) for i in range(1)] * 2
        on = [lsb("on%d" % i, [128, 4, 128]) for i in range(1)] * 2
        banks = [ps.enter_context(nc.psum_tensor("pb%d" % i, [128, 512], F32)) for i in range(8)]
        bctr = [0]

        def bank():
            i = bctr[0] % 8
            bctr[0] += 1
            return banks[i], "pb%d" % i
        rr = {"am": 0, "ama": 0, "g": 0, "v": 0}

        S.op("pool", lambda e: e.memset(ST[:], 0.0), writes=["ST"])
        S.op("pool", lambda e: e.memset(STb[:], 0.0), writes=["STb"])
        S.op("pool", lambda e: e.memset(GS[:], 0.0), writes=["GS"])
        S.op("pool", lambda e: e.memset(GSb[:], 0.0), writes=["GSb"])

        XBCv = XBC_T.rearrange("(b p) t -> p b t", p=128)
        QKVv = QKV_T.rearrange("(b p) t -> p b t", p=128)

        def loads(tt):
            i = tt % 2
            S.dma(smt[i][:], SM[tt * 128:(tt + 1) * 128, :], reads=["SM"], writes=["p3sm%d" % i])
            S.dma(xbct[i][:], XBCv[:, :, tt * 128:(tt + 1) * 128], reads=["xbc_T"], writes=["p3xbc%d" % i])
            S.dma(qkvt[i][:], QKVv[:, :, tt * 128:(tt + 1) * 128], reads=["qkv_T"], writes=["p3qkv%d" % i])

        def load_z(tt):
            S.dma(zt[0][:], Z[tt * 128:(tt + 1) * 128, :], reads=["Z"], writes=["p3z0"])

        def bc(ap2, n, w):
            return ap2.unsqueeze(2).to_broadcast([128, n, w])

        def v3(ap2, h=4):
            return ap2.rearrange("p (h w) -> p h w", h=h)

        loads(0)
        load_z(0)
        for tt in range(NT):
            i = tt % 2
            sm, smk = smt[i], "p3sm%d" % i
            xbc, xbk = xbct[i], "p3xbc%d" % i
            qkv, qkk = qkvt[i], "p3qkv%d" % i
            z, zk = zt[0], "p3z0"
            y, yk = ytile, "p3y"
            if tt + 1 < NT:
                loads(tt + 1)
            bD, bDk = bank()
            S.op("pe", lambda e, bD=bD, sm=sm: e.matmul(bD[:, 0:48], Umat, sm[:, 48:96], start=True, stop=True),
                 reads=[smk, "cst"], writes=[bDk])
            S.op("pe", lambda e, bD=bD, sm=sm: e.matmul(bD[:, 64:112], ones, sm[:, 48:96], start=True, stop=True),
                 reads=[smk, "cst"], writes=[bDk])
            S.op("act", lambda e, bD=bD: e.activation(out=c_sb[:], in_=bD[:, 0:48], func=AF.Identity), writes=[bDk, "c_sb"])
            S.op("act", lambda e, bD=bD: e.activation(out=e_sb[:], in_=bD[:, 0:48], func=AF.Exp), writes=[bDk, "e_sb"])
            S.op("act", lambda e, bD=bD: e.activation(out=dA_sb[:], in_=bD[:, 64:112], func=AF.Exp), writes=[bDk, "dA_sb"])
            S.op("act", lambda e, bD=bD: e.activation(out=f_sb[:], in_=bD[:, 64:112], func=AF.Identity), writes=[bDk, "f_sb"])
            S.op("dve", lambda e: e.tensor_tensor(out=f_sb[:], in0=f_sb[:], in1=c_sb[:], op=ALU.subtract),
                 reads=["c_sb", "f_sb"], writes=["f_sb"])
            S.op("act", lambda e: e.activation(out=f_sb[:], in_=f_sb[:], func=AF.Exp), reads=["f_sb"], writes=["f_sb"])
            S.op("pool", lambda e, sm=sm: e.tensor_scalar(out=nbeta[:], in0=sm[:, 32:48], scalar1=-1.0, scalar2=None, op0=ALU.mult),
                 reads=[smk], writes=["nbeta"])
            jobs = [(xbc, xbk, 0, xs_tok, "xs_tok", 2), (xbc, xbk, 16, b_tok, "b_tok", 1),
                    (qkv, qkk, 8, k_tok, "k_tok", 1), (qkv, qkk, 16, v_tok, "v_tok", 2)]
            nev = 0
            for (src, srck, b0, dst, dstk, nq) in jobs:
                for q8 in range(nq):
                    bk_, bkk = bank()
                    bb = bk_[:].bitcast(BF16)
                    for a in range(8):
                        blk = b0 + q8 * 8 + a
                        S.op("pe", lambda e, bb=bb, a=a, src=src, blk=blk: e.transpose(
                            bb[:, a * 128:(a + 1) * 128], src[:, blk, :], identb), reads=[srck, "cstb"], writes=[bkk])
                    if nev % 2 == 0:
                        S.op("act", lambda e, bb=bb, dst=dst, q8=q8: e.activation(
                            out=dst[:, q8 * 1024:(q8 + 1) * 1024], in_=bb, func=AF.Identity), writes=[bkk, dstk])
                    else:
                        S.op("dve", lambda e, bb=bb, dst=dst, q8=q8: e.tensor_copy(
                            out=dst[:, q8 * 1024:(q8 + 1) * 1024], in_=bb), writes=[bkk, dstk])
                    nev += 1

            for half in range(2):
                bk_, bkk = bank()
                for j in range(4):
                    hq = half * 4 + j
                    S.op("pe", lambda e, bk_=bk_, j=j, hq=hq, qkv=qkv: e.matmul(
                        bk_[:, j * 128:(j + 1) * 128], qkv[:, 8 + hq, :], qkv[:, 8 + hq, :], start=True, stop=True),
                        reads=[qkk], writes=[bkk])
                S.op("dve", lambda e, bk_=bk_, half=half: e.tensor_tensor(
                    out=KKm[:, half * 4:(half + 1) * 4, :], in0=v3(bk_[:]), in1=SUm.unsqueeze(1).to_broadcast([128, 4, 128]), op=ALU.mult),
                    reads=["cst"], writes=[bkk, "KKm"])
                bq_, bqk = bank()
                for j in range(4):
                    hq = half * 4 + j
                    S.op("pe", lambda e, bq_=bq_, j=j, hq=hq, qkv=qkv: e.matmul(
                        bq_[:, j * 128:(j + 1) * 128], qkv[:, 8 + hq, :], qkv[:, hq, :], start=True, stop=True),
                        reads=[qkk], writes=[bqk])
                S.op("dve", lambda e, bq_=bq_, half=half: e.tensor_tensor(
                    out=QKm[:, half * 4:(half + 1) * 4, :], in0=v3(bq_[:]), in1=Umat.unsqueeze(1).to_broadcast([128, 4, 128]), op=ALU.mult),
                    reads=["cst"], writes=[bqk, "QKm"])
            def fl(t):
                return t[:].rearrange("p h w -> p (h w)")

            def bcm(plane):
                return cst[:, plane, :].unsqueeze(1).to_broadcast([128, 4, 128])

            def build_DT(cols, r, sm, smk):
                ra = rr["ama"] % 2
                rr["ama"] += 1
                S.op("dve", lambda e: e.tensor_tensor(out=Am[ra][:], in0=bcm(C_GT), in1=bc(sm[:, cols:cols + 4], 4, 128), op=ALU.mult),
                     reads=[smk, "cst"], writes=["Am%d" % ra])
                sg, sgk = bank()
                for j in range(4):
                    S.op("pe", lambda e, sg=sg, j=j: e.matmul(sg[:, j * 128:(j + 1) * 128], Am[ra][:, j, :], Umat, start=True, stop=True),
                         reads=["Am%d" % ra, "cst"], writes=[sgk])
                S.op("act", lambda e, sg=sg: e.activation(out=fl(DT[r]), in_=sg[:], func=AF.Exp), writes=[sgk, "DT%d" % r])

            def gdn_quad(q, s_, sm=sm, smk=smk, qkv=qkv, qkk=qkk, z=z, zk=zk, y=y, yk=yk):
                X, XT, Dv, DvT, E, F, XM = IB[s_]
                kX, kXT, kDv, kDvT, kE, kF, kXM = [("ib", s_, n_) for n_ in range(7)]
                qs = slice(q * 4, (q + 1) * 4)
                r = rr["am"] % 4
                rr["am"] += 1
                build_DT(80 + q * 4, r, sm, smk)
                for j in range(4):
                    hv = q * 4 + j
                    hq = hv // 2
                    S.op("dve", lambda e, j=j, hv=hv, hq=hq: e.scalar_tensor_tensor(
                        out=X[:, j, :], in0=KKm[:, hq, :], scalar=nbeta[:, hv:hv + 1], in1=DT[r][:, j, :], op0=ALU.mult, op1=ALU.mult),
                        reads=["KKm", "nbeta", "DT%d" % r], writes=[kX])
                S.op("dve", lambda e: e.tensor_tensor(
                    out=attnT[:, qs, :].rearrange("p (a b) w -> p a b w", a=2),
                    in0=QKm[:, 2 * q:2 * q + 2, :].unsqueeze(2).to_broadcast([128, 2, 2, 128]),
                    in1=DT[r][:].rearrange("p (a b) w -> p a b w", a=2), op=ALU.mult),
                    reads=["QKm", "DT%d" % r], writes=[("attnT", q)])
                yield

                def mm4(lhs, lk, rhs, rk):
                    b_, bk = bank()
                    for j in range(4):
                        S.op("pe", lambda e, b_=b_, j=j: e.matmul(b_[:, j * 128:(j + 1) * 128], lhs[:, j, :], rhs[:, j, :], start=True, stop=True),
                             reads=[lk, rk], writes=[bk])
                    return b_, bk

                def tr4(src, sk):
                    b_, bk = bank()
                    for j in range(4):
                        S.op("pe", lambda e, b_=b_, j=j: e.transpose(b_[:, j * 128:(j + 1) * 128], src[:, j, :], ident),
                             reads=[sk, "cst"], writes=[bk])
                    return b_, bk

                def ev_act(b_, bk, dst, dk):
                    S.op("act", lambda e: e.activation(out=fl(dst), in_=b_[:], func=AF.Identity), writes=[bk, dk])

                def ev_dve(b_, bk, dst, dk):
                    S.op("dve", lambda e: e.tensor_copy(out=fl(dst), in_=b_[:]), writes=[bk, dk])

                def acc_dve(b_, bk, dst, dk, out=None, ok=None):
                    o_ = fl(dst) if out is None else out
                    S.op("dve", lambda e: e.tensor_tensor(out=o_, in0=fl(dst), in1=b_[:], op=ALU.add),
                         reads=[dk], writes=[bk, dk if ok is None else ok])

                b_, bk = tr4(X, kX)
                ev_act(b_, bk, XT, kXT)
                S.op("dve", lambda e: e.tensor_tensor(out=E[:], in0=X[:], in1=bcm(C_BD8), op=ALU.mult), reads=[kX, "cst"], writes=[kE])
                S.op("dve", lambda e: e.tensor_tensor(out=Dv[:], in0=E[:], in1=bcm(C_ID), op=ALU.add), reads=[kE, "cst"], writes=[kDv])
                yield
                S.op("dve", lambda e: e.tensor_tensor(
                    out=XM[:], in0=XT[:].unsqueeze(1).to_broadcast([128, 5, 4, 128]),
                    in1=cst[:, C_BD8:C_BD8 + 5, :].unsqueeze(2).to_broadcast([128, 5, 4, 128]), op=ALU.mult),
                    reads=[kXT, "cst"], writes=[kXM])
                yield
                X0T = XM[:, 0, :, :]
                S.op("dve", lambda e: e.tensor_tensor(out=DvT[:], in0=X0T, in1=bcm(C_ID), op=ALU.add), reads=[kXM, "cst"], writes=[kDvT])
                b1, b1k = mm4(X0T, kXM, E, kE)
                b2, b2k = mm4(E, kE, X0T, kXM)
                ev_dve(b1, b1k, F, kF)
                ev_act(b2, b2k, X, kX)
                yield
                b1, b1k = mm4(X, kX, Dv, kDv)
                b2, b2k = mm4(Dv, kDv, X, kX)
                acc_dve(b1, b1k, Dv, kDv)
                acc_dve(b2, b2k, DvT, kDvT)
                b3_, b3k_ = mm4(F, kF, X, kX)
                ev_act(b3_, b3k_, E, kE)
                yield
                b1, b1k = mm4(E, kE, Dv, kDv)
                b2, b2k = mm4(Dv, kDv, E, kE)
                acc_dve(b1, b1k, Dv, kDv)
                acc_dve(b2, b2k, DvT, kDvT)
                yield
                for n, b in enumerate((8, 16, 32, 64)):
                    b_, bk = mm4(XM[:, 1 + n, :, :], kXM, Dv, kDv)
                    ev_act(b_, bk, E, kE)
                    yield
                    b1, b1k = mm4(DvT, kDvT, E, kE)
                    if b != 64:
                        b2, b2k = mm4(E, kE, DvT, kDvT)
                        acc_dve(b1, b1k, Dv, kDv)
                        acc_dve(b2, b2k, DvT, kDvT)
                        yield
                    else:
                        acc_dve(b1, b1k, Dv, kDv, out=T2T[:, qs, :].rearrange("p h w -> p (h w)"), ok=("T2T", q))
                        yield
                kq = k_tok[:, 2 * q * 128:(2 * q + 2) * 128].rearrange("p (a w) -> p a w", a=2).unsqueeze(2).to_broadcast([128, 2, 2, 128])
                S.op("pool", lambda e: e.tensor_tensor(
                    out=ke[:, qs, :].rearrange("p (a b) w -> p a b w", a=2), in0=kq,
                    in1=e_sb[:, 32 + q * 4:36 + q * 4].rearrange("p (a b) -> p a b", a=2).unsqueeze(3).to_broadcast([128, 2, 2, 128]),
                    op=ALU.mult), reads=["k_tok", "e_sb"], writes=[("ke", q)])
                S.op("pool", lambda e: e.tensor_tensor(
                    out=kf[:, qs, :].rearrange("p (a b) w -> p a b w", a=2), in0=kq,
                    in1=f_sb[:, 32 + q * 4:36 + q * 4].rearrange("p (a b) -> p a b", a=2).unsqueeze(3).to_broadcast([128, 2, 2, 128]),
                    op=ALU.mult), reads=["k_tok", "f_sb"], writes=[("kf", q)])
                wp, wpk = bank()
                for j in range(4):
                    hv = q * 4 + j
                    S.op("pe", lambda e, wp=wp, j=j, hv=hv: e.matmul(wp[:, j * 128:(j + 1) * 128], ke[:, hv, :], T2T[:, hv, :],
                                                                     start=True, stop=True),
                         reads=[("ke", q), ("T2T", q)], writes=[wpk])
                S.op("act", lambda e, wp=wp: e.activation(out=nwT[:, qs, :].rearrange("p h w -> p (h w)"), in_=wp[:],
                                                        func=AF.Identity, scale=-1.0), writes=[wpk, ("nwT", q)])
                yield
                qs = slice(q * 4, (q + 1) * 4)
                vi = rr["v"] % 2
                rr["v"] += 1
                vp, vpk = bank()
                for j in range(4):
                    hv = q * 4 + j
                    S.op("pe", lambda e, vp=vp, j=j, hv=hv: e.matmul(vp[:, j * 128:(j + 1) * 128], T2T[:, hv, :],
                                                                     v_tok[:, hv * 128:(hv + 1) * 128], start=True, stop=False),
                         reads=[("T2T", q), "v_tok"], writes=[vpk])
                    S.op("pe", lambda e, vp=vp, j=j, hv=hv: e.matmul(vp[:, j * 128:(j + 1) * 128], nwT[:, hv, :], GSb[:, hv, :],
                                                                     start=False, stop=True),
                         reads=[("nwT", q), ("GSb", q)], writes=[vpk])
                for j in range(4):
                    hv = q * 4 + j
                    S.op("act", lambda e, vp=vp, j=j, hv=hv, vi=vi: e.activation(
                        out=vnew[vi][:, j, :], in_=vp[:, j * 128:(j + 1) * 128], func=AF.Identity, scale=sm[:, 32 + hv:33 + hv]),
                        reads=[smk], writes=[vpk, "vnew%d" % vi])
                yield
                oi, oik = bank()
                for j in range(4):
                    hv = q * 4 + j
                    hq = hv // 2
                    S.op("pe", lambda e, oi=oi, j=j, hv=hv, hq=hq: e.matmul(oi[:, j * 128:(j + 1) * 128], qkv[:, hq, :], GSb[:, hv, :],
                                                                               start=True, stop=True),
                         reads=[qkk, ("GSb", q)], writes=[oik])
                oa, oak = bank()
                for j in range(4):
                    hv = q * 4 + j
                    S.op("pe", lambda e, oa=oa, j=j, hv=hv, vi=vi: e.matmul(oa[:, j * 128:(j + 1) * 128], attnT[:, hv, :], vnew[vi][:, j, :],
                                                                           start=True, stop=True),
                         reads=[("attnT", q), "vnew%d" % vi], writes=[oak])
                sn, snk = bank()
                for j in range(4):
                    hv = q * 4 + j
                    S.op("pe", lambda e, sn=sn, j=j, hv=hv, vi=vi: e.matmul(sn[:, j * 128:(j + 1) * 128], kf[:, hv, :], vnew[vi][:, j, :],
                                                                           start=True, stop=True),
                         reads=[("kf", q), "vnew%d" % vi], writes=[snk])
                for j in range(4):
                    hv = q * 4 + j
                    S.op("act", lambda e, oi=oi, j=j, hv=hv, vi=vi: e.activation(
                        out=osb[vi][:, j, :], in_=oi[:, j * 128:(j + 1) * 128], func=AF.Identity, scale=e_sb[:, 32 + hv:33 + hv]),
                        reads=["e_sb"], writes=[oik, "osb0"])
                S.op("dve", lambda e, oa=oa, vi=vi: e.tensor_tensor(
                    out=osb[vi][:].rearrange("p h w -> p (h w)"), in0=osb[vi][:].rearrange("p h w -> p (h w)"), in1=oa[:], op=ALU.add),
                    reads=["osb0"], writes=[oak, "osb0"])
                for j in range(4):
                    S.op("act", lambda e, j=j, vi=vi: e.activation(out=on[vi][:, j, :], in_=osb[vi][:, j, :], func=AF.Square,
                                                                 accum_out=ssq[vi][:, j:j + 1]),
                         reads=["osb0"], writes=["on0", "ssq%d" % vi])
                rsqrt_to(S, cst, ssq[vi][:], "ssq%d" % vi, ssq[vi][:], "ssq%d" % vi, 1.0 / 128)
                for j in range(4):
                    hv = q * 4 + j
                    S.op("dve", lambda e, j=j, vi=vi: e.scalar_tensor_tensor(
                        out=on[vi][:, j, :], in0=osb[vi][:, j, :], scalar=ssq[vi][:, j:j + 1], in1=rows[:, R_NW2:R_NW2 + 128],
                        op0=ALU.mult, op1=ALU.mult), reads=["osb0", "ssq%d" % vi, "rows"], writes=["on0"])
                S.op("pool", lambda e, vi=vi: e.tensor_tensor(
                    out=y[:, 2048 + q * 512:2048 + (q + 1) * 512], in0=on[vi][:].rearrange("p h w -> p (h w)"),
                    in1=z[:, 2048 + q * 512:2048 + (q + 1) * 512], op=ALU.mult), reads=["on0", zk], writes=[yk])
                for j in range(4):
                    hv = q * 4 + j
                    S.op("dve", lambda e, sn=sn, j=j, hv=hv: e.scalar_tensor_tensor(
                        out=GS[:, hv, :], in0=GS[:, hv, :], scalar=dA_sb[:, 32 + hv:33 + hv], in1=sn[:, j * 128:(j + 1) * 128],
                        op0=ALU.mult, op1=ALU.add), reads=[("GS", q), "dA_sb"], writes=[snk, ("GS", q)])
                S.op("act", lambda e, qs=qs: e.activation(out=GSb[:, qs, :], in_=GS[:, qs, :], func=AF.Identity),
                     reads=[("GS", q)], writes=[("GSb", q)])
                yield

            def ssd_group(g, sm=sm, smk=smk, xbc=xbc, xbk=xbk, z=z, zk=zk, y=y, yk=yk):
                gi = rr["g"] % 2
                rr["g"] += 1
                r = rr["am"] % 4
                rr["am"] += 1
                b2, b2k = bank()
                S.op("pe", lambda e: e.matmul(b2[:, 0:128], xbc[:, 16 + g, :], xbc[:, 24 + g, :], start=True, stop=True),
                     reads=[xbk], writes=[b2k])
                S.op("dve", lambda e: e.tensor_tensor(out=CBTm[gi][:], in0=b2[:, 0:128], in1=Umat, op=ALU.mult),
                     reads=["cst"], writes=[b2k, "CBTm%d" % gi])
                xs3 = v3(xs_tok[:, g * 256:(g + 1) * 256])
                S.op("dve", lambda e: e.tensor_tensor(
                    out=v3(xdt[gi][:]), in0=xs3, in1=bc(sm[:, g * 4:(g + 1) * 4], 4, 64), op=ALU.mult),
                    reads=["xs_tok", smk], writes=["xdt%d" % gi])
                S.op("dve", lambda e: e.tensor_tensor(
                    out=v3(xw[gi][:]), in0=v3(xdt[gi][:]), in1=bc(f_sb[:, g * 4:(g + 1) * 4], 4, 64), op=ALU.mult),
                    reads=["xdt%d" % gi, "f_sb"], writes=["xw%d" % gi])
                S.op("pool", lambda e: e.tensor_tensor(
                    out=v3(xsD[gi][:]), in0=xs3, in1=bc(rows[:, R_D1 + g * 4:R_D1 + (g + 1) * 4], 4, 64), op=ALU.mult),
                    reads=["xs_tok", "rows"], writes=["xsD%d" % gi])
                build_DT(48 + g * 4, r, sm, smk)
                yield
                S.op("dve", lambda e: e.tensor_tensor(out=MT[gi][:], in0=DT[r][:],
                                                      in1=CBTm[gi][:].unsqueeze(1).to_broadcast([128, 4, 128]), op=ALU.mult),
                     reads=["CBTm%d" % gi, "DT%d" % r], writes=["MT%d" % gi])
                b3, b3k = bank()
                for h in range(4):
                    S.op("pe", lambda e, h=h: e.matmul(b3[:, h * 64:(h + 1) * 64], MT[gi][:, h, :],
                                                       xdt[gi][:, h * 64:(h + 1) * 64], start=True, stop=True),
                         reads=["MT%d" % gi, "xdt%d" % gi], writes=[b3k])
                S.op("pe", lambda e: e.matmul(b3[:, 256:512], xbc[:, 24 + g, :], STb[:, g, :], start=True, stop=True),
                     reads=[xbk, ("STb", g)], writes=[b3k])
                b4, b4k = bank()
                S.op("pe", lambda e: e.matmul(b4[:, 256:512], b_tok[:, g * 128:(g + 1) * 128], xw[gi][:], start=True, stop=True),
                     reads=["b_tok", "xw%d" % gi], writes=[b4k])
                S.op("dve", lambda e: e.tensor_tensor(
                    out=v3(yacc[gi][:]), in0=v3(b3[:, 256:512]), in1=bc(e_sb[:, g * 4:(g + 1) * 4], 4, 64), op=ALU.mult),
                    reads=["e_sb"], writes=[b3k, "yacc%d" % gi])
                S.op("dve", lambda e: e.tensor_tensor(out=yacc[gi][:], in0=yacc[gi][:], in1=b3[:, 0:256], op=ALU.add),
                     reads=["yacc%d" % gi], writes=[b3k, "yacc%d" % gi])
                S.op("pool", lambda e: e.tensor_tensor(out=yacc[gi][:], in0=yacc[gi][:], in1=xsD[gi][:], op=ALU.add),
                     reads=["yacc%d" % gi, "xsD%d" % gi], writes=["yacc%d" % gi])
                S.op("dve", lambda e: e.tensor_tensor(out=yz[gi][:], in0=yacc[gi][:], in1=z[:, g * 256:(g + 1) * 256], op=ALU.mult),
                     reads=["yacc%d" % gi, zk], writes=["yz%d" % gi])
                S.op("act", lambda e: e.activation(out=xsD[gi][:], in_=yz[gi][:], func=AF.Square, accum_out=ssq[gi][:, 0:1]),
                     reads=["yz%d" % gi], writes=["xsD%d" % gi, "ssq%d" % gi])
                rsqrt_to(S, cst, ssq[gi][:, 0:1], "ssq%d" % gi, ssq[gi][:, 0:1], "ssq%d" % gi, 1.0 / 256)
                S.op("dve", lambda e: e.scalar_tensor_tensor(
                    out=y[:, g * 256:(g + 1) * 256], in0=yz[gi][:], scalar=ssq[gi][:, 0:1],
                    in1=rows[:, R_NW1 + g * 256:R_NW1 + (g + 1) * 256], op0=ALU.mult, op1=ALU.mult),
                    reads=["yz%d" % gi, "ssq%d" % gi, "rows"], writes=[yk])
                S.op("pool", lambda e: e.tensor_tensor(
                    out=v3(ST[:, g, :]), in0=v3(ST[:, g, :]), in1=bc(dA_sb[:, g * 4:(g + 1) * 4], 4, 64), op=ALU.mult),
                    reads=[("ST", g), "dA_sb"], writes=[("ST", g)])
                S.op("dve", lambda e: e.tensor_tensor(out=ST[:, g, :], in0=ST[:, g, :], in1=b4[:, 256:512], op=ALU.add),
                     reads=[("ST", g)], writes=[b4k, ("ST", g)])
                S.op("act", lambda e: e.activation(out=STb[:, g, :], in_=ST[:, g, :], func=AF.Identity),
                     reads=[("ST", g)], writes=[("STb", g)])
                yield

            pend_g = [gdn_quad(q, q % 2) for q in range(4)]
            pend_s = [ssd_group(g) for g in range(8)]
            live = []
            slots = {"g": 0, "s": 0}

            def refill():
                while slots["g"] < 2 and pend_g:
                    live.append(("g", pend_g.pop(0)))
                    slots["g"] += 1
                while slots["s"] < 2 and pend_s:
                    live.append(("s", pend_s.pop(0)))
                    slots["s"] += 1
            refill()
            while live:
                for item in list(live):
                    kind, g_ = item
                    try:
                        next(g_)
                    except StopIteration:
                        live.remove(item)
                        slots[kind] -= 1
                refill()

            S.dma(Y[tt * 128:(tt + 1) * 128, :], y[:], reads=[yk], writes=["Y"])
            if tt + 1 < NT:
                load_z(tt + 1)
        S.barrier()


def load_weight_bf16(nc, S, dst, dstk, src_v, nk, ncols, stg, stgk):
    cnt = 0
    kc = 2
    for k0 in range(0, nk, kc):
        for c0 in range(0, ncols, 512):
            s_ = stg[cnt % 2]
            sk = stgk[cnt % 2]
            cnt += 1
            S.dma(s_[:], src_v[:, k0:k0 + kc, c0:c0 + 512], writes=[sk])
            if cnt % 2 == 0:
                S.op("act", lambda e, s_=s_, k0=k0, c0=c0: e.activation(out=dst[:, k0:k0 + kc, c0:c0 + 512], in_=s_[:], func=AF.Identity),
                     reads=[sk], writes=[dstk])
            else:
                S.op("dve", lambda e, s_=s_, k0=k0, c0=c0: e.tensor_copy(out=dst[:, k0:k0 + kc, c0:c0 + 512], in_=s_[:]),
                     reads=[sk], writes=[dstk])


def phase4a(nc, S, T, Y, G_T, XT, X2T, w_su, w_gu, w_out, cst, cstb, pv):
    BW = 256
    NBW = T // BW
    NA = BW // 128
    identb = cstb[:, C_ID, :]
    onesD = cst[:, C_ONED, :]
    with ExitStack() as ps:
        lsb = lambda name, shape, dt=F32: ps.enter_context(nc.sbuf_tensor(name, shape, dt))
        Wsu = lsb("Wsu", [128, 16, D], BF16)
        Wgu = lsb("Wgu", [128, 16, D], BF16)
        Wo = lsb("Wo", [128, 8, D], BF16)
        stg = [lsb("stg%d" % i, [128, 2, 512]) for i in range(2)]
        ytoks = [lsb("ytok%d" % i, [128, NA, 4096], BF16) for i in range(2)]
        yT = lsb("yT", [128, 32, BW], BF16)
        gTs = [lsb("gT%d" % i, [128, 16, BW], BF16) for i in range(2)]
        xTs = [lsb("xT4_%d" % i, [128, 8, BW]) for i in range(2)]
        t1 = [lsb("m_t1_%d" % i, [128, BW]) for i in range(2)]
        t2 = [lsb("m_t2_%d" % i, [128, BW]) for i in range(2)]
        mT = lsb("mT", [128, 8, BW], BF16)
        m2s = lsb("m2s", [128, 8, BW])
        sqm = lsb("sqm", [128, 8, BW])
        rstd = lsb("rstd4", [128, BW])
        pu = [ps.enter_context(nc.psum_tensor("pu%d" % i, [128, 512], F32)) for i in range(4)]
        pt = [ps.enter_context(nc.psum_tensor("ptb%d" % i, [128, 1024], BF16)) for i in range(2)]
        pss = ps.enter_context(nc.psum_tensor("pss4", [128, 512], F32))
        load_weight_bf16(nc, S, Wsu, "Wsu", w_su.rearrange("(k p) f -> p k f", p=128), 16, D, stg, ["stg0", "stg1"])
        load_weight_bf16(nc, S, Wgu, "Wgu", w_gu.rearrange("(k p) f -> p k f", p=128), 16, D, stg, ["stg0", "stg1"])
        load_weight_bf16(nc, S, Wo, "Wo", w_out.rearrange("(k p) f -> p k f", p=128), 8, D, stg, ["stg0", "stg1"])
        GTv = G_T.rearrange("(b p) t -> p b t", p=128)
        XTv = XT.rearrange("(k p) t -> p k t", p=128)
        X2Tv = X2T.rearrange("(k p) t -> p k t", p=128)
        pc = 0
        def loads4(nb):
            t0 = nb * BW
            i = nb % 2
            S.dma(ytoks[i][:], Y[t0:t0 + BW, :].rearrange("(a p) f -> p a f", p=128), reads=["Y"], writes=["ytok%d" % i])
            S.dma(gTs[i][:], GTv[:, :, t0:t0 + BW], reads=["G_T"], writes=["gT%d" % i])
            S.dma(xTs[i][:], XTv[:, :, t0:t0 + BW], reads=["XT"], writes=["xT4_%d" % i])
        loads4(0)
        for nb in range(NBW):
            t0 = nb * BW
            ytok, gT, xT = ytoks[nb % 2], gTs[nb % 2], xTs[nb % 2]
            ytk, gTk, xTk = "ytok%d" % (nb % 2), "gT%d" % (nb % 2), "xT4_%d" % (nb % 2)
            if nb + 1 < NBW:
                loads4(nb + 1)
            for cb in range(32):
                p_ = pt[cb % 2]
                pk = "ptb%d" % (cb % 2)
                for a in range(NA):
                    S.op("pe", lambda e, p_=p_, a=a, cb=cb, ytok=ytok: e.transpose(p_[:, a * 128:(a + 1) * 128],
                                                                      ytok[:, a, cb * 128:(cb + 1) * 128], identb),
                         reads=[ytk, "cstb"], writes=[pk])
                if cb % 2 == 0:
                    S.op("act", lambda e, p_=p_, cb=cb: e.activation(out=yT[:, cb, :], in_=p_[:, 0:BW], func=AF.Identity),
                         reads=[pk], writes=["yT"])
                else:
                    S.op("dve", lambda e, p_=p_, cb=cb: e.tensor_copy(out=yT[:, cb, :], in_=p_[:, 0:BW]), reads=[pk], writes=["yT"])
            for blk in range(8):
                p1 = pu[pc % 4]
                p1k = "pu%d" % (pc % 4)
                pc += 1
                p2 = pu[pc % 4]
                p2k = "pu%d" % (pc % 4)
                pc += 1
                for k in range(16):
                    S.op("pe", lambda e, p1=p1, k=k, blk=blk: e.matmul(p1[:, 0:BW], Wsu[:, k, blk * 128:(blk + 1) * 128], yT[:, k, :],
                                                                       start=(k == 0), stop=(k == 15)),
                         reads=["Wsu", "yT"], writes=[p1k])
                for k in range(16):
                    S.op("pe", lambda e, p2=p2, k=k, blk=blk: e.matmul(p2[:, 0:BW], Wgu[:, k, blk * 128:(blk + 1) * 128], yT[:, 16 + k, :],
                                                                       start=(k == 0), stop=(k == 15)),
                         reads=["Wgu", "yT"], writes=[p2k])
                a1 = t1[blk % 2]
                a1k = "m_t1_%d" % (blk % 2)
                a2 = t2[blk % 2]
                a2k = "m_t2_%d" % (blk % 2)
                S.op("dve", lambda e, p1=p1, a1=a1, blk=blk, gT=gT: e.tensor_tensor(out=a1[:], in0=p1[:, 0:BW], in1=gT[:, blk, :], op=ALU.mult),
                     reads=[p1k, gTk], writes=[a1k])
                S.op("dve", lambda e, p2=p2, a2=a2, blk=blk, gT=gT: e.tensor_tensor(out=a2[:], in0=p2[:, 0:BW], in1=gT[:, 8 + blk, :], op=ALU.mult),
                     reads=[p2k, gTk], writes=[a2k])
                S.op("pool", lambda e, a1=a1, a2=a2, blk=blk: e.tensor_tensor(out=mT[:, blk, :], in0=a1[:], in1=a2[:], op=ALU.add),
                     reads=[a1k, a2k], writes=["mT"])
            for blk in range(8):
                p1 = pu[pc % 4]
                p1k = "pu%d" % (pc % 4)
                pc += 1
                for k in range(8):
                    S.op("pe", lambda e, p1=p1, k=k, blk=blk: e.matmul(p1[:, 0:BW], Wo[:, k, blk * 128:(blk + 1) * 128], mT[:, k, :],
                                                                       start=(k == 0), stop=(k == 7)),
                         reads=["Wo", "mT"], writes=[p1k])
                S.op("act", lambda e, p1=p1, blk=blk: e.activation(out=m2s[:, blk, :], in_=p1[:, 0:BW], func=AF.Identity),
                     reads=[p1k], writes=["m2s"])
                S.op("act", lambda e, p1=p1, blk=blk: e.activation(out=sqm[:, blk, :], in_=p1[:, 0:BW], func=AF.Square),
                     reads=[p1k], writes=["sqm"])
            for k in range(8):
                S.op("pe", lambda e, k=k: e.matmul(pss[:, 0:BW], onesD, sqm[:, k, :], start=(k == 0), stop=(k == 7)),
                     reads=["sqm", "cst"], writes=["pss4"])
            rsqrt_to(S, cst, rstd[:], "rstd4", pss[:, 0:BW], "pss4")
            for blk in range(8):
                a1 = t1[blk % 2]
                a1k = "m_t1_%d" % (blk % 2)
                S.op("pool", lambda e, a1=a1, blk=blk: e.tensor_tensor(out=a1[:], in0=m2s[:, blk, :], in1=rstd[:], op=ALU.mult),
                     reads=["m2s", "rstd4"], writes=[a1k])
                S.op("dve", lambda e, a1=a1, blk=blk, xT=xT: e.scalar_tensor_tensor(
                    out=xT[:, blk, :], in0=a1[:], scalar=pv[:, 2, blk:blk + 1], in1=xT[:, blk, :], op0=ALU.mult, op1=ALU.add),
                    reads=[a1k, "pv", xTk], writes=[xTk])
            S.dma(X2Tv[:, :, t0:t0 + BW], xT[:], reads=[xTk], writes=["X2T"])
        S.barrier()


def phase4b(nc, S, T, X2T, out, w_up, w_dn, cst, pv, normT):
    BW = 256
    NBW = T // BW
    NA = BW // 128
    ident = cst[:, C_ID, :]
    onesD = cst[:, C_ONED, :]
    with ExitStack() as ps:
        lsb = lambda name, shape, dt=F32: ps.enter_context(nc.sbuf_tensor(name, shape, dt))
        Wup = lsb("Wup", [128, 8, 4096], BF16)
        Wdn = lsb("Wdn", [128, 32, D], BF16)
        stg = [lsb("stgb%d" % i, [128, 2, 512]) for i in range(2)]
        xTs = [lsb("x5T%d" % i, [128, 8, BW]) for i in range(2)]
        sq = lsb("sq5", [128, 8, BW])
        rstd = lsb("rstd5", [128, BW])
        tmp = [lsb("tmp5_%d" % i, [128, BW]) for i in range(2)]
        h2T = lsb("h2T", [128, 8, BW], BF16)
        rl = [lsb("rl%d" % i, [128, BW]) for i in range(2)]
        actT = lsb("actT", [128, 32, BW], BF16)
        dns = lsb("dns", [128, 8, BW])
        otok = [lsb("otok%d" % i, [128, 512]) for i in range(2)]
        pu = [ps.enter_context(nc.psum_tensor("p5u%d" % i, [128, 512], F32)) for i in range(4)]
        pss = ps.enter_context(nc.psum_tensor("pss5", [128, 512], F32))
        po = [ps.enter_context(nc.psum_tensor("p5o%d" % i, [128, 512], F32)) for i in range(2)]
        load_weight_bf16(nc, S, Wup, "Wup", w_up.rearrange("(k p) f -> p k f", p=128), 8, 4096, stg, ["stgb0", "stgb1"])
        load_weight_bf16(nc, S, Wdn, "Wdn", w_dn.rearrange("(k p) f -> p k f", p=128), 32, D, stg, ["stgb0", "stgb1"])
        X2Tv = X2T.rearrange("(k p) t -> p k t", p=128)
        pc = 0
        oc = 0
        S.dma(xTs[0][:], X2Tv[:, :, 0:BW], reads=["X2T"], writes=["x5T0"])
        for nb in range(NBW):
            t0 = nb * BW
            xT = xTs[nb % 2]
            xk = "x5T%d" % (nb % 2)
            if nb + 1 < NBW:
                S.dma(xTs[(nb + 1) % 2][:], X2Tv[:, :, t0 + BW:t0 + 2 * BW], reads=["X2T"], writes=["x5T%d" % ((nb + 1) % 2)])
            normT(xT, xk, sq, "sq5", rstd, "rstd5", pss, "pss5", tmp, ["tmp5_0", "tmp5_1"],
                  lambda k: h2T[:, k, :], "h2T", 3, 4, BW)
            for blk in range(32):
                p1 = pu[pc % 4]
                p1k = "p5u%d" % (pc % 4)
                pc += 1
                for k in range(8):
                    S.op("pe", lambda e, p1=p1, k=k, blk=blk: e.matmul(p1[:, 0:BW], Wup[:, k, blk * 128:(blk + 1) * 128], h2T[:, k, :],
                                                                       start=(k == 0), stop=(k == 7)),
                         reads=["Wup", "h2T"], writes=[p1k])
                r_ = rl[blk % 2]
                rk = "rl%d" % (blk % 2)
                S.op("act", lambda e, p1=p1, r_=r_: e.activation(out=r_[:], in_=p1[:, 0:BW], func=AF.Relu), reads=[p1k], writes=[rk])
                eng = "pool" if blk % 2 == 0 else "dve"
                S.op(eng, lambda e, r_=r_, blk=blk: e.tensor_tensor(out=actT[:, blk, :], in0=r_[:], in1=r_[:], op=ALU.mult),
                     reads=[rk], writes=["actT"])
            for blk in range(8):
                p1 = pu[pc % 4]
                p1k = "p5u%d" % (pc % 4)
                pc += 1
                for k in range(32):
                    S.op("pe", lambda e, p1=p1, k=k, blk=blk: e.matmul(p1[:, 0:BW], Wdn[:, k, blk * 128:(blk + 1) * 128], actT[:, k, :],
                                                                       start=(k == 0), stop=(k == 31)),
                         reads=["Wdn", "actT"], writes=[p1k])
                S.op("act", lambda e, p1=p1, blk=blk: e.activation(out=dns[:, blk, :], in_=p1[:, 0:BW], func=AF.Identity),
                     reads=[p1k], writes=["dns"])
                S.op("act", lambda e, p1=p1, blk=blk: e.activation(out=sq[:, blk, :], in_=p1[:, 0:BW], func=AF.Square),
                     reads=[p1k], writes=["sq5"])
            for k in range(8):
                S.op("pe", lambda e, k=k: e.matmul(pss[:, 0:BW], onesD, sq[:, k, :], start=(k == 0), stop=(k == 7)),
                     reads=["sq5", "cst"], writes=["pss5"])
            rsqrt_to(S, cst, rstd[:], "rstd5", pss[:, 0:BW], "pss5")
            for blk in range(8):
                t_ = tmp[blk % 2]
                tk = "tmp5_%d" % (blk % 2)
                S.op("pool", lambda e, t_=t_, blk=blk: e.tensor_tensor(out=t_[:], in0=dns[:, blk, :], in1=rstd[:], op=ALU.mult),
                     reads=["dns", "rstd5"], writes=[tk])
                S.op("dve", lambda e, t_=t_, blk=blk, xT=xT: e.scalar_tensor_tensor(
                    out=dns[:, blk, :], in0=t_[:], scalar=pv[:, 5, blk:blk + 1], in1=xT[:, blk, :], op0=ALU.mult, op1=ALU.add),
                    reads=[tk, "pv", xk, "dns"], writes=["dns"])
            for a in range(NA):
                for half in range(2):
                    ot = otok[oc % 2]
                    otk = "otok%d" % (oc % 2)
                    oc += 1
                    p_ = po[half]
                    pk = "p5o%d" % half
                    for b4 in range(4):
                        blk = half * 4 + b4
                        S.op("pe", lambda e, p_=p_, b4=b4, blk=blk, a=a: e.transpose(
                            p_[:, b4 * 128:(b4 + 1) * 128], dns[:, blk, a * 128:(a + 1) * 128], ident),
                            reads=["dns", "cst"], writes=[pk])
                    if half == 0:
                        S.op("act", lambda e, p_=p_, ot=ot: e.activation(out=ot[:], in_=p_[:], func=AF.Identity),
                             reads=[pk], writes=[otk])
                    else:
                        S.op("dve", lambda e, p_=p_, ot=ot: e.tensor_copy(out=ot[:], in_=p_[:]), reads=[pk], writes=[otk])
                    S.dma(out[t0 + a * 128:t0 + (a + 1) * 128, half * 512:(half + 1) * 512], ot[:], reads=[otk], writes=["out"])
        S.barrier()


def host_inputs(inputs, b, T):
    f = lambda a: np.ascontiguousarray(np.asarray(a, dtype=np.float32))
    col = lambda v: f(np.asarray(v).reshape(-1, 128).T)
    nw = np.stack([col(inputs["norm_mix_pre"][0]), col(inputs["norm_mix_post"][0]),
                   col(inputs["norm_mlp_pre"][0]), col(inputs["norm_mlp_post"][0])], axis=1)
    cws = np.concatenate([np.asarray(inputs["ssm_conv_w"][0]), np.asarray(inputs["ssm_conv_b"])], axis=0)
    cws = cws.reshape(5, 32, 128).transpose(2, 1, 0)
    cwg = np.asarray(inputs["gdn_conv_w"][0]).reshape(4, 32, 128).transpose(2, 1, 0)
    rowv = np.concatenate([np.asarray(inputs["ssm_dt_bias"][0]), np.asarray(inputs["ssm_A_log"][0]), np.asarray(inputs["ssm_D"][0]),
                           np.asarray(inputs["gdn_dt_bias"][0]), np.asarray(inputs["gdn_A_log"][0]),
                           np.asarray(inputs["ssm_norm_w"][0]), np.asarray(inputs["gdn_norm_w"][0])])[None, :]
    return {
        "x": f(np.asarray(inputs["x"])[b, :T]),
        "c_col": col(np.asarray(inputs["c"])[b]),
        "w_ada": f(inputs["w_ada"][0]),
        "b_ada_col": col(inputs["b_ada"][0]),
        "nw_col": f(nw),
        "w_in": f(inputs["w_in"][0]),
        "cw_ssm": f(cws),
        "cw_gdn": f(cwg),
        "rowv": f(rowv),
        "w_su": f(inputs["w_ssm_up"][0]),
        "w_gu": f(inputs["w_gdn_up"][0]),
        "w_out": f(inputs["w_out"][0]),
        "w_up": f(inputs["w_mlp_up"][0]),
        "w_dn": f(inputs["w_mlp_down"][0]),
        "consts": make_consts(),
    }


def kernel(**inputs):
    T = 4096
    nc, S = build(T)
    shared = None
    in_maps = []
    for b in range(8):
        m = host_inputs(inputs, b, T)
        if shared is None:
            shared = m
        else:
            for k in m:
                if k not in ("x", "c_col"):
                    m[k] = shared[k]
        in_maps.append(m)
    res = run_bass_kernel_spmd(nc, in_maps, core_ids=list(range(8)))
    return np.stack([np.asarray(r["out"], dtype=np.float32) for r in res.results], axis=0)
```

```python
import numpy as np
import ml_dtypes
from contextlib import ExitStack
import concourse.bass as bass
import concourse.mybir as mybir
from concourse.bass_utils import run_bass_kernel_spmd

F32 = mybir.dt.float32
BF16 = mybir.dt.bfloat16
AF = mybir.ActivationFunctionType
ALU = mybir.AluOpType

D = 1024
EPS = 1e-6
COMPUTE = ("pe", "act", "dve", "pool")
NSLOT = 8


STRICT = False


class Sched:
    def __init__(self, nc, st):
        self.nc = nc
        self.st = st
        self.streams = {e: [] for e in ("pe", "act", "dve", "pool", "sp")}
        self.sem = {}
        for e in COMPUTE:
            self.sem[e] = st.enter_context(nc.semaphore("c_" + e))
        for i in range(NSLOT):
            self.sem[("sp", i)] = st.enter_context(nc.semaphore("d_sp%d" % i))
        self.count = {k: 0 for k in self.sem}
        self.dma_idx = 0
        self.known = {e: {} for e in self.streams}
        self.clock = {}
        self.last_write = {}
        self.readers = {}
        self.ninstr = 0
        self.nwaits = 0

    def _need(self, eng, ev, waits):
        c, n = ev
        if self.known[eng].get(c, 0) >= n:
            return
        if waits.get(c, 0) < n:
            waits[c] = n

    def _deps(self, eng, reads, writes):
        waits = {}
        for k in reads:
            ev = self.last_write.get(k)
            if ev is not None:
                if ev[0] == eng and eng == "pe":
                    continue
                self._need(eng, ev, waits)
        for k in writes:
            ev = self.last_write.get(k)
            if ev is not None and (STRICT and eng != "pe" or not (ev[0] == eng and eng in COMPUTE)):
                self._need(eng, ev, waits)
            for rv in self.readers.get(k, ()):
                if rv[0] == eng and eng in COMPUTE and not STRICT:
                    continue
                self._need(eng, rv, waits)
        return waits

    def _apply(self, eng, waits):
        kn = self.known[eng]
        for c, n in waits.items():
            ck = self.clock.get((c, n))
            if ck:
                for cc, nn in ck.items():
                    if kn.get(cc, 0) < nn:
                        kn[cc] = nn
            if kn.get(c, 0) < n:
                kn[c] = n

    def _record(self, ev, eng, reads, writes):
        ck = dict(self.known[eng])
        ck[ev[0]] = ev[1]
        self.clock[ev] = ck
        for k in reads:
            self.readers.setdefault(k, []).append(ev)
        for k in writes:
            self.last_write[k] = ev
            self.readers[k] = []

    def op(self, eng, fn, reads=(), writes=()):
        waits = self._deps(eng, reads, writes)
        self._apply(eng, waits)
        self.count[eng] += 1
        ev = (eng, self.count[eng])
        self._record(ev, eng, reads, writes)
        self.streams[eng].append((list(waits.items()), fn, (eng, 1)))
        self.ninstr += 1
        self.nwaits += len(waits)
        return ev

    def dma(self, out, in_, reads=(), writes=()):
        q = "sp"
        slot = (q, self.dma_idx % NSLOT)
        self.dma_idx += 1
        waits = self._deps(q, reads, writes)
        if self.count[slot] > 0:
            self._need(q, (slot, self.count[slot]), waits)
        self._apply(q, waits)
        self.count[slot] += 1
        ev = (slot, self.count[slot])
        self._record(ev, q, reads, writes)
        fn = lambda e, out=out, in_=in_: e.dma_start(out=out, in_=in_)
        self.streams[q].append((list(waits.items()), fn, (slot, 16)))
        self.ninstr += 1
        self.nwaits += len(waits)
        return ev

    def barrier(self):
        for eng in self.streams:
            waits = {}
            for c, n in self.count.items():
                if n > 0 and c != eng:
                    self._need(eng, (c, n), waits)
            self._apply(eng, waits)
            self.streams[eng].append((list(waits.items()), None, None))
        self.last_write = {}
        self.readers = {}

    def emit(self):
        nc = self.nc
        block = self.st.enter_context(nc.Block())
        sem = self.sem

        def run(stream):
            def body(e):
                for waits, fn, inc in stream:
                    for c, n in waits:
                        e.wait_ge(sem[c], n * (1 if c in COMPUTE else 16))
                    if fn is not None:
                        fn(e).then_inc(sem[inc[0]], inc[1])
            return body

        block.tensor(run(self.streams["pe"]))
        block.scalar(run(self.streams["act"]))
        block.vector(run(self.streams["dve"]))
        block.gpsimd(run(self.streams["pool"]))
        block.sync(run(self.streams["sp"]))


OFF_Z1 = 0
OFF_XBC = 2048
OFF_DT = 6144
OFF_QKV = 6176
OFF_Z2 = 10272
OFF_B = 12320
OFF_A = 12336
OFF_GS = 12352
OFF_GG = 13376

C_ID, C_U, C_GT, C_SU, C_ONE, C_ONED, C_EPS, C_BD8, C_CMT, NCONST = 0, 1, 2, 3, 4, 5, 6, 7, 8, 12
R_DTB1, R_AL1, R_D1, R_DTB2, R_AL2, R_NW1, R_NW2, RLEN = 0, 32, 64, 96, 112, 128, 2176, 2304


def make_consts():
    k = np.arange(128)[:, None]
    l = np.arange(128)[None, :]
    c = np.zeros((128, NCONST, 128), np.float32)
    c[:, C_ID] = (k == l)
    c[:, C_U] = (k <= l)
    c[:, C_GT] = (k > l)
    c[:, C_SU] = (l > k)
    c[:, C_ONE] = 1.0
    c[:, C_ONED] = 1.0 / D
    c[:, C_EPS] = EPS
    c[:, C_BD8] = (k // 8 == l // 8)
    for n, b in enumerate((8, 16, 32, 64)):
        cm = ((k // (2 * b) == l // (2 * b)) & ((k // b) % 2 == 0) & ((l // b) % 2 == 1))
        c[:, C_CMT + n] = cm.T
    return c


def rsqrt_to(S, cst, dst, dstk, src, srck, scale=1.0):
    S.op("act", lambda e: e.activation(out=dst, in_=src, func=AF.Sqrt, bias=cst[:, C_EPS, 0:1], scale=scale),
         reads=[srck, "cst"], writes=[dstk])
    S.op("dve", lambda e: e.reciprocal(out=dst, in_=dst), reads=[dstk], writes=[dstk])


def build(T, phases=(0, 1, 2, 3, 4, 5), debug=False):
    NT = T // 128
    NB = T // 512
    nc = bass.Bass("TRN2", target_bir_lowering=False)

    def din(name, shape, dt=F32):
        return nc.dram_tensor(name, shape, dt, kind="ExternalInput").ap()

    def dscr(name, shape, dt):
        return nc.dram_tensor(name, shape, dt, kind="ExternalOutput").ap()

    x = din("x", [T, D])
    c_col = din("c_col", [128, 8])
    w_ada = din("w_ada", [D, 6 * D])
    b_ada_col = din("b_ada_col", [128, 48])
    nw_col = din("nw_col", [128, 4, 8])
    w_in = din("w_in", [D, 14400])
    cw_ssm = din("cw_ssm", [128, 32, 5])
    cw_gdn = din("cw_gdn", [128, 32, 4])
    rowv = din("rowv", [1, RLEN])
    w_su = din("w_su", [2048, D])
    w_gu = din("w_gu", [2048, D])
    w_out = din("w_out", [D, D])
    w_up = din("w_up", [D, 4096])
    w_dn = din("w_dn", [4096, D])
    consts = din("consts", [128, NCONST, 128])
    out = nc.dram_tensor("out", [T, D], F32, kind="ExternalOutput").ap()

    XT = dscr("s_xt", [D, T], F32)
    XBC_T = dscr("s_xbct", [4096, T], BF16)
    QKV_T = dscr("s_qkvt", [4096, T], BF16)
    G_T = dscr("s_gt", [2048, T], BF16)
    Z = dscr("s_z", [T, 4096], BF16)
    SM = dscr("s_sm", [T, 96], F32)
    if debug:
        Y = dscr("s_y", [T, 4096], BF16)
        X2T = dscr("s_x2t", [D, T], F32)
        MOD = dscr("s_mod", [128, 48], F32)
    else:
        Y = Z
        X2T = XT
        MOD = None

    with ExitStack() as st:
        S = Sched(nc, st)
        sb = lambda name, shape, dt=F32: st.enter_context(nc.sbuf_tensor(name, shape, dt))
        cst = sb("cst", [128, NCONST, 128])
        cstb = sb("cstb", [128, 1, 128], BF16)
        pv = sb("pv", [128, 6, 8])

        S.dma(cst[:], consts, writes=["cst"])
        S.op("pool", lambda e: e.tensor_copy(out=cstb[:], in_=cst[:, 0:1, :]), reads=["cst"], writes=["cstb"])
        ident = cst[:, C_ID, :]
        identb = cstb[:, C_ID, :]
        Umat = cst[:, C_U, :]
        GTm = cst[:, C_GT, :]
        SUm = cst[:, C_SU, :]
        ones = cst[:, C_ONE, :]
        onesD = cst[:, C_ONED, :]
        if 0 in phases:
            with ExitStack() as ps:
                lsb = lambda name, shape, dt=F32: ps.enter_context(nc.sbuf_tensor(name, shape, dt))
                cact = lsb("cact", [128, 8])
                csig = lsb("csig", [128, 8])
                wa = [lsb("wa%d" % i, [128, 8, 512]) for i in range(2)]
                modsb = lsb("modsb", [128, 48])
                bada = lsb("bada", [128, 48])
                nwc = lsb("nwc", [128, 4, 8])
                modps = ps.enter_context(nc.psum_tensor("modps", [128, 512], F32))
                S.dma(cact[:], c_col, writes=["cact"])
                S.dma(bada[:], b_ada_col, writes=["bada"])
                S.dma(nwc[:], nw_col, writes=["nwc"])
                S.op("act", lambda e: e.activation(out=csig[:], in_=cact[:], func=AF.Sigmoid), reads=["cact"], writes=["csig"])
                S.op("dve", lambda e: e.tensor_tensor(out=cact[:], in0=cact[:], in1=csig[:], op=ALU.mult),
                     reads=["cact", "csig"], writes=["cact"])
                wav = w_ada.rearrange("(k p) f -> p k f", p=128)
                for fb in range(12):
                    w = wa[fb % 2]
                    wk = "wa%d" % (fb % 2)
                    S.dma(w[:], wav[:, :, fb * 512:(fb + 1) * 512], writes=[wk])
                    for j in range(4):
                        col = fb * 4 + j
                        for k in range(8):
                            S.op("pe", lambda e, w=w, j=j, k=k, col=col: e.matmul(
                                modps[:, col:col + 1], w[:, k, j * 128:(j + 1) * 128], cact[:, k:k + 1],
                                start=(k == 0), stop=(k == 7)), reads=[wk, "cact"], writes=["modps"])
                S.op("dve", lambda e: e.tensor_tensor(out=modsb[:], in0=modps[:, 0:48], in1=bada[:], op=ALU.add),
                     reads=["modps", "bada"], writes=["modsb"])
                S.op("dve", lambda e: e.scalar_tensor_tensor(out=pv[:, 0, :], in0=modsb[:, 8:16], scalar=1.0, in1=nwc[:, 0, :],
                                                             op0=ALU.add, op1=ALU.mult), reads=["modsb", "nwc"], writes=["pv"])
                S.op("dve", lambda e: e.tensor_copy(out=pv[:, 1, :], in_=modsb[:, 0:8]), reads=["modsb"], writes=["pv"])
                S.op("dve", lambda e: e.tensor_tensor(out=pv[:, 2, :], in0=modsb[:, 16:24], in1=nwc[:, 1, :], op=ALU.mult),
                     reads=["modsb", "nwc"], writes=["pv"])
                S.op("dve", lambda e: e.scalar_tensor_tensor(out=pv[:, 3, :], in0=modsb[:, 32:40], scalar=1.0, in1=nwc[:, 2, :],
                                                             op0=ALU.add, op1=ALU.mult), reads=["modsb", "nwc"], writes=["pv"])
                S.op("dve", lambda e: e.tensor_copy(out=pv[:, 4, :], in_=modsb[:, 24:32]), reads=["modsb"], writes=["pv"])
                S.op("dve", lambda e: e.tensor_tensor(out=pv[:, 5, :], in0=modsb[:, 40:48], in1=nwc[:, 3, :], op=ALU.mult),
                     reads=["modsb", "nwc"], writes=["pv"])
                if debug:
                    S.dma(MOD, modsb[:], reads=["modsb"], writes=["MOD"])
                S.barrier()

        def normT(xT, xk, sq, sqk, rstd, rstdk, ssps, sspsk, tmp, tmpk, hdst, hk, ia, ish, W=512):
            S.op("act", lambda e: e.activation(out=sq[:], in_=xT[:], func=AF.Square), reads=[xk], writes=[sqk])
            for k in range(8):
                S.op("pe", lambda e, k=k: e.matmul(ssps[:, 0:W], onesD, sq[:, k, :], start=(k == 0), stop=(k == 7)),
                     reads=[sqk, "cst"], writes=[sspsk])
            rsqrt_to(S, cst, rstd[:], rstdk, ssps[:, 0:W], sspsk)
            for k in range(8):
                t = tmp[k % 2]
                tk = tmpk[k % 2]
                S.op("dve", lambda e, k=k, t=t: e.tensor_tensor(out=t[:], in0=xT[:, k, :], in1=rstd[:], op=ALU.mult),
                     reads=[xk, rstdk], writes=[tk])
                S.op("act", lambda e, k=k, t=t: e.activation(out=hdst(k), in_=t[:], func=AF.Identity,
                                                           bias=pv[:, ish, k:k + 1], scale=pv[:, ia, k:k + 1]),
                     reads=[tk, "pv"], writes=[hk])

        hT_cm = None
        hst = ExitStack()
        if 1 in phases or 2 in phases:
            hT_cm = hst.enter_context(nc.sbuf_tensor("hT", [128, 8, T], BF16))

        if 1 in phases:
            with ExitStack() as ps:
                lsb = lambda name, shape, dt=F32: ps.enter_context(nc.sbuf_tensor(name, shape, dt))
                xtok = [lsb("xtok%d" % i, [128, 4, D]) for i in range(2)]
                xTb = [lsb("xTb%d" % i, [128, 8, 512]) for i in range(2)]
                sq = lsb("sq1", [128, 8, 512])
                rstd = lsb("rstd1", [128, 512])
                tmp = [lsb("tmp1_%d" % i, [128, 512]) for i in range(2)]
                tps = [ps.enter_context(nc.psum_tensor("tps%d" % i, [128, 512], F32)) for i in range(4)]
                ssps = ps.enter_context(nc.psum_tensor("ssps1", [128, 512], F32))
                XTv = XT.rearrange("(k p) t -> p k t", p=128)
                for nb in range(NB):
                    xt = xtok[nb % 2]
                    xtk = "xtok%d" % (nb % 2)
                    xT = xTb[nb % 2]
                    xTk = "xTb%d" % (nb % 2)
                    S.dma(xt[:], x[nb * 512:(nb + 1) * 512, :].rearrange("(a p) f -> p a f", p=128), writes=[xtk])
                    for k in range(8):
                        tp = tps[k % 4]
                        tpk = "tps%d" % (k % 4)
                        for a in range(4):
                            S.op("pe", lambda e, tp=tp, a=a, k=k, xt=xt: e.transpose(
                                tp[:, a * 128:(a + 1) * 128], xt[:, a, k * 128:(k + 1) * 128], ident),
                                reads=[xtk, "cst"], writes=[tpk])
                        if k % 2 == 0:
                            S.op("act", lambda e, tp=tp, k=k, xT=xT: e.activation(out=xT[:, k, :], in_=tp[:], func=AF.Identity),
                                 reads=[tpk], writes=[xTk])
                        else:
                            S.op("dve", lambda e, tp=tp, k=k, xT=xT: e.tensor_copy(out=xT[:, k, :], in_=tp[:]),
                                 reads=[tpk], writes=[xTk])
                    S.dma(XTv[:, :, nb * 512:(nb + 1) * 512], xT[:], reads=[xTk], writes=["XT"])
                    normT(xT, xTk, sq, "sq1", rstd, "rstd1", ssps, "ssps1", tmp, ["tmp1_0", "tmp1_1"],
                          lambda k, nb=nb: hT_cm[:, k, nb * 512:(nb + 1) * 512], ("hT", nb), 0, 1)
                S.barrier()

        if 2 in phases:
            hkeys = [("hT", nb) for nb in range(NB)]
            with ExitStack() as ps:
                lsb = lambda name, shape, dt=F32: ps.enter_context(nc.sbuf_tensor(name, shape, dt))
                wst = [lsb("wst%d" % i, [128, 8, 128]) for i in range(2)]
                wbf = [lsb("wbf%d" % i, [128, 8, 128], BF16) for i in range(2)]
                pc = [lsb("pc%d" % i, [128, T + 3]) for i in range(2)]
                accs = [lsb("acc%d" % i, [128, T]) for i in range(2)]
                sq2 = lsb("sq2", [128, T])
                rs = lsb("rs", [128, T])
                ob = [lsb("ob%d" % i, [128, T], BF16) for i in range(2)]
                cws = lsb("cws", [128, 32, 5])
                cwg = lsb("cwg", [128, 32, 4])
                pps = [ps.enter_context(nc.psum_tensor("pps%d" % i, [128, 512], F32)) for i in range(4)]
                sps = [ps.enter_context(nc.psum_tensor("sps%d" % i, [128, 512], F32)) for i in range(2)]
                S.dma(cws[:], cw_ssm, writes=["cws"])
                S.dma(cwg[:], cw_gdn, writes=["cwg"])
                for i in range(2):
                    S.op("pool", lambda e, i=i: e.memset(pc[i][:, 0:3], 0.0), writes=["pc%d" % i])
                w_in_v = w_in.rearrange("(k p) f -> p k f", p=128)
                XBCv = XBC_T.rearrange("(b p) t -> b p t", p=128)
                QKVv = QKV_T.rearrange("(b p) t -> b p t", p=128)
                GTv = G_T.rearrange("(b p) t -> b p t", p=128)
                blocks = []
                for cb in range(32):
                    blocks.append(("xbc", cb, OFF_XBC + cb * 128))
                for cb in range(32):
                    blocks.append(("qkv", cb, OFF_QKV + cb * 128))
                for cb in range(8):
                    blocks.append(("gate", cb, OFF_GS + cb * 128))
                for cb in range(8):
                    blocks.append(("gate", 8 + cb, OFF_GG + cb * 128))
                pcount = 0
                pcnt = [0]

                def bufs(bi):
                    return (accs[bi % 2], "acc%d" % (bi % 2), pc[bi % 2], "pc%d" % (bi % 2), ob[bi % 2], "ob%d" % (bi % 2),
                            wbf[bi % 2], "wbf%d" % (bi % 2))

                def wload(bi):
                    off = blocks[bi][2]
                    ws, wsk, wb, wbk = wst[bi % 2], "wst%d" % (bi % 2), wbf[bi % 2], "wbf%d" % (bi % 2)
                    S.dma(ws[:], w_in_v[:, :, off:off + 128], writes=[wsk])
                    S.op("pool", lambda e: e.tensor_copy(out=wb[:], in_=ws[:]), reads=[wsk], writes=[wbk])

                def front(bi):
                    kind, cb, off = blocks[bi]
                    acc, acck, p_, pk, o_, ok, wb, wbk = bufs(bi)
                    if bi + 1 < len(blocks):
                        wload(bi + 1)
                    for tb in range(NB):
                        pp = pps[pcnt[0] % 4]
                        ppk = "pps%d" % (pcnt[0] % 4)
                        pcnt[0] += 1
                        for k in range(8):
                            S.op("pe", lambda e, pp=pp, k=k, tb=tb: e.matmul(
                                pp[:], wb[:, k, :], hT_cm[:, k, tb * 512:(tb + 1) * 512], start=(k == 0), stop=(k == 7)),
                                reads=[wbk, hkeys[tb]], writes=[ppk])
                        if kind == "gate":
                            S.op("act", lambda e, pp=pp, tb=tb: e.activation(
                                out=o_[:, tb * 512:(tb + 1) * 512], in_=pp[:], func=AF.Sigmoid), reads=[ppk], writes=[ok])
                        else:
                            S.op("act", lambda e, pp=pp, tb=tb: e.activation(
                                out=p_[:, 3 + tb * 512:3 + (tb + 1) * 512], in_=pp[:], func=AF.Identity), reads=[ppk], writes=[pk])
                            if kind == "xbc":
                                S.op("act", lambda e, pp=pp, tb=tb: e.activation(
                                    out=acc[:, tb * 512:(tb + 1) * 512], in_=pp[:], func=AF.Identity,
                                    bias=cws[:, cb, 4:5], scale=cws[:, cb, 3:4]), reads=[ppk, "cws"], writes=[acck])
                            else:
                                S.op("act", lambda e, pp=pp, tb=tb: e.activation(
                                    out=acc[:, tb * 512:(tb + 1) * 512], in_=pp[:], func=AF.Identity,
                                    scale=cwg[:, cb, 3:4]), reads=[ppk, "cwg"], writes=[acck])

                def conv(bi):
                    kind, cb, off = blocks[bi]
                    if kind == "gate":
                        return
                    acc, acck, p_, pk, o_, ok, wb, wbk = bufs(bi)
                    cwt = cws if kind == "xbc" else cwg
                    cwk = "cws" if kind == "xbc" else "cwg"
                    for j in range(1, 4):
                        S.op("dve", lambda e, j=j: e.scalar_tensor_tensor(
                            out=acc[:], in0=p_[:, 3 - j:3 - j + T], scalar=cwt[:, cb, 3 - j:4 - j], in1=acc[:],
                            op0=ALU.mult, op1=ALU.add), reads=[pk, cwk, acck], writes=[acck])

                def post(bi):
                    kind, cb, off = blocks[bi]
                    acc, acck, p_, pk, o_, ok, wb, wbk = bufs(bi)
                    if kind == "gate":
                        S.dma(GTv[cb], o_[:], reads=[ok], writes=["G_T"])
                        return
                    if kind == "qkv" and cb < 16:
                        S.op("act", lambda e: e.activation(out=acc[:], in_=acc[:], func=AF.Silu), reads=[acck], writes=[acck])
                        S.op("act", lambda e: e.activation(out=sq2[:], in_=acc[:], func=AF.Square), reads=[acck], writes=["sq2"])
                        for tb in range(NB):
                            sp = sps[tb % 2]
                            spk = "sps%d" % (tb % 2)
                            S.op("pe", lambda e, sp=sp, tb=tb: e.matmul(sp[:], ones, sq2[:, tb * 512:(tb + 1) * 512],
                                                                        start=True, stop=True),
                                 reads=["sq2", "cst"], writes=[spk])
                            rsqrt_to(S, cst, rs[:, tb * 512:(tb + 1) * 512], "rs", sp[:], spk)
                        qs = (128.0 ** -0.5) if cb < 8 else 1.0
                        S.op("dve", lambda e: e.scalar_tensor_tensor(
                            out=o_[:], in0=acc[:], scalar=qs, in1=rs[:], op0=ALU.mult, op1=ALU.mult),
                            reads=[acck, "rs"], writes=[ok])
                    else:
                        S.op("act", lambda e: e.activation(out=o_[:], in_=acc[:], func=AF.Silu), reads=[acck], writes=[ok])
                    dst = XBCv[cb] if kind == "xbc" else QKVv[cb]
                    S.dma(dst, o_[:], reads=[ok], writes=[kind + "_T"])

                wload(0)
                for bi in range(len(blocks)):
                    front(bi)
                    if bi > 0:
                        post(bi - 1)
                    conv(bi)
                post(len(blocks) - 1)
                S.barrier()

            with ExitStack() as ps:
                lsb = lambda name, shape, dt=F32: ps.enter_context(nc.sbuf_tensor(name, shape, dt))
                wzs = [lsb("wzs%d" % i, [128, 8, 512]) for i in range(2)]
                wzb = [lsb("wzb%d" % i, [128, 8, 512], BF16) for i in range(2)]
                zb = [lsb("zb%d" % i, [128, 512], BF16) for i in range(3)]
                wss = lsb("wss", [128, 8, 64])
                wsb = lsb("wsb", [128, 8, 64], BF16)
                smt = [lsb("smt%d" % i, [128, 96]) for i in range(2)]
                t1 = [lsb("t1_%d" % i, [128, 48]) for i in range(2)]
                rows = lsb("rows2", [128, RLEN])
                arow = lsb("arow", [128, 48])
                S.dma(rows[:], rowv.partition_broadcast(128), writes=["rows"])
                S.op("act", lambda e: e.activation(out=arow[:, 0:32], in_=rows[:, R_AL1:R_AL1 + 32], func=AF.Exp),
                     reads=["rows"], writes=["arow"])
                S.op("act", lambda e: e.activation(out=arow[:, 32:48], in_=rows[:, R_AL2:R_AL2 + 16], func=AF.Exp),
                     reads=["rows"], writes=["arow"])
                S.op("dve", lambda e: e.tensor_scalar(out=arow[:], in0=arow[:], scalar1=-1.0, scalar2=None, op0=ALU.mult),
                     reads=["arow"], writes=["arow"])
                pps = [ps.enter_context(nc.psum_tensor("zps%d" % i, [128, 512], F32)) for i in range(4)]
                sps = [ps.enter_context(nc.psum_tensor("smps%d" % i, [128, 512], F32)) for i in range(2)]
                w_in_v = w_in.rearrange("(k p) f -> p k f", p=128)
                pcount = 0
                for blk in range(8):
                    off = (OFF_Z1 + blk * 512) if blk < 4 else (OFF_Z2 + (blk - 4) * 512)
                    ws = wzs[blk % 2]
                    wsk = "wzs%d" % (blk % 2)
                    wb = wzb[blk % 2]
                    wbk = "wzb%d" % (blk % 2)
                    if blk == 0:
                        S.dma(ws[:], w_in_v[:, :, off:off + 512], writes=[wsk])
                    if blk + 1 < 8:
                        noff = (OFF_Z1 + (blk + 1) * 512) if blk + 1 < 4 else (OFF_Z2 + (blk + 1 - 4) * 512)
                        S.dma(wzs[(blk + 1) % 2][:], w_in_v[:, :, noff:noff + 512], writes=["wzs%d" % ((blk + 1) % 2)])
                    S.op("act", lambda e, ws=ws, wb=wb: e.activation(out=wb[:], in_=ws[:], func=AF.Identity), reads=[wsk], writes=[wbk])
                    for tt in range(NT):
                        pp = pps[pcount % 4]
                        ppk = "zps%d" % (pcount % 4)
                        z_ = zb[pcount % 3]
                        zk = "zb%d" % (pcount % 3)
                        pcount += 1
                        for k in range(8):
                            S.op("pe", lambda e, pp=pp, wb=wb, k=k, tt=tt: e.matmul(
                                pp[:], hT_cm[:, k, tt * 128:(tt + 1) * 128], wb[:, k, :], start=(k == 0), stop=(k == 7)),
                                reads=[wbk, hkeys[tt // 4]], writes=[ppk])
                        S.op("act", lambda e, pp=pp, z_=z_: e.activation(out=z_[:], in_=pp[:], func=AF.Silu), reads=[ppk], writes=[zk])
                        S.dma(Z[tt * 128:(tt + 1) * 128, blk * 512:(blk + 1) * 512], z_[:], reads=[zk], writes=["Z"])
                S.dma(wss[:, :, 0:32], w_in_v[:, :, OFF_DT:OFF_DT + 32], writes=["wss"])
                S.dma(wss[:, :, 32:64], w_in_v[:, :, OFF_B:OFF_B + 32], writes=["wss"])
                S.op("pool", lambda e: e.tensor_copy(out=wsb[:], in_=wss[:]), reads=["wss"], writes=["wsb"])
                for tt in range(NT):
                    sp = sps[tt % 2]
                    spk = "smps%d" % (tt % 2)
                    sm_ = smt[tt % 2]
                    smk = "smt%d" % (tt % 2)
                    t_ = t1[tt % 2]
                    tk = "t1_%d" % (tt % 2)
                    for k in range(8):
                        S.op("pe", lambda e, sp=sp, k=k, tt=tt: e.matmul(
                            sp[:, 0:64], hT_cm[:, k, tt * 128:(tt + 1) * 128], wsb[:, k, :], start=(k == 0), stop=(k == 7)),
                            reads=["wsb", hkeys[tt // 4]], writes=[spk])
                    S.op("dve", lambda e, sp=sp, t_=t_: e.tensor_tensor(out=t_[:, 0:32], in0=sp[:, 0:32],
                                                                        in1=rows[:, R_DTB1:R_DTB1 + 32], op=ALU.add),
                         reads=["rows"], writes=[spk, tk])
                    S.op("dve", lambda e, sp=sp, t_=t_: e.tensor_tensor(out=t_[:, 32:48], in0=sp[:, 48:64],
                                                                        in1=rows[:, R_DTB2:R_DTB2 + 16], op=ALU.add),
                         reads=["rows"], writes=[spk, tk])
                    S.op("act", lambda e, t_=t_: e.activation(out=t_[:], in_=t_[:], func=AF.Exp), reads=[tk], writes=[tk])
                    S.op("dve", lambda e, t_=t_: e.tensor_scalar(out=t_[:], in0=t_[:], scalar1=1.0, scalar2=None, op0=ALU.add),
                         reads=[tk], writes=[tk])
                    S.op("act", lambda e, t_=t_: e.activation(out=t_[:], in_=t_[:], func=AF.Ln), reads=[tk], writes=[tk])
                    S.op("act", lambda e, sp=sp, sm_=sm_: e.activation(out=sm_[:, 32:48], in_=sp[:, 32:48], func=AF.Sigmoid),
                         writes=[spk, smk])
                    S.op("dve", lambda e, t_=t_, sm_=sm_: e.tensor_copy(out=sm_[:, 0:32], in_=t_[:, 0:32]), reads=[tk], writes=[smk])
                    S.op("dve", lambda e, t_=t_, sm_=sm_: e.tensor_tensor(out=sm_[:, 48:96], in0=t_[:], in1=arow[:], op=ALU.mult),
                         reads=[tk, "arow"], writes=[smk])
                    S.dma(SM[tt * 128:(tt + 1) * 128, :], sm_[:], reads=[smk], writes=["SM"])
                S.barrier()

        hst.close()
        if 3 in phases:
            phase3(nc, S, st, T, XBC_T, QKV_T, Z, SM, Y, cst, cstb, rowv)
        if 4 in phases:
            phase4a(nc, S, T, Y, G_T, XT, X2T, w_su, w_gu, w_out, cst, cstb, pv)
        if 5 in phases:
            phase4b(nc, S, T, X2T, out, w_up, w_dn, cst, pv, normT)
        else:
            pass
        S.barrier()
        S.emit()
    return nc, S


def phase3(nc, S, st_outer, T, XBC_T, QKV_T, Z, SM, Y, cst, cstb, rowv):
    NT = T // 128
    ident = cst[:, C_ID, :]
    identb = cstb[:, C_ID, :]
    Umat = cst[:, C_U, :]
    GTm = cst[:, C_GT, :]
    SUm = cst[:, C_SU, :]
    ones = cst[:, C_ONE, :]
    with ExitStack() as ps:
        lsb = lambda name, shape, dt=F32: ps.enter_context(nc.sbuf_tensor(name, shape, dt))
        smt = [lsb("p3sm%d" % i, [128, 96]) for i in range(2)]
        rows = lsb("rows3", [128, RLEN])
        S.dma(rows[:], rowv.partition_broadcast(128), writes=["rows"])
        xbct = [lsb("p3xbc%d" % i, [128, 32, 128], BF16) for i in range(2)]
        qkvt = [lsb("p3qkv%d" % i, [128, 32, 128], BF16) for i in range(2)]
        zt = [lsb("p3z%d" % i, [128, 4096], BF16) for i in range(1)]
        ytile = lsb("p3y", [128, 4096], BF16)
        xs_tok = lsb("xs_tok", [128, 2048], BF16)
        b_tok = lsb("b_tok", [128, 1024], BF16)
        k_tok = lsb("k_tok", [128, 1024], BF16)
        v_tok = lsb("v_tok", [128, 2048], BF16)
        c_sb = lsb("c_sb", [128, 48])
        e_sb = lsb("e_sb", [128, 48])
        f_sb = lsb("f_sb", [128, 48])
        dA_sb = lsb("dA_sb", [128, 48])
        nbeta = lsb("nbeta", [128, 16])
        ST = lsb("ST", [128, 8, 256])
        STb = lsb("STb", [128, 8, 256], BF16)
        GS = lsb("GS", [128, 16, 128])
        GSb = lsb("GSb", [128, 16, 128], BF16)
        IB = [[lsb("ib%d_%d" % (s_, n_), [128, 4, 128]) for n_ in range(7)]
              + [lsb("ibb%d_%d" % (s_, n_), [128, 4, 128], BF16) for n_ in range(3)]
              + [lsb("ibm%d" % s_, [128, 4, 4, 128], BF16)] for s_ in range(2)]
        attnT = lsb("attnT", [128, 16, 128], BF16)
        T2T = lsb("T2T", [128, 16, 128], BF16)
        ke = lsb("ke", [128, 16, 128], BF16)
        kf = lsb("kf", [128, 16, 128], BF16)
        nwT = lsb("nwT", [128, 16, 128], BF16)
        KKm = lsb("KKm", [128, 8, 128])
        QKm = lsb("QKm", [128, 8, 128])
        Am = [lsb("Am%d" % i, [128, 4, 128]) for i in range(2)]
        DT = [lsb("DT%d" % i, [128, 4, 128]) for i in range(4)]
        MT = [lsb("MT%d" % i, [128, 4, 128], BF16) for i in range(2)]
        CBTm = [lsb("CBTm%d" % i, [128, 128]) for i in range(2)]
        xdt = [lsb("xdt%d" % i, [128, 256], BF16) for i in range(2)]
        xw = [lsb("xw%d" % i, [128, 256], BF16) for i in range(2)]
        xsD = [lsb("xsD%d" % i, [128, 256]) for i in range(2)]
        yacc = [lsb("yacc%d" % i, [128, 256]) for i in range(2)]
        yz = [lsb("yz%d" % i, [128, 256]) for i in range(2)]
        ssq = [lsb("ssq%d" % i, [128, 4]) for i in range(2)]
        vnew = [lsb("vnew%d" % i, [128, 4, 128], BF16) for i in range(2)]
        osb = [lsb("osb%d" % i, [128, 4, 128]) for i in range(1)] * 2
        on = [lsb("on%d" % i, [128, 4, 128]) for i in range(1)] * 2
        banks = [ps.enter_context(nc.psum_tensor("pb%d" % i, [128, 512], F32)) for i in range(8)]
        bctr = [0]

        def bank():
            i = bctr[0] % 8
            bctr[0] += 1
            return banks[i], "pb%d" % i
        rr = {"am": 0, "ama": 0, "g": 0, "v": 0}

        S.op("pool", lambda e: e.memset(ST[:], 0.0), writes=["ST"])
        S.op("pool", lambda e: e.memset(STb[:], 0.0), writes=["STb"])
        S.op("pool", lambda e: e.memset(GS[:], 0.0), writes=["GS"])
        S.op("pool", lambda e: e.memset(GSb[:], 0.0), writes=["GSb"])

        XBCv = XBC_T.rearrange("(b p) t -> p b t", p=128)
        QKVv = QKV_T.rearrange("(b p) t -> p b t", p=128)

        def loads(tt):
            i = tt % 2
            S.dma(smt[i][:], SM[tt * 128:(tt + 1) * 128, :], reads=["SM"], writes=["p3sm%d" % i])
            S.dma(xbct[i][:], XBCv[:, :, tt * 128:(tt + 1) * 128], reads=["xbc_T"], writes=["p3xbc%d" % i])
            S.dma(qkvt[i][:], QKVv[:, :, tt * 128:(tt + 1) * 128], reads=["qkv_T"], writes=["p3qkv%d" % i])

        def load_z(tt):
            S.dma(zt[0][:], Z[tt * 128:(tt + 1) * 128, :], reads=["Z"], writes=["p3z0"])

        def bc(ap2, n, w):
            return ap2.unsqueeze(2).to_broadcast([128, n, w])

        def v3(ap2, h=4):
            return ap2.rearrange("p (h w) -> p h w", h=h)

        loads(0)
        load_z(0)
        for tt in range(NT):
            i = tt % 2
            sm, smk = smt[i], "p3sm%d" % i
            xbc, xbk = xbct[i], "p3xbc%d" % i
            qkv, qkk = qkvt[i], "p3qkv%d" % i
            z, zk = zt[0], "p3z0"
            y, yk = ytile, "p3y"
            if tt + 1 < NT:
                loads(tt + 1)
            bD, bDk = bank()
            S.op("pe", lambda e, bD=bD, sm=sm: e.matmul(bD[:, 0:48], Umat, sm[:, 48:96], start=True, stop=True),
                 reads=[smk, "cst"], writes=[bDk])
            S.op("pe", lambda e, bD=bD, sm=sm: e.matmul(bD[:, 64:112], ones, sm[:, 48:96], start=True, stop=True),
                 reads=[smk, "cst"], writes=[bDk])
            S.op("act", lambda e, bD=bD: e.activation(out=c_sb[:], in_=bD[:, 0:48], func=AF.Identity), writes=[bDk, "c_sb"])
            S.op("act", lambda e, bD=bD: e.activation(out=e_sb[:], in_=bD[:, 0:48], func=AF.Exp), writes=[bDk, "e_sb"])
            S.op("act", lambda e, bD=bD: e.activation(out=dA_sb[:], in_=bD[:, 64:112], func=AF.Exp), writes=[bDk, "dA_sb"])
            S.op("act", lambda e, bD=bD: e.activation(out=f_sb[:], in_=bD[:, 64:112], func=AF.Identity), writes=[bDk, "f_sb"])
            S.op("dve", lambda e: e.tensor_tensor(out=f_sb[:], in0=f_sb[:], in1=c_sb[:], op=ALU.subtract),
                 reads=["c_sb", "f_sb"], writes=["f_sb"])
            S.op("act", lambda e: e.activation(out=f_sb[:], in_=f_sb[:], func=AF.Exp), reads=["f_sb"], writes=["f_sb"])
            S.op("pool", lambda e, sm=sm: e.tensor_scalar(out=nbeta[:], in0=sm[:, 32:48], scalar1=-1.0, scalar2=None, op0=ALU.mult),
                 reads=[smk], writes=["nbeta"])
            jobs = [(xbc, xbk, 0, xs_tok, "xs_tok", 2), (xbc, xbk, 16, b_tok, "b_tok", 1),
                    (qkv, qkk, 8, k_tok, "k_tok", 1), (qkv, qkk, 16, v_tok, "v_tok", 2)]
            nev = 0
            for (src, srck, b0, dst, dstk, nq) in jobs:
                for q8 in range(nq):
                    bk_, bkk = bank()
                    bb = bk_[:].bitcast(BF16)
                    for a in range(8):
                        blk = b0 + q8 * 8 + a
                        S.op("pe", lambda e, bb=bb, a=a, src=src, blk=blk: e.transpose(
                            bb[:, a * 128:(a + 1) * 128], src[:, blk, :], identb), reads=[srck, "cstb"], writes=[bkk])
                    if nev % 2 == 0:
                        S.op("act", lambda e, bb=bb, dst=dst, q8=q8: e.activation(
                            out=dst[:, q8 * 1024:(q8 + 1) * 1024], in_=bb, func=AF.Identity), writes=[bkk, dstk])
                    else:
                        S.op("dve", lambda e, bb=bb, dst=dst, q8=q8: e.tensor_copy(
                            out=dst[:, q8 * 1024:(q8 + 1) * 1024], in_=bb), writes=[bkk, dstk])
                    nev += 1

            for half in range(2):
                bk_, bkk = bank()
                for j in range(4):
                    hq = half * 4 + j
                    S.op("pe", lambda e, bk_=bk_, j=j, hq=hq, qkv=qkv: e.matmul(
                        bk_[:, j * 128:(j + 1) * 128], qkv[:, 8 + hq, :], qkv[:, 8 + hq, :], start=True, stop=True),
                        reads=[qkk], writes=[bkk])
                S.op("dve", lambda e, bk_=bk_, half=half: e.tensor_tensor(
                    out=KKm[:, half * 4:(half + 1) * 4, :], in0=v3(bk_[:]), in1=SUm.unsqueeze(1).to_broadcast([128, 4, 128]), op=ALU.mult),
                    reads=["cst"], writes=[bkk, "KKm"])
                bq_, bqk = bank()
                for j in range(4):
                    hq = half * 4 + j
                    S.op("pe", lambda e, bq_=bq_, j=j, hq=hq, qkv=qkv: e.matmul(
                        bq_[:, j * 128:(j + 1) * 128], qkv[:, 8 + hq, :], qkv[:, hq, :], start=True, stop=True),
                        reads=[qkk], writes=[bqk])
                S.op("dve", lambda e, bq_=bq_, half=half: e.tensor_tensor(
                    out=QKm[:, half * 4:(half + 1) * 4, :], in0=v3(bq_[:]), in1=Umat.unsqueeze(1).to_broadcast([128, 4, 128]), op=ALU.mult),
                    reads=["cst"], writes=[bqk, "QKm"])
            def fl(t):
                return t[:].rearrange("p h w -> p (h w)")

            def bcm(plane):
                return cst[:, plane, :].unsqueeze(1).to_broadcast([128, 4, 128])

            def build_DT(cols, r, sm, smk):
                ra = rr["ama"] % 2
                rr["ama"] += 1
                S.op("dve", lambda e: e.tensor_tensor(out=Am[ra][:], in0=bcm(C_GT), in1=bc(sm[:, cols:cols + 4], 4, 128), op=ALU.mult),
                     reads=[smk, "cst"], writes=["Am%d" % ra])
                sg, sgk = bank()
                for j in range(4):
                    S.op("pe", lambda e, sg=sg, j=j: e.matmul(sg[:, j * 128:(j + 1) * 128], Am[ra][:, j, :], Umat, start=True, stop=True),
                         reads=["Am%d" % ra, "cst"], writes=[sgk])
                S.op("act", lambda e, sg=sg: e.activation(out=fl(DT[r]), in_=sg[:], func=AF.Exp), writes=[sgk, "DT%d" % r])

            def gdn_quad(q, s_, sm=sm, smk=smk, qkv=qkv, qkk=qkk, z=z, zk=zk, y=y, yk=yk):
                X, XT, Dv, DvT, E, F, X0T, Dvb, DvTb, Eb, XMb = IB[s_]
                kX, kXT, kDv, kDvT, kE, kF, kXM, kDvb, kDvTb, kEb, kXMb = [("ib", s_, n_) for n_ in range(11)]
                qs = slice(q * 4, (q + 1) * 4)
                r = rr["am"] % 4
                rr["am"] += 1
                build_DT(80 + q * 4, r, sm, smk)
                for j in range(4):
                    hv = q * 4 + j
                    hq = hv // 2
                    S.op("dve", lambda e, j=j, hv=hv, hq=hq: e.scalar_tensor_tensor(
                        out=X[:, j, :], in0=KKm[:, hq, :], scalar=nbeta[:, hv:hv + 1], in1=DT[r][:, j, :], op0=ALU.mult, op1=ALU.mult),
                        reads=["KKm", "nbeta", "DT%d" % r], writes=[kX])
                S.op("dve", lambda e: e.tensor_tensor(
                    out=attnT[:, qs, :].rearrange("p (a b) w -> p a b w", a=2),
                    in0=QKm[:, 2 * q:2 * q + 2, :].unsqueeze(2).to_broadcast([128, 2, 2, 128]),
                    in1=DT[r][:].rearrange("p (a b) w -> p a b w", a=2), op=ALU.mult),
                    reads=["QKm", "DT%d" % r], writes=[("attnT", q)])
                yield

                def mm4(lhs, lk, rhs, rk):
                    b_, bk = bank()
                    for j in range(4):
                        S.op("pe", lambda e, b_=b_, j=j: e.matmul(b_[:, j * 128:(j + 1) * 128], lhs[:, j, :], rhs[:, j, :], start=True, stop=True),
                             reads=[lk, rk], writes=[bk])
                    return b_, bk

                def tr4(src, sk):
                    b_, bk = bank()
                    for j in range(4):
                        S.op("pe", lambda e, b_=b_, j=j: e.transpose(b_[:, j * 128:(j + 1) * 128], src[:, j, :], ident),
                             reads=[sk, "cst"], writes=[bk])
                    return b_, bk

                def ev_act(b_, bk, dst, dk):
                    S.op("act", lambda e: e.activation(out=fl(dst), in_=b_[:], func=AF.Identity), writes=[bk, dk])

                def ev_dve(b_, bk, dst, dk):
                    S.op("dve", lambda e: e.tensor_copy(out=fl(dst), in_=b_[:]), writes=[bk, dk])

                def acc_dve(b_, bk, dst, dk, out=None, ok=None):
                    o_ = fl(dst) if out is None else out
                    S.op("dve", lambda e: e.tensor_tensor(out=o_, in0=fl(dst), in1=b_[:], op=ALU.add),
                         reads=[dk], writes=[bk, dk if ok is None else ok])

                b_, bk = tr4(X, kX)
                ev_act(b_, bk, XT, kXT)
                S.op("dve", lambda e: e.tensor_tensor(out=E[:], in0=X[:], in1=bcm(C_BD8), op=ALU.mult), reads=[kX, "cst"], writes=[kE])
                S.op("dve", lambda e: e.tensor_tensor(out=Dv[:], in0=E[:], in1=bcm(C_ID), op=ALU.add), reads=[kE, "cst"], writes=[kDv])
                yield
                S.op("dve", lambda e: e.tensor_tensor(out=X0T[:], in0=XT[:], in1=bcm(C_BD8), op=ALU.mult), reads=[kXT, "cst"], writes=[kXM])
                S.op("dve", lambda e: e.tensor_tensor(
                    out=XMb[:], in0=XT[:].unsqueeze(1).to_broadcast([128, 4, 4, 128]),
                    in1=cst[:, C_CMT:C_CMT + 4, :].unsqueeze(2).to_broadcast([128, 4, 4, 128]), op=ALU.mult),
                    reads=[kXT, "cst"], writes=[kXMb])
                yield
                S.op("dve", lambda e: e.tensor_tensor(out=DvT[:], in0=X0T[:], in1=bcm(C_ID), op=ALU.add), reads=[kXM, "cst"], writes=[kDvT])
                b1, b1k = mm4(X0T, kXM, E, kE)
                b2, b2k = mm4(E, kE, X0T, kXM)
                ev_dve(b1, b1k, F, kF)
                ev_act(b2, b2k, X, kX)
                yield
                b1, b1k = mm4(X, kX, Dv, kDv)
                b2, b2k = mm4(Dv, kDv, X, kX)
                acc_dve(b1, b1k, Dv, kDv)
                acc_dve(b2, b2k, DvT, kDvT)
                b3_, b3k_ = mm4(F, kF, X, kX)
                ev_act(b3_, b3k_, E, kE)
                yield
                b1, b1k = mm4(E, kE, Dv, kDv)
                b2, b2k = mm4(Dv, kDv, E, kE)
                acc_dve(b1, b1k, Dv, kDv, out=fl(Dvb), ok=kDvb)
                acc_dve(b2, b2k, DvT, kDvT, out=fl(DvTb), ok=kDvTb)
                yield
                for n, b in enumerate((8, 16, 32, 64)):
                    b_, bk = mm4(XMb[:, n, :, :], kXMb, Dvb, kDvb)
                    S.op("act", lambda e, b_=b_: e.activation(out=fl(Eb), in_=b_[:], func=AF.Identity), writes=[bk, kEb])
                    yield
                    b1, b1k = mm4(DvTb, kDvTb, Eb, kEb)
                    if b != 64:
                        b2, b2k = mm4(Eb, kEb, DvTb, kDvTb)
                        acc_dve(b1, b1k, Dvb, kDvb)
                        acc_dve(b2, b2k, DvTb, kDvTb)
                        yield
                    else:
                        acc_dve(b1, b1k, Dvb, kDvb, out=T2T[:, qs, :].rearrange("p h w -> p (h w)"), ok=("T2T", q))
                        yield
                kq = k_tok[:, 2 * q * 128:(2 * q + 2) * 128].rearrange("p (a w) -> p a w", a=2).unsqueeze(2).to_broadcast([128, 2, 2, 128])
                S.op("pool", lambda e: e.tensor_tensor(
                    out=ke[:, qs, :].rearrange("p (a b) w -> p a b w", a=2), in0=kq,
                    in1=e_sb[:, 32 + q * 4:36 + q * 4].rearrange("p (a b) -> p a b", a=2).unsqueeze(3).to_broadcast([128, 2, 2, 128]),
                    op=ALU.mult), reads=["k_tok", "e_sb"], writes=[("ke", q)])
                S.op("pool", lambda e: e.tensor_tensor(
                    out=kf[:, qs, :].rearrange("p (a b) w -> p a b w", a=2), in0=kq,
                    in1=f_sb[:, 32 + q * 4:36 + q * 4].rearrange("p (a b) -> p a b", a=2).unsqueeze(3).to_broadcast([128, 2, 2, 128]),
                    op=ALU.mult), reads=["k_tok", "f_sb"], writes=[("kf", q)])
                wp, wpk = bank()
                for j in range(4):
                    hv = q * 4 + j
                    S.op("pe", lambda e, wp=wp, j=j, hv=hv: e.matmul(wp[:, j * 128:(j + 1) * 128], ke[:, hv, :], T2T[:, hv, :],
                                                                     start=True, stop=True),
                         reads=[("ke", q), ("T2T", q)], writes=[wpk])
                S.op("act", lambda e, wp=wp: e.activation(out=nwT[:, qs, :].rearrange("p h w -> p (h w)"), in_=wp[:],
                                                        func=AF.Identity, scale=-1.0), writes=[wpk, ("nwT", q)])
                yield
                qs = slice(q * 4, (q + 1) * 4)
                vi = rr["v"] % 2
                rr["v"] += 1
                vp, vpk = bank()
                for j in range(4):
                    hv = q * 4 + j
                    S.op("pe", lambda e, vp=vp, j=j, hv=hv: e.matmul(vp[:, j * 128:(j + 1) * 128], T2T[:, hv, :],
                                                                     v_tok[:, hv * 128:(hv + 1) * 128], start=True, stop=False),
                         reads=[("T2T", q), "v_tok"], writes=[vpk])
                    S.op("pe", lambda e, vp=vp, j=j, hv=hv: e.matmul(vp[:, j * 128:(j + 1) * 128], nwT[:, hv, :], GSb[:, hv, :],
                                                                     start=False, stop=True),
                         reads=[("nwT", q), ("GSb", q)], writes=[vpk])
                for j in range(4):
                    hv = q * 4 + j
                    S.op("act", lambda e, vp=vp, j=j, hv=hv, vi=vi: e.activation(
                        out=vnew[vi][:, j, :], in_=vp[:, j * 128:(j + 1) * 128], func=AF.Identity, scale=sm[:, 32 + hv:33 + hv]),
                        reads=[smk], writes=[vpk, "vnew%d" % vi])
                yield
                oi, oik = bank()
                for j in range(4):
                    hv = q * 4 + j
                    hq = hv // 2
                    S.op("pe", lambda e, oi=oi, j=j, hv=hv, hq=hq: e.matmul(oi[:, j * 128:(j + 1) * 128], qkv[:, hq, :], GSb[:, hv, :],
                                                                               start=True, stop=True),
                         reads=[qkk, ("GSb", q)], writes=[oik])
                oa, oak = bank()
                for j in range(4):
                    hv = q * 4 + j
                    S.op("pe", lambda e, oa=oa, j=j, hv=hv, vi=vi: e.matmul(oa[:, j * 128:(j + 1) * 128], attnT[:, hv, :], vnew[vi][:, j, :],
                                                                           start=True, stop=True),
                         reads=[("attnT", q), "vnew%d" % vi], writes=[oak])
                sn, snk = bank()
                for j in range(4):
                    hv = q * 4 + j
                    S.op("pe", lambda e, sn=sn, j=j, hv=hv, vi=vi: e.matmul(sn[:, j * 128:(j + 1) * 128], kf[:, hv, :], vnew[vi][:, j, :],
                                                                           start=True, stop=True),
                         reads=[("kf", q), "vnew%d" % vi], writes=[snk])
                for j in range(4):
                    hv = q * 4 + j
                    S.op("act", lambda e, oi=oi, j=j, hv=hv, vi=vi: e.activation(
                        out=osb[vi][:, j, :], in_=oi[:, j * 128:(j + 1) * 128], func=AF.Identity, scale=e_sb[:, 32 + hv:33 + hv]),
                        reads=["e_sb"], writes=[oik, "osb0"])
                S.op("dve", lambda e, oa=oa, vi=vi: e.tensor_tensor(
                    out=osb[vi][:].rearrange("p h w -> p (h w)"), in0=osb[vi][:].rearrange("p h w -> p (h w)"), in1=oa[:], op=ALU.add),
                    reads=["osb0"], writes=[oak, "osb0"])
                for j in range(4):
                    S.op("act", lambda e, j=j, vi=vi: e.activation(out=on[vi][:, j, :], in_=osb[vi][:, j, :], func=AF.Square,
                                                                 accum_out=ssq[vi][:, j:j + 1]),
                         reads=["osb0"], writes=["on0", "ssq%d" % vi])
                rsqrt_to(S, cst, ssq[vi][:], "ssq%d" % vi, ssq[vi][:], "ssq%d" % vi, 1.0 / 128)
                for j in range(4):
                    hv = q * 4 + j
                    S.op("dve", lambda e, j=j, vi=vi: e.scalar_tensor_tensor(
                        out=on[vi][:, j, :], in0=osb[vi][:, j, :], scalar=ssq[vi][:, j:j + 1], in1=rows[:, R_NW2:R_NW2 + 128],
                        op0=ALU.mult, op1=ALU.mult), reads=["osb0", "ssq%d" % vi, "rows"], writes=["on0"])
                S.op("pool", lambda e, vi=vi: e.tensor_tensor(
                    out=y[:, 2048 + q * 512:2048 + (q + 1) * 512], in0=on[vi][:].rearrange("p h w -> p (h w)"),
                    in1=z[:, 2048 + q * 512:2048 + (q + 1) * 512], op=ALU.mult), reads=["on0", zk], writes=[yk])
                for j in range(4):
                    hv = q * 4 + j
                    S.op("dve", lambda e, sn=sn, j=j, hv=hv: e.scalar_tensor_tensor(
                        out=GS[:, hv, :], in0=GS[:, hv, :], scalar=dA_sb[:, 32 + hv:33 + hv], in1=sn[:, j * 128:(j + 1) * 128],
                        op0=ALU.mult, op1=ALU.add), reads=[("GS", q), "dA_sb"], writes=[snk, ("GS", q)])
                S.op("act", lambda e, qs=qs: e.activation(out=GSb[:, qs, :], in_=GS[:, qs, :], func=AF.Identity),
                     reads=[("GS", q)], writes=[("GSb", q)])
                yield

            def ssd_group(g, sm=sm, smk=smk, xbc=xbc, xbk=xbk, z=z, zk=zk, y=y, yk=yk):
                gi = rr["g"] % 2
                rr["g"] += 1
                r = rr["am"] % 4
                rr["am"] += 1
                b2, b2k = bank()
                S.op("pe", lambda e: e.matmul(b2[:, 0:128], xbc[:, 16 + g, :], xbc[:, 24 + g, :], start=True, stop=True),
                     reads=[xbk], writes=[b2k])
                S.op("dve", lambda e: e.tensor_tensor(out=CBTm[gi][:], in0=b2[:, 0:128], in1=Umat, op=ALU.mult),
                     reads=["cst"], writes=[b2k, "CBTm%d" % gi])
                xs3 = v3(xs_tok[:, g * 256:(g + 1) * 256])
                S.op("dve", lambda e: e.tensor_tensor(
                    out=v3(xdt[gi][:]), in0=xs3, in1=bc(sm[:, g * 4:(g + 1) * 4], 4, 64), op=ALU.mult),
                    reads=["xs_tok", smk], writes=["xdt%d" % gi])
                S.op("dve", lambda e: e.tensor_tensor(
                    out=v3(xw[gi][:]), in0=v3(xdt[gi][:]), in1=bc(f_sb[:, g * 4:(g + 1) * 4], 4, 64), op=ALU.mult),
                    reads=["xdt%d" % gi, "f_sb"], writes=["xw%d" % gi])
                S.op("pool", lambda e: e.tensor_tensor(
                    out=v3(xsD[gi][:]), in0=xs3, in1=bc(rows[:, R_D1 + g * 4:R_D1 + (g + 1) * 4], 4, 64), op=ALU.mult),
                    reads=["xs_tok", "rows"], writes=["xsD%d" % gi])
                build_DT(48 + g * 4, r, sm, smk)
                yield
                S.op("dve", lambda e: e.tensor_tensor(out=MT[gi][:], in0=DT[r][:],
                                                      in1=CBTm[gi][:].unsqueeze(1).to_broadcast([128, 4, 128]), op=ALU.mult),
                     reads=["CBTm%d" % gi, "DT%d" % r], writes=["MT%d" % gi])
                b3, b3k = bank()
                for h in range(4):
                    S.op("pe", lambda e, h=h: e.matmul(b3[:, h * 64:(h + 1) * 64], MT[gi][:, h, :],
                                                       xdt[gi][:, h * 64:(h + 1) * 64], start=True, stop=True),
                         reads=["MT%d" % gi, "xdt%d" % gi], writes=[b3k])
                S.op("pe", lambda e: e.matmul(b3[:, 256:512], xbc[:, 24 + g, :], STb[:, g, :], start=True, stop=True),
                     reads=[xbk, ("STb", g)], writes=[b3k])
                b4, b4k = bank()
                S.op("pe", lambda e: e.matmul(b4[:, 256:512], b_tok[:, g * 128:(g + 1) * 128], xw[gi][:], start=True, stop=True),
                     reads=["b_tok", "xw%d" % gi], writes=[b4k])
                S.op("dve", lambda e: e.tensor_tensor(
                    out=v3(yacc[gi][:]), in0=v3(b3[:, 256:512]), in1=bc(e_sb[:, g * 4:(g + 1) * 4], 4, 64), op=ALU.mult),
                    reads=["e_sb"], writes=[b3k, "yacc%d" % gi])
                S.op("dve", lambda e: e.tensor_tensor(out=yacc[gi][:], in0=yacc[gi][:], in1=b3[:, 0:256], op=ALU.add),
                     reads=["yacc%d" % gi], writes=[b3k, "yacc%d" % gi])
                S.op("pool", lambda e: e.tensor_tensor(out=yacc[gi][:], in0=yacc[gi][:], in1=xsD[gi][:], op=ALU.add),
                     reads=["yacc%d" % gi, "xsD%d" % gi], writes=["yacc%d" % gi])
                S.op("dve", lambda e: e.tensor_tensor(out=yz[gi][:], in0=yacc[gi][:], in1=z[:, g * 256:(g + 1) * 256], op=ALU.mult),
                     reads=["yacc%d" % gi, zk], writes=["yz%d" % gi])
                S.op("act", lambda e: e.activation(out=xsD[gi][:], in_=yz[gi][:], func=AF.Square, accum_out=ssq[gi][:, 0:1]),
                     reads=["yz%d" % gi], writes=["xsD%d" % gi, "ssq%d" % gi])
                rsqrt_to(S, cst, ssq[gi][:, 0:1], "ssq%d" % gi, ssq[gi][:, 0:1], "ssq%d" % gi, 1.0 / 256)
                S.op("dve", lambda e: e.scalar_tensor_tensor(
                    out=y[:, g * 256:(g + 1) * 256], in0=yz[gi][:], scalar=ssq[gi][:, 0:1],
                    in1=rows[:, R_NW1 + g * 256:R_NW1 + (g + 1) * 256], op0=ALU.mult, op1=ALU.mult),
                    reads=["yz%d" % gi, "ssq%d" % gi, "rows"], writes=[yk])
                S.op("pool", lambda e: e.tensor_tensor(
                    out=v3(ST[:, g, :]), in0=v3(ST[:, g, :]), in1=bc(dA_sb[:, g * 4:(g + 1) * 4], 4, 64), op=ALU.mult),
                    reads=[("ST", g), "dA_sb"], writes=[("ST", g)])
                S.op("dve", lambda e: e.tensor_tensor(out=ST[:, g, :], in0=ST[:, g, :], in1=b4[:, 256:512], op=ALU.add),
                     reads=[("ST", g)], writes=[b4k, ("ST", g)])
                S.op("act", lambda e: e.activation(out=STb[:, g, :], in_=ST[:, g, :], func=AF.Identity),
                     reads=[("ST", g)], writes=[("STb", g)])
                yield

            pend_g = [gdn_quad(q, q % 2) for q in range(4)]
            pend_s = [ssd_group(g) for g in range(8)]
            live = []
            slots = {"g": 0, "s": 0}

            def refill():
                while slots["g"] < 2 and pend_g:
                    live.append(("g", pend_g.pop(0)))
                    slots["g"] += 1
                while slots["s"] < 2 and pend_s:
                    live.append(("s", pend_s.pop(0)))
                    slots["s"] += 1
            refill()
            while live:
                for item in list(live):
                    kind, g_ = item
                    try:
                        next(g_)
                    except StopIteration:
                        live.remove(item)
                        slots[kind] -= 1
                refill()

            S.dma(Y[tt * 128:(tt + 1) * 128, :], y[:], reads=[yk], writes=["Y"])
            if tt + 1 < NT:
                load_z(tt + 1)
        S.barrier()


def load_weight_bf16(nc, S, dst, dstk, src_v, nk, ncols, stg, stgk):
    cnt = 0
    kc = 2
    for k0 in range(0, nk, kc):
        for c0 in range(0, ncols, 512):
            s_ = stg[cnt % 2]
            sk = stgk[cnt % 2]
            cnt += 1
            S.dma(s_[:], src_v[:, k0:k0 + kc, c0:c0 + 512], writes=[sk])
            if cnt % 2 == 0:
                S.op("act", lambda e, s_=s_, k0=k0, c0=c0: e.activation(out=dst[:, k0:k0 + kc, c0:c0 + 512], in_=s_[:], func=AF.Identity),
                     reads=[sk], writes=[dstk])
            else:
                S.op("dve", lambda e, s_=s_, k0=k0, c0=c0: e.tensor_copy(out=dst[:, k0:k0 + kc, c0:c0 + 512], in_=s_[:]),
                     reads=[sk], writes=[dstk])


def phase4a(nc, S, T, Y, G_T, XT, X2T, w_su, w_gu, w_out, cst, cstb, pv):
    BW = 256
    NBW = T // BW
    NA = BW // 128
    identb = cstb[:, C_ID, :]
    onesD = cst[:, C_ONED, :]
    with ExitStack() as ps:
        lsb = lambda name, shape, dt=F32: ps.enter_context(nc.sbuf_tensor(name, shape, dt))
        Wsu = lsb("Wsu", [128, 16, D], BF16)
        Wgu = lsb("Wgu", [128, 16, D], BF16)
        Wo = lsb("Wo", [128, 8, D], BF16)
        stg = [lsb("stg%d" % i, [128, 2, 512]) for i in range(2)]
        ytoks = [lsb("ytok%d" % i, [128, NA, 4096], BF16) for i in range(2)]
        yT = lsb("yT", [128, 32, BW], BF16)
        gTs = [lsb("gT%d" % i, [128, 16, BW], BF16) for i in range(2)]
        xTs = [lsb("xT4_%d" % i, [128, 8, BW]) for i in range(2)]
        t1 = [lsb("m_t1_%d" % i, [128, BW]) for i in range(2)]
        t2 = [lsb("m_t2_%d" % i, [128, BW]) for i in range(2)]
        mT = lsb("mT", [128, 8, BW], BF16)
        m2s = lsb("m2s", [128, 8, BW])
        sqm = lsb("sqm", [128, 8, BW])
        rstd = lsb("rstd4", [128, BW])
        pu = [ps.enter_context(nc.psum_tensor("pu%d" % i, [128, 512], F32)) for i in range(4)]
        pt = [ps.enter_context(nc.psum_tensor("ptb%d" % i, [128, 1024], BF16)) for i in range(2)]
        pss = ps.enter_context(nc.psum_tensor("pss4", [128, 512], F32))
        load_weight_bf16(nc, S, Wsu, "Wsu", w_su.rearrange("(k p) f -> p k f", p=128), 16, D, stg, ["stg0", "stg1"])
        load_weight_bf16(nc, S, Wgu, "Wgu", w_gu.rearrange("(k p) f -> p k f", p=128), 16, D, stg, ["stg0", "stg1"])
        load_weight_bf16(nc, S, Wo, "Wo", w_out.rearrange("(k p) f -> p k f", p=128), 8, D, stg, ["stg0", "stg1"])
        GTv = G_T.rearrange("(b p) t -> p b t", p=128)
        XTv = XT.rearrange("(k p) t -> p k t", p=128)
        X2Tv = X2T.rearrange("(k p) t -> p k t", p=128)
        pc = 0
        def loads4(nb):
            t0 = nb * BW
            i = nb % 2
            S.dma(ytoks[i][:], Y[t0:t0 + BW, :].rearrange("(a p) f -> p a f", p=128), reads=["Y"], writes=["ytok%d" % i])
            S.dma(gTs[i][:], GTv[:, :, t0:t0 + BW], reads=["G_T"], writes=["gT%d" % i])
            S.dma(xTs[i][:], XTv[:, :, t0:t0 + BW], reads=["XT"], writes=["xT4_%d" % i])
        loads4(0)
        for nb in range(NBW):
            t0 = nb * BW
            ytok, gT, xT = ytoks[nb % 2], gTs[nb % 2], xTs[nb % 2]
            ytk, gTk, xTk = "ytok%d" % (nb % 2), "gT%d" % (nb % 2), "xT4_%d" % (nb % 2)
            if nb + 1 < NBW:
                loads4(nb + 1)
            for cb in range(32):
                p_ = pt[cb % 2]
                pk = "ptb%d" % (cb % 2)
                for a in range(NA):
                    S.op("pe", lambda e, p_=p_, a=a, cb=cb, ytok=ytok: e.transpose(p_[:, a * 128:(a + 1) * 128],
                                                                      ytok[:, a, cb * 128:(cb + 1) * 128], identb),
                         reads=[ytk, "cstb"], writes=[pk])
                if cb % 2 == 0:
                    S.op("act", lambda e, p_=p_, cb=cb: e.activation(out=yT[:, cb, :], in_=p_[:, 0:BW], func=AF.Identity),
                         reads=[pk], writes=["yT"])
                else:
                    S.op("dve", lambda e, p_=p_, cb=cb: e.tensor_copy(out=yT[:, cb, :], in_=p_[:, 0:BW]), reads=[pk], writes=["yT"])
            for blk in range(8):
                p1 = pu[pc % 4]
                p1k = "pu%d" % (pc % 4)
                pc += 1
                p2 = pu[pc % 4]
                p2k = "pu%d" % (pc % 4)
                pc += 1
                for k in range(16):
                    S.op("pe", lambda e, p1=p1, k=k, blk=blk: e.matmul(p1[:, 0:BW], Wsu[:, k, blk * 128:(blk + 1) * 128], yT[:, k, :],
                                                                       start=(k == 0), stop=(k == 15)),
                         reads=["Wsu", "yT"], writes=[p1k])
                for k in range(16):
                    S.op("pe", lambda e, p2=p2, k=k, blk=blk: e.matmul(p2[:, 0:BW], Wgu[:, k, blk * 128:(blk + 1) * 128], yT[:, 16 + k, :],
                                                                       start=(k == 0), stop=(k == 15)),
                         reads=["Wgu", "yT"], writes=[p2k])
                a1 = t1[blk % 2]
                a1k = "m_t1_%d" % (blk % 2)
                a2 = t2[blk % 2]
                a2k = "m_t2_%d" % (blk % 2)
                S.op("dve", lambda e, p1=p1, a1=a1, blk=blk, gT=gT: e.tensor_tensor(out=a1[:], in0=p1[:, 0:BW], in1=gT[:, blk, :], op=ALU.mult),
                     reads=[p1k, gTk], writes=[a1k])
                S.op("dve", lambda e, p2=p2, a2=a2, blk=blk, gT=gT: e.tensor_tensor(out=a2[:], in0=p2[:, 0:BW], in1=gT[:, 8 + blk, :], op=ALU.mult),
                     reads=[p2k, gTk], writes=[a2k])
                S.op("dve", lambda e, a1=a1, a2=a2, blk=blk: e.tensor_tensor(out=mT[:, blk, :], in0=a1[:], in1=a2[:], op=ALU.add),
                     reads=[a1k, a2k], writes=["mT"])
            for blk in range(8):
                p1 = pu[pc % 4]
                p1k = "pu%d" % (pc % 4)
                pc += 1
                for k in range(8):
                    S.op("pe", lambda e, p1=p1, k=k, blk=blk: e.matmul(p1[:, 0:BW], Wo[:, k, blk * 128:(blk + 1) * 128], mT[:, k, :],
                                                                       start=(k == 0), stop=(k == 7)),
                         reads=["Wo", "mT"], writes=[p1k])
                S.op("act", lambda e, p1=p1, blk=blk: e.activation(out=m2s[:, blk, :], in_=p1[:, 0:BW], func=AF.Identity),
                     reads=[p1k], writes=["m2s"])
                S.op("act", lambda e, p1=p1, blk=blk: e.activation(out=sqm[:, blk, :], in_=p1[:, 0:BW], func=AF.Square),
                     reads=[p1k], writes=["sqm"])
            for k in range(8):
                S.op("pe", lambda e, k=k: e.matmul(pss[:, 0:BW], onesD, sqm[:, k, :], start=(k == 0), stop=(k == 7)),
                     reads=["sqm", "cst"], writes=["pss4"])
            rsqrt_to(S, cst, rstd[:], "rstd4", pss[:, 0:BW], "pss4")
            for blk in range(8):
                a1 = t1[blk % 2]
                a1k = "m_t1_%d" % (blk % 2)
                S.op("dve", lambda e, a1=a1, blk=blk: e.tensor_tensor(out=a1[:], in0=m2s[:, blk, :], in1=rstd[:], op=ALU.mult),
                     reads=["m2s", "rstd4"], writes=[a1k])
                S.op("dve", lambda e, a1=a1, blk=blk, xT=xT: e.scalar_tensor_tensor(
                    out=xT[:, blk, :], in0=a1[:], scalar=pv[:, 2, blk:blk + 1], in1=xT[:, blk, :], op0=ALU.mult, op1=ALU.add),
                    reads=[a1k, "pv", xTk], writes=[xTk])
            S.dma(X2Tv[:, :, t0:t0 + BW], xT[:], reads=[xTk], writes=["X2T"])
        S.barrier()


def phase4b(nc, S, T, X2T, out, w_up, w_dn, cst, pv, normT):
    BW = 256
    NBW = T // BW
    NA = BW // 128
    ident = cst[:, C_ID, :]
    onesD = cst[:, C_ONED, :]
    with ExitStack() as ps:
        lsb = lambda name, shape, dt=F32: ps.enter_context(nc.sbuf_tensor(name, shape, dt))
        Wup = lsb("Wup", [128, 8, 4096], BF16)
        Wdn = lsb("Wdn", [128, 32, D], BF16)
        stg = [lsb("stgb%d" % i, [128, 2, 512]) for i in range(2)]
        xTs = [lsb("x5T%d" % i, [128, 8, BW]) for i in range(2)]
        sq = lsb("sq5", [128, 8, BW])
        rstd = lsb("rstd5", [128, BW])
        tmp = [lsb("tmp5_%d" % i, [128, BW]) for i in range(2)]
        h2T = lsb("h2T", [128, 8, BW], BF16)
        rl = [lsb("rl%d" % i, [128, BW]) for i in range(2)]
        actT = lsb("actT", [128, 32, BW], BF16)
        dns = lsb("dns", [128, 8, BW])
        otok = [lsb("otok%d" % i, [128, 512]) for i in range(2)]
        pu = [ps.enter_context(nc.psum_tensor("p5u%d" % i, [128, 512], F32)) for i in range(4)]
        pss = ps.enter_context(nc.psum_tensor("pss5", [128, 512], F32))
        po = [ps.enter_context(nc.psum_tensor("p5o%d" % i, [128, 512], F32)) for i in range(2)]
        load_weight_bf16(nc, S, Wup, "Wup", w_up.rearrange("(k p) f -> p k f", p=128), 8, 4096, stg, ["stgb0", "stgb1"])
        load_weight_bf16(nc, S, Wdn, "Wdn", w_dn.rearrange("(k p) f -> p k f", p=128), 32, D, stg, ["stgb0", "stgb1"])
        X2Tv = X2T.rearrange("(k p) t -> p k t", p=128)
        pc = 0
        oc = 0
        S.dma(xTs[0][:], X2Tv[:, :, 0:BW], reads=["X2T"], writes=["x5T0"])
        for nb in range(NBW):
            t0 = nb * BW
            xT = xTs[nb % 2]
            xk = "x5T%d" % (nb % 2)
            if nb + 1 < NBW:
                S.dma(xTs[(nb + 1) % 2][:], X2Tv[:, :, t0 + BW:t0 + 2 * BW], reads=["X2T"], writes=["x5T%d" % ((nb + 1) % 2)])
            normT(xT, xk, sq, "sq5", rstd, "rstd5", pss, "pss5", tmp, ["tmp5_0", "tmp5_1"],
                  lambda k: h2T[:, k, :], "h2T", 3, 4, BW)
            for blk in range(32):
                p1 = pu[pc % 4]
                p1k = "p5u%d" % (pc % 4)
                pc += 1
                for k in range(8):
                    S.op("pe", lambda e, p1=p1, k=k, blk=blk: e.matmul(p1[:, 0:BW], Wup[:, k, blk * 128:(blk + 1) * 128], h2T[:, k, :],
                                                                       start=(k == 0), stop=(k == 7)),
                         reads=["Wup", "h2T"], writes=[p1k])
                r_ = rl[blk % 2]
                rk = "rl%d" % (blk % 2)
                S.op("act", lambda e, p1=p1, r_=r_: e.activation(out=r_[:], in_=p1[:, 0:BW], func=AF.Relu), reads=[p1k], writes=[rk])
                eng = "pool" if blk % 4 == 0 else "dve"
                S.op(eng, lambda e, r_=r_, blk=blk: e.tensor_tensor(out=actT[:, blk, :], in0=r_[:], in1=r_[:], op=ALU.mult),
                     reads=[rk], writes=["actT"])
            for blk in range(8):
                p1 = pu[pc % 4]
                p1k = "p5u%d" % (pc % 4)
                pc += 1
                for k in range(32):
                    S.op("pe", lambda e, p1=p1, k=k, blk=blk: e.matmul(p1[:, 0:BW], Wdn[:, k, blk * 128:(blk + 1) * 128], actT[:, k, :],
                                                                       start=(k == 0), stop=(k == 31)),
                         reads=["Wdn", "actT"], writes=[p1k])
                S.op("act", lambda e, p1=p1, blk=blk: e.activation(out=dns[:, blk, :], in_=p1[:, 0:BW], func=AF.Identity),
                     reads=[p1k], writes=["dns"])
                S.op("act", lambda e, p1=p1, blk=blk: e.activation(out=sq[:, blk, :], in_=p1[:, 0:BW], func=AF.Square),
                     reads=[p1k], writes=["sq5"])
            for k in range(8):
                S.op("pe", lambda e, k=k: e.matmul(pss[:, 0:BW], onesD, sq[:, k, :], start=(k == 0), stop=(k == 7)),
                     reads=["sq5", "cst"], writes=["pss5"])
            rsqrt_to(S, cst, rstd[:], "rstd5", pss[:, 0:BW], "pss5")
            for blk in range(8):
                t_ = tmp[blk % 2]
                tk = "tmp5_%d" % (blk % 2)
                S.op("dve", lambda e, t_=t_, blk=blk: e.tensor_tensor(out=t_[:], in0=dns[:, blk, :], in1=rstd[:], op=ALU.mult),
                     reads=["dns", "rstd5"], writes=[tk])
                S.op("dve", lambda e, t_=t_, blk=blk, xT=xT: e.scalar_tensor_tensor(
                    out=dns[:, blk, :], in0=t_[:], scalar=pv[:, 5, blk:blk + 1], in1=xT[:, blk, :], op0=ALU.mult, op1=ALU.add),
                    reads=[tk, "pv", xk, "dns"], writes=["dns"])
            for a in range(NA):
                for half in range(2):
                    ot = otok[oc % 2]
                    otk = "otok%d" % (oc % 2)
                    oc += 1
                    p_ = po[half]
                    pk = "p5o%d" % half
                    for b4 in range(4):
                        blk = half * 4 + b4
                        S.op("pe", lambda e, p_=p_, b4=b4, blk=blk, a=a: e.transpose(
                            p_[:, b4 * 128:(b4 + 1) * 128], dns[:, blk, a * 128:(a + 1) * 128], ident),
                            reads=["dns", "cst"], writes=[pk])
                    if half == 0:
                        S.op("act", lambda e, p_=p_, ot=ot: e.activation(out=ot[:], in_=p_[:], func=AF.Identity),
                             reads=[pk], writes=[otk])
                    else:
                        S.op("dve", lambda e, p_=p_, ot=ot: e.tensor_copy(out=ot[:], in_=p_[:]), reads=[pk], writes=[otk])
                    S.dma(out[t0 + a * 128:t0 + (a + 1) * 128, half * 512:(half + 1) * 512], ot[:], reads=[otk], writes=["out"])
        S.barrier()


def host_inputs(inputs, b, T):
    f = lambda a: np.ascontiguousarray(np.asarray(a, dtype=np.float32))
    col = lambda v: f(np.asarray(v).reshape(-1, 128).T)
    nw = np.stack([col(inputs["norm_mix_pre"][0]), col(inputs["norm_mix_post"][0]),
                   col(inputs["norm_mlp_pre"][0]), col(inputs["norm_mlp_post"][0])], axis=1)
    cws = np.concatenate([np.asarray(inputs["ssm_conv_w"][0]), np.asarray(inputs["ssm_conv_b"])], axis=0)
    cws = cws.reshape(5, 32, 128).transpose(2, 1, 0)
    cwg = np.asarray(inputs["gdn_conv_w"][0]).reshape(4, 32, 128).transpose(2, 1, 0)
    rowv = np.concatenate([np.asarray(inputs["ssm_dt_bias"][0]), np.asarray(inputs["ssm_A_log"][0]), np.asarray(inputs["ssm_D"][0]),
                           np.asarray(inputs["gdn_dt_bias"][0]), np.asarray(inputs["gdn_A_log"][0]),
                           np.asarray(inputs["ssm_norm_w"][0]), np.asarray(inputs["gdn_norm_w"][0])])[None, :]
    return {
        "x": f(np.asarray(inputs["x"])[b, :T]),
        "c_col": col(np.asarray(inputs["c"])[b]),
        "w_ada": f(inputs["w_ada"][0]),
        "b_ada_col": col(inputs["b_ada"][0]),
        "nw_col": f(nw),
        "w_in": f(inputs["w_in"][0]),
        "cw_ssm": f(cws),
        "cw_gdn": f(cwg),
        "rowv": f(rowv),
        "w_su": f(inputs["w_ssm_up"][0]),
        "w_gu": f(inputs["w_gdn_up"][0]),
        "w_out": f(inputs["w_out"][0]),
        "w_up": f(inputs["w_mlp_up"][0]),
        "w_dn": f(inputs["w_mlp_down"][0]),
        "consts": make_consts(),
    }


def kernel(**inputs):
    T = 4096
    nc, S = build(T)
    shared = None
    in_maps = []
    for b in range(8):
        m = host_inputs(inputs, b, T)
        if shared is None:
            shared = m
        else:
            for k in m:
                if k not in ("x", "c_col"):
                    m[k] = shared[k]
        in_maps.append(m)
    res = run_bass_kernel_spmd(nc, in_maps, core_ids=list(range(8)))
    return np.stack([np.asarray(r["out"], dtype=np.float32) for r in res.results], axis=0)
```

```python
import numpy as np
import ml_dtypes
from contextlib import ExitStack
import concourse.bass as bass
import concourse.mybir as mybir
from concourse.bass_utils import run_bass_kernel_spmd

F32 = mybir.dt.float32
BF16 = mybir.dt.bfloat16
AF = mybir.ActivationFunctionType
ALU = mybir.AluOpType

D = 1024
EPS = 1e-6
COMPUTE = ("pe", "act", "dve", "pool")
NSLOT = 8


STRICT = False


class Sched:
    def __init__(self, nc, st):
        self.nc = nc
        self.st = st
        self.streams = {e: [] for e in ("pe", "act", "dve", "pool", "sp")}
        self.sem = {}
        for e in COMPUTE:
            self.sem[e] = st.enter_context(nc.semaphore("c_" + e))
        for i in range(NSLOT):
            self.sem[("sp", i)] = st.enter_context(nc.semaphore("d_sp%d" % i))
        self.count = {k: 0 for k in self.sem}
        self.dma_idx = 0
        self.known = {e: {} for e in self.streams}
        self.clock = {}
        self.last_write = {}
        self.readers = {}
        self.ninstr = 0
        self.nwaits = 0

    def _need(self, eng, ev, waits):
        c, n = ev
        if self.known[eng].get(c, 0) >= n:
            return
        if waits.get(c, 0) < n:
            waits[c] = n

    def _deps(self, eng, reads, writes):
        waits = {}
        for k in reads:
            ev = self.last_write.get(k)
            if ev is not None:
                if ev[0] == eng and eng == "pe":
                    continue
                self._need(eng, ev, waits)
        for k in writes:
            ev = self.last_write.get(k)
            if ev is not None and (STRICT and eng != "pe" or not (ev[0] == eng and eng in COMPUTE)):
                self._need(eng, ev, waits)
            for rv in self.readers.get(k, ()):
                if rv[0] == eng and eng in COMPUTE and not STRICT:
                    continue
                self._need(eng, rv, waits)
        return waits

    def _apply(self, eng, waits):
        kn = self.known[eng]
        for c, n in waits.items():
            ck = self.clock.get((c, n))
            if ck:
                for cc, nn in ck.items():
                    if kn.get(cc, 0) < nn:
                        kn[cc] = nn
            if kn.get(c, 0) < n:
                kn[c] = n

    def _record(self, ev, eng, reads, writes):
        ck = dict(self.known[eng])
        ck[ev[0]] = ev[1]
        self.clock[ev] = ck
        for k in reads:
            self.readers.setdefault(k, []).append(ev)
        for k in writes:
            self.last_write[k] = ev
            self.readers[k] = []

    def op(self, eng, fn, reads=(), writes=()):
        waits = self._deps(eng, reads, writes)
        self._apply(eng, waits)
        self.count[eng] += 1
        ev = (eng, self.count[eng])
        self._record(ev, eng, reads, writes)
        self.streams[eng].append((list(waits.items()), fn, (eng, 1)))
        self.ninstr += 1
        self.nwaits += len(waits)
        return ev

    def dma(self, out, in_, reads=(), writes=()):
        q = "sp"
        slot = (q, self.dma_idx % NSLOT)
        self.dma_idx += 1
        waits = self._deps(q, reads, writes)
        if self.count[slot] > 0:
            self._need(q, (slot, self.count[slot]), waits)
        self._apply(q, waits)
        self.count[slot] += 1
        ev = (slot, self.count[slot])
        self._record(ev, q, reads, writes)
        fn = lambda e, out=out, in_=in_: e.dma_start(out=out, in_=in_)
        self.streams[q].append((list(waits.items()), fn, (slot, 16)))
        self.ninstr += 1
        self.nwaits += len(waits)
        return ev

    def barrier(self):
        for eng in self.streams:
            waits = {}
            for c, n in self.count.items():
                if n > 0 and c != eng:
                    self._need(eng, (c, n), waits)
            self._apply(eng, waits)
            self.streams[eng].append((list(waits.items()), None, None))
        self.last_write = {}
        self.readers = {}

    def emit(self):
        nc = self.nc
        block = self.st.enter_context(nc.Block())
        sem = self.sem

        def run(stream):
            def body(e):
                for waits, fn, inc in stream:
                    for c, n in waits:
                        e.wait_ge(sem[c], n * (1 if c in COMPUTE else 16))
                    if fn is not None:
                        fn(e).then_inc(sem[inc[0]], inc[1])
            return body

        block.tensor(run(self.streams["pe"]))
        block.scalar(run(self.streams["act"]))
        block.vector(run(self.streams["dve"]))
        block.gpsimd(run(self.streams["pool"]))
        block.sync(run(self.streams["sp"]))


OFF_Z1 = 0
OFF_XBC = 2048
OFF_DT = 6144
OFF_QKV = 6176
OFF_Z2 = 10272
OFF_B = 12320
OFF_A = 12336
OFF_GS = 12352
OFF_GG = 13376

C_ID, C_U, C_GT, C_SU, C_ONE, C_ONED, C_EPS, C_BD8, C_CMT, NCONST = 0, 1, 2, 3, 4, 5, 6, 7, 8, 14
R_DTB1, R_AL1, R_D1, R_DTB2, R_AL2, R_NW1, R_NW2, RLEN = 0, 32, 64, 96, 112, 128, 2176, 2304


def make_consts():
    k = np.arange(128)[:, None]
    l = np.arange(128)[None, :]
    c = np.zeros((128, NCONST, 128), np.float32)
    c[:, C_ID] = (k == l)
    c[:, C_U] = (k <= l)
    c[:, C_GT] = (k > l)
    c[:, C_SU] = (l > k)
    c[:, C_ONE] = 1.0
    c[:, C_ONED] = 1.0 / D
    c[:, C_EPS] = EPS
    c[:, C_BD8] = (k // 2 == l // 2)
    for n, b in enumerate((2, 4, 8, 16, 32, 64)):
        cm = ((k // (2 * b) == l // (2 * b)) & ((k // b) % 2 == 0) & ((l // b) % 2 == 1))
        c[:, C_CMT + n] = cm.T
    return c


def rsqrt_to(S, cst, dst, dstk, src, srck, scale=1.0):
    S.op("act", lambda e: e.activation(out=dst, in_=src, func=AF.Sqrt, bias=cst[:, C_EPS, 0:1], scale=scale),
         reads=[srck, "cst"], writes=[dstk])
    S.op("dve", lambda e: e.reciprocal(out=dst, in_=dst), reads=[dstk], writes=[dstk])


def build(T, phases=(0, 1, 2, 3, 4, 5), debug=False):
    NT = T // 128
    NB = T // 512
    nc = bass.Bass("TRN2", target_bir_lowering=False)

    def din(name, shape, dt=F32):
        return nc.dram_tensor(name, shape, dt, kind="ExternalInput").ap()

    def dscr(name, shape, dt):
        return nc.dram_tensor(name, shape, dt, kind="ExternalOutput").ap()

    x = din("x", [T, D])
    c_col = din("c_col", [128, 8])
    w_ada = din("w_ada", [D, 6 * D])
    b_ada_col = din("b_ada_col", [128, 48])
    nw_col = din("nw_col", [128, 4, 8])
    w_in = din("w_in", [D, 14400])
    cw_ssm = din("cw_ssm", [128, 32, 5])
    cw_gdn = din("cw_gdn", [128, 32, 4])
    rowv = din("rowv", [1, RLEN])
    w_su = din("w_su", [2048, D])
    w_gu = din("w_gu", [2048, D])
    w_out = din("w_out", [D, D])
    w_up = din("w_up", [D, 4096])
    w_dn = din("w_dn", [4096, D])
    consts = din("consts", [128, NCONST, 128])
    out = nc.dram_tensor("out", [T, D], F32, kind="ExternalOutput").ap()

    XT = dscr("s_xt", [D, T], F32)
    XBC_T = dscr("s_xbct", [4096, T], BF16)
    QKV_T = dscr("s_qkvt", [4096, T], BF16)
    G_T = dscr("s_gt", [2048, T], BF16)
    Z = dscr("s_z", [T, 4096], BF16)
    SM = dscr("s_sm", [T, 96], F32)
    if debug:
        Y = dscr("s_y", [T, 4096], BF16)
        X2T = dscr("s_x2t", [D, T], F32)
        MOD = dscr("s_mod", [128, 48], F32)
    else:
        Y = Z
        X2T = XT
        MOD = None

    with ExitStack() as st:
        S = Sched(nc, st)
        sb = lambda name, shape, dt=F32: st.enter_context(nc.sbuf_tensor(name, shape, dt))
        cst = sb("cst", [128, NCONST, 128])
        cstb = sb("cstb", [128, 1, 128], BF16)
        pv = sb("pv", [128, 6, 8])

        S.dma(cst[:], consts, writes=["cst"])
        S.op("pool", lambda e: e.tensor_copy(out=cstb[:], in_=cst[:, 0:1, :]), reads=["cst"], writes=["cstb"])
        ident = cst[:, C_ID, :]
        identb = cstb[:, C_ID, :]
        Umat = cst[:, C_U, :]
        GTm = cst[:, C_GT, :]
        SUm = cst[:, C_SU, :]
        ones = cst[:, C_ONE, :]
        onesD = cst[:, C_ONED, :]
        if 0 in phases:
            with ExitStack() as ps:
                lsb = lambda name, shape, dt=F32: ps.enter_context(nc.sbuf_tensor(name, shape, dt))
                cact = lsb("cact", [128, 8])
                csig = lsb("csig", [128, 8])
                wa = [lsb("wa%d" % i, [128, 8, 512]) for i in range(2)]
                modsb = lsb("modsb", [128, 48])
                bada = lsb("bada", [128, 48])
                nwc = lsb("nwc", [128, 4, 8])
                modps = ps.enter_context(nc.psum_tensor("modps", [128, 512], F32))
                S.dma(cact[:], c_col, writes=["cact"])
                S.dma(bada[:], b_ada_col, writes=["bada"])
                S.dma(nwc[:], nw_col, writes=["nwc"])
                S.op("act", lambda e: e.activation(out=csig[:], in_=cact[:], func=AF.Sigmoid), reads=["cact"], writes=["csig"])
                S.op("dve", lambda e: e.tensor_tensor(out=cact[:], in0=cact[:], in1=csig[:], op=ALU.mult),
                     reads=["cact", "csig"], writes=["cact"])
                wav = w_ada.rearrange("(k p) f -> p k f", p=128)
                for fb in range(12):
                    w = wa[fb % 2]
                    wk = "wa%d" % (fb % 2)
                    S.dma(w[:], wav[:, :, fb * 512:(fb + 1) * 512], writes=[wk])
                    for j in range(4):
                        col = fb * 4 + j
                        for k in range(8):
                            S.op("pe", lambda e, w=w, j=j, k=k, col=col: e.matmul(
                                modps[:, col:col + 1], w[:, k, j * 128:(j + 1) * 128], cact[:, k:k + 1],
                                start=(k == 0), stop=(k == 7)), reads=[wk, "cact"], writes=["modps"])
                S.op("dve", lambda e: e.tensor_tensor(out=modsb[:], in0=modps[:, 0:48], in1=bada[:], op=ALU.add),
                     reads=["modps", "bada"], writes=["modsb"])
                S.op("dve", lambda e: e.scalar_tensor_tensor(out=pv[:, 0, :], in0=modsb[:, 8:16], scalar=1.0, in1=nwc[:, 0, :],
                                                             op0=ALU.add, op1=ALU.mult), reads=["modsb", "nwc"], writes=["pv"])
                S.op("dve", lambda e: e.tensor_copy(out=pv[:, 1, :], in_=modsb[:, 0:8]), reads=["modsb"], writes=["pv"])
                S.op("dve", lambda e: e.tensor_tensor(out=pv[:, 2, :], in0=modsb[:, 16:24], in1=nwc[:, 1, :], op=ALU.mult),
                     reads=["modsb", "nwc"], writes=["pv"])
                S.op("dve", lambda e: e.scalar_tensor_tensor(out=pv[:, 3, :], in0=modsb[:, 32:40], scalar=1.0, in1=nwc[:, 2, :],
                                                             op0=ALU.add, op1=ALU.mult), reads=["modsb", "nwc"], writes=["pv"])
                S.op("dve", lambda e: e.tensor_copy(out=pv[:, 4, :], in_=modsb[:, 24:32]), reads=["modsb"], writes=["pv"])
                S.op("dve", lambda e: e.tensor_tensor(out=pv[:, 5, :], in0=modsb[:, 40:48], in1=nwc[:, 3, :], op=ALU.mult),
                     reads=["modsb", "nwc"], writes=["pv"])
                if debug:
                    S.dma(MOD, modsb[:], reads=["modsb"], writes=["MOD"])
                S.barrier()

        def normT(xT, xk, sq, sqk, rstd, rstdk, ssps, sspsk, tmp, tmpk, hdst, hk, ia, ish, W=512):
            S.op("act", lambda e: e.activation(out=sq[:], in_=xT[:], func=AF.Square), reads=[xk], writes=[sqk])
            for k in range(8):
                S.op("pe", lambda e, k=k: e.matmul(ssps[:, 0:W], onesD, sq[:, k, :], start=(k == 0), stop=(k == 7)),
                     reads=[sqk, "cst"], writes=[sspsk])
            rsqrt_to(S, cst, rstd[:], rstdk, ssps[:, 0:W], sspsk)
            for k in range(8):
                t = tmp[k % 2]
                tk = tmpk[k % 2]
                S.op("dve", lambda e, k=k, t=t: e.tensor_tensor(out=t[:], in0=xT[:, k, :], in1=rstd[:], op=ALU.mult),
                     reads=[xk, rstdk], writes=[tk])
                S.op("act", lambda e, k=k, t=t: e.activation(out=hdst(k), in_=t[:], func=AF.Identity,
                                                           bias=pv[:, ish, k:k + 1], scale=pv[:, ia, k:k + 1]),
                     reads=[tk, "pv"], writes=[hk])

        hT_cm = None
        hst = ExitStack()
        if 1 in phases or 2 in phases:
            hT_cm = hst.enter_context(nc.sbuf_tensor("hT", [128, 8, T], BF16))

        if 1 in phases:
            with ExitStack() as ps:
                lsb = lambda name, shape, dt=F32: ps.enter_context(nc.sbuf_tensor(name, shape, dt))
                xtok = [lsb("xtok%d" % i, [128, 4, D]) for i in range(2)]
                xTb = [lsb("xTb%d" % i, [128, 8, 512]) for i in range(2)]
                sq = lsb("sq1", [128, 8, 512])
                rstd = lsb("rstd1", [128, 512])
                tmp = [lsb("tmp1_%d" % i, [128, 512]) for i in range(2)]
                tps = [ps.enter_context(nc.psum_tensor("tps%d" % i, [128, 512], F32)) for i in range(4)]
                ssps = ps.enter_context(nc.psum_tensor("ssps1", [128, 512], F32))
                XTv = XT.rearrange("(k p) t -> p k t", p=128)
                for nb in range(NB):
                    xt = xtok[nb % 2]
                    xtk = "xtok%d" % (nb % 2)
                    xT = xTb[nb % 2]
                    xTk = "xTb%d" % (nb % 2)
                    S.dma(xt[:], x[nb * 512:(nb + 1) * 512, :].rearrange("(a p) f -> p a f", p=128), writes=[xtk])
                    for k in range(8):
                        tp = tps[k % 4]
                        tpk = "tps%d" % (k % 4)
                        for a in range(4):
                            S.op("pe", lambda e, tp=tp, a=a, k=k, xt=xt: e.transpose(
                                tp[:, a * 128:(a + 1) * 128], xt[:, a, k * 128:(k + 1) * 128], ident),
                                reads=[xtk, "cst"], writes=[tpk])
                        if k % 2 == 0:
                            S.op("act", lambda e, tp=tp, k=k, xT=xT: e.activation(out=xT[:, k, :], in_=tp[:], func=AF.Identity),
                                 reads=[tpk], writes=[xTk])
                        else:
                            S.op("dve", lambda e, tp=tp, k=k, xT=xT: e.tensor_copy(out=xT[:, k, :], in_=tp[:]),
                                 reads=[tpk], writes=[xTk])
                    S.dma(XTv[:, :, nb * 512:(nb + 1) * 512], xT[:], reads=[xTk], writes=["XT"])
                    normT(xT, xTk, sq, "sq1", rstd, "rstd1", ssps, "ssps1", tmp, ["tmp1_0", "tmp1_1"],
                          lambda k, nb=nb: hT_cm[:, k, nb * 512:(nb + 1) * 512], ("hT", nb), 0, 1)
                S.barrier()

        if 2 in phases:
            hkeys = [("hT", nb) for nb in range(NB)]
            with ExitStack() as ps:
                lsb = lambda name, shape, dt=F32: ps.enter_context(nc.sbuf_tensor(name, shape, dt))
                wst = [lsb("wst%d" % i, [128, 8, 128]) for i in range(2)]
                wbf = [lsb("wbf%d" % i, [128, 8, 128], BF16) for i in range(2)]
                pc = [lsb("pc%d" % i, [128, T + 3]) for i in range(2)]
                accs = [lsb("acc%d" % i, [128, T]) for i in range(2)]
                sq2 = lsb("sq2", [128, T])
                rs = lsb("rs", [128, T])
                ob = [lsb("ob%d" % i, [128, T], BF16) for i in range(2)]
                cws = lsb("cws", [128, 32, 5])
                cwg = lsb("cwg", [128, 32, 4])
                pps = [ps.enter_context(nc.psum_tensor("pps%d" % i, [128, 512], F32)) for i in range(4)]
                sps = [ps.enter_context(nc.psum_tensor("sps%d" % i, [128, 512], F32)) for i in range(2)]
                S.dma(cws[:], cw_ssm, writes=["cws"])
                S.dma(cwg[:], cw_gdn, writes=["cwg"])
                for i in range(2):
                    S.op("pool", lambda e, i=i: e.memset(pc[i][:, 0:3], 0.0), writes=["pc%d" % i])
                w_in_v = w_in.rearrange("(k p) f -> p k f", p=128)
                XBCv = XBC_T.rearrange("(b p) t -> b p t", p=128)
                QKVv = QKV_T.rearrange("(b p) t -> b p t", p=128)
                GTv = G_T.rearrange("(b p) t -> b p t", p=128)
                blocks = []
                for cb in range(32):
                    blocks.append(("xbc", cb, OFF_XBC + cb * 128))
                for cb in range(32):
                    blocks.append(("qkv", cb, OFF_QKV + cb * 128))
                for cb in range(8):
                    blocks.append(("gate", cb, OFF_GS + cb * 128))
                for cb in range(8):
                    blocks.append(("gate", 8 + cb, OFF_GG + cb * 128))
                pcount = 0
                pcnt = [0]

                def bufs(bi):
                    return (accs[bi % 2], "acc%d" % (bi % 2), pc[bi % 2], "pc%d" % (bi % 2), ob[bi % 2], "ob%d" % (bi % 2),
                            wbf[bi % 2], "wbf%d" % (bi % 2))

                def wload(bi):
                    off = blocks[bi][2]
                    ws, wsk, wb, wbk = wst[bi % 2], "wst%d" % (bi % 2), wbf[bi % 2], "wbf%d" % (bi % 2)
                    S.dma(ws[:], w_in_v[:, :, off:off + 128], writes=[wsk])
                    S.op("pool", lambda e: e.tensor_copy(out=wb[:], in_=ws[:]), reads=[wsk], writes=[wbk])

                def front(bi):
                    kind, cb, off = blocks[bi]
                    acc, acck, p_, pk, o_, ok, wb, wbk = bufs(bi)
                    if bi + 1 < len(blocks):
                        wload(bi + 1)
                    for tb in range(NB):
                        pp = pps[pcnt[0] % 4]
                        ppk = "pps%d" % (pcnt[0] % 4)
                        pcnt[0] += 1
                        for k in range(8):
                            S.op("pe", lambda e, pp=pp, k=k, tb=tb: e.matmul(
                                pp[:], wb[:, k, :], hT_cm[:, k, tb * 512:(tb + 1) * 512], start=(k == 0), stop=(k == 7)),
                                reads=[wbk, hkeys[tb]], writes=[ppk])
                        if kind == "gate":
                            S.op("act", lambda e, pp=pp, tb=tb: e.activation(
                                out=o_[:, tb * 512:(tb + 1) * 512], in_=pp[:], func=AF.Sigmoid), reads=[ppk], writes=[ok])
                        else:
                            S.op("act", lambda e, pp=pp, tb=tb: e.activation(
                                out=p_[:, 3 + tb * 512:3 + (tb + 1) * 512], in_=pp[:], func=AF.Identity), reads=[ppk], writes=[pk])
                            if kind == "xbc":
                                S.op("act", lambda e, pp=pp, tb=tb: e.activation(
                                    out=acc[:, tb * 512:(tb + 1) * 512], in_=pp[:], func=AF.Identity,
                                    bias=cws[:, cb, 4:5], scale=cws[:, cb, 3:4]), reads=[ppk, "cws"], writes=[acck])
                            else:
                                S.op("act", lambda e, pp=pp, tb=tb: e.activation(
                                    out=acc[:, tb * 512:(tb + 1) * 512], in_=pp[:], func=AF.Identity,
                                    scale=cwg[:, cb, 3:4]), reads=[ppk, "cwg"], writes=[acck])

                def conv(bi):
                    kind, cb, off = blocks[bi]
                    if kind == "gate":
                        return
                    acc, acck, p_, pk, o_, ok, wb, wbk = bufs(bi)
                    cwt = cws if kind == "xbc" else cwg
                    cwk = "cws" if kind == "xbc" else "cwg"
                    for j in range(1, 4):
                        S.op("dve", lambda e, j=j: e.scalar_tensor_tensor(
                            out=acc[:], in0=p_[:, 3 - j:3 - j + T], scalar=cwt[:, cb, 3 - j:4 - j], in1=acc[:],
                            op0=ALU.mult, op1=ALU.add), reads=[pk, cwk, acck], writes=[acck])

                def post(bi):
                    kind, cb, off = blocks[bi]
                    acc, acck, p_, pk, o_, ok, wb, wbk = bufs(bi)
                    if kind == "gate":
                        S.dma(GTv[cb], o_[:], reads=[ok], writes=["G_T"])
                        return
                    if kind == "qkv" and cb < 16:
                        S.op("act", lambda e: e.activation(out=acc[:], in_=acc[:], func=AF.Silu), reads=[acck], writes=[acck])
                        S.op("act", lambda e: e.activation(out=sq2[:], in_=acc[:], func=AF.Square), reads=[acck], writes=["sq2"])
                        for tb in range(NB):
                            sp = sps[tb % 2]
                            spk = "sps%d" % (tb % 2)
                            S.op("pe", lambda e, sp=sp, tb=tb: e.matmul(sp[:], ones, sq2[:, tb * 512:(tb + 1) * 512],
                                                                        start=True, stop=True),
                                 reads=["sq2", "cst"], writes=[spk])
                            rsqrt_to(S, cst, rs[:, tb * 512:(tb + 1) * 512], "rs", sp[:], spk)
                        qs = (128.0 ** -0.5) if cb < 8 else 1.0
                        S.op("dve", lambda e: e.scalar_tensor_tensor(
                            out=o_[:], in0=acc[:], scalar=qs, in1=rs[:], op0=ALU.mult, op1=ALU.mult),
                            reads=[acck, "rs"], writes=[ok])
                    else:
                        S.op("act", lambda e: e.activation(out=o_[:], in_=acc[:], func=AF.Silu), reads=[acck], writes=[ok])
                    dst = XBCv[cb] if kind == "xbc" else QKVv[cb]
                    S.dma(dst, o_[:], reads=[ok], writes=[kind + "_T"])

                wload(0)
                for bi in range(len(blocks)):
                    front(bi)
                    if bi > 0:
                        post(bi - 1)
                    conv(bi)
                post(len(blocks) - 1)
                S.barrier()

            with ExitStack() as ps:
                lsb = lambda name, shape, dt=F32: ps.enter_context(nc.sbuf_tensor(name, shape, dt))
                wzs = [lsb("wzs%d" % i, [128, 8, 512]) for i in range(2)]
                wzb = [lsb("wzb%d" % i, [128, 8, 512], BF16) for i in range(2)]
                zb = [lsb("zb%d" % i, [128, 512], BF16) for i in range(3)]
                wss = lsb("wss", [128, 8, 64])
                wsb = lsb("wsb", [128, 8, 64], BF16)
                smt = [lsb("smt%d" % i, [128, 96]) for i in range(2)]
                t1 = [lsb("t1_%d" % i, [128, 48]) for i in range(2)]
                rows = lsb("rows2", [128, RLEN])
                arow = lsb("arow", [128, 48])
                S.dma(rows[:], rowv.partition_broadcast(128), writes=["rows"])
                S.op("act", lambda e: e.activation(out=arow[:, 0:32], in_=rows[:, R_AL1:R_AL1 + 32], func=AF.Exp),
                     reads=["rows"], writes=["arow"])
                S.op("act", lambda e: e.activation(out=arow[:, 32:48], in_=rows[:, R_AL2:R_AL2 + 16], func=AF.Exp),
                     reads=["rows"], writes=["arow"])
                S.op("dve", lambda e: e.tensor_scalar(out=arow[:], in0=arow[:], scalar1=-1.0, scalar2=None, op0=ALU.mult),
                     reads=["arow"], writes=["arow"])
                pps = [ps.enter_context(nc.psum_tensor("zps%d" % i, [128, 512], F32)) for i in range(4)]
                sps = [ps.enter_context(nc.psum_tensor("smps%d" % i, [128, 512], F32)) for i in range(2)]
                w_in_v = w_in.rearrange("(k p) f -> p k f", p=128)
                pcount = 0
                for blk in range(8):
                    off = (OFF_Z1 + blk * 512) if blk < 4 else (OFF_Z2 + (blk - 4) * 512)
                    ws = wzs[blk % 2]
                    wsk = "wzs%d" % (blk % 2)
                    wb = wzb[blk % 2]
                    wbk = "wzb%d" % (blk % 2)
                    if blk == 0:
                        S.dma(ws[:], w_in_v[:, :, off:off + 512], writes=[wsk])
                    if blk + 1 < 8:
                        noff = (OFF_Z1 + (blk + 1) * 512) if blk + 1 < 4 else (OFF_Z2 + (blk + 1 - 4) * 512)
                        S.dma(wzs[(blk + 1) % 2][:], w_in_v[:, :, noff:noff + 512], writes=["wzs%d" % ((blk + 1) % 2)])
                    S.op("act", lambda e, ws=ws, wb=wb: e.activation(out=wb[:], in_=ws[:], func=AF.Identity), reads=[wsk], writes=[wbk])
                    for tt in range(NT):
                        pp = pps[pcount % 4]
                        ppk = "zps%d" % (pcount % 4)
                        z_ = zb[pcount % 3]
                        zk = "zb%d" % (pcount % 3)
                        pcount += 1
                        for k in range(8):
                            S.op("pe", lambda e, pp=pp, wb=wb, k=k, tt=tt: e.matmul(
                                pp[:], hT_cm[:, k, tt * 128:(tt + 1) * 128], wb[:, k, :], start=(k == 0), stop=(k == 7)),
                                reads=[wbk, hkeys[tt // 4]], writes=[ppk])
                        S.op("act", lambda e, pp=pp, z_=z_: e.activation(out=z_[:], in_=pp[:], func=AF.Silu), reads=[ppk], writes=[zk])
                        S.dma(Z[tt * 128:(tt + 1) * 128, blk * 512:(blk + 1) * 512], z_[:], reads=[zk], writes=["Z"])
                S.dma(wss[:, :, 0:32], w_in_v[:, :, OFF_DT:OFF_DT + 32], writes=["wss"])
                S.dma(wss[:, :, 32:64], w_in_v[:, :, OFF_B:OFF_B + 32], writes=["wss"])
                S.op("pool", lambda e: e.tensor_copy(out=wsb[:], in_=wss[:]), reads=["wss"], writes=["wsb"])
                for tt in range(NT):
                    sp = sps[tt % 2]
                    spk = "smps%d" % (tt % 2)
                    sm_ = smt[tt % 2]
                    smk = "smt%d" % (tt % 2)
                    t_ = t1[tt % 2]
                    tk = "t1_%d" % (tt % 2)
                    for k in range(8):
                        S.op("pe", lambda e, sp=sp, k=k, tt=tt: e.matmul(
                            sp[:, 0:64], hT_cm[:, k, tt * 128:(tt + 1) * 128], wsb[:, k, :], start=(k == 0), stop=(k == 7)),
                            reads=["wsb", hkeys[tt // 4]], writes=[spk])
                    S.op("dve", lambda e, sp=sp, t_=t_: e.tensor_tensor(out=t_[:, 0:32], in0=sp[:, 0:32],
                                                                        in1=rows[:, R_DTB1:R_DTB1 + 32], op=ALU.add),
                         reads=["rows"], writes=[spk, tk])
                    S.op("dve", lambda e, sp=sp, t_=t_: e.tensor_tensor(out=t_[:, 32:48], in0=sp[:, 48:64],
                                                                        in1=rows[:, R_DTB2:R_DTB2 + 16], op=ALU.add),
                         reads=["rows"], writes=[spk, tk])
                    S.op("act", lambda e, t_=t_: e.activation(out=t_[:], in_=t_[:], func=AF.Exp), reads=[tk], writes=[tk])
                    S.op("dve", lambda e, t_=t_: e.tensor_scalar(out=t_[:], in0=t_[:], scalar1=1.0, scalar2=None, op0=ALU.add),
                         reads=[tk], writes=[tk])
                    S.op("act", lambda e, t_=t_: e.activation(out=t_[:], in_=t_[:], func=AF.Ln), reads=[tk], writes=[tk])
                    S.op("act", lambda e, sp=sp, sm_=sm_: e.activation(out=sm_[:, 32:48], in_=sp[:, 32:48], func=AF.Sigmoid),
                         writes=[spk, smk])
                    S.op("dve", lambda e, t_=t_, sm_=sm_: e.tensor_copy(out=sm_[:, 0:32], in_=t_[:, 0:32]), reads=[tk], writes=[smk])
                    S.op("dve", lambda e, t_=t_, sm_=sm_: e.tensor_tensor(out=sm_[:, 48:96], in0=t_[:], in1=arow[:], op=ALU.mult),
                         reads=[tk, "arow"], writes=[smk])
                    S.dma(SM[tt * 128:(tt + 1) * 128, :], sm_[:], reads=[smk], writes=["SM"])
                S.barrier()

        hst.close()
        if 3 in phases:
            phase3(nc, S, st, T, XBC_T, QKV_T, Z, SM, Y, cst, cstb, rowv)
        if 4 in phases:
            phase4a(nc, S, T, Y, G_T, XT, X2T, w_su, w_gu, w_out, cst, cstb, pv)
        if 5 in phases:
            phase4b(nc, S, T, X2T, out, w_up, w_dn, cst, pv, normT)
        else:
            pass
        S.barrier()
        S.emit()
    return nc, S


def phase3(nc, S, st_outer, T, XBC_T, QKV_T, Z, SM, Y, cst, cstb, rowv):
    NT = T // 128
    ident = cst[:, C_ID, :]
    identb = cstb[:, C_ID, :]
    Umat = cst[:, C_U, :]
    GTm = cst[:, C_GT, :]
    SUm = cst[:, C_SU, :]
    ones = cst[:, C_ONE, :]
    with ExitStack() as ps:
        lsb = lambda name, shape, dt=F32: ps.enter_context(nc.sbuf_tensor(name, shape, dt))
        smt = [lsb("p3sm%d" % i, [128, 96]) for i in range(2)]
        rows = lsb("rows3", [128, RLEN])
        S.dma(rows[:], rowv.partition_broadcast(128), writes=["rows"])
        xbct = [lsb("p3xbc%d" % i, [128, 32, 128], BF16) for i in range(2)]
        qkvt = [lsb("p3qkv%d" % i, [128, 32, 128], BF16) for i in range(2)]
        zt = [lsb("p3z%d" % i, [128, 4096], BF16) for i in range(1)]
        ytile = lsb("p3y", [128, 4096], BF16)
        xs_tok = lsb("xs_tok", [128, 2048], BF16)
        b_tok = lsb("b_tok", [128, 1024], BF16)
        k_tok = lsb("k_tok", [128, 1024], BF16)
        v_tok = lsb("v_tok", [128, 2048], BF16)
        c_sb = lsb("c_sb", [128, 48])
        e_sb = lsb("e_sb", [128, 48])
        f_sb = lsb("f_sb", [128, 48])
        dA_sb = lsb("dA_sb", [128, 48])
        nbeta = lsb("nbeta", [128, 16])
        ST = lsb("ST", [128, 8, 256])
        STb = lsb("STb", [128, 8, 256], BF16)
        GS = lsb("GS", [128, 16, 128])
        GSb = lsb("GSb", [128, 16, 128], BF16)
        IB = [[lsb("ibb%d_%d" % (s_, n_), [128, 4, 128], BF16) for n_ in range(5)]
              + [lsb("ibm%d" % s_, [128, 6, 4, 128], BF16)] for s_ in range(2)]
        attnT = lsb("attnT", [128, 16, 128], BF16)
        T2T = lsb("T2T", [128, 16, 128], BF16)
        ke = lsb("ke", [128, 16, 128], BF16)
        kf = lsb("kf", [128, 16, 128], BF16)
        nwT = lsb("nwT", [128, 16, 128], BF16)
        KKm = lsb("KKm", [128, 8, 128])
        QKm = lsb("QKm", [128, 8, 128])
        Am = [lsb("Am%d" % i, [128, 4, 128]) for i in range(2)]
        DT = [lsb("DT%d" % i, [128, 4, 128]) for i in range(4)]
        MT = [lsb("MT%d" % i, [128, 4, 128], BF16) for i in range(2)]
        CBTm = [lsb("CBTm%d" % i, [128, 128]) for i in range(2)]
        xdt = [lsb("xdt%d" % i, [128, 256], BF16) for i in range(2)]
        xw = [lsb("xw%d" % i, [128, 256], BF16) for i in range(2)]
        xsD = [lsb("xsD%d" % i, [128, 256]) for i in range(2)]
        yacc = [lsb("yacc%d" % i, [128, 256]) for i in range(2)]
        yz = [lsb("yz%d" % i, [128, 256]) for i in range(2)]
        ssq = [lsb("ssq%d" % i, [128, 4]) for i in range(2)]
        vnew = [lsb("vnew%d" % i, [128, 4, 128], BF16) for i in range(2)]
        osb = [lsb("osb%d" % i, [128, 4, 128]) for i in range(1)] * 2
        on = [lsb("on%d" % i, [128, 4, 128]) for i in range(1)] * 2
        banks = [ps.enter_context(nc.psum_tensor("pb%d" % i, [128, 512], F32)) for i in range(8)]
        bctr = [0]

        def bank():
            i = bctr[0] % 8
            bctr[0] += 1
            return banks[i], "pb%d" % i
        rr = {"am": 0, "ama": 0, "g": 0, "v": 0}

        S.op("pool", lambda e: e.memset(ST[:], 0.0), writes=["ST"])
        S.op("pool", lambda e: e.memset(STb[:], 0.0), writes=["STb"])
        S.op("pool", lambda e: e.memset(GS[:], 0.0), writes=["GS"])
        S.op("pool", lambda e: e.memset(GSb[:], 0.0), writes=["GSb"])

        XBCv = XBC_T.rearrange("(b p) t -> p b t", p=128)
        QKVv = QKV_T.rearrange("(b p) t -> p b t", p=128)

        def loads(tt):
            i = tt % 2
            S.dma(smt[i][:], SM[tt * 128:(tt + 1) * 128, :], reads=["SM"], writes=["p3sm%d" % i])
            S.dma(xbct[i][:], XBCv[:, :, tt * 128:(tt + 1) * 128], reads=["xbc_T"], writes=["p3xbc%d" % i])
            S.dma(qkvt[i][:], QKVv[:, :, tt * 128:(tt + 1) * 128], reads=["qkv_T"], writes=["p3qkv%d" % i])

        def load_z(tt):
            S.dma(zt[0][:], Z[tt * 128:(tt + 1) * 128, :], reads=["Z"], writes=["p3z0"])

        def bc(ap2, n, w):
            return ap2.unsqueeze(2).to_broadcast([128, n, w])

        def v3(ap2, h=4):
            return ap2.rearrange("p (h w) -> p h w", h=h)

        loads(0)
        load_z(0)
        for tt in range(NT):
            i = tt % 2
            sm, smk = smt[i], "p3sm%d" % i
            xbc, xbk = xbct[i], "p3xbc%d" % i
            qkv, qkk = qkvt[i], "p3qkv%d" % i
            z, zk = zt[0], "p3z0"
            y, yk = ytile, "p3y"
            if tt + 1 < NT:
                loads(tt + 1)
            bD, bDk = bank()
            S.op("pe", lambda e, bD=bD, sm=sm: e.matmul(bD[:, 0:48], Umat, sm[:, 48:96], start=True, stop=True),
                 reads=[smk, "cst"], writes=[bDk])
            S.op("pe", lambda e, bD=bD, sm=sm: e.matmul(bD[:, 64:112], ones, sm[:, 48:96], start=True, stop=True),
                 reads=[smk, "cst"], writes=[bDk])
            S.op("act", lambda e, bD=bD: e.activation(out=c_sb[:], in_=bD[:, 0:48], func=AF.Identity), writes=[bDk, "c_sb"])
            S.op("act", lambda e, bD=bD: e.activation(out=e_sb[:], in_=bD[:, 0:48], func=AF.Exp), writes=[bDk, "e_sb"])
            S.op("act", lambda e, bD=bD: e.activation(out=dA_sb[:], in_=bD[:, 64:112], func=AF.Exp), writes=[bDk, "dA_sb"])
            S.op("act", lambda e, bD=bD: e.activation(out=f_sb[:], in_=bD[:, 64:112], func=AF.Identity), writes=[bDk, "f_sb"])
            S.op("dve", lambda e: e.tensor_tensor(out=f_sb[:], in0=f_sb[:], in1=c_sb[:], op=ALU.subtract),
                 reads=["c_sb", "f_sb"], writes=["f_sb"])
            S.op("act", lambda e: e.activation(out=f_sb[:], in_=f_sb[:], func=AF.Exp), reads=["f_sb"], writes=["f_sb"])
            S.op("pool", lambda e, sm=sm: e.tensor_scalar(out=nbeta[:], in0=sm[:, 32:48], scalar1=-1.0, scalar2=None, op0=ALU.mult),
                 reads=[smk], writes=["nbeta"])
            jobs = [(xbc, xbk, 0, xs_tok, "xs_tok", 2), (xbc, xbk, 16, b_tok, "b_tok", 1),
                    (qkv, qkk, 8, k_tok, "k_tok", 1), (qkv, qkk, 16, v_tok, "v_tok", 2)]
            nev = 0
            for (src, srck, b0, dst, dstk, nq) in jobs:
                for q8 in range(nq):
                    bk_, bkk = bank()
                    bb = bk_[:].bitcast(BF16)
                    for a in range(8):
                        blk = b0 + q8 * 8 + a
                        S.op("pe", lambda e, bb=bb, a=a, src=src, blk=blk: e.transpose(
                            bb[:, a * 128:(a + 1) * 128], src[:, blk, :], identb), reads=[srck, "cstb"], writes=[bkk])
                    if nev % 2 == 0:
                        S.op("act", lambda e, bb=bb, dst=dst, q8=q8: e.activation(
                            out=dst[:, q8 * 1024:(q8 + 1) * 1024], in_=bb, func=AF.Identity), writes=[bkk, dstk])
                    else:
                        S.op("dve", lambda e, bb=bb, dst=dst, q8=q8: e.tensor_copy(
                            out=dst[:, q8 * 1024:(q8 + 1) * 1024], in_=bb), writes=[bkk, dstk])
                    nev += 1

            for half in range(2):
                bk_, bkk = bank()
                for j in range(4):
                    hq = half * 4 + j
                    S.op("pe", lambda e, bk_=bk_, j=j, hq=hq, qkv=qkv: e.matmul(
                        bk_[:, j * 128:(j + 1) * 128], qkv[:, 8 + hq, :], qkv[:, 8 + hq, :], start=True, stop=True),
                        reads=[qkk], writes=[bkk])
                S.op("dve", lambda e, bk_=bk_, half=half: e.tensor_tensor(
                    out=KKm[:, half * 4:(half + 1) * 4, :], in0=v3(bk_[:]), in1=SUm.unsqueeze(1).to_broadcast([128, 4, 128]), op=ALU.mult),
                    reads=["cst"], writes=[bkk, "KKm"])
                bq_, bqk = bank()
                for j in range(4):
                    hq = half * 4 + j
                    S.op("pe", lambda e, bq_=bq_, j=j, hq=hq, qkv=qkv: e.matmul(
                        bq_[:, j * 128:(j + 1) * 128], qkv[:, 8 + hq, :], qkv[:, hq, :], start=True, stop=True),
                        reads=[qkk], writes=[bqk])
                S.op("dve", lambda e, bq_=bq_, half=half: e.tensor_tensor(
                    out=QKm[:, half * 4:(half + 1) * 4, :], in0=v3(bq_[:]), in1=Umat.unsqueeze(1).to_broadcast([128, 4, 128]), op=ALU.mult),
                    reads=["cst"], writes=[bqk, "QKm"])
            def fl(t):
                return t[:].rearrange("p h w -> p (h w)")

            def bcm(plane):
                return cst[:, plane, :].unsqueeze(1).to_broadcast([128, 4, 128])

            def build_DT(cols, r, sm, smk):
                ra = rr["ama"] % 2
                rr["ama"] += 1
                S.op("dve", lambda e: e.tensor_tensor(out=Am[ra][:], in0=bcm(C_GT), in1=bc(sm[:, cols:cols + 4], 4, 128), op=ALU.mult),
                     reads=[smk, "cst"], writes=["Am%d" % ra])
                sg, sgk = bank()
                for j in range(4):
                    S.op("pe", lambda e, sg=sg, j=j: e.matmul(sg[:, j * 128:(j + 1) * 128], Am[ra][:, j, :], Umat, start=True, stop=True),
                         reads=["Am%d" % ra, "cst"], writes=[sgk])
                S.op("act", lambda e, sg=sg: e.activation(out=fl(DT[r]), in_=sg[:], func=AF.Exp), writes=[sgk, "DT%d" % r])

            def gdn_quad(q, s_, sm=sm, smk=smk, qkv=qkv, qkk=qkk, z=z, zk=zk, y=y, yk=yk):
                Xb, XTb, Dvb, DvTb, Eb, XMb = IB[s_]
                kX, kXT, kDvb, kDvTb, kEb, kXMb = [("ib", s_, n_) for n_ in range(6)]
                qs = slice(q * 4, (q + 1) * 4)
                r = rr["am"] % 4
                rr["am"] += 1
                build_DT(80 + q * 4, r, sm, smk)
                for j in range(4):
                    hv = q * 4 + j
                    hq = hv // 2
                    S.op("dve", lambda e, j=j, hv=hv, hq=hq: e.scalar_tensor_tensor(
                        out=Xb[:, j, :], in0=KKm[:, hq, :], scalar=nbeta[:, hv:hv + 1], in1=DT[r][:, j, :], op0=ALU.mult, op1=ALU.mult),
                        reads=["KKm", "nbeta", "DT%d" % r], writes=[kX])
                S.op("dve", lambda e: e.tensor_tensor(
                    out=attnT[:, qs, :].rearrange("p (a b) w -> p a b w", a=2),
                    in0=QKm[:, 2 * q:2 * q + 2, :].unsqueeze(2).to_broadcast([128, 2, 2, 128]),
                    in1=DT[r][:].rearrange("p (a b) w -> p a b w", a=2), op=ALU.mult),
                    reads=["QKm", "DT%d" % r], writes=[("attnT", q)])
                yield

                def mm4(lhs, lk, rhs, rk):
                    b_, bk = bank()
                    for j in range(4):
                        S.op("pe", lambda e, b_=b_, j=j: e.matmul(b_[:, j * 128:(j + 1) * 128], lhs[:, j, :], rhs[:, j, :], start=True, stop=True),
                             reads=[lk, rk], writes=[bk])
                    return b_, bk

                def acc_dve(b_, bk, dst, dk, out=None, ok=None):
                    o_ = fl(dst) if out is None else out
                    S.op("dve", lambda e: e.tensor_tensor(out=o_, in0=fl(dst), in1=b_[:], op=ALU.add),
                         reads=[dk], writes=[bk, dk if ok is None else ok])

                tb_, tbk = bank()
                tbb = tb_[:].bitcast(BF16)
                for j in range(4):
                    S.op("pe", lambda e, j=j: e.transpose(tbb[:, j * 128:(j + 1) * 128], Xb[:, j, :], identb),
                         reads=[kX, "cstb"], writes=[tbk])
                S.op("act", lambda e: e.activation(out=fl(XTb), in_=tbb[:, 0:512], func=AF.Identity), writes=[tbk, kXT])
                S.op("dve", lambda e: e.tensor_tensor(out=Dvb[:], in0=Xb[:], in1=bcm(C_BD8), op=ALU.mult), reads=[kX, "cst"], writes=[kDvb])
                S.op("dve", lambda e: e.tensor_tensor(out=Dvb[:], in0=Dvb[:], in1=bcm(C_ID), op=ALU.add), reads=[kDvb, "cst"], writes=[kDvb])
                yield
                S.op("dve", lambda e: e.tensor_tensor(out=DvTb[:], in0=XTb[:], in1=bcm(C_BD8), op=ALU.mult), reads=[kXT, "cst"], writes=[kDvTb])
                S.op("dve", lambda e: e.tensor_tensor(out=DvTb[:], in0=DvTb[:], in1=bcm(C_ID), op=ALU.add), reads=[kDvTb, "cst"], writes=[kDvTb])
                S.op("dve", lambda e: e.tensor_tensor(
                    out=XMb[:], in0=XTb[:].unsqueeze(1).to_broadcast([128, 6, 4, 128]),
                    in1=cst[:, C_CMT:C_CMT + 6, :].unsqueeze(2).to_broadcast([128, 6, 4, 128]), op=ALU.mult),
                    reads=[kXT, "cst"], writes=[kXMb])
                yield
                for n, b in enumerate((2, 4, 8, 16, 32, 64)):
                    b_, bk = mm4(XMb[:, n, :, :], kXMb, Dvb, kDvb)
                    S.op("act", lambda e, b_=b_: e.activation(out=fl(Eb), in_=b_[:], func=AF.Identity), writes=[bk, kEb])
                    yield
                    b1, b1k = mm4(DvTb, kDvTb, Eb, kEb)
                    if b != 64:
                        b2, b2k = mm4(Eb, kEb, DvTb, kDvTb)
                        acc_dve(b1, b1k, Dvb, kDvb)
                        acc_dve(b2, b2k, DvTb, kDvTb)
                        yield
                    else:
                        acc_dve(b1, b1k, Dvb, kDvb, out=T2T[:, qs, :].rearrange("p h w -> p (h w)"), ok=("T2T", q))
                        yield
                kq = k_tok[:, 2 * q * 128:(2 * q + 2) * 128].rearrange("p (a w) -> p a w", a=2).unsqueeze(2).to_broadcast([128, 2, 2, 128])
                S.op("pool", lambda e: e.tensor_tensor(
                    out=ke[:, qs, :].rearrange("p (a b) w -> p a b w", a=2), in0=kq,
                    in1=e_sb[:, 32 + q * 4:36 + q * 4].rearrange("p (a b) -> p a b", a=2).unsqueeze(3).to_broadcast([128, 2, 2, 128]),
                    op=ALU.mult), reads=["k_tok", "e_sb"], writes=[("ke", q)])
                S.op("pool", lambda e: e.tensor_tensor(
                    out=kf[:, qs, :].rearrange("p (a b) w -> p a b w", a=2), in0=kq,
                    in1=f_sb[:, 32 + q * 4:36 + q * 4].rearrange("p (a b) -> p a b", a=2).unsqueeze(3).to_broadcast([128, 2, 2, 128]),
                    op=ALU.mult), reads=["k_tok", "f_sb"], writes=[("kf", q)])
                wp, wpk = bank()
                for j in range(4):
                    hv = q * 4 + j
                    S.op("pe", lambda e, wp=wp, j=j, hv=hv: e.matmul(wp[:, j * 128:(j + 1) * 128], ke[:, hv, :], T2T[:, hv, :],
                                                                     start=True, stop=True),
                         reads=[("ke", q), ("T2T", q)], writes=[wpk])
                S.op("act", lambda e, wp=wp: e.activation(out=nwT[:, qs, :].rearrange("p h w -> p (h w)"), in_=wp[:],
                                                        func=AF.Identity, scale=-1.0), writes=[wpk, ("nwT", q)])
                yield
                qs = slice(q * 4, (q + 1) * 4)
                vi = rr["v"] % 2
                rr["v"] += 1
                vp, vpk = bank()
                for j in range(4):
                    hv = q * 4 + j
                    S.op("pe", lambda e, vp=vp, j=j, hv=hv: e.matmul(vp[:, j * 128:(j + 1) * 128], T2T[:, hv, :],
                                                                     v_tok[:, hv * 128:(hv + 1) * 128], start=True, stop=False),
                         reads=[("T2T", q), "v_tok"], writes=[vpk])
                    S.op("pe", lambda e, vp=vp, j=j, hv=hv: e.matmul(vp[:, j * 128:(j + 1) * 128], nwT[:, hv, :], GSb[:, hv, :],
                                                                     start=False, stop=True),
                         reads=[("nwT", q), ("GSb", q)], writes=[vpk])
                for j in range(4):
                    hv = q * 4 + j
                    S.op("act", lambda e, vp=vp, j=j, hv=hv, vi=vi: e.activation(
                        out=vnew[vi][:, j, :], in_=vp[:, j * 128:(j + 1) * 128], func=AF.Identity, scale=sm[:, 32 + hv:33 + hv]),
                        reads=[smk], writes=[vpk, "vnew%d" % vi])
                yield
                oi, oik = bank()
                for j in range(4):
                    hv = q * 4 + j
                    hq = hv // 2
                    S.op("pe", lambda e, oi=oi, j=j, hv=hv, hq=hq: e.matmul(oi[:, j * 128:(j + 1) * 128], qkv[:, hq, :], GSb[:, hv, :],
                                                                               start=True, stop=True),
                         reads=[qkk, ("GSb", q)], writes=[oik])
                oa, oak = bank()
                for j in range(4):
                    hv = q * 4 + j
                    S.op("pe", lambda e, oa=oa, j=j, hv=hv, vi=vi: e.matmul(oa[:, j * 128:(j + 1) * 128], attnT[:, hv, :], vnew[vi][:, j, :],
                                                                           start=True, stop=True),
                         reads=[("attnT", q), "vnew%d" % vi], writes=[oak])
                sn, snk = bank()
                for j in range(4):
                    hv = q * 4 + j
                    S.op("pe", lambda e, sn=sn, j=j, hv=hv, vi=vi: e.matmul(sn[:, j * 128:(j + 1) * 128], kf[:, hv, :], vnew[vi][:, j, :],
                                                                           start=True, stop=True),
                         reads=[("kf", q), "vnew%d" % vi], writes=[snk])
                for j in range(4):
                    hv = q * 4 + j
                    S.op("act", lambda e, oi=oi, j=j, hv=hv, vi=vi: e.activation(
                        out=osb[vi][:, j, :], in_=oi[:, j * 128:(j + 1) * 128], func=AF.Identity, scale=e_sb[:, 32 + hv:33 + hv]),
                        reads=["e_sb"], writes=[oik, "osb0"])
                S.op("dve", lambda e, oa=oa, vi=vi: e.tensor_tensor(
                    out=osb[vi][:].rearrange("p h w -> p (h w)"), in0=osb[vi][:].rearrange("p h w -> p (h w)"), in1=oa[:], op=ALU.add),
                    reads=["osb0"], writes=[oak, "osb0"])
                for j in range(4):
                    S.op("act", lambda e, j=j, vi=vi: e.activation(out=on[vi][:, j, :], in_=osb[vi][:, j, :], func=AF.Square,
                                                                 accum_out=ssq[vi][:, j:j + 1]),
                         reads=["osb0"], writes=["on0", "ssq%d" % vi])
                rsqrt_to(S, cst, ssq[vi][:], "ssq%d" % vi, ssq[vi][:], "ssq%d" % vi, 1.0 / 128)
                for j in range(4):
                    hv = q * 4 + j
                    S.op("dve", lambda e, j=j, vi=vi: e.scalar_tensor_tensor(
                        out=on[vi][:, j, :], in0=osb[vi][:, j, :], scalar=ssq[vi][:, j:j + 1], in1=rows[:, R_NW2:R_NW2 + 128],
                        op0=ALU.mult, op1=ALU.mult), reads=["osb0", "ssq%d" % vi, "rows"], writes=["on0"])
                S.op("pool", lambda e, vi=vi: e.tensor_tensor(
                    out=y[:, 2048 + q * 512:2048 + (q + 1) * 512], in0=on[vi][:].rearrange("p h w -> p (h w)"),
                    in1=z[:, 2048 + q * 512:2048 + (q + 1) * 512], op=ALU.mult), reads=["on0", zk], writes=[yk])
                for j in range(4):
                    hv = q * 4 + j
                    S.op("dve", lambda e, sn=sn, j=j, hv=hv: e.scalar_tensor_tensor(
                        out=GS[:, hv, :], in0=GS[:, hv, :], scalar=dA_sb[:, 32 + hv:33 + hv], in1=sn[:, j * 128:(j + 1) * 128],
                        op0=ALU.mult, op1=ALU.add), reads=[("GS", q), "dA_sb"], writes=[snk, ("GS", q)])
                S.op("act", lambda e, qs=qs: e.activation(out=GSb[:, qs, :], in_=GS[:, qs, :], func=AF.Identity),
                     reads=[("GS", q)], writes=[("GSb", q)])
                yield

            def ssd_group(g, sm=sm, smk=smk, xbc=xbc, xbk=xbk, z=z, zk=zk, y=y, yk=yk):
                gi = rr["g"] % 2
                rr["g"] += 1
                r = rr["am"] % 4
                rr["am"] += 1
                b2, b2k = bank()
                S.op("pe", lambda e: e.matmul(b2[:, 0:128], xbc[:, 16 + g, :], xbc[:, 24 + g, :], start=True, stop=True),
                     reads=[xbk], writes=[b2k])
                S.op("dve", lambda e: e.tensor_tensor(out=CBTm[gi][:], in0=b2[:, 0:128], in1=Umat, op=ALU.mult),
                     reads=["cst"], writes=[b2k, "CBTm%d" % gi])
                xs3 = v3(xs_tok[:, g * 256:(g + 1) * 256])
                S.op("dve", lambda e: e.tensor_tensor(
                    out=v3(xdt[gi][:]), in0=xs3, in1=bc(sm[:, g * 4:(g + 1) * 4], 4, 64), op=ALU.mult),
                    reads=["xs_tok", smk], writes=["xdt%d" % gi])
                S.op("dve", lambda e: e.tensor_tensor(
                    out=v3(xw[gi][:]), in0=v3(xdt[gi][:]), in1=bc(f_sb[:, g * 4:(g + 1) * 4], 4, 64), op=ALU.mult),
                    reads=["xdt%d" % gi, "f_sb"], writes=["xw%d" % gi])
                S.op("pool", lambda e: e.tensor_tensor(
                    out=v3(xsD[gi][:]), in0=xs3, in1=bc(rows[:, R_D1 + g * 4:R_D1 + (g + 1) * 4], 4, 64), op=ALU.mult),
                    reads=["xs_tok", "rows"], writes=["xsD%d" % gi])
                build_DT(48 + g * 4, r, sm, smk)
                yield
                S.op("dve", lambda e: e.tensor_tensor(out=MT[gi][:], in0=DT[r][:],
                                                      in1=CBTm[gi][:].unsqueeze(1).to_broadcast([128, 4, 128]), op=ALU.mult),
                     reads=["CBTm%d" % gi, "DT%d" % r], writes=["MT%d" % gi])
                b3, b3k = bank()
                for h in range(4):
                    S.op("pe", lambda e, h=h: e.matmul(b3[:, h * 64:(h + 1) * 64], MT[gi][:, h, :],
                                                       xdt[gi][:, h * 64:(h + 1) * 64], start=True, stop=True),
                         reads=["MT%d" % gi, "xdt%d" % gi], writes=[b3k])
                S.op("pe", lambda e: e.matmul(b3[:, 256:512], xbc[:, 24 + g, :], STb[:, g, :], start=True, stop=True),
                     reads=[xbk, ("STb", g)], writes=[b3k])
                b4, b4k = bank()
                S.op("pe", lambda e: e.matmul(b4[:, 256:512], b_tok[:, g * 128:(g + 1) * 128], xw[gi][:], start=True, stop=True),
                     reads=["b_tok", "xw%d" % gi], writes=[b4k])
                S.op("dve", lambda e: e.tensor_tensor(
                    out=v3(yacc[gi][:]), in0=v3(b3[:, 256:512]), in1=bc(e_sb[:, g * 4:(g + 1) * 4], 4, 64), op=ALU.mult),
                    reads=["e_sb"], writes=[b3k, "yacc%d" % gi])
                S.op("dve", lambda e: e.tensor_tensor(out=yacc[gi][:], in0=yacc[gi][:], in1=b3[:, 0:256], op=ALU.add),
                     reads=["yacc%d" % gi], writes=[b3k, "yacc%d" % gi])
                S.op("pool", lambda e: e.tensor_tensor(out=yacc[gi][:], in0=yacc[gi][:], in1=xsD[gi][:], op=ALU.add),
                     reads=["yacc%d" % gi, "xsD%d" % gi], writes=["yacc%d" % gi])
                S.op("dve", lambda e: e.tensor_tensor(out=yz[gi][:], in0=yacc[gi][:], in1=z[:, g * 256:(g + 1) * 256], op=ALU.mult),
                     reads=["yacc%d" % gi, zk], writes=["yz%d" % gi])
                S.op("act", lambda e: e.activation(out=xsD[gi][:], in_=yz[gi][:], func=AF.Square, accum_out=ssq[gi][:, 0:1]),
                     reads=["yz%d" % gi], writes=["xsD%d" % gi, "ssq%d" % gi])
                rsqrt_to(S, cst, ssq[gi][:, 0:1], "ssq%d" % gi, ssq[gi][:, 0:1], "ssq%d" % gi, 1.0 / 256)
                S.op("dve", lambda e: e.scalar_tensor_tensor(
                    out=y[:, g * 256:(g + 1) * 256], in0=yz[gi][:], scalar=ssq[gi][:, 0:1],
                    in1=rows[:, R_NW1 + g * 256:R_NW1 + (g + 1) * 256], op0=ALU.mult, op1=ALU.mult),
                    reads=["yz%d" % gi, "ssq%d" % gi, "rows"], writes=[yk])
                S.op("pool", lambda e: e.tensor_tensor(
                    out=v3(ST[:, g, :]), in0=v3(ST[:, g, :]), in1=bc(dA_sb[:, g * 4:(g + 1) * 4], 4, 64), op=ALU.mult),
                    reads=[("ST", g), "dA_sb"], writes=[("ST", g)])
                S.op("dve", lambda e: e.tensor_tensor(out=ST[:, g, :], in0=ST[:, g, :], in1=b4[:, 256:512], op=ALU.add),
                     reads=[("ST", g)], writes=[b4k, ("ST", g)])
                S.op("act", lambda e: e.activation(out=STb[:, g, :], in_=ST[:, g, :], func=AF.Identity),
                     reads=[("ST", g)], writes=[("STb", g)])
                yield

            pend_g = [gdn_quad(q, q % 2) for q in range(4)]
            pend_s = [ssd_group(g) for g in range(8)]
            live = []
            slots = {"g": 0, "s": 0}

            def refill():
                while slots["g"] < 2 and pend_g:
                    live.append(("g", pend_g.pop(0)))
                    slots["g"] += 1
                while slots["s"] < 2 and pend_s:
                    live.append(("s", pend_s.pop(0)))
                    slots["s"] += 1
            refill()
            while live:
                for item in list(live):
                    kind, g_ = item
                    try:
                        next(g_)
                    except StopIteration:
                        live.remove(item)
                        slots[kind] -= 1
                refill()

            S.dma(Y[tt * 128:(tt + 1) * 128, :], y[:], reads=[yk], writes=["Y"])
            if tt + 1 < NT:
                load_z(tt + 1)
        S.barrier()


def load_weight_bf16(nc, S, dst, dstk, src_v, nk, ncols, stg, stgk):
    cnt = 0
    kc = 2
    for k0 in range(0, nk, kc):
        for c0 in range(0, ncols, 512):
            s_ = stg[cnt % 2]
            sk = stgk[cnt % 2]
            cnt += 1
            S.dma(s_[:], src_v[:, k0:k0 + kc, c0:c0 + 512], writes=[sk])
            if cnt % 2 == 0:
                S.op("act", lambda e, s_=s_, k0=k0, c0=c0: e.activation(out=dst[:, k0:k0 + kc, c0:c0 + 512], in_=s_[:], func=AF.Identity),
                     reads=[sk], writes=[dstk])
            else:
                S.op("dve", lambda e, s_=s_, k0=k0, c0=c0: e.tensor_copy(out=dst[:, k0:k0 + kc, c0:c0 + 512], in_=s_[:]),
                     reads=[sk], writes=[dstk])


def phase4a(nc, S, T, Y, G_T, XT, X2T, w_su, w_gu, w_out, cst, cstb, pv):
    BW = 256
    NBW = T // BW
    NA = BW // 128
    identb = cstb[:, C_ID, :]
    onesD = cst[:, C_ONED, :]
    with ExitStack() as ps:
        lsb = lambda name, shape, dt=F32: ps.enter_context(nc.sbuf_tensor(name, shape, dt))
        Wsu = lsb("Wsu", [128, 16, D], BF16)
        Wgu = lsb("Wgu", [128, 16, D], BF16)
        Wo = lsb("Wo", [128, 8, D], BF16)
        stg = [lsb("stg%d" % i, [128, 2, 512]) for i in range(2)]
        ytoks = [lsb("ytok%d" % i, [128, NA, 4096], BF16) for i in range(2)]
        yT = lsb("yT", [128, 32, BW], BF16)
        gTs = [lsb("gT%d" % i, [128, 16, BW], BF16) for i in range(2)]
        xTs = [lsb("xT4_%d" % i, [128, 8, BW]) for i in range(2)]
        t1 = [lsb("m_t1_%d" % i, [128, BW]) for i in range(2)]
        t2 = [lsb("m_t2_%d" % i, [128, BW]) for i in range(2)]
        mT = lsb("mT", [128, 8, BW], BF16)
        m2s = lsb("m2s", [128, 8, BW])
        sqm = lsb("sqm", [128, 8, BW])
        rstd = lsb("rstd4", [128, BW])
        pu = [ps.enter_context(nc.psum_tensor("pu%d" % i, [128, 512], F32)) for i in range(4)]
        pt = [ps.enter_context(nc.psum_tensor("ptb%d" % i, [128, 1024], BF16)) for i in range(2)]
        pss = ps.enter_context(nc.psum_tensor("pss4", [128, 512], F32))
        load_weight_bf16(nc, S, Wsu, "Wsu", w_su.rearrange("(k p) f -> p k f", p=128), 16, D, stg, ["stg0", "stg1"])
        load_weight_bf16(nc, S, Wgu, "Wgu", w_gu.rearrange("(k p) f -> p k f", p=128), 16, D, stg, ["stg0", "stg1"])
        load_weight_bf16(nc, S, Wo, "Wo", w_out.rearrange("(k p) f -> p k f", p=128), 8, D, stg, ["stg0", "stg1"])
        GTv = G_T.rearrange("(b p) t -> p b t", p=128)
        XTv = XT.rearrange("(k p) t -> p k t", p=128)
        X2Tv = X2T.rearrange("(k p) t -> p k t", p=128)
        pc = 0
        def loads4(nb):
            t0 = nb * BW
            i = nb % 2
            S.dma(ytoks[i][:], Y[t0:t0 + BW, :].rearrange("(a p) f -> p a f", p=128), reads=["Y"], writes=["ytok%d" % i])
            S.dma(gTs[i][:], GTv[:, :, t0:t0 + BW], reads=["G_T"], writes=["gT%d" % i])
            S.dma(xTs[i][:], XTv[:, :, t0:t0 + BW], reads=["XT"], writes=["xT4_%d" % i])
        loads4(0)
        for nb in range(NBW):
            t0 = nb * BW
            ytok, gT, xT = ytoks[nb % 2], gTs[nb % 2], xTs[nb % 2]
            ytk, gTk, xTk = "ytok%d" % (nb % 2), "gT%d" % (nb % 2), "xT4_%d" % (nb % 2)
            if nb + 1 < NBW:
                loads4(nb + 1)
            for cb in range(32):
                p_ = pt[cb % 2]
                pk = "ptb%d" % (cb % 2)
                for a in range(NA):
                    S.op("pe", lambda e, p_=p_, a=a, cb=cb, ytok=ytok: e.transpose(p_[:, a * 128:(a + 1) * 128],
                                                                      ytok[:, a, cb * 128:(cb + 1) * 128], identb),
                         reads=[ytk, "cstb"], writes=[pk])
                if cb % 2 == 0:
                    S.op("act", lambda e, p_=p_, cb=cb: e.activation(out=yT[:, cb, :], in_=p_[:, 0:BW], func=AF.Identity),
                         reads=[pk], writes=["yT"])
                else:
                    S.op("dve", lambda e, p_=p_, cb=cb: e.tensor_copy(out=yT[:, cb, :], in_=p_[:, 0:BW]), reads=[pk], writes=["yT"])
            for blk in range(8):
                p1 = pu[pc % 4]
                p1k = "pu%d" % (pc % 4)
                pc += 1
                p2 = pu[pc % 4]
                p2k = "pu%d" % (pc % 4)
                pc += 1
                for k in range(16):
                    S.op("pe", lambda e, p1=p1, k=k, blk=blk: e.matmul(p1[:, 0:BW], Wsu[:, k, blk * 128:(blk + 1) * 128], yT[:, k, :],
                                                                       start=(k == 0), stop=(k == 15)),
                         reads=["Wsu", "yT"], writes=[p1k])
                for k in range(16):
                    S.op("pe", lambda e, p2=p2, k=k, blk=blk: e.matmul(p2[:, 0:BW], Wgu[:, k, blk * 128:(blk + 1) * 128], yT[:, 16 + k, :],
                                                                       start=(k == 0), stop=(k == 15)),
                         reads=["Wgu", "yT"], writes=[p2k])
                a1 = t1[blk % 2]
                a1k = "m_t1_%d" % (blk % 2)
                a2 = t2[blk % 2]
                a2k = "m_t2_%d" % (blk % 2)
                S.op("dve", lambda e, p1=p1, a1=a1, blk=blk, gT=gT: e.tensor_tensor(out=a1[:], in0=p1[:, 0:BW], in1=gT[:, blk, :], op=ALU.mult),
                     reads=[p1k, gTk], writes=[a1k])
                S.op("dve", lambda e, p2=p2, a2=a2, blk=blk, gT=gT: e.tensor_tensor(out=a2[:], in0=p2[:, 0:BW], in1=gT[:, 8 + blk, :], op=ALU.mult),
                     reads=[p2k, gTk], writes=[a2k])
                S.op("dve", lambda e, a1=a1, a2=a2, blk=blk: e.tensor_tensor(out=mT[:, blk, :], in0=a1[:], in1=a2[:], op=ALU.add),
                     reads=[a1k, a2k], writes=["mT"])
            for blk in range(8):
                p1 = pu[pc % 4]
                p1k = "pu%d" % (pc % 4)
                pc += 1
                for k in range(8):
                    S.op("pe", lambda e, p1=p1, k=k, blk=blk: e.matmul(p1[:, 0:BW], Wo[:, k, blk * 128:(blk + 1) * 128], mT[:, k, :],
                                                                       start=(k == 0), stop=(k == 7)),
                         reads=["Wo", "mT"], writes=[p1k])
                S.op("act", lambda e, p1=p1, blk=blk: e.activation(out=m2s[:, blk, :], in_=p1[:, 0:BW], func=AF.Identity),
                     reads=[p1k], writes=["m2s"])
                S.op("act", lambda e, p1=p1, blk=blk: e.activation(out=sqm[:, blk, :], in_=p1[:, 0:BW], func=AF.Square),
                     reads=[p1k], writes=["sqm"])
            for k in range(8):
                S.op("pe", lambda e, k=k: e.matmul(pss[:, 0:BW], onesD, sqm[:, k, :], start=(k == 0), stop=(k == 7)),
                     reads=["sqm", "cst"], writes=["pss4"])
            rsqrt_to(S, cst, rstd[:], "rstd4", pss[:, 0:BW], "pss4")
            for blk in range(8):
                a1 = t1[blk % 2]
                a1k = "m_t1_%d" % (blk % 2)
                S.op("dve", lambda e, a1=a1, blk=blk: e.tensor_tensor(out=a1[:], in0=m2s[:, blk, :], in1=rstd[:], op=ALU.mult),
                     reads=["m2s", "rstd4"], writes=[a1k])
                S.op("dve", lambda e, a1=a1, blk=blk, xT=xT: e.scalar_tensor_tensor(
                    out=xT[:, blk, :], in0=a1[:], scalar=pv[:, 2, blk:blk + 1], in1=xT[:, blk, :], op0=ALU.mult, op1=ALU.add),
                    reads=[a1k, "pv", xTk], writes=[xTk])
            S.dma(X2Tv[:, :, t0:t0 + BW], xT[:], reads=[xTk], writes=["X2T"])
        S.barrier()


def phase4b(nc, S, T, X2T, out, w_up, w_dn, cst, pv, normT):
    BW = 256
    NBW = T // BW
    NA = BW // 128
    ident = cst[:, C_ID, :]
    onesD = cst[:, C_ONED, :]
    with ExitStack() as ps:
        lsb = lambda name, shape, dt=F32: ps.enter_context(nc.sbuf_tensor(name, shape, dt))
        Wup = lsb("Wup", [128, 8, 4096], BF16)
        Wdn = lsb("Wdn", [128, 32, D], BF16)
        stg = [lsb("stgb%d" % i, [128, 2, 512]) for i in range(2)]
        xTs = [lsb("x5T%d" % i, [128, 8, BW]) for i in range(2)]
        sq = lsb("sq5", [128, 8, BW])
        rstd = lsb("rstd5", [128, BW])
        tmp = [lsb("tmp5_%d" % i, [128, BW]) for i in range(2)]
        h2T = lsb("h2T", [128, 8, BW], BF16)
        rl = [lsb("rl%d" % i, [128, BW]) for i in range(2)]
        actT = lsb("actT", [128, 32, BW], BF16)
        dns = lsb("dns", [128, 8, BW])
        otok = [lsb("otok%d" % i, [128, 512]) for i in range(2)]
        pu = [ps.enter_context(nc.psum_tensor("p5u%d" % i, [128, 512], F32)) for i in range(4)]
        pss = ps.enter_context(nc.psum_tensor("pss5", [128, 512], F32))
        po = [ps.enter_context(nc.psum_tensor("p5o%d" % i, [128, 512], F32)) for i in range(2)]
        load_weight_bf16(nc, S, Wup, "Wup", w_up.rearrange("(k p) f -> p k f", p=128), 8, 4096, stg, ["stgb0", "stgb1"])
        load_weight_bf16(nc, S, Wdn, "Wdn", w_dn.rearrange("(k p) f -> p k f", p=128), 32, D, stg, ["stgb0", "stgb1"])
        X2Tv = X2T.rearrange("(k p) t -> p k t", p=128)
        pc = 0
        oc = 0
        S.dma(xTs[0][:], X2Tv[:, :, 0:BW], reads=["X2T"], writes=["x5T0"])
        for nb in range(NBW):
            t0 = nb * BW
            xT = xTs[nb % 2]
            xk = "x5T%d" % (nb % 2)
            if nb + 1 < NBW:
                S.dma(xTs[(nb + 1) % 2][:], X2Tv[:, :, t0 + BW:t0 + 2 * BW], reads=["X2T"], writes=["x5T%d" % ((nb + 1) % 2)])
            normT(xT, xk, sq, "sq5", rstd, "rstd5", pss, "pss5", tmp, ["tmp5_0", "tmp5_1"],
                  lambda k: h2T[:, k, :], "h2T", 3, 4, BW)
            for blk in range(32):
                p1 = pu[pc % 4]
                p1k = "p5u%d" % (pc % 4)
                pc += 1
                for k in range(8):
                    S.op("pe", lambda e, p1=p1, k=k, blk=blk: e.matmul(p1[:, 0:BW], Wup[:, k, blk * 128:(blk + 1) * 128], h2T[:, k, :],
                                                                       start=(k == 0), stop=(k == 7)),
                         reads=["Wup", "h2T"], writes=[p1k])
                r_ = rl[blk % 2]
                rk = "rl%d" % (blk % 2)
                S.op("act", lambda e, p1=p1, r_=r_: e.activation(out=r_[:], in_=p1[:, 0:BW], func=AF.Relu), reads=[p1k], writes=[rk])
                eng = "pool" if blk % 4 == 0 else "dve"
                S.op(eng, lambda e, r_=r_, blk=blk: e.tensor_tensor(out=actT[:, blk, :], in0=r_[:], in1=r_[:], op=ALU.mult),
                     reads=[rk], writes=["actT"])
            for blk in range(8):
                p1 = pu[pc % 4]
                p1k = "p5u%d" % (pc % 4)
                pc += 1
                for k in range(32):
                    S.op("pe", lambda e, p1=p1, k=k, blk=blk: e.matmul(p1[:, 0:BW], Wdn[:, k, blk * 128:(blk + 1) * 128], actT[:, k, :],
                                                                       start=(k == 0), stop=(k == 31)),
                         reads=["Wdn", "actT"], writes=[p1k])
                S.op("act", lambda e, p1=p1, blk=blk: e.activation(out=dns[:, blk, :], in_=p1[:, 0:BW], func=AF.Identity),
                     reads=[p1k], writes=["dns"])
                S.op("act", lambda e, p1=p1, blk=blk: e.activation(out=sq[:, blk, :], in_=p1[:, 0:BW], func=AF.Square),
                     reads=[p1k], writes=["sq5"])
            for k in range(8):
                S.op("pe", lambda e, k=k: e.matmul(pss[:, 0:BW], onesD, sq[:, k, :], start=(k == 0), stop=(k == 7)),
                     reads=["sq5", "cst"], writes=["pss5"])
            rsqrt_to(S, cst, rstd[:], "rstd5", pss[:, 0:BW], "pss5")
            for blk in range(8):
                t_ = tmp[blk % 2]
                tk = "tmp5_%d" % (blk % 2)
                S.op("dve", lambda e, t_=t_, blk=blk: e.tensor_tensor(out=t_[:], in0=dns[:, blk, :], in1=rstd[:], op=ALU.mult),
                     reads=["dns", "rstd5"], writes=[tk])
                S.op("dve", lambda e, t_=t_, blk=blk, xT=xT: e.scalar_tensor_tensor(
                    out=dns[:, blk, :], in0=t_[:], scalar=pv[:, 5, blk:blk + 1], in1=xT[:, blk, :], op0=ALU.mult, op1=ALU.add),
                    reads=[tk, "pv", xk, "dns"], writes=["dns"])
            for a in range(NA):
                for half in range(2):
                    ot = otok[oc % 2]
                    otk = "otok%d" % (oc % 2)
                    oc += 1
                    p_ = po[half]
                    pk = "p5o%d" % half
                    for b4 in range(4):
                        blk = half * 4 + b4
                        S.op("pe", lambda e, p_=p_, b4=b4, blk=blk, a=a: e.transpose(
                            p_[:, b4 * 128:(b4 + 1) * 128], dns[:, blk, a * 128:(a + 1) * 128], ident),
                            reads=["dns", "cst"], writes=[pk])
                    if half == 0:
                        S.op("act", lambda e, p_=p_, ot=ot: e.activation(out=ot[:], in_=p_[:], func=AF.Identity),
                             reads=[pk], writes=[otk])
                    else:
                        S.op("dve", lambda e, p_=p_, ot=ot: e.tensor_copy(out=ot[:], in_=p_[:]), reads=[pk], writes=[otk])
                    S.dma(out[t0 + a * 128:t0 + (a + 1) * 128, half * 512:(half + 1) * 512], ot[:], reads=[otk], writes=["out"])
        S.barrier()


def host_inputs(inputs, b, T):
    f = lambda a: np.ascontiguousarray(np.asarray(a, dtype=np.float32))
    col = lambda v: f(np.asarray(v).reshape(-1, 128).T)
    nw = np.stack([col(inputs["norm_mix_pre"][0]), col(inputs["norm_mix_post"][0]),
                   col(inputs["norm_mlp_pre"][0]), col(inputs["norm_mlp_post"][0])], axis=1)
    cws = np.concatenate([np.asarray(inputs["ssm_conv_w"][0]), np.asarray(inputs["ssm_conv_b"])], axis=0)
    cws = cws.reshape(5, 32, 128).transpose(2, 1, 0)
    cwg = np.asarray(inputs["gdn_conv_w"][0]).reshape(4, 32, 128).transpose(2, 1, 0)
    rowv = np.concatenate([np.asarray(inputs["ssm_dt_bias"][0]), np.asarray(inputs["ssm_A_log"][0]), np.asarray(inputs["ssm_D"][0]),
                           np.asarray(inputs["gdn_dt_bias"][0]), np.asarray(inputs["gdn_A_log"][0]),
                           np.asarray(inputs["ssm_norm_w"][0]), np.asarray(inputs["gdn_norm_w"][0])])[None, :]
    return {
        "x": f(np.asarray(inputs["x"])[b, :T]),
        "c_col": col(np.asarray(inputs["c"])[b]),
        "w_ada": f(inputs["w_ada"][0]),
        "b_ada_col": col(inputs["b_ada"][0]),
        "nw_col": f(nw),
        "w_in": f(inputs["w_in"][0]),
        "cw_ssm": f(cws),
        "cw_gdn": f(cwg),
        "rowv": f(rowv),
        "w_su": f(inputs["w_ssm_up"][0]),
        "w_gu": f(inputs["w_gdn_up"][0]),
        "w_out": f(inputs["w_out"][0]),
        "w_up": f(inputs["w_mlp_up"][0]),
        "w_dn": f(inputs["w_mlp_down"][0]),
        "consts": make_consts(),
    }


def kernel(**inputs):
    T = 4096
    nc, S = build(T)
    shared = None
    in_maps = []
    for b in range(8):
        m = host_inputs(inputs, b, T)
        if shared is None:
            shared = m
        else:
            for k in m:
                if k not in ("x", "c_col"):
                    m[k] = shared[k]
        in_maps.append(m)
    res = run_bass_kernel_spmd(nc, in_maps, core_ids=list(range(8)))
    return np.stack([np.asarray(r["out"], dtype=np.float32) for r in res.results], axis=0)
```

```python
import numpy as np
import ml_dtypes
from contextlib import ExitStack
import concourse.bass as bass
import concourse.mybir as mybir
from concourse.bass_utils import run_bass_kernel_spmd

F32 = mybir.dt.float32
BF16 = mybir.dt.bfloat16
AF = mybir.ActivationFunctionType
ALU = mybir.AluOpType

D = 1024
EPS = 1e-6
COMPUTE = ("pe", "act", "dve", "pool")
NSLOT = 8


STRICT = False


class Sched:
    def __init__(self, nc, st):
        self.nc = nc
        self.st = st
        self.streams = {e: [] for e in ("pe", "act", "dve", "pool", "sp")}
        self.sem = {}
        for e in COMPUTE:
            self.sem[e] = st.enter_context(nc.semaphore("c_" + e))
        for i in range(NSLOT):
            self.sem[("sp", i)] = st.enter_context(nc.semaphore("d_sp%d" % i))
        self.count = {k: 0 for k in self.sem}
        self.dma_idx = 0
        self.known = {e: {} for e in self.streams}
        self.clock = {}
        self.last_write = {}
        self.readers = {}
        self.ninstr = 0
        self.nwaits = 0

    def _need(self, eng, ev, waits):
        c, n = ev
        if self.known[eng].get(c, 0) >= n:
            return
        if waits.get(c, 0) < n:
            waits[c] = n

    def _deps(self, eng, reads, writes):
        waits = {}
        for k in reads:
            ev = self.last_write.get(k)
            if ev is not None:
                if ev[0] == eng and eng == "pe":
                    continue
                self._need(eng, ev, waits)
        for k in writes:
            ev = self.last_write.get(k)
            if ev is not None and (STRICT and eng != "pe" or not (ev[0] == eng and eng in COMPUTE)):
                self._need(eng, ev, waits)
            for rv in self.readers.get(k, ()):
                if rv[0] == eng and eng in COMPUTE and not STRICT:
                    continue
                self._need(eng, rv, waits)
        return waits

    def _apply(self, eng, waits):
        kn = self.known[eng]
        for c, n in waits.items():
            ck = self.clock.get((c, n))
            if ck:
                for cc, nn in ck.items():
                    if kn.get(cc, 0) < nn:
                        kn[cc] = nn
            if kn.get(c, 0) < n:
                kn[c] = n

    def _record(self, ev, eng, reads, writes):
        ck = dict(self.known[eng])
        ck[ev[0]] = ev[1]
        self.clock[ev] = ck
        for k in reads:
            self.readers.setdefault(k, []).append(ev)
        for k in writes:
            self.last_write[k] = ev
            self.readers[k] = []

    def op(self, eng, fn, reads=(), writes=()):
        waits = self._deps(eng, reads, writes)
        self._apply(eng, waits)
        self.count[eng] += 1
        ev = (eng, self.count[eng])
        self._record(ev, eng, reads, writes)
        self.streams[eng].append((list(waits.items()), fn, (eng, 1)))
        self.ninstr += 1
        self.nwaits += len(waits)
        return ev

    def dma(self, out, in_, reads=(), writes=()):
        q = "sp"
        slot = (q, self.dma_idx % NSLOT)
        self.dma_idx += 1
        waits = self._deps(q, reads, writes)
        if self.count[slot] > 0:
            self._need(q, (slot, self.count[slot]), waits)
        self._apply(q, waits)
        self.count[slot] += 1
        ev = (slot, self.count[slot])
        self._record(ev, q, reads, writes)
        fn = lambda e, out=out, in_=in_: e.dma_start(out=out, in_=in_)
        self.streams[q].append((list(waits.items()), fn, (slot, 16)))
        self.ninstr += 1
        self.nwaits += len(waits)
        return ev

    def barrier(self):
        for eng in self.streams:
            waits = {}
            for c, n in self.count.items():
                if n > 0 and c != eng:
                    self._need(eng, (c, n), waits)
            self._apply(eng, waits)
            self.streams[eng].append((list(waits.items()), None, None))
        self.last_write = {}
        self.readers = {}

    def emit(self):
        nc = self.nc
        block = self.st.enter_context(nc.Block())
        sem = self.sem

        def run(stream):
            def body(e):
                for waits, fn, inc in stream:
                    for c, n in waits:
                        e.wait_ge(sem[c], n * (1 if c in COMPUTE else 16))
                    if fn is not None:
                        fn(e).then_inc(sem[inc[0]], inc[1])
            return body

        block.tensor(run(self.streams["pe"]))
        block.scalar(run(self.streams["act"]))
        block.vector(run(self.streams["dve"]))
        block.gpsimd(run(self.streams["pool"]))
        block.sync(run(self.streams["sp"]))


OFF_Z1 = 0
OFF_XBC = 2048
OFF_DT = 6144
OFF_QKV = 6176
OFF_Z2 = 10272
OFF_B = 12320
OFF_A = 12336
OFF_GS = 12352
OFF_GG = 13376

C_ID, C_U, C_GT, C_SU, C_ONE, C_ONED, C_EPS, C_BD8, C_CMT, NCONST = 0, 1, 2, 3, 4, 5, 6, 7, 8, 14
R_DTB1, R_AL1, R_D1, R_DTB2, R_AL2, R_NW1, R_NW2, RLEN = 0, 32, 64, 96, 112, 128, 2176, 2304


def make_consts():
    k = np.arange(128)[:, None]
    l = np.arange(128)[None, :]
    c = np.zeros((128, NCONST, 128), np.float32)
    c[:, C_ID] = (k == l)
    c[:, C_U] = (k <= l)
    c[:, C_GT] = (k > l)
    c[:, C_SU] = (l > k)
    c[:, C_ONE] = 1.0
    c[:, C_ONED] = 1.0 / D
    c[:, C_EPS] = EPS
    c[:, C_BD8] = (k // 2 == l // 2)
    for n, b in enumerate((2, 4, 8, 16, 32, 64)):
        cm = ((k // (2 * b) == l // (2 * b)) & ((k // b) % 2 == 0) & ((l // b) % 2 == 1))
        c[:, C_CMT + n] = cm.T
    return c


def rsqrt_to(S, cst, dst, dstk, src, srck, scale=1.0):
    S.op("act", lambda e: e.activation(out=dst, in_=src, func=AF.Sqrt, bias=cst[:, C_EPS, 0:1], scale=scale),
         reads=[srck, "cst"], writes=[dstk])
    S.op("dve", lambda e: e.reciprocal(out=dst, in_=dst), reads=[dstk], writes=[dstk])


def build(T, phases=(0, 1, 2, 3, 4, 5), debug=False):
    NT = T // 128
    NB = T // 512
    nc = bass.Bass("TRN2", target_bir_lowering=False)

    def din(name, shape, dt=F32):
        return nc.dram_tensor(name, shape, dt, kind="ExternalInput").ap()

    def dscr(name, shape, dt):
        return nc.dram_tensor(name, shape, dt, kind="ExternalOutput").ap()

    x = din("x", [T, D])
    c_col = din("c_col", [128, 8])
    w_ada = din("w_ada", [D, 6 * D])
    b_ada_col = din("b_ada_col", [128, 48])
    nw_col = din("nw_col", [128, 4, 8])
    w_in = din("w_in", [D, 14400])
    cw_ssm = din("cw_ssm", [128, 32, 5])
    cw_gdn = din("cw_gdn", [128, 32, 4])
    rowv = din("rowv", [1, RLEN])
    w_su = din("w_su", [2048, D])
    w_gu = din("w_gu", [2048, D])
    w_out = din("w_out", [D, D])
    w_up = din("w_up", [D, 4096])
    w_dn = din("w_dn", [4096, D])
    consts = din("consts", [128, NCONST, 128])
    out = nc.dram_tensor("out", [T, D], F32, kind="ExternalOutput").ap()

    XT = dscr("s_xt", [D, T], F32)
    XBC_T = dscr("s_xbct", [4096, T], BF16)
    QKV_T = dscr("s_qkvt", [4096, T], BF16)
    G_T = dscr("s_gt", [2048, T], BF16)
    Z = dscr("s_z", [T, 4096], BF16)
    SM = dscr("s_sm", [T, 96], F32)
    if debug:
        Y = dscr("s_y", [T, 4096], BF16)
        X2T = dscr("s_x2t", [D, T], F32)
        MOD = dscr("s_mod", [128, 48], F32)
    else:
        Y = Z
        X2T = XT
        MOD = None

    with ExitStack() as st:
        S = Sched(nc, st)
        sb = lambda name, shape, dt=F32: st.enter_context(nc.sbuf_tensor(name, shape, dt))
        cst = sb("cst", [128, NCONST, 128])
        cstb = sb("cstb", [128, 1, 128], BF16)
        pv = sb("pv", [128, 6, 8])

        S.dma(cst[:], consts, writes=["cst"])
        S.op("pool", lambda e: e.tensor_copy(out=cstb[:], in_=cst[:, 0:1, :]), reads=["cst"], writes=["cstb"])
        ident = cst[:, C_ID, :]
        identb = cstb[:, C_ID, :]
        Umat = cst[:, C_U, :]
        GTm = cst[:, C_GT, :]
        SUm = cst[:, C_SU, :]
        ones = cst[:, C_ONE, :]
        onesD = cst[:, C_ONED, :]
        if 0 in phases:
            with ExitStack() as ps:
                lsb = lambda name, shape, dt=F32: ps.enter_context(nc.sbuf_tensor(name, shape, dt))
                cact = lsb("cact", [128, 8])
                csig = lsb("csig", [128, 8])
                wa = [lsb("wa%d" % i, [128, 8, 512]) for i in range(2)]
                modsb = lsb("modsb", [128, 48])
                bada = lsb("bada", [128, 48])
                nwc = lsb("nwc", [128, 4, 8])
                modps = ps.enter_context(nc.psum_tensor("modps", [128, 512], F32))
                S.dma(cact[:], c_col, writes=["cact"])
                S.dma(bada[:], b_ada_col, writes=["bada"])
                S.dma(nwc[:], nw_col, writes=["nwc"])
                S.op("act", lambda e: e.activation(out=csig[:], in_=cact[:], func=AF.Sigmoid), reads=["cact"], writes=["csig"])
                S.op("dve", lambda e: e.tensor_tensor(out=cact[:], in0=cact[:], in1=csig[:], op=ALU.mult),
                     reads=["cact", "csig"], writes=["cact"])
                wav = w_ada.rearrange("(k p) f -> p k f", p=128)
                for fb in range(12):
                    w = wa[fb % 2]
                    wk = "wa%d" % (fb % 2)
                    S.dma(w[:], wav[:, :, fb * 512:(fb + 1) * 512], writes=[wk])
                    for j in range(4):
                        col = fb * 4 + j
                        for k in range(8):
                            S.op("pe", lambda e, w=w, j=j, k=k, col=col: e.matmul(
                                modps[:, col:col + 1], w[:, k, j * 128:(j + 1) * 128], cact[:, k:k + 1],
                                start=(k == 0), stop=(k == 7)), reads=[wk, "cact"], writes=["modps"])
                S.op("dve", lambda e: e.tensor_tensor(out=modsb[:], in0=modps[:, 0:48], in1=bada[:], op=ALU.add),
                     reads=["modps", "bada"], writes=["modsb"])
                S.op("dve", lambda e: e.scalar_tensor_tensor(out=pv[:, 0, :], in0=modsb[:, 8:16], scalar=1.0, in1=nwc[:, 0, :],
                                                             op0=ALU.add, op1=ALU.mult), reads=["modsb", "nwc"], writes=["pv"])
                S.op("dve", lambda e: e.tensor_copy(out=pv[:, 1, :], in_=modsb[:, 0:8]), reads=["modsb"], writes=["pv"])
                S.op("dve", lambda e: e.tensor_tensor(out=pv[:, 2, :], in0=modsb[:, 16:24], in1=nwc[:, 1, :], op=ALU.mult),
                     reads=["modsb", "nwc"], writes=["pv"])
                S.op("dve", lambda e: e.scalar_tensor_tensor(out=pv[:, 3, :], in0=modsb[:, 32:40], scalar=1.0, in1=nwc[:, 2, :],
                                                             op0=ALU.add, op1=ALU.mult), reads=["modsb", "nwc"], writes=["pv"])
                S.op("dve", lambda e: e.tensor_copy(out=pv[:, 4, :], in_=modsb[:, 24:32]), reads=["modsb"], writes=["pv"])
                S.op("dve", lambda e: e.tensor_tensor(out=pv[:, 5, :], in0=modsb[:, 40:48], in1=nwc[:, 3, :], op=ALU.mult),
                     reads=["modsb", "nwc"], writes=["pv"])
                if debug:
                    S.dma(MOD, modsb[:], reads=["modsb"], writes=["MOD"])
                S.barrier()

        def normT(xT, xk, sq, sqk, rstd, rstdk, ssps, sspsk, tmp, tmpk, hdst, hk, ia, ish, W=512):
            S.op("act", lambda e: e.activation(out=sq[:], in_=xT[:], func=AF.Square), reads=[xk], writes=[sqk])
            for k in range(8):
                S.op("pe", lambda e, k=k: e.matmul(ssps[:, 0:W], onesD, sq[:, k, :], start=(k == 0), stop=(k == 7)),
                     reads=[sqk, "cst"], writes=[sspsk])
            rsqrt_to(S, cst, rstd[:], rstdk, ssps[:, 0:W], sspsk)
            for k in range(8):
                t = tmp[k % 2]
                tk = tmpk[k % 2]
                S.op("dve", lambda e, k=k, t=t: e.tensor_tensor(out=t[:], in0=xT[:, k, :], in1=rstd[:], op=ALU.mult),
                     reads=[xk, rstdk], writes=[tk])
                S.op("act", lambda e, k=k, t=t: e.activation(out=hdst(k), in_=t[:], func=AF.Identity,
                                                           bias=pv[:, ish, k:k + 1], scale=pv[:, ia, k:k + 1]),
                     reads=[tk, "pv"], writes=[hk])

        hT_cm = None
        hst = ExitStack()
        if 1 in phases or 2 in phases:
            hT_cm = hst.enter_context(nc.sbuf_tensor("hT", [128, 8, T], BF16))

        if 1 in phases:
            with ExitStack() as ps:
                lsb = lambda name, shape, dt=F32: ps.enter_context(nc.sbuf_tensor(name, shape, dt))
                xtok = [lsb("xtok%d" % i, [128, 4, D]) for i in range(2)]
                xTb = [lsb("xTb%d" % i, [128, 8, 512]) for i in range(2)]
                sq = lsb("sq1", [128, 8, 512])
                rstd = lsb("rstd1", [128, 512])
                tmp = [lsb("tmp1_%d" % i, [128, 512]) for i in range(2)]
                tps = [ps.enter_context(nc.psum_tensor("tps%d" % i, [128, 512], F32)) for i in range(4)]
                ssps = ps.enter_context(nc.psum_tensor("ssps1", [128, 512], F32))
                XTv = XT.rearrange("(k p) t -> p k t", p=128)
                for nb in range(NB):
                    xt = xtok[nb % 2]
                    xtk = "xtok%d" % (nb % 2)
                    xT = xTb[nb % 2]
                    xTk = "xTb%d" % (nb % 2)
                    S.dma(xt[:], x[nb * 512:(nb + 1) * 512, :].rearrange("(a p) f -> p a f", p=128), writes=[xtk])
                    for k in range(8):
                        tp = tps[k % 4]
                        tpk = "tps%d" % (k % 4)
                        for a in range(4):
                            S.op("pe", lambda e, tp=tp, a=a, k=k, xt=xt: e.transpose(
                                tp[:, a * 128:(a + 1) * 128], xt[:, a, k * 128:(k + 1) * 128], ident),
                                reads=[xtk, "cst"], writes=[tpk])
                        if k % 2 == 0:
                            S.op("act", lambda e, tp=tp, k=k, xT=xT: e.activation(out=xT[:, k, :], in_=tp[:], func=AF.Identity),
                                 reads=[tpk], writes=[xTk])
                        else:
                            S.op("dve", lambda e, tp=tp, k=k, xT=xT: e.tensor_copy(out=xT[:, k, :], in_=tp[:]),
                                 reads=[tpk], writes=[xTk])
                    S.dma(XTv[:, :, nb * 512:(nb + 1) * 512], xT[:], reads=[xTk], writes=["XT"])
                    normT(xT, xTk, sq, "sq1", rstd, "rstd1", ssps, "ssps1", tmp, ["tmp1_0", "tmp1_1"],
                          lambda k, nb=nb: hT_cm[:, k, nb * 512:(nb + 1) * 512], ("hT", nb), 0, 1)
                S.barrier()

        if 2 in phases:
            hkeys = [("hT", nb) for nb in range(NB)]
            with ExitStack() as ps:
                lsb = lambda name, shape, dt=F32: ps.enter_context(nc.sbuf_tensor(name, shape, dt))
                wst = [lsb("wst%d" % i, [128, 8, 128]) for i in range(2)]
                wbf = [lsb("wbf%d" % i, [128, 8, 128], BF16) for i in range(2)]
                pc = [lsb("pc%d" % i, [128, T + 3]) for i in range(2)]
                accs = [lsb("acc%d" % i, [128, T]) for i in range(2)]
                sq2 = lsb("sq2", [128, T])
                rs = lsb("rs", [128, T])
                ob = [lsb("ob%d" % i, [128, T], BF16) for i in range(2)]
                cws = lsb("cws", [128, 32, 5])
                cwg = lsb("cwg", [128, 32, 4])
                pps = [ps.enter_context(nc.psum_tensor("pps%d" % i, [128, 512], F32)) for i in range(4)]
                sps = [ps.enter_context(nc.psum_tensor("sps%d" % i, [128, 512], F32)) for i in range(2)]
                S.dma(cws[:], cw_ssm, writes=["cws"])
                S.dma(cwg[:], cw_gdn, writes=["cwg"])
                for i in range(2):
                    S.op("pool", lambda e, i=i: e.memset(pc[i][:, 0:3], 0.0), writes=["pc%d" % i])
                w_in_v = w_in.rearrange("(k p) f -> p k f", p=128)
                XBCv = XBC_T.rearrange("(b p) t -> b p t", p=128)
                QKVv = QKV_T.rearrange("(b p) t -> b p t", p=128)
                GTv = G_T.rearrange("(b p) t -> b p t", p=128)
                blocks = []
                for cb in range(32):
                    blocks.append(("xbc", cb, OFF_XBC + cb * 128))
                for cb in range(32):
                    blocks.append(("qkv", cb, OFF_QKV + cb * 128))
                for cb in range(8):
                    blocks.append(("gate", cb, OFF_GS + cb * 128))
                for cb in range(8):
                    blocks.append(("gate", 8 + cb, OFF_GG + cb * 128))
                pcount = 0
                pcnt = [0]

                def bufs(bi):
                    return (accs[bi % 2], "acc%d" % (bi % 2), pc[bi % 2], "pc%d" % (bi % 2), ob[bi % 2], "ob%d" % (bi % 2),
                            wbf[bi % 2], "wbf%d" % (bi % 2))

                def wload(bi):
                    off = blocks[bi][2]
                    ws, wsk, wb, wbk = wst[bi % 2], "wst%d" % (bi % 2), wbf[bi % 2], "wbf%d" % (bi % 2)
                    S.dma(ws[:], w_in_v[:, :, off:off + 128], writes=[wsk])
                    S.op("pool", lambda e: e.tensor_copy(out=wb[:], in_=ws[:]), reads=[wsk], writes=[wbk])

                def front(bi):
                    kind, cb, off = blocks[bi]
                    acc, acck, p_, pk, o_, ok, wb, wbk = bufs(bi)
                    if bi + 1 < len(blocks):
                        wload(bi + 1)
                    for tb in range(NB):
                        pp = pps[pcnt[0] % 4]
                        ppk = "pps%d" % (pcnt[0] % 4)
                        pcnt[0] += 1
                        for k in range(8):
                            S.op("pe", lambda e, pp=pp, k=k, tb=tb: e.matmul(
                                pp[:], wb[:, k, :], hT_cm[:, k, tb * 512:(tb + 1) * 512], start=(k == 0), stop=(k == 7)),
                                reads=[wbk, hkeys[tb]], writes=[ppk])
                        if kind == "gate":
                            S.op("act", lambda e, pp=pp, tb=tb: e.activation(
                                out=o_[:, tb * 512:(tb + 1) * 512], in_=pp[:], func=AF.Sigmoid), reads=[ppk], writes=[ok])
                        else:
                            S.op("act", lambda e, pp=pp, tb=tb: e.activation(
                                out=p_[:, 3 + tb * 512:3 + (tb + 1) * 512], in_=pp[:], func=AF.Identity), reads=[ppk], writes=[pk])
                            if kind == "xbc":
                                S.op("act", lambda e, pp=pp, tb=tb: e.activation(
                                    out=acc[:, tb * 512:(tb + 1) * 512], in_=pp[:], func=AF.Identity,
                                    bias=cws[:, cb, 4:5], scale=cws[:, cb, 3:4]), reads=[ppk, "cws"], writes=[acck])
                            else:
                                S.op("act", lambda e, pp=pp, tb=tb: e.activation(
                                    out=acc[:, tb * 512:(tb + 1) * 512], in_=pp[:], func=AF.Identity,
                                    scale=cwg[:, cb, 3:4]), reads=[ppk, "cwg"], writes=[acck])

                def conv(bi):
                    kind, cb, off = blocks[bi]
                    if kind == "gate":
                        return
                    acc, acck, p_, pk, o_, ok, wb, wbk = bufs(bi)
                    cwt = cws if kind == "xbc" else cwg
                    cwk = "cws" if kind == "xbc" else "cwg"
                    for j in range(1, 4):
                        S.op("dve", lambda e, j=j: e.scalar_tensor_tensor(
                            out=acc[:], in0=p_[:, 3 - j:3 - j + T], scalar=cwt[:, cb, 3 - j:4 - j], in1=acc[:],
                            op0=ALU.mult, op1=ALU.add), reads=[pk, cwk, acck], writes=[acck])

                def post(bi):
                    kind, cb, off = blocks[bi]
                    acc, acck, p_, pk, o_, ok, wb, wbk = bufs(bi)
                    if kind == "gate":
                        S.dma(GTv[cb], o_[:], reads=[ok], writes=["G_T"])
                        return
                    if kind == "qkv" and cb < 16:
                        S.op("act", lambda e: e.activation(out=acc[:], in_=acc[:], func=AF.Silu), reads=[acck], writes=[acck])
                        S.op("act", lambda e: e.activation(out=sq2[:], in_=acc[:], func=AF.Square), reads=[acck], writes=["sq2"])
                        for tb in range(NB):
                            sp = sps[tb % 2]
                            spk = "sps%d" % (tb % 2)
                            S.op("pe", lambda e, sp=sp, tb=tb: e.matmul(sp[:], ones, sq2[:, tb * 512:(tb + 1) * 512],
                                                                        start=True, stop=True),
                                 reads=["sq2", "cst"], writes=[spk])
                            rsqrt_to(S, cst, rs[:, tb * 512:(tb + 1) * 512], "rs", sp[:], spk)
                        qs = (128.0 ** -0.5) if cb < 8 else 1.0
                        S.op("dve", lambda e: e.scalar_tensor_tensor(
                            out=o_[:], in0=acc[:], scalar=qs, in1=rs[:], op0=ALU.mult, op1=ALU.mult),
                            reads=[acck, "rs"], writes=[ok])
                    else:
                        S.op("act", lambda e: e.activation(out=o_[:], in_=acc[:], func=AF.Silu), reads=[acck], writes=[ok])
                    dst = XBCv[cb] if kind == "xbc" else QKVv[cb]
                    S.dma(dst, o_[:], reads=[ok], writes=[kind + "_T"])

                wload(0)
                for bi in range(len(blocks)):
                    front(bi)
                    if bi > 0:
                        post(bi - 1)
                    conv(bi)
                post(len(blocks) - 1)
                S.barrier()

            with ExitStack() as ps:
                lsb = lambda name, shape, dt=F32: ps.enter_context(nc.sbuf_tensor(name, shape, dt))
                wzs = [lsb("wzs%d" % i, [128, 8, 512]) for i in range(2)]
                wzb = [lsb("wzb%d" % i, [128, 8, 512], BF16) for i in range(2)]
                zb = [lsb("zb%d" % i, [128, 512], BF16) for i in range(3)]
                wss = lsb("wss", [128, 8, 64])
                wsb = lsb("wsb", [128, 8, 64], BF16)
                smt = [lsb("smt%d" % i, [128, 96]) for i in range(2)]
                t1 = [lsb("t1_%d" % i, [128, 48]) for i in range(2)]
                rows = lsb("rows2", [128, RLEN])
                arow = lsb("arow", [128, 48])
                S.dma(rows[:], rowv.partition_broadcast(128), writes=["rows"])
                S.op("act", lambda e: e.activation(out=arow[:, 0:32], in_=rows[:, R_AL1:R_AL1 + 32], func=AF.Exp),
                     reads=["rows"], writes=["arow"])
                S.op("act", lambda e: e.activation(out=arow[:, 32:48], in_=rows[:, R_AL2:R_AL2 + 16], func=AF.Exp),
                     reads=["rows"], writes=["arow"])
                S.op("dve", lambda e: e.tensor_scalar(out=arow[:], in0=arow[:], scalar1=-1.0, scalar2=None, op0=ALU.mult),
                     reads=["arow"], writes=["arow"])
                pps = [ps.enter_context(nc.psum_tensor("zps%d" % i, [128, 512], F32)) for i in range(4)]
                sps = [ps.enter_context(nc.psum_tensor("smps%d" % i, [128, 512], F32)) for i in range(2)]
                w_in_v = w_in.rearrange("(k p) f -> p k f", p=128)
                pcount = 0
                for blk in range(8):
                    off = (OFF_Z1 + blk * 512) if blk < 4 else (OFF_Z2 + (blk - 4) * 512)
                    ws = wzs[blk % 2]
                    wsk = "wzs%d" % (blk % 2)
                    wb = wzb[blk % 2]
                    wbk = "wzb%d" % (blk % 2)
                    if blk == 0:
                        S.dma(ws[:], w_in_v[:, :, off:off + 512], writes=[wsk])
                    if blk + 1 < 8:
                        noff = (OFF_Z1 + (blk + 1) * 512) if blk + 1 < 4 else (OFF_Z2 + (blk + 1 - 4) * 512)
                        S.dma(wzs[(blk + 1) % 2][:], w_in_v[:, :, noff:noff + 512], writes=["wzs%d" % ((blk + 1) % 2)])
                    S.op("act", lambda e, ws=ws, wb=wb: e.activation(out=wb[:], in_=ws[:], func=AF.Identity), reads=[wsk], writes=[wbk])
                    for tt in range(NT):
                        pp = pps[pcount % 4]
                        ppk = "zps%d" % (pcount % 4)
                        z_ = zb[pcount % 3]
                        zk = "zb%d" % (pcount % 3)
                        pcount += 1
                        for k in range(8):
                            S.op("pe", lambda e, pp=pp, wb=wb, k=k, tt=tt: e.matmul(
                                pp[:], hT_cm[:, k, tt * 128:(tt + 1) * 128], wb[:, k, :], start=(k == 0), stop=(k == 7)),
                                reads=[wbk, hkeys[tt // 4]], writes=[ppk])
                        S.op("act", lambda e, pp=pp, z_=z_: e.activation(out=z_[:], in_=pp[:], func=AF.Silu), reads=[ppk], writes=[zk])
                        S.dma(Z[tt * 128:(tt + 1) * 128, blk * 512:(blk + 1) * 512], z_[:], reads=[zk], writes=["Z"])
                S.dma(wss[:, :, 0:32], w_in_v[:, :, OFF_DT:OFF_DT + 32], writes=["wss"])
                S.dma(wss[:, :, 32:64], w_in_v[:, :, OFF_B:OFF_B + 32], writes=["wss"])
                S.op("pool", lambda e: e.tensor_copy(out=wsb[:], in_=wss[:]), reads=["wss"], writes=["wsb"])
                for tt in range(NT):
                    sp = sps[tt % 2]
                    spk = "smps%d" % (tt % 2)
                    sm_ = smt[tt % 2]
                    smk = "smt%d" % (tt % 2)
                    t_ = t1[tt % 2]
                    tk = "t1_%d" % (tt % 2)
                    for k in range(8):
                        S.op("pe", lambda e, sp=sp, k=k, tt=tt: e.matmul(
                            sp[:, 0:64], hT_cm[:, k, tt * 128:(tt + 1) * 128], wsb[:, k, :], start=(k == 0), stop=(k == 7)),
                            reads=["wsb", hkeys[tt // 4]], writes=[spk])
                    S.op("dve", lambda e, sp=sp, t_=t_: e.tensor_tensor(out=t_[:, 0:32], in0=sp[:, 0:32],
                                                                        in1=rows[:, R_DTB1:R_DTB1 + 32], op=ALU.add),
                         reads=["rows"], writes=[spk, tk])
                    S.op("dve", lambda e, sp=sp, t_=t_: e.tensor_tensor(out=t_[:, 32:48], in0=sp[:, 48:64],
                                                                        in1=rows[:, R_DTB2:R_DTB2 + 16], op=ALU.add),
                         reads=["rows"], writes=[spk, tk])
                    S.op("act", lambda e, t_=t_: e.activation(out=t_[:], in_=t_[:], func=AF.Exp), reads=[tk], writes=[tk])
                    S.op("dve", lambda e, t_=t_: e.tensor_scalar(out=t_[:], in0=t_[:], scalar1=1.0, scalar2=None, op0=ALU.add),
                         reads=[tk], writes=[tk])
                    S.op("act", lambda e, t_=t_: e.activation(out=t_[:], in_=t_[:], func=AF.Ln), reads=[tk], writes=[tk])
                    S.op("act", lambda e, sp=sp, sm_=sm_: e.activation(out=sm_[:, 32:48], in_=sp[:, 32:48], func=AF.Sigmoid),
                         writes=[spk, smk])
                    S.op("dve", lambda e, t_=t_, sm_=sm_: e.tensor_copy(out=sm_[:, 0:32], in_=t_[:, 0:32]), reads=[tk], writes=[smk])
                    S.op("dve", lambda e, t_=t_, sm_=sm_: e.tensor_tensor(out=sm_[:, 48:96], in0=t_[:], in1=arow[:], op=ALU.mult),
                         reads=[tk, "arow"], writes=[smk])
                    S.dma(SM[tt * 128:(tt + 1) * 128, :], sm_[:], reads=[smk], writes=["SM"])
                S.barrier()

        hst.close()
        if 3 in phases:
            phase3(nc, S, st, T, XBC_T, QKV_T, Z, SM, Y, cst, cstb, rowv)
        if 4 in phases:
            phase4a(nc, S, T, Y, G_T, XT, X2T, w_su, w_gu, w_out, cst, cstb, pv)
        if 5 in phases:
            phase4b(nc, S, T, X2T, out, w_up, w_dn, cst, pv, normT)
        else:
            pass
        S.barrier()
        S.emit()
    return nc, S


def phase3(nc, S, st_outer, T, XBC_T, QKV_T, Z, SM, Y, cst, cstb, rowv):
    NT = T // 128
    ident = cst[:, C_ID, :]
    identb = cstb[:, C_ID, :]
    Umat = cst[:, C_U, :]
    GTm = cst[:, C_GT, :]
    SUm = cst[:, C_SU, :]
    ones = cst[:, C_ONE, :]
    with ExitStack() as ps:
        lsb = lambda name, shape, dt=F32: ps.enter_context(nc.sbuf_tensor(name, shape, dt))
        smt = [lsb("p3sm%d" % i, [128, 96]) for i in range(2)]
        rows = lsb("rows3", [128, RLEN])
        S.dma(rows[:], rowv.partition_broadcast(128), writes=["rows"])
        xbct = [lsb("p3xbc%d" % i, [128, 32, 128], BF16) for i in range(2)]
        qkvt = [lsb("p3qkv%d" % i, [128, 32, 128], BF16) for i in range(2)]
        zt = [lsb("p3z%d" % i, [128, 4096], BF16) for i in range(1)]
        ytile = lsb("p3y", [128, 4096], BF16)
        xs_tok = lsb("xs_tok", [128, 2048], BF16)
        b_tok = lsb("b_tok", [128, 1024], BF16)
        k_tok = lsb("k_tok", [128, 1024], BF16)
        v_tok = lsb("v_tok", [128, 2048], BF16)
        c_sb = lsb("c_sb", [128, 48])
        e_sb = lsb("e_sb", [128, 48])
        f_sb = lsb("f_sb", [128, 48])
        dA_sb = lsb("dA_sb", [128, 48])
        nbeta = lsb("nbeta", [128, 16])
        ST = lsb("ST", [128, 8, 256])
        STb = lsb("STb", [128, 8, 256], BF16)
        GS = lsb("GS", [128, 16, 128])
        GSb = lsb("GSb", [128, 16, 128], BF16)
        IB = [[lsb("ibb%d_%d" % (s_, n_), [128, 4, 128], BF16) for n_ in range(5)]
              + [lsb("ibm%d" % s_, [128, 6, 4, 128], BF16)] for s_ in range(2)]
        attnT = lsb("attnT", [128, 16, 128], BF16)
        T2T = lsb("T2T", [128, 16, 128], BF16)
        ke = lsb("ke", [128, 16, 128], BF16)
        kf = lsb("kf", [128, 16, 128], BF16)
        nwT = lsb("nwT", [128, 16, 128], BF16)
        KKm = lsb("KKm", [128, 8, 128])
        QKm = lsb("QKm", [128, 8, 128])
        Am = [lsb("Am%d" % i, [128, 4, 128]) for i in range(2)]
        DT = [lsb("DT%d" % i, [128, 4, 128]) for i in range(4)]
        MT = [lsb("MT%d" % i, [128, 4, 128], BF16) for i in range(2)]
        CBTm = [lsb("CBTm%d" % i, [128, 128]) for i in range(2)]
        xdt = [lsb("xdt%d" % i, [128, 256], BF16) for i in range(2)]
        xw = [lsb("xw%d" % i, [128, 256], BF16) for i in range(2)]
        xsD = [lsb("xsD%d" % i, [128, 256]) for i in range(2)]
        yacc = [lsb("yacc%d" % i, [128, 256]) for i in range(2)]
        yz = [lsb("yz%d" % i, [128, 256]) for i in range(2)]
        ssq = [lsb("ssq%d" % i, [128, 4]) for i in range(2)]
        ssqs = [lsb("ssqs%d" % i, [128, 4]) for i in range(2)]
        vnew = [lsb("vnew%d" % i, [128, 4, 128], BF16) for i in range(2)]
        osb = [lsb("osb%d" % i, [128, 4, 128]) for i in range(2)]
        on = [lsb("on%d" % i, [128, 4, 128]) for i in range(2)]
        banks = [ps.enter_context(nc.psum_tensor("pb%d" % i, [128, 512], F32)) for i in range(8)]
        bctr = [0]

        def bank():
            i = bctr[0] % 8
            bctr[0] += 1
            return banks[i], "pb%d" % i
        rr = {"am": 0, "ama": 0, "g": 0, "v": 0}

        S.op("pool", lambda e: e.memset(ST[:], 0.0), writes=["ST"])
        S.op("pool", lambda e: e.memset(STb[:], 0.0), writes=["STb"])
        S.op("pool", lambda e: e.memset(GS[:], 0.0), writes=["GS"])
        S.op("pool", lambda e: e.memset(GSb[:], 0.0), writes=["GSb"])

        XBCv = XBC_T.rearrange("(b p) t -> p b t", p=128)
        QKVv = QKV_T.rearrange("(b p) t -> p b t", p=128)

        def loads(tt):
            i = tt % 2
            S.dma(smt[i][:], SM[tt * 128:(tt + 1) * 128, :], reads=["SM"], writes=["p3sm%d" % i])
            S.dma(xbct[i][:], XBCv[:, :, tt * 128:(tt + 1) * 128], reads=["xbc_T"], writes=["p3xbc%d" % i])
            S.dma(qkvt[i][:], QKVv[:, :, tt * 128:(tt + 1) * 128], reads=["qkv_T"], writes=["p3qkv%d" % i])

        def load_z(tt):
            S.dma(zt[0][:], Z[tt * 128:(tt + 1) * 128, :], reads=["Z"], writes=["p3z0"])

        def bc(ap2, n, w):
            return ap2.unsqueeze(2).to_broadcast([128, n, w])

        def v3(ap2, h=4):
            return ap2.rearrange("p (h w) -> p h w", h=h)

        loads(0)
        load_z(0)
        for tt in range(NT):
            i = tt % 2
            sm, smk = smt[i], "p3sm%d" % i
            xbc, xbk = xbct[i], "p3xbc%d" % i
            qkv, qkk = qkvt[i], "p3qkv%d" % i
            z, zk = zt[0], "p3z0"
            y, yk = ytile, "p3y"
            if tt + 1 < NT:
                loads(tt + 1)
            bD, bDk = bank()
            S.op("pe", lambda e, bD=bD, sm=sm: e.matmul(bD[:, 0:48], Umat, sm[:, 48:96], start=True, stop=True),
                 reads=[smk, "cst"], writes=[bDk])
            S.op("pe", lambda e, bD=bD, sm=sm: e.matmul(bD[:, 64:112], ones, sm[:, 48:96], start=True, stop=True),
                 reads=[smk, "cst"], writes=[bDk])
            S.op("act", lambda e, bD=bD: e.activation(out=c_sb[:], in_=bD[:, 0:48], func=AF.Identity), writes=[bDk, "c_sb"])
            S.op("act", lambda e, bD=bD: e.activation(out=e_sb[:], in_=bD[:, 0:48], func=AF.Exp), writes=[bDk, "e_sb"])
            S.op("act", lambda e, bD=bD: e.activation(out=dA_sb[:], in_=bD[:, 64:112], func=AF.Exp), writes=[bDk, "dA_sb"])
            S.op("act", lambda e, bD=bD: e.activation(out=f_sb[:], in_=bD[:, 64:112], func=AF.Identity), writes=[bDk, "f_sb"])
            S.op("dve", lambda e: e.tensor_tensor(out=f_sb[:], in0=f_sb[:], in1=c_sb[:], op=ALU.subtract),
                 reads=["c_sb", "f_sb"], writes=["f_sb"])
            S.op("act", lambda e: e.activation(out=f_sb[:], in_=f_sb[:], func=AF.Exp), reads=["f_sb"], writes=["f_sb"])
            S.op("pool", lambda e, sm=sm: e.tensor_scalar(out=nbeta[:], in0=sm[:, 32:48], scalar1=-1.0, scalar2=None, op0=ALU.mult),
                 reads=[smk], writes=["nbeta"])
            jobs = [(xbc, xbk, 0, xs_tok, "xs_tok", 2), (xbc, xbk, 16, b_tok, "b_tok", 1),
                    (qkv, qkk, 8, k_tok, "k_tok", 1), (qkv, qkk, 16, v_tok, "v_tok", 2)]
            nev = 0
            for (src, srck, b0, dst, dstk, nq) in jobs:
                for q8 in range(nq):
                    bk_, bkk = bank()
                    bb = bk_[:].bitcast(BF16)
                    for a in range(8):
                        blk = b0 + q8 * 8 + a
                        S.op("pe", lambda e, bb=bb, a=a, src=src, blk=blk: e.transpose(
                            bb[:, a * 128:(a + 1) * 128], src[:, blk, :], identb), reads=[srck, "cstb"], writes=[bkk])
                    if nev % 2 == 0:
                        S.op("act", lambda e, bb=bb, dst=dst, q8=q8: e.activation(
                            out=dst[:, q8 * 1024:(q8 + 1) * 1024], in_=bb, func=AF.Identity), writes=[bkk, dstk])
                    else:
                        S.op("dve", lambda e, bb=bb, dst=dst, q8=q8: e.tensor_copy(
                            out=dst[:, q8 * 1024:(q8 + 1) * 1024], in_=bb), writes=[bkk, dstk])
                    nev += 1

            for half in range(2):
                bk_, bkk = bank()
                for j in range(4):
                    hq = half * 4 + j
                    S.op("pe", lambda e, bk_=bk_, j=j, hq=hq, qkv=qkv: e.matmul(
                        bk_[:, j * 128:(j + 1) * 128], qkv[:, 8 + hq, :], qkv[:, 8 + hq, :], start=True, stop=True),
                        reads=[qkk], writes=[bkk])
                S.op("dve", lambda e, bk_=bk_, half=half: e.tensor_tensor(
                    out=KKm[:, half * 4:(half + 1) * 4, :], in0=v3(bk_[:]), in1=SUm.unsqueeze(1).to_broadcast([128, 4, 128]), op=ALU.mult),
                    reads=["cst"], writes=[bkk, "KKm"])
                bq_, bqk = bank()
                for j in range(4):
                    hq = half * 4 + j
                    S.op("pe", lambda e, bq_=bq_, j=j, hq=hq, qkv=qkv: e.matmul(
                        bq_[:, j * 128:(j + 1) * 128], qkv[:, 8 + hq, :], qkv[:, hq, :], start=True, stop=True),
                        reads=[qkk], writes=[bqk])
                S.op("dve", lambda e, bq_=bq_, half=half: e.tensor_tensor(
                    out=QKm[:, half * 4:(half + 1) * 4, :], in0=v3(bq_[:]), in1=Umat.unsqueeze(1).to_broadcast([128, 4, 128]), op=ALU.mult),
                    reads=["cst"], writes=[bqk, "QKm"])
            def fl(t):
                return t[:].rearrange("p h w -> p (h w)")

            def bcm(plane):
                return cst[:, plane, :].unsqueeze(1).to_broadcast([128, 4, 128])

            def build_DT(cols, r, sm, smk):
                ra = rr["ama"] % 2
                rr["ama"] += 1
                S.op("dve", lambda e: e.tensor_tensor(out=Am[ra][:], in0=bcm(C_GT), in1=bc(sm[:, cols:cols + 4], 4, 128), op=ALU.mult),
                     reads=[smk, "cst"], writes=["Am%d" % ra])
                sg, sgk = bank()
                for j in range(4):
                    S.op("pe", lambda e, sg=sg, j=j: e.matmul(sg[:, j * 128:(j + 1) * 128], Am[ra][:, j, :], Umat, start=True, stop=True),
                         reads=["Am%d" % ra, "cst"], writes=[sgk])
                S.op("act", lambda e, sg=sg: e.activation(out=fl(DT[r]), in_=sg[:], func=AF.Exp), writes=[sgk, "DT%d" % r])

            def gdn_quad(q, s_, sm=sm, smk=smk, qkv=qkv, qkk=qkk, z=z, zk=zk, y=y, yk=yk):
                Xb, XTb, Dvb, DvTb, Eb, XMb = IB[s_]
                kX, kXT, kDvb, kDvTb, kEb, kXMb = [("ib", s_, n_) for n_ in range(6)]
                qs = slice(q * 4, (q + 1) * 4)
                r = rr["am"] % 4
                rr["am"] += 1
                build_DT(80 + q * 4, r, sm, smk)
                for j in range(4):
                    hv = q * 4 + j
                    hq = hv // 2
                    S.op("dve", lambda e, j=j, hv=hv, hq=hq: e.scalar_tensor_tensor(
                        out=Xb[:, j, :], in0=KKm[:, hq, :], scalar=nbeta[:, hv:hv + 1], in1=DT[r][:, j, :], op0=ALU.mult, op1=ALU.mult),
                        reads=["KKm", "nbeta", "DT%d" % r], writes=[kX])
                S.op("dve", lambda e: e.tensor_tensor(
                    out=attnT[:, qs, :].rearrange("p (a b) w -> p a b w", a=2),
                    in0=QKm[:, 2 * q:2 * q + 2, :].unsqueeze(2).to_broadcast([128, 2, 2, 128]),
                    in1=DT[r][:].rearrange("p (a b) w -> p a b w", a=2), op=ALU.mult),
                    reads=["QKm", "DT%d" % r], writes=[("attnT", q)])
                yield

                def mm4(lhs, lk, rhs, rk):
                    b_, bk = bank()
                    for j in range(4):
                        S.op("pe", lambda e, b_=b_, j=j: e.matmul(b_[:, j * 128:(j + 1) * 128], lhs[:, j, :], rhs[:, j, :], start=True, stop=True),
                             reads=[lk, rk], writes=[bk])
                    return b_, bk

                def acc_dve(b_, bk, dst, dk, out=None, ok=None):
                    o_ = fl(dst) if out is None else out
                    S.op("dve", lambda e: e.tensor_tensor(out=o_, in0=fl(dst), in1=b_[:], op=ALU.add),
                         reads=[dk], writes=[bk, dk if ok is None else ok])

                tb_, tbk = bank()
                tbb = tb_[:].bitcast(BF16)
                for j in range(4):
                    S.op("pe", lambda e, j=j: e.transpose(tbb[:, j * 128:(j + 1) * 128], Xb[:, j, :], identb),
                         reads=[kX, "cstb"], writes=[tbk])
                S.op("act", lambda e: e.activation(out=fl(XTb), in_=tbb[:, 0:512], func=AF.Identity), writes=[tbk, kXT])
                S.op("dve", lambda e: e.tensor_tensor(out=Dvb[:], in0=Xb[:], in1=bcm(C_BD8), op=ALU.mult), reads=[kX, "cst"], writes=[kDvb])
                S.op("dve", lambda e: e.tensor_tensor(out=Dvb[:], in0=Dvb[:], in1=bcm(C_ID), op=ALU.add), reads=[kDvb, "cst"], writes=[kDvb])
                yield
                S.op("dve", lambda e: e.tensor_tensor(out=DvTb[:], in0=XTb[:], in1=bcm(C_BD8), op=ALU.mult), reads=[kXT, "cst"], writes=[kDvTb])
                S.op("dve", lambda e: e.tensor_tensor(out=DvTb[:], in0=DvTb[:], in1=bcm(C_ID), op=ALU.add), reads=[kDvTb, "cst"], writes=[kDvTb])
                S.op("dve", lambda e: e.tensor_tensor(
                    out=XMb[:], in0=XTb[:].unsqueeze(1).to_broadcast([128, 6, 4, 128]),
                    in1=cst[:, C_CMT:C_CMT + 6, :].unsqueeze(2).to_broadcast([128, 6, 4, 128]), op=ALU.mult),
                    reads=[kXT, "cst"], writes=[kXMb])
                yield
                for n, b in enumerate((2, 4, 8, 16, 32, 64)):
                    b_, bk = mm4(XMb[:, n, :, :], kXMb, Dvb, kDvb)
                    S.op("act", lambda e, b_=b_: e.activation(out=fl(Eb), in_=b_[:], func=AF.Identity), writes=[bk, kEb])
                    yield
                    b1, b1k = mm4(DvTb, kDvTb, Eb, kEb)
                    if b != 64:
                        b2, b2k = mm4(Eb, kEb, DvTb, kDvTb)
                        acc_dve(b1, b1k, Dvb, kDvb)
                        acc_dve(b2, b2k, DvTb, kDvTb)
                        yield
                    else:
                        acc_dve(b1, b1k, Dvb, kDvb, out=T2T[:, qs, :].rearrange("p h w -> p (h w)"), ok=("T2T", q))
                        yield
                kq = k_tok[:, 2 * q * 128:(2 * q + 2) * 128].rearrange("p (a w) -> p a w", a=2).unsqueeze(2).to_broadcast([128, 2, 2, 128])
                S.op("pool", lambda e: e.tensor_tensor(
                    out=ke[:, qs, :].rearrange("p (a b) w -> p a b w", a=2), in0=kq,
                    in1=e_sb[:, 32 + q * 4:36 + q * 4].rearrange("p (a b) -> p a b", a=2).unsqueeze(3).to_broadcast([128, 2, 2, 128]),
                    op=ALU.mult), reads=["k_tok", "e_sb"], writes=[("ke", q)])
                S.op("pool", lambda e: e.tensor_tensor(
                    out=kf[:, qs, :].rearrange("p (a b) w -> p a b w", a=2), in0=kq,
                    in1=f_sb[:, 32 + q * 4:36 + q * 4].rearrange("p (a b) -> p a b", a=2).unsqueeze(3).to_broadcast([128, 2, 2, 128]),
                    op=ALU.mult), reads=["k_tok", "f_sb"], writes=[("kf", q)])
                wp, wpk = bank()
                for j in range(4):
                    hv = q * 4 + j
                    S.op("pe", lambda e, wp=wp, j=j, hv=hv: e.matmul(wp[:, j * 128:(j + 1) * 128], ke[:, hv, :], T2T[:, hv, :],
                                                                     start=True, stop=True),
                         reads=[("ke", q), ("T2T", q)], writes=[wpk])
                S.op("act", lambda e, wp=wp: e.activation(out=nwT[:, qs, :].rearrange("p h w -> p (h w)"), in_=wp[:],
                                                        func=AF.Identity, scale=-1.0), writes=[wpk, ("nwT", q)])
                yield
                qs = slice(q * 4, (q + 1) * 4)
                vi = rr["v"] % 2
                rr["v"] += 1
                vp, vpk = bank()
                for j in range(4):
                    hv = q * 4 + j
                    S.op("pe", lambda e, vp=vp, j=j, hv=hv: e.matmul(vp[:, j * 128:(j + 1) * 128], T2T[:, hv, :],
                                                                     v_tok[:, hv * 128:(hv + 1) * 128], start=True, stop=False),
                         reads=[("T2T", q), "v_tok"], writes=[vpk])
                    S.op("pe", lambda e, vp=vp, j=j, hv=hv: e.matmul(vp[:, j * 128:(j + 1) * 128], nwT[:, hv, :], GSb[:, hv, :],
                                                                     start=False, stop=True),
                         reads=[("nwT", q), ("GSb", q)], writes=[vpk])
                for j in range(4):
                    hv = q * 4 + j
                    S.op("act", lambda e, vp=vp, j=j, hv=hv, vi=vi: e.activation(
                        out=vnew[vi][:, j, :], in_=vp[:, j * 128:(j + 1) * 128], func=AF.Identity, scale=sm[:, 32 + hv:33 + hv]),
                        reads=[smk], writes=[vpk, "vnew%d" % vi])
                yield
                oi, oik = bank()
                for j in range(4):
                    hv = q * 4 + j
                    hq = hv // 2
                    S.op("pe", lambda e, oi=oi, j=j, hv=hv, hq=hq: e.matmul(oi[:, j * 128:(j + 1) * 128], qkv[:, hq, :], GSb[:, hv, :],
                                                                               start=True, stop=True),
                         reads=[qkk, ("GSb", q)], writes=[oik])
                oa, oak = bank()
                for j in range(4):
                    hv = q * 4 + j
                    S.op("pe", lambda e, oa=oa, j=j, hv=hv, vi=vi: e.matmul(oa[:, j * 128:(j + 1) * 128], attnT[:, hv, :], vnew[vi][:, j, :],
                                                                           start=True, stop=True),
                         reads=[("attnT", q), "vnew%d" % vi], writes=[oak])
                sn, snk = bank()
                for j in range(4):
                    hv = q * 4 + j
                    S.op("pe", lambda e, sn=sn, j=j, hv=hv, vi=vi: e.matmul(sn[:, j * 128:(j + 1) * 128], kf[:, hv, :], vnew[vi][:, j, :],
                                                                           start=True, stop=True),
                         reads=[("kf", q), "vnew%d" % vi], writes=[snk])
                for j in range(4):
                    hv = q * 4 + j
                    S.op("act", lambda e, oi=oi, j=j, hv=hv, vi=vi: e.activation(
                        out=osb[vi][:, j, :], in_=oi[:, j * 128:(j + 1) * 128], func=AF.Identity, scale=e_sb[:, 32 + hv:33 + hv]),
                        reads=["e_sb"], writes=[oik, "osb%d" % vi])
                S.op("dve", lambda e, oa=oa, vi=vi: e.tensor_tensor(
                    out=osb[vi][:].rearrange("p h w -> p (h w)"), in0=osb[vi][:].rearrange("p h w -> p (h w)"), in1=oa[:], op=ALU.add),
                    reads=["osb%d" % vi], writes=[oak, "osb%d" % vi])
                for j in range(4):
                    hv = q * 4 + j
                    S.op("dve", lambda e, sn=sn, j=j, hv=hv: e.scalar_tensor_tensor(
                        out=GS[:, hv, :], in0=GS[:, hv, :], scalar=dA_sb[:, 32 + hv:33 + hv], in1=sn[:, j * 128:(j + 1) * 128],
                        op0=ALU.mult, op1=ALU.add), reads=[("GS", q), "dA_sb"], writes=[snk, ("GS", q)])
                S.op("act", lambda e, qs=qs: e.activation(out=GSb[:, qs, :], in_=GS[:, qs, :], func=AF.Identity),
                     reads=[("GS", q)], writes=[("GSb", q)])
                yield
                for j in range(4):
                    S.op("act", lambda e, j=j, vi=vi: e.activation(out=on[vi][:, j, :], in_=osb[vi][:, j, :], func=AF.Square,
                                                                 accum_out=ssq[vi][:, j:j + 1]),
                         reads=["osb%d" % vi], writes=["on%d" % vi, "ssq%d" % vi])
                S.op("act", lambda e, vi=vi: e.activation(out=ssq[vi][:], in_=ssq[vi][:], func=AF.Sqrt, bias=cst[:, C_EPS, 0:1], scale=1.0 / 128),
                     reads=["ssq%d" % vi, "cst"], writes=["ssq%d" % vi])
                yield
                S.op("dve", lambda e, vi=vi: e.reciprocal(out=ssq[vi][:], in_=ssq[vi][:]), reads=["ssq%d" % vi], writes=["ssq%d" % vi])
                for j in range(4):
                    hv = q * 4 + j
                    S.op("dve", lambda e, j=j, vi=vi: e.scalar_tensor_tensor(
                        out=on[vi][:, j, :], in0=osb[vi][:, j, :], scalar=ssq[vi][:, j:j + 1], in1=rows[:, R_NW2:R_NW2 + 128],
                        op0=ALU.mult, op1=ALU.mult), reads=["osb%d" % vi, "ssq%d" % vi, "rows"], writes=["on%d" % vi])
                S.op("pool", lambda e, vi=vi: e.tensor_tensor(
                    out=y[:, 2048 + q * 512:2048 + (q + 1) * 512], in0=on[vi][:].rearrange("p h w -> p (h w)"),
                    in1=z[:, 2048 + q * 512:2048 + (q + 1) * 512], op=ALU.mult), reads=["on%d" % vi, zk], writes=[yk])
                yield

            def ssd_group(g, sm=sm, smk=smk, xbc=xbc, xbk=xbk, z=z, zk=zk, y=y, yk=yk):
                gi = rr["g"] % 2
                rr["g"] += 1
                r = rr["am"] % 4
                rr["am"] += 1
                b2, b2k = bank()
                S.op("pe", lambda e: e.matmul(b2[:, 0:128], xbc[:, 16 + g, :], xbc[:, 24 + g, :], start=True, stop=True),
                     reads=[xbk], writes=[b2k])
                S.op("dve", lambda e: e.tensor_tensor(out=CBTm[gi][:], in0=b2[:, 0:128], in1=Umat, op=ALU.mult),
                     reads=["cst"], writes=[b2k, "CBTm%d" % gi])
                xs3 = v3(xs_tok[:, g * 256:(g + 1) * 256])
                S.op("dve", lambda e: e.tensor_tensor(
                    out=v3(xdt[gi][:]), in0=xs3, in1=bc(sm[:, g * 4:(g + 1) * 4], 4, 64), op=ALU.mult),
                    reads=["xs_tok", smk], writes=["xdt%d" % gi])
                S.op("dve", lambda e: e.tensor_tensor(
                    out=v3(xw[gi][:]), in0=v3(xdt[gi][:]), in1=bc(f_sb[:, g * 4:(g + 1) * 4], 4, 64), op=ALU.mult),
                    reads=["xdt%d" % gi, "f_sb"], writes=["xw%d" % gi])
                S.op("pool", lambda e: e.tensor_tensor(
                    out=v3(xsD[gi][:]), in0=xs3, in1=bc(rows[:, R_D1 + g * 4:R_D1 + (g + 1) * 4], 4, 64), op=ALU.mult),
                    reads=["xs_tok", "rows"], writes=["xsD%d" % gi])
                build_DT(48 + g * 4, r, sm, smk)
                yield
                S.op("dve", lambda e: e.tensor_tensor(out=MT[gi][:], in0=DT[r][:],
                                                      in1=CBTm[gi][:].unsqueeze(1).to_broadcast([128, 4, 128]), op=ALU.mult),
                     reads=["CBTm%d" % gi, "DT%d" % r], writes=["MT%d" % gi])
                yield
                b3, b3k = bank()
                for h in range(4):
                    S.op("pe", lambda e, h=h: e.matmul(b3[:, h * 64:(h + 1) * 64], MT[gi][:, h, :],
                                                       xdt[gi][:, h * 64:(h + 1) * 64], start=True, stop=True),
                         reads=["MT%d" % gi, "xdt%d" % gi], writes=[b3k])
                S.op("pe", lambda e: e.matmul(b3[:, 256:512], xbc[:, 24 + g, :], STb[:, g, :], start=True, stop=True),
                     reads=[xbk, ("STb", g)], writes=[b3k])
                b4, b4k = bank()
                S.op("pe", lambda e: e.matmul(b4[:, 256:512], b_tok[:, g * 128:(g + 1) * 128], xw[gi][:], start=True, stop=True),
                     reads=["b_tok", "xw%d" % gi], writes=[b4k])
                S.op("dve", lambda e: e.tensor_tensor(
                    out=v3(yacc[gi][:]), in0=v3(b3[:, 256:512]), in1=bc(e_sb[:, g * 4:(g + 1) * 4], 4, 64), op=ALU.mult),
                    reads=["e_sb"], writes=[b3k, "yacc%d" % gi])
                S.op("dve", lambda e: e.tensor_tensor(out=yacc[gi][:], in0=yacc[gi][:], in1=b3[:, 0:256], op=ALU.add),
                     reads=["yacc%d" % gi], writes=[b3k, "yacc%d" % gi])
                S.op("pool", lambda e: e.tensor_tensor(
                    out=v3(ST[:, g, :]), in0=v3(ST[:, g, :]), in1=bc(dA_sb[:, g * 4:(g + 1) * 4], 4, 64), op=ALU.mult),
                    reads=[("ST", g), "dA_sb"], writes=[("ST", g)])
                S.op("dve", lambda e: e.tensor_tensor(out=ST[:, g, :], in0=ST[:, g, :], in1=b4[:, 256:512], op=ALU.add),
                     reads=[("ST", g)], writes=[b4k, ("ST", g)])
                S.op("act", lambda e: e.activation(out=STb[:, g, :], in_=ST[:, g, :], func=AF.Identity),
                     reads=[("ST", g)], writes=[("STb", g)])
                yield
                S.op("pool", lambda e: e.tensor_tensor(out=yacc[gi][:], in0=yacc[gi][:], in1=xsD[gi][:], op=ALU.add),
                     reads=["yacc%d" % gi, "xsD%d" % gi], writes=["yacc%d" % gi])
                yield
                S.op("dve", lambda e: e.tensor_tensor(out=yz[gi][:], in0=yacc[gi][:], in1=z[:, g * 256:(g + 1) * 256], op=ALU.mult),
                     reads=["yacc%d" % gi, zk], writes=["yz%d" % gi])
                yield
                S.op("act", lambda e: e.activation(out=xsD[gi][:], in_=yz[gi][:], func=AF.Square, accum_out=ssqs[gi][:, 0:1]),
                     reads=["yz%d" % gi], writes=["xsD%d" % gi, "ssqs%d" % gi])
                S.op("act", lambda e: e.activation(out=ssqs[gi][:, 0:1], in_=ssqs[gi][:, 0:1], func=AF.Sqrt, bias=cst[:, C_EPS, 0:1], scale=1.0 / 256),
                     reads=["ssqs%d" % gi, "cst"], writes=["ssqs%d" % gi])
                yield
                S.op("dve", lambda e: e.reciprocal(out=ssqs[gi][:, 0:1], in_=ssqs[gi][:, 0:1]), reads=["ssqs%d" % gi], writes=["ssqs%d" % gi])
                S.op("dve", lambda e: e.scalar_tensor_tensor(
                    out=y[:, g * 256:(g + 1) * 256], in0=yz[gi][:], scalar=ssqs[gi][:, 0:1],
                    in1=rows[:, R_NW1 + g * 256:R_NW1 + (g + 1) * 256], op0=ALU.mult, op1=ALU.mult),
                    reads=["yz%d" % gi, "ssqs%d" % gi, "rows"], writes=[yk])
                yield

            pend_g = [gdn_quad(q, q % 2) for q in range(4)]
            pend_s = [ssd_group(g) for g in range(8)]
            live = []
            slots = {"g": 0, "s": 0}

            def refill():
                while slots["g"] < 2 and pend_g:
                    live.append(("g", pend_g.pop(0)))
                    slots["g"] += 1
                while slots["s"] < 2 and pend_s:
                    live.append(("s", pend_s.pop(0)))
                    slots["s"] += 1
            refill()
            while live:
                for item in list(live):
                    kind, g_ = item
                    try:
                        next(g_)
                    except StopIteration:
                        live.remove(item)
                        slots[kind] -= 1
                refill()

            S.dma(Y[tt * 128:(tt + 1) * 128, :], y[:], reads=[yk], writes=["Y"])
            if tt + 1 < NT:
                load_z(tt + 1)
        S.barrier()


def load_weight_bf16(nc, S, dst, dstk, src_v, nk, ncols, stg, stgk):
    cnt = 0
    kc = 2
    for k0 in range(0, nk, kc):
        for c0 in range(0, ncols, 512):
            s_ = stg[cnt % 2]
            sk = stgk[cnt % 2]
            cnt += 1
            S.dma(s_[:], src_v[:, k0:k0 + kc, c0:c0 + 512], writes=[sk])
            if cnt % 2 == 0:
                S.op("act", lambda e, s_=s_, k0=k0, c0=c0: e.activation(out=dst[:, k0:k0 + kc, c0:c0 + 512], in_=s_[:], func=AF.Identity),
                     reads=[sk], writes=[dstk])
            else:
                S.op("dve", lambda e, s_=s_, k0=k0, c0=c0: e.tensor_copy(out=dst[:, k0:k0 + kc, c0:c0 + 512], in_=s_[:]),
                     reads=[sk], writes=[dstk])


def phase4a(nc, S, T, Y, G_T, XT, X2T, w_su, w_gu, w_out, cst, cstb, pv):
    BW = 256
    NBW = T // BW
    NA = BW // 128
    identb = cstb[:, C_ID, :]
    onesD = cst[:, C_ONED, :]
    with ExitStack() as ps:
        lsb = lambda name, shape, dt=F32: ps.enter_context(nc.sbuf_tensor(name, shape, dt))
        Wsu = lsb("Wsu", [128, 16, D], BF16)
        Wgu = lsb("Wgu", [128, 16, D], BF16)
        Wo = lsb("Wo", [128, 8, D], BF16)
        stg = [lsb("stg%d" % i, [128, 2, 512]) for i in range(2)]
        ytoks = [lsb("ytok%d" % i, [128, NA, 4096], BF16) for i in range(2)]
        yT = lsb("yT", [128, 32, BW], BF16)
        gTs = [lsb("gT%d" % i, [128, 16, BW], BF16) for i in range(2)]
        xTs = [lsb("xT4_%d" % i, [128, 8, BW]) for i in range(2)]
        t1 = [lsb("m_t1_%d" % i, [128, BW]) for i in range(2)]
        t2 = [lsb("m_t2_%d" % i, [128, BW]) for i in range(2)]
        mT = lsb("mT", [128, 8, BW], BF16)
        m2s = lsb("m2s", [128, 8, BW])
        sqm = lsb("sqm", [128, 8, BW])
        rstd = lsb("rstd4", [128, BW])
        pu = [ps.enter_context(nc.psum_tensor("pu%d" % i, [128, 512], F32)) for i in range(4)]
        pt = [ps.enter_context(nc.psum_tensor("ptb%d" % i, [128, 1024], BF16)) for i in range(2)]
        pss = ps.enter_context(nc.psum_tensor("pss4", [128, 512], F32))
        load_weight_bf16(nc, S, Wsu, "Wsu", w_su.rearrange("(k p) f -> p k f", p=128), 16, D, stg, ["stg0", "stg1"])
        load_weight_bf16(nc, S, Wgu, "Wgu", w_gu.rearrange("(k p) f -> p k f", p=128), 16, D, stg, ["stg0", "stg1"])
        load_weight_bf16(nc, S, Wo, "Wo", w_out.rearrange("(k p) f -> p k f", p=128), 8, D, stg, ["stg0", "stg1"])
        GTv = G_T.rearrange("(b p) t -> p b t", p=128)
        XTv = XT.rearrange("(k p) t -> p k t", p=128)
        X2Tv = X2T.rearrange("(k p) t -> p k t", p=128)
        pc = 0
        def loads4(nb):
            t0 = nb * BW
            i = nb % 2
            S.dma(ytoks[i][:], Y[t0:t0 + BW, :].rearrange("(a p) f -> p a f", p=128), reads=["Y"], writes=["ytok%d" % i])
            S.dma(gTs[i][:], GTv[:, :, t0:t0 + BW], reads=["G_T"], writes=["gT%d" % i])
            S.dma(xTs[i][:], XTv[:, :, t0:t0 + BW], reads=["XT"], writes=["xT4_%d" % i])
        loads4(0)
        for nb in range(NBW):
            t0 = nb * BW
            ytok, gT, xT = ytoks[nb % 2], gTs[nb % 2], xTs[nb % 2]
            ytk, gTk, xTk = "ytok%d" % (nb % 2), "gT%d" % (nb % 2), "xT4_%d" % (nb % 2)
            if nb + 1 < NBW:
                loads4(nb + 1)
            for cb in range(32):
                p_ = pt[cb % 2]
                pk = "ptb%d" % (cb % 2)
                for a in range(NA):
                    S.op("pe", lambda e, p_=p_, a=a, cb=cb, ytok=ytok: e.transpose(p_[:, a * 128:(a + 1) * 128],
                                                                      ytok[:, a, cb * 128:(cb + 1) * 128], identb),
                         reads=[ytk, "cstb"], writes=[pk])
                if cb % 2 == 0:
                    S.op("act", lambda e, p_=p_, cb=cb: e.activation(out=yT[:, cb, :], in_=p_[:, 0:BW], func=AF.Identity),
                         reads=[pk], writes=["yT"])
                else:
                    S.op("dve", lambda e, p_=p_, cb=cb: e.tensor_copy(out=yT[:, cb, :], in_=p_[:, 0:BW]), reads=[pk], writes=["yT"])
            for blk in range(8):
                p1 = pu[pc % 4]
                p1k = "pu%d" % (pc % 4)
                pc += 1
                p2 = pu[pc % 4]
                p2k = "pu%d" % (pc % 4)
                pc += 1
                for k in range(16):
                    S.op("pe", lambda e, p1=p1, k=k, blk=blk: e.matmul(p1[:, 0:BW], Wsu[:, k, blk * 128:(blk + 1) * 128], yT[:, k, :],
                                                                       start=(k == 0), stop=(k == 15)),
                         reads=["Wsu", "yT"], writes=[p1k])
                for k in range(16):
                    S.op("pe", lambda e, p2=p2, k=k, blk=blk: e.matmul(p2[:, 0:BW], Wgu[:, k, blk * 128:(blk + 1) * 128], yT[:, 16 + k, :],
                                                                       start=(k == 0), stop=(k == 15)),
                         reads=["Wgu", "yT"], writes=[p2k])
                a1 = t1[blk % 2]
                a1k = "m_t1_%d" % (blk % 2)
                a2 = t2[blk % 2]
                a2k = "m_t2_%d" % (blk % 2)
                S.op("dve", lambda e, p1=p1, a1=a1, blk=blk, gT=gT: e.tensor_tensor(out=a1[:], in0=p1[:, 0:BW], in1=gT[:, blk, :], op=ALU.mult),
                     reads=[p1k, gTk], writes=[a1k])
                S.op("dve", lambda e, p2=p2, a2=a2, blk=blk, gT=gT: e.tensor_tensor(out=a2[:], in0=p2[:, 0:BW], in1=gT[:, 8 + blk, :], op=ALU.mult),
                     reads=[p2k, gTk], writes=[a2k])
                S.op("dve", lambda e, a1=a1, a2=a2, blk=blk: e.tensor_tensor(out=mT[:, blk, :], in0=a1[:], in1=a2[:], op=ALU.add),
                     reads=[a1k, a2k], writes=["mT"])
            for blk in range(8):
                p1 = pu[pc % 4]
                p1k = "pu%d" % (pc % 4)
                pc += 1
                for k in range(8):
                    S.op("pe", lambda e, p1=p1, k=k, blk=blk: e.matmul(p1[:, 0:BW], Wo[:, k, blk * 128:(blk + 1) * 128], mT[:, k, :],
                                                                       start=(k == 0), stop=(k == 7)),
                         reads=["Wo", "mT"], writes=[p1k])
                S.op("act", lambda e, p1=p1, blk=blk: e.activation(out=m2s[:, blk, :], in_=p1[:, 0:BW], func=AF.Identity),
                     reads=[p1k], writes=["m2s"])
                S.op("act", lambda e, p1=p1, blk=blk: e.activation(out=sqm[:, blk, :], in_=p1[:, 0:BW], func=AF.Square),
                     reads=[p1k], writes=["sqm"])
            for k in range(8):
                S.op("pe", lambda e, k=k: e.matmul(pss[:, 0:BW], onesD, sqm[:, k, :], start=(k == 0), stop=(k == 7)),
                     reads=["sqm", "cst"], writes=["pss4"])
            rsqrt_to(S, cst, rstd[:], "rstd4", pss[:, 0:BW], "pss4")
            for blk in range(8):
                a1 = t1[blk % 2]
                a1k = "m_t1_%d" % (blk % 2)
                S.op("dve", lambda e, a1=a1, blk=blk: e.tensor_tensor(out=a1[:], in0=m2s[:, blk, :], in1=rstd[:], op=ALU.mult),
                     reads=["m2s", "rstd4"], writes=[a1k])
                S.op("dve", lambda e, a1=a1, blk=blk, xT=xT: e.scalar_tensor_tensor(
                    out=xT[:, blk, :], in0=a1[:], scalar=pv[:, 2, blk:blk + 1], in1=xT[:, blk, :], op0=ALU.mult, op1=ALU.add),
                    reads=[a1k, "pv", xTk], writes=[xTk])
            S.dma(X2Tv[:, :, t0:t0 + BW], xT[:], reads=[xTk], writes=["X2T"])
        S.barrier()


def phase4b(nc, S, T, X2T, out, w_up, w_dn, cst, pv, normT):
    BW = 256
    NBW = T // BW
    NA = BW // 128
    ident = cst[:, C_ID, :]
    onesD = cst[:, C_ONED, :]
    with ExitStack() as ps:
        lsb = lambda name, shape, dt=F32: ps.enter_context(nc.sbuf_tensor(name, shape, dt))
        Wup = lsb("Wup", [128, 8, 4096], BF16)
        Wdn = lsb("Wdn", [128, 32, D], BF16)
        stg = [lsb("stgb%d" % i, [128, 2, 512]) for i in range(2)]
        xTs = [lsb("x5T%d" % i, [128, 8, BW]) for i in range(2)]
        sq = lsb("sq5", [128, 8, BW])
        rstd = lsb("rstd5", [128, BW])
        tmp = [lsb("tmp5_%d" % i, [128, BW]) for i in range(2)]
        h2T = lsb("h2T", [128, 8, BW], BF16)
        rl = [lsb("rl%d" % i, [128, BW]) for i in range(2)]
        actT = lsb("actT", [128, 32, BW], BF16)
        dns = lsb("dns", [128, 8, BW])
        otok = [lsb("otok%d" % i, [128, 512]) for i in range(2)]
        pu = [ps.enter_context(nc.psum_tensor("p5u%d" % i, [128, 512], F32)) for i in range(4)]
        pss = ps.enter_context(nc.psum_tensor("pss5", [128, 512], F32))
        po = [ps.enter_context(nc.psum_tensor("p5o%d" % i, [128, 512], F32)) for i in range(2)]
        load_weight_bf16(nc, S, Wup, "Wup", w_up.rearrange("(k p) f -> p k f", p=128), 8, 4096, stg, ["stgb0", "stgb1"])
        load_weight_bf16(nc, S, Wdn, "Wdn", w_dn.rearrange("(k p) f -> p k f", p=128), 32, D, stg, ["stgb0", "stgb1"])
        X2Tv = X2T.rearrange("(k p) t -> p k t", p=128)
        pc = 0
        oc = 0
        S.dma(xTs[0][:], X2Tv[:, :, 0:BW], reads=["X2T"], writes=["x5T0"])
        for nb in range(NBW):
            t0 = nb * BW
            xT = xTs[nb % 2]
            xk = "x5T%d" % (nb % 2)
            if nb + 1 < NBW:
                S.dma(xTs[(nb + 1) % 2][:], X2Tv[:, :, t0 + BW:t0 + 2 * BW], reads=["X2T"], writes=["x5T%d" % ((nb + 1) % 2)])
            normT(xT, xk, sq, "sq5", rstd, "rstd5", pss, "pss5", tmp, ["tmp5_0", "tmp5_1"],
                  lambda k: h2T[:, k, :], "h2T", 3, 4, BW)
            for blk in range(32):
                p1 = pu[pc % 4]
                p1k = "p5u%d" % (pc % 4)
                pc += 1
                for k in range(8):
                    S.op("pe", lambda e, p1=p1, k=k, blk=blk: e.matmul(p1[:, 0:BW], Wup[:, k, blk * 128:(blk + 1) * 128], h2T[:, k, :],
                                                                       start=(k == 0), stop=(k == 7)),
                         reads=["Wup", "h2T"], writes=[p1k])
                r_ = rl[blk % 2]
                rk = "rl%d" % (blk % 2)
                S.op("act", lambda e, p1=p1, r_=r_: e.activation(out=r_[:], in_=p1[:, 0:BW], func=AF.Relu), reads=[p1k], writes=[rk])
                eng = "pool" if blk % 4 == 0 else "dve"
                S.op(eng, lambda e, r_=r_, blk=blk: e.tensor_tensor(out=actT[:, blk, :], in0=r_[:], in1=r_[:], op=ALU.mult),
                     reads=[rk], writes=["actT"])
            for blk in range(8):
                p1 = pu[pc % 4]
                p1k = "p5u%d" % (pc % 4)
                pc += 1
                for k in range(32):
                    S.op("pe", lambda e, p1=p1, k=k, blk=blk: e.matmul(p1[:, 0:BW], Wdn[:, k, blk * 128:(blk + 1) * 128], actT[:, k, :],
                                                                       start=(k == 0), stop=(k == 31)),
                         reads=["Wdn", "actT"], writes=[p1k])
                S.op("act", lambda e, p1=p1, blk=blk: e.activation(out=dns[:, blk, :], in_=p1[:, 0:BW], func=AF.Identity),
                     reads=[p1k], writes=["dns"])
                S.op("act", lambda e, p1=p1, blk=blk: e.activation(out=sq[:, blk, :], in_=p1[:, 0:BW], func=AF.Square),
                     reads=[p1k], writes=["sq5"])
            for k in range(8):
                S.op("pe", lambda e, k=k: e.matmul(pss[:, 0:BW], onesD, sq[:, k, :], start=(k == 0), stop=(k == 7)),
                     reads=["sq5", "cst"], writes=["pss5"])
            rsqrt_to(S, cst, rstd[:], "rstd5", pss[:, 0:BW], "pss5")
            for blk in range(8):
                t_ = tmp[blk % 2]
                tk = "tmp5_%d" % (blk % 2)
                S.op("dve", lambda e, t_=t_, blk=blk: e.tensor_tensor(out=t_[:], in0=dns[:, blk, :], in1=rstd[:], op=ALU.mult),
                     reads=["dns", "rstd5"], writes=[tk])
                S.op("dve", lambda e, t_=t_, blk=blk, xT=xT: e.scalar_tensor_tensor(
                    out=dns[:, blk, :], in0=t_[:], scalar=pv[:, 5, blk:blk + 1], in1=xT[:, blk, :], op0=ALU.mult, op1=ALU.add),
                    reads=[tk, "pv", xk, "dns"], writes=["dns"])
            for a in range(NA):
                for half in range(2):
                    ot = otok[oc % 2]
                    otk = "otok%d" % (oc % 2)
                    oc += 1
                    p_ = po[half]
                    pk = "p5o%d" % half
                    for b4 in range(4):
                        blk = half * 4 + b4
                        S.op("pe", lambda e, p_=p_, b4=b4, blk=blk, a=a: e.transpose(
                            p_[:, b4 * 128:(b4 + 1) * 128], dns[:, blk, a * 128:(a + 1) * 128], ident),
                            reads=["dns", "cst"], writes=[pk])
                    if half == 0:
                        S.op("act", lambda e, p_=p_, ot=ot: e.activation(out=ot[:], in_=p_[:], func=AF.Identity),
                             reads=[pk], writes=[otk])
                    else:
                        S.op("dve", lambda e, p_=p_, ot=ot: e.tensor_copy(out=ot[:], in_=p_[:]), reads=[pk], writes=[otk])
                    S.dma(out[t0 + a * 128:t0 + (a + 1) * 128, half * 512:(half + 1) * 512], ot[:], reads=[otk], writes=["out"])
        S.barrier()


def host_inputs(inputs, b, T):
    f = lambda a: np.ascontiguousarray(np.asarray(a, dtype=np.float32))
    col = lambda v: f(np.asarray(v).reshape(-1, 128).T)
    nw = np.stack([col(inputs["norm_mix_pre"][0]), col(inputs["norm_mix_post"][0]),
                   col(inputs["norm_mlp_pre"][0]), col(inputs["norm_mlp_post"][0])], axis=1)
    cws = np.concatenate([np.asarray(inputs["ssm_conv_w"][0]), np.asarray(inputs["ssm_conv_b"])], axis=0)
    cws = cws.reshape(5, 32, 128).transpose(2, 1, 0)
    cwg = np.asarray(inputs["gdn_conv_w"][0]).reshape(4, 32, 128).transpose(2, 1, 0)
    rowv = np.concatenate([np.asarray(inputs["ssm_dt_bias"][0]), np.asarray(inputs["ssm_A_log"][0]), np.asarray(inputs["ssm_D"][0]),
                           np.asarray(inputs["gdn_dt_bias"][0]), np.asarray(inputs["gdn_A_log"][0]),
                           np.asarray(inputs["ssm_norm_w"][0]), np.asarray(inputs["gdn_norm_w"][0])])[None, :]
    return {
        "x": f(np.asarray(inputs["x"])[b, :T]),
        "c_col": col(np.asarray(inputs["c"])[b]),
        "w_ada": f(inputs["w_ada"][0]),
        "b_ada_col": col(inputs["b_ada"][0]),
        "nw_col": f(nw),
        "w_in": f(inputs["w_in"][0]),
        "cw_ssm": f(cws),
        "cw_gdn": f(cwg),
        "rowv": f(rowv),
        "w_su": f(inputs["w_ssm_up"][0]),
        "w_gu": f(inputs["w_gdn_up"][0]),
        "w_out": f(inputs["w_out"][0]),
        "w_up": f(inputs["w_mlp_up"][0]),
        "w_dn": f(inputs["w_mlp_down"][0]),
        "consts": make_consts(),
    }


def kernel(**inputs):
    T = 4096
    nc, S = build(T)
    shared = None
    in_maps = []
    for b in range(8):
        m = host_inputs(inputs, b, T)
        if shared is None:
            shared = m
        else:
            for k in m:
                if k not in ("x", "c_col"):
                    m[k] = shared[k]
        in_maps.append(m)
    res = run_bass_kernel_spmd(nc, in_maps, core_ids=list(range(8)))
    return np.stack([np.asarray(r["out"], dtype=np.float32) for r in res.results], axis=0)
```

```python
import numpy as np
import ml_dtypes
from contextlib import ExitStack
import concourse.bass as bass
import concourse.mybir as mybir
from concourse.bass_utils import run_bass_kernel_spmd

F32 = mybir.dt.float32
BF16 = mybir.dt.bfloat16
AF = mybir.ActivationFunctionType
ALU = mybir.AluOpType

D = 1024
EPS = 1e-6
COMPUTE = ("pe", "act", "dve", "pool")
NSLOT = 8


STRICT = False


class Sched:
    def __init__(self, nc, st):
        self.nc = nc
        self.st = st
        self.streams = {e: [] for e in ("pe", "act", "dve", "pool", "sp")}
        self.sem = {}
        for e in COMPUTE:
            self.sem[e] = st.enter_context(nc.semaphore("c_" + e))
        for i in range(NSLOT):
            self.sem[("sp", i)] = st.enter_context(nc.semaphore("d_sp%d" % i))
        self.count = {k: 0 for k in self.sem}
        self.dma_idx = 0
        self.known = {e: {} for e in self.streams}
        self.clock = {}
        self.last_write = {}
        self.readers = {}
        self.ninstr = 0
        self.nwaits = 0

    def _need(self, eng, ev, waits):
        c, n = ev
        if self.known[eng].get(c, 0) >= n:
            return
        if waits.get(c, 0) < n:
            waits[c] = n

    def _deps(self, eng, reads, writes):
        waits = {}
        for k in reads:
            ev = self.last_write.get(k)
            if ev is not None:
                if ev[0] == eng and eng == "pe":
                    continue
                self._need(eng, ev, waits)
        for k in writes:
            ev = self.last_write.get(k)
            if ev is not None and (STRICT and eng != "pe" or not (ev[0] == eng and eng in COMPUTE)):
                self._need(eng, ev, waits)
            for rv in self.readers.get(k, ()):
                if rv[0] == eng and eng in COMPUTE and not STRICT:
                    continue
                self._need(eng, rv, waits)
        return waits

    def _apply(self, eng, waits):
        kn = self.known[eng]
        for c, n in waits.items():
            ck = self.clock.get((c, n))
            if ck:
                for cc, nn in ck.items():
                    if kn.get(cc, 0) < nn:
                        kn[cc] = nn
            if kn.get(c, 0) < n:
                kn[c] = n

    def _record(self, ev, eng, reads, writes):
        ck = dict(self.known[eng])
        ck[ev[0]] = ev[1]
        self.clock[ev] = ck
        for k in reads:
            self.readers.setdefault(k, []).append(ev)
        for k in writes:
            self.last_write[k] = ev
            self.readers[k] = []

    def op(self, eng, fn, reads=(), writes=()):
        waits = self._deps(eng, reads, writes)
        self._apply(eng, waits)
        self.count[eng] += 1
        ev = (eng, self.count[eng])
        self._record(ev, eng, reads, writes)
        self.streams[eng].append((list(waits.items()), fn, (eng, 1)))
        self.ninstr += 1
        self.nwaits += len(waits)
        return ev

    def dma(self, out, in_, reads=(), writes=()):
        q = "sp"
        slot = (q, self.dma_idx % NSLOT)
        self.dma_idx += 1
        waits = self._deps(q, reads, writes)
        if self.count[slot] > 0:
            self._need(q, (slot, self.count[slot]), waits)
        self._apply(q, waits)
        self.count[slot] += 1
        ev = (slot, self.count[slot])
        self._record(ev, q, reads, writes)
        fn = lambda e, out=out, in_=in_: e.dma_start(out=out, in_=in_)
        self.streams[q].append((list(waits.items()), fn, (slot, 16)))
        self.ninstr += 1
        self.nwaits += len(waits)
        return ev

    def barrier(self):
        for eng in self.streams:
            waits = {}
            for c, n in self.count.items():
                if n > 0 and c != eng:
                    self._need(eng, (c, n), waits)
            self._apply(eng, waits)
            self.streams[eng].append((list(waits.items()), None, None))
        self.last_write = {}
        self.readers = {}

    def emit(self):
        nc = self.nc
        block = self.st.enter_context(nc.Block())
        sem = self.sem

        def run(stream):
            def body(e):
                for waits, fn, inc in stream:
                    for c, n in waits:
                        e.wait_ge(sem[c], n * (1 if c in COMPUTE else 16))
                    if fn is not None:
                        fn(e).then_inc(sem[inc[0]], inc[1])
            return body

        block.tensor(run(self.streams["pe"]))
        block.scalar(run(self.streams["act"]))
        block.vector(run(self.streams["dve"]))
        block.gpsimd(run(self.streams["pool"]))
        block.sync(run(self.streams["sp"]))


OFF_Z1 = 0
OFF_XBC = 2048
OFF_DT = 6144
OFF_QKV = 6176
OFF_Z2 = 10272
OFF_B = 12320
OFF_A = 12336
OFF_GS = 12352
OFF_GG = 13376

C_ID, C_U, C_GT, C_SU, C_ONE, C_ONED, C_EPS, C_BD8, C_CMT, NCONST = 0, 1, 2, 3, 4, 5, 6, 7, 8, 14
R_DTB1, R_AL1, R_D1, R_DTB2, R_AL2, R_NW1, R_NW2, RLEN = 0, 32, 64, 96, 112, 128, 2176, 2304


def make_consts():
    k = np.arange(128)[:, None]
    l = np.arange(128)[None, :]
    c = np.zeros((128, NCONST, 128), np.float32)
    c[:, C_ID] = (k == l)
    c[:, C_U] = (k <= l)
    c[:, C_GT] = (k > l)
    c[:, C_SU] = (l > k)
    c[:, C_ONE] = 1.0
    c[:, C_ONED] = 1.0 / D
    c[:, C_EPS] = EPS
    c[:, C_BD8] = (k // 2 == l // 2)
    for n, b in enumerate((2, 4, 8, 16, 32, 64)):
        cm = ((k // (2 * b) == l // (2 * b)) & ((k // b) % 2 == 0) & ((l // b) % 2 == 1))
        c[:, C_CMT + n] = cm.T
    return c


def rsqrt_to(S, cst, dst, dstk, src, srck, scale=1.0):
    S.op("act", lambda e: e.activation(out=dst, in_=src, func=AF.Sqrt, bias=cst[:, C_EPS, 0:1], scale=scale),
         reads=[srck, "cst"], writes=[dstk])
    S.op("dve", lambda e: e.reciprocal(out=dst, in_=dst), reads=[dstk], writes=[dstk])


def build(T, phases=(0, 1, 2, 3, 4, 5), debug=False):
    NT = T // 128
    NB = T // 512
    nc = bass.Bass("TRN2", target_bir_lowering=False)

    def din(name, shape, dt=F32):
        return nc.dram_tensor(name, shape, dt, kind="ExternalInput").ap()

    def dscr(name, shape, dt):
        return nc.dram_tensor(name, shape, dt, kind="ExternalOutput").ap()

    x = din("x", [T, D])
    c_col = din("c_col", [128, 8])
    w_ada = din("w_ada", [D, 6 * D])
    b_ada_col = din("b_ada_col", [128, 48])
    nw_col = din("nw_col", [128, 4, 8])
    w_in = din("w_in", [D, 14400])
    cw_ssm = din("cw_ssm", [128, 32, 5])
    cw_gdn = din("cw_gdn", [128, 32, 4])
    rowv = din("rowv", [1, RLEN])
    w_su = din("w_su", [2048, D])
    w_gu = din("w_gu", [2048, D])
    w_out = din("w_out", [D, D])
    w_up = din("w_up", [D, 4096])
    w_dn = din("w_dn", [4096, D])
    consts = din("consts", [128, NCONST, 128])
    out = nc.dram_tensor("out", [T, D], F32, kind="ExternalOutput").ap()

    XT = dscr("s_xt", [D, T], F32)
    XBC_T = dscr("s_xbct", [4096, T], BF16)
    QKV_T = dscr("s_qkvt", [4096, T], BF16)
    G_T = dscr("s_gt", [2048, T], BF16)
    Z = dscr("s_z", [T, 4096], BF16)
    SM = dscr("s_sm", [T, 96], F32)
    if debug:
        Y = dscr("s_y", [T, 4096], BF16)
        X2T = dscr("s_x2t", [D, T], F32)
        MOD = dscr("s_mod", [128, 48], F32)
    else:
        Y = Z
        X2T = XT
        MOD = None

    with ExitStack() as st:
        S = Sched(nc, st)
        sb = lambda name, shape, dt=F32: st.enter_context(nc.sbuf_tensor(name, shape, dt))
        cst = sb("cst", [128, NCONST, 128])
        cstb = sb("cstb", [128, 1, 128], BF16)
        pv = sb("pv", [128, 6, 8])

        S.dma(cst[:], consts, writes=["cst"])
        S.op("pool", lambda e: e.tensor_copy(out=cstb[:], in_=cst[:, 0:1, :]), reads=["cst"], writes=["cstb"])
        ident = cst[:, C_ID, :]
        identb = cstb[:, C_ID, :]
        Umat = cst[:, C_U, :]
        GTm = cst[:, C_GT, :]
        SUm = cst[:, C_SU, :]
        ones = cst[:, C_ONE, :]
        onesD = cst[:, C_ONED, :]
        if 0 in phases:
            with ExitStack() as ps:
                lsb = lambda name, shape, dt=F32: ps.enter_context(nc.sbuf_tensor(name, shape, dt))
                cact = lsb("cact", [128, 8])
                csig = lsb("csig", [128, 8])
                wa = [lsb("wa%d" % i, [128, 8, 512]) for i in range(2)]
                modsb = lsb("modsb", [128, 48])
                bada = lsb("bada", [128, 48])
                nwc = lsb("nwc", [128, 4, 8])
                modps = ps.enter_context(nc.psum_tensor("modps", [128, 512], F32))
                S.dma(cact[:], c_col, writes=["cact"])
                S.dma(bada[:], b_ada_col, writes=["bada"])
                S.dma(nwc[:], nw_col, writes=["nwc"])
                S.op("act", lambda e: e.activation(out=csig[:], in_=cact[:], func=AF.Sigmoid), reads=["cact"], writes=["csig"])
                S.op("dve", lambda e: e.tensor_tensor(out=cact[:], in0=cact[:], in1=csig[:], op=ALU.mult),
                     reads=["cact", "csig"], writes=["cact"])
                wav = w_ada.rearrange("(k p) f -> p k f", p=128)
                for fb in range(12):
                    w = wa[fb % 2]
                    wk = "wa%d" % (fb % 2)
                    S.dma(w[:], wav[:, :, fb * 512:(fb + 1) * 512], writes=[wk])
                    for j in range(4):
                        col = fb * 4 + j
                        for k in range(8):
                            S.op("pe", lambda e, w=w, j=j, k=k, col=col: e.matmul(
                                modps[:, col:col + 1], w[:, k, j * 128:(j + 1) * 128], cact[:, k:k + 1],
                                start=(k == 0), stop=(k == 7)), reads=[wk, "cact"], writes=["modps"])
                S.op("dve", lambda e: e.tensor_tensor(out=modsb[:], in0=modps[:, 0:48], in1=bada[:], op=ALU.add),
                     reads=["modps", "bada"], writes=["modsb"])
                S.op("dve", lambda e: e.scalar_tensor_tensor(out=pv[:, 0, :], in0=modsb[:, 8:16], scalar=1.0, in1=nwc[:, 0, :],
                                                             op0=ALU.add, op1=ALU.mult), reads=["modsb", "nwc"], writes=["pv"])
                S.op("dve", lambda e: e.tensor_copy(out=pv[:, 1, :], in_=modsb[:, 0:8]), reads=["modsb"], writes=["pv"])
                S.op("dve", lambda e: e.tensor_tensor(out=pv[:, 2, :], in0=modsb[:, 16:24], in1=nwc[:, 1, :], op=ALU.mult),
                     reads=["modsb", "nwc"], writes=["pv"])
                S.op("dve", lambda e: e.scalar_tensor_tensor(out=pv[:, 3, :], in0=modsb[:, 32:40], scalar=1.0, in1=nwc[:, 2, :],
                                                             op0=ALU.add, op1=ALU.mult), reads=["modsb", "nwc"], writes=["pv"])
                S.op("dve", lambda e: e.tensor_copy(out=pv[:, 4, :], in_=modsb[:, 24:32]), reads=["modsb"], writes=["pv"])
                S.op("dve", lambda e: e.tensor_tensor(out=pv[:, 5, :], in0=modsb[:, 40:48], in1=nwc[:, 3, :], op=ALU.mult),
                     reads=["modsb", "nwc"], writes=["pv"])
                if debug:
                    S.dma(MOD, modsb[:], reads=["modsb"], writes=["MOD"])
                S.barrier()

        def normT(xT, xk, sq, sqk, rstd, rstdk, ssps, sspsk, tmp, tmpk, hdst, hk, ia, ish, W=512):
            S.op("act", lambda e: e.activation(out=sq[:], in_=xT[:], func=AF.Square), reads=[xk], writes=[sqk])
            for k in range(8):
                S.op("pe", lambda e, k=k: e.matmul(ssps[:, 0:W], onesD, sq[:, k, :], start=(k == 0), stop=(k == 7)),
                     reads=[sqk, "cst"], writes=[sspsk])
            rsqrt_to(S, cst, rstd[:], rstdk, ssps[:, 0:W], sspsk)
            for k in range(8):
                t = tmp[k % 2]
                tk = tmpk[k % 2]
                S.op("dve", lambda e, k=k, t=t: e.tensor_tensor(out=t[:], in0=xT[:, k, :], in1=rstd[:], op=ALU.mult),
                     reads=[xk, rstdk], writes=[tk])
                S.op("act", lambda e, k=k, t=t: e.activation(out=hdst(k), in_=t[:], func=AF.Identity,
                                                           bias=pv[:, ish, k:k + 1], scale=pv[:, ia, k:k + 1]),
                     reads=[tk, "pv"], writes=[hk])

        hT_cm = None
        hst = ExitStack()
        if 1 in phases or 2 in phases:
            hT_cm = hst.enter_context(nc.sbuf_tensor("hT", [128, 8, T], BF16))

        if 1 in phases:
            with ExitStack() as ps:
                lsb = lambda name, shape, dt=F32: ps.enter_context(nc.sbuf_tensor(name, shape, dt))
                xtok = [lsb("xtok%d" % i, [128, 4, D]) for i in range(2)]
                xTb = [lsb("xTb%d" % i, [128, 8, 512]) for i in range(2)]
                sq = lsb("sq1", [128, 8, 512])
                rstd = lsb("rstd1", [128, 512])
                tmp = [lsb("tmp1_%d" % i, [128, 512]) for i in range(2)]
                tps = [ps.enter_context(nc.psum_tensor("tps%d" % i, [128, 512], F32)) for i in range(4)]
                ssps = ps.enter_context(nc.psum_tensor("ssps1", [128, 512], F32))
                XTv = XT.rearrange("(k p) t -> p k t", p=128)
                for nb in range(NB):
                    xt = xtok[nb % 2]
                    xtk = "xtok%d" % (nb % 2)
                    xT = xTb[nb % 2]
                    xTk = "xTb%d" % (nb % 2)
                    S.dma(xt[:], x[nb * 512:(nb + 1) * 512, :].rearrange("(a p) f -> p a f", p=128), writes=[xtk])
                    for k in range(8):
                        tp = tps[k % 4]
                        tpk = "tps%d" % (k % 4)
                        for a in range(4):
                            S.op("pe", lambda e, tp=tp, a=a, k=k, xt=xt: e.transpose(
                                tp[:, a * 128:(a + 1) * 128], xt[:, a, k * 128:(k + 1) * 128], ident),
                                reads=[xtk, "cst"], writes=[tpk])
                        if k % 2 == 0:
                            S.op("act", lambda e, tp=tp, k=k, xT=xT: e.activation(out=xT[:, k, :], in_=tp[:], func=AF.Identity),
                                 reads=[tpk], writes=[xTk])
                        else:
                            S.op("dve", lambda e, tp=tp, k=k, xT=xT: e.tensor_copy(out=xT[:, k, :], in_=tp[:]),
                                 reads=[tpk], writes=[xTk])
                    S.dma(XTv[:, :, nb * 512:(nb + 1) * 512], xT[:], reads=[xTk], writes=["XT"])
                    normT(xT, xTk, sq, "sq1", rstd, "rstd1", ssps, "ssps1", tmp, ["tmp1_0", "tmp1_1"],
                          lambda k, nb=nb: hT_cm[:, k, nb * 512:(nb + 1) * 512], ("hT", nb), 0, 1)
                S.barrier()

        if 2 in phases:
            hkeys = [("hT", nb) for nb in range(NB)]
            with ExitStack() as ps:
                lsb = lambda name, shape, dt=F32: ps.enter_context(nc.sbuf_tensor(name, shape, dt))
                wst = [lsb("wst%d" % i, [128, 8, 128]) for i in range(2)]
                wbf = [lsb("wbf%d" % i, [128, 8, 128], BF16) for i in range(2)]
                pc = [lsb("pc%d" % i, [128, T + 3]) for i in range(2)]
                accs = [lsb("acc%d" % i, [128, T]) for i in range(2)]
                sq2 = lsb("sq2", [128, T])
                rs = lsb("rs", [128, T])
                ob = [lsb("ob%d" % i, [128, T], BF16) for i in range(2)]
                cws = lsb("cws", [128, 32, 5])
                cwg = lsb("cwg", [128, 32, 4])
                pps = [ps.enter_context(nc.psum_tensor("pps%d" % i, [128, 512], F32)) for i in range(4)]
                sps = [ps.enter_context(nc.psum_tensor("sps%d" % i, [128, 512], F32)) for i in range(2)]
                S.dma(cws[:], cw_ssm, writes=["cws"])
                S.dma(cwg[:], cw_gdn, writes=["cwg"])
                for i in range(2):
                    S.op("pool", lambda e, i=i: e.memset(pc[i][:, 0:3], 0.0), writes=["pc%d" % i])
                w_in_v = w_in.rearrange("(k p) f -> p k f", p=128)
                XBCv = XBC_T.rearrange("(b p) t -> b p t", p=128)
                QKVv = QKV_T.rearrange("(b p) t -> b p t", p=128)
                GTv = G_T.rearrange("(b p) t -> b p t", p=128)
                blocks = []
                for cb in range(32):
                    blocks.append(("xbc", cb, OFF_XBC + cb * 128))
                for cb in range(32):
                    blocks.append(("qkv", cb, OFF_QKV + cb * 128))
                for cb in range(8):
                    blocks.append(("gate", cb, OFF_GS + cb * 128))
                for cb in range(8):
                    blocks.append(("gate", 8 + cb, OFF_GG + cb * 128))
                pcount = 0
                pcnt = [0]

                def bufs(bi):
                    return (accs[bi % 2], "acc%d" % (bi % 2), pc[bi % 2], "pc%d" % (bi % 2), ob[bi % 2], "ob%d" % (bi % 2),
                            wbf[bi % 2], "wbf%d" % (bi % 2))

                def wload(bi):
                    off = blocks[bi][2]
                    ws, wsk, wb, wbk = wst[bi % 2], "wst%d" % (bi % 2), wbf[bi % 2], "wbf%d" % (bi % 2)
                    S.dma(ws[:], w_in_v[:, :, off:off + 128], writes=[wsk])
                    S.op("pool", lambda e: e.tensor_copy(out=wb[:], in_=ws[:]), reads=[wsk], writes=[wbk])

                def front(bi):
                    kind, cb, off = blocks[bi]
                    acc, acck, p_, pk, o_, ok, wb, wbk = bufs(bi)
                    if bi + 1 < len(blocks):
                        wload(bi + 1)
                    for tb in range(NB):
                        pp = pps[pcnt[0] % 4]
                        ppk = "pps%d" % (pcnt[0] % 4)
                        pcnt[0] += 1
                        for k in range(8):
                            S.op("pe", lambda e, pp=pp, k=k, tb=tb: e.matmul(
                                pp[:], wb[:, k, :], hT_cm[:, k, tb * 512:(tb + 1) * 512], start=(k == 0), stop=(k == 7)),
                                reads=[wbk, hkeys[tb]], writes=[ppk])
                        if kind == "gate":
                            S.op("act", lambda e, pp=pp, tb=tb: e.activation(
                                out=o_[:, tb * 512:(tb + 1) * 512], in_=pp[:], func=AF.Sigmoid), reads=[ppk], writes=[ok])
                        else:
                            S.op("act", lambda e, pp=pp, tb=tb: e.activation(
                                out=p_[:, 3 + tb * 512:3 + (tb + 1) * 512], in_=pp[:], func=AF.Identity), reads=[ppk], writes=[pk])
                            if kind == "xbc":
                                S.op("act", lambda e, pp=pp, tb=tb: e.activation(
                                    out=acc[:, tb * 512:(tb + 1) * 512], in_=pp[:], func=AF.Identity,
                                    bias=cws[:, cb, 4:5], scale=cws[:, cb, 3:4]), reads=[ppk, "cws"], writes=[acck])
                            else:
                                S.op("act", lambda e, pp=pp, tb=tb: e.activation(
                                    out=acc[:, tb * 512:(tb + 1) * 512], in_=pp[:], func=AF.Identity,
                                    scale=cwg[:, cb, 3:4]), reads=[ppk, "cwg"], writes=[acck])

                def conv(bi):
                    kind, cb, off = blocks[bi]
                    if kind == "gate":
                        return
                    acc, acck, p_, pk, o_, ok, wb, wbk = bufs(bi)
                    cwt = cws if kind == "xbc" else cwg
                    cwk = "cws" if kind == "xbc" else "cwg"
                    for j in range(1, 4):
                        S.op("dve", lambda e, j=j: e.scalar_tensor_tensor(
                            out=acc[:], in0=p_[:, 3 - j:3 - j + T], scalar=cwt[:, cb, 3 - j:4 - j], in1=acc[:],
                            op0=ALU.mult, op1=ALU.add), reads=[pk, cwk, acck], writes=[acck])

                def post(bi):
                    kind, cb, off = blocks[bi]
                    acc, acck, p_, pk, o_, ok, wb, wbk = bufs(bi)
                    if kind == "gate":
                        S.dma(GTv[cb], o_[:], reads=[ok], writes=["G_T"])
                        return
                    if kind == "qkv" and cb < 16:
                        S.op("act", lambda e: e.activation(out=acc[:], in_=acc[:], func=AF.Silu), reads=[acck], writes=[acck])
                        S.op("act", lambda e: e.activation(out=sq2[:], in_=acc[:], func=AF.Square), reads=[acck], writes=["sq2"])
                        for tb in range(NB):
                            sp = sps[tb % 2]
                            spk = "sps%d" % (tb % 2)
                            S.op("pe", lambda e, sp=sp, tb=tb: e.matmul(sp[:], ones, sq2[:, tb * 512:(tb + 1) * 512],
                                                                        start=True, stop=True),
                                 reads=["sq2", "cst"], writes=[spk])
                            rsqrt_to(S, cst, rs[:, tb * 512:(tb + 1) * 512], "rs", sp[:], spk)
                        qs = (128.0 ** -0.5) if cb < 8 else 1.0
                        S.op("dve", lambda e: e.scalar_tensor_tensor(
                            out=o_[:], in0=acc[:], scalar=qs, in1=rs[:], op0=ALU.mult, op1=ALU.mult),
                            reads=[acck, "rs"], writes=[ok])
                    else:
                        S.op("act", lambda e: e.activation(out=o_[:], in_=acc[:], func=AF.Silu), reads=[acck], writes=[ok])
                    dst = XBCv[cb] if kind == "xbc" else QKVv[cb]
                    S.dma(dst, o_[:], reads=[ok], writes=[kind + "_T"])

                wload(0)
                for bi in range(len(blocks)):
                    front(bi)
                    if bi > 0:
                        post(bi - 1)
                    conv(bi)
                post(len(blocks) - 1)
                S.barrier()

            with ExitStack() as ps:
                lsb = lambda name, shape, dt=F32: ps.enter_context(nc.sbuf_tensor(name, shape, dt))
                wzs = [lsb("wzs%d" % i, [128, 8, 512]) for i in range(2)]
                wzb = [lsb("wzb%d" % i, [128, 8, 512], BF16) for i in range(2)]
                zb = [lsb("zb%d" % i, [128, 512], BF16) for i in range(3)]
                wss = lsb("wss", [128, 8, 64])
                wsb = lsb("wsb", [128, 8, 64], BF16)
                smt = [lsb("smt%d" % i, [128, 96]) for i in range(2)]
                t1 = [lsb("t1_%d" % i, [128, 48]) for i in range(2)]
                rows = lsb("rows2", [128, RLEN])
                arow = lsb("arow", [128, 48])
                S.dma(rows[:], rowv.partition_broadcast(128), writes=["rows"])
                S.op("act", lambda e: e.activation(out=arow[:, 0:32], in_=rows[:, R_AL1:R_AL1 + 32], func=AF.Exp),
                     reads=["rows"], writes=["arow"])
                S.op("act", lambda e: e.activation(out=arow[:, 32:48], in_=rows[:, R_AL2:R_AL2 + 16], func=AF.Exp),
                     reads=["rows"], writes=["arow"])
                S.op("dve", lambda e: e.tensor_scalar(out=arow[:], in0=arow[:], scalar1=-1.0, scalar2=None, op0=ALU.mult),
                     reads=["arow"], writes=["arow"])
                pps = [ps.enter_context(nc.psum_tensor("zps%d" % i, [128, 512], F32)) for i in range(4)]
                sps = [ps.enter_context(nc.psum_tensor("smps%d" % i, [128, 512], F32)) for i in range(2)]
                w_in_v = w_in.rearrange("(k p) f -> p k f", p=128)
                pcount = 0
                for blk in range(8):
                    off = (OFF_Z1 + blk * 512) if blk < 4 else (OFF_Z2 + (blk - 4) * 512)
                    ws = wzs[blk % 2]
                    wsk = "wzs%d" % (blk % 2)
                    wb = wzb[blk % 2]
                    wbk = "wzb%d" % (blk % 2)
                    if blk == 0:
                        S.dma(ws[:], w_in_v[:, :, off:off + 512], writes=[wsk])
                    if blk + 1 < 8:
                        noff = (OFF_Z1 + (blk + 1) * 512) if blk + 1 < 4 else (OFF_Z2 + (blk + 1 - 4) * 512)
                        S.dma(wzs[(blk + 1) % 2][:], w_in_v[:, :, noff:noff + 512], writes=["wzs%d" % ((blk + 1) % 2)])
                    S.op("act", lambda e, ws=ws, wb=wb: e.activation(out=wb[:], in_=ws[:], func=AF.Identity), reads=[wsk], writes=[wbk])
                    for tt in range(NT):
                        pp = pps[pcount % 4]
                        ppk = "zps%d" % (pcount % 4)
                        z_ = zb[pcount % 3]
                        zk = "zb%d" % (pcount % 3)
                        pcount += 1
                        for k in range(8):
                            S.op("pe", lambda e, pp=pp, wb=wb, k=k, tt=tt: e.matmul(
                                pp[:], hT_cm[:, k, tt * 128:(tt + 1) * 128], wb[:, k, :], start=(k == 0), stop=(k == 7)),
                                reads=[wbk, hkeys[tt // 4]], writes=[ppk])
                        S.op("act", lambda e, pp=pp, z_=z_: e.activation(out=z_[:], in_=pp[:], func=AF.Silu), reads=[ppk], writes=[zk])
                        S.dma(Z[tt * 128:(tt + 1) * 128, blk * 512:(blk + 1) * 512], z_[:], reads=[zk], writes=["Z"])
                S.dma(wss[:, :, 0:32], w_in_v[:, :, OFF_DT:OFF_DT + 32], writes=["wss"])
                S.dma(wss[:, :, 32:64], w_in_v[:, :, OFF_B:OFF_B + 32], writes=["wss"])
                S.op("pool", lambda e: e.tensor_copy(out=wsb[:], in_=wss[:]), reads=["wss"], writes=["wsb"])
                for tt in range(NT):
                    sp = sps[tt % 2]
                    spk = "smps%d" % (tt % 2)
                    sm_ = smt[tt % 2]
                    smk = "smt%d" % (tt % 2)
                    t_ = t1[tt % 2]
                    tk = "t1_%d" % (tt % 2)
                    for k in range(8):
                        S.op("pe", lambda e, sp=sp, k=k, tt=tt: e.matmul(
                            sp[:, 0:64], hT_cm[:, k, tt * 128:(tt + 1) * 128], wsb[:, k, :], start=(k == 0), stop=(k == 7)),
                            reads=["wsb", hkeys[tt // 4]], writes=[spk])
                    S.op("dve", lambda e, sp=sp, t_=t_: e.tensor_tensor(out=t_[:, 0:32], in0=sp[:, 0:32],
                                                                        in1=rows[:, R_DTB1:R_DTB1 + 32], op=ALU.add),
                         reads=["rows"], writes=[spk, tk])
                    S.op("dve", lambda e, sp=sp, t_=t_: e.tensor_tensor(out=t_[:, 32:48], in0=sp[:, 48:64],
                                                                        in1=rows[:, R_DTB2:R_DTB2 + 16], op=ALU.add),
                         reads=["rows"], writes=[spk, tk])
                    S.op("act", lambda e, t_=t_: e.activation(out=t_[:], in_=t_[:], func=AF.Exp), reads=[tk], writes=[tk])
                    S.op("dve", lambda e, t_=t_: e.tensor_scalar(out=t_[:], in0=t_[:], scalar1=1.0, scalar2=None, op0=ALU.add),
                         reads=[tk], writes=[tk])
                    S.op("act", lambda e, t_=t_: e.activation(out=t_[:], in_=t_[:], func=AF.Ln), reads=[tk], writes=[tk])
                    S.op("act", lambda e, sp=sp, sm_=sm_: e.activation(out=sm_[:, 32:48], in_=sp[:, 32:48], func=AF.Sigmoid),
                         writes=[spk, smk])
                    S.op("dve", lambda e, t_=t_, sm_=sm_: e.tensor_copy(out=sm_[:, 0:32], in_=t_[:, 0:32]), reads=[tk], writes=[smk])
                    S.op("dve", lambda e, t_=t_, sm_=sm_: e.tensor_tensor(out=sm_[:, 48:96], in0=t_[:], in1=arow[:], op=ALU.mult),
                         reads=[tk, "arow"], writes=[smk])
                    S.dma(SM[tt * 128:(tt + 1) * 128, :], sm_[:], reads=[smk], writes=["SM"])
                S.barrier()

        hst.close()
        if 3 in phases:
            phase3(nc, S, st, T, XBC_T, QKV_T, Z, SM, Y, cst, cstb, rowv)
        if 4 in phases:
            phase4a(nc, S, T, Y, G_T, XT, X2T, w_su, w_gu, w_out, cst, cstb, pv)
        if 5 in phases:
            phase4b(nc, S, T, X2T, out, w_up, w_dn, cst, pv, normT)
        else:
            pass
        S.barrier()
        S.emit()
    return nc, S


def phase3(nc, S, st_outer, T, XBC_T, QKV_T, Z, SM, Y, cst, cstb, rowv):
    NT = T // 128
    ident = cst[:, C_ID, :]
    identb = cstb[:, C_ID, :]
    Umat = cst[:, C_U, :]
    GTm = cst[:, C_GT, :]
    SUm = cst[:, C_SU, :]
    ones = cst[:, C_ONE, :]
    with ExitStack() as ps:
        lsb = lambda name, shape, dt=F32: ps.enter_context(nc.sbuf_tensor(name, shape, dt))
        smt = [lsb("p3sm%d" % i, [128, 96]) for i in range(2)]
        rows = lsb("rows3", [128, RLEN])
        S.dma(rows[:], rowv.partition_broadcast(128), writes=["rows"])
        xbct = [lsb("p3xbc%d" % i, [128, 32, 128], BF16) for i in range(2)]
        qkvt = [lsb("p3qkv%d" % i, [128, 32, 128], BF16) for i in range(2)]
        zt = [lsb("p3z%d" % i, [128, 4096], BF16) for i in range(1)]
        ytile = lsb("p3y", [128, 4096], BF16)
        xs_tok = lsb("xs_tok", [128, 2048], BF16)
        b_tok = lsb("b_tok", [128, 1024], BF16)
        k_tok = lsb("k_tok", [128, 1024], BF16)
        v_tok = lsb("v_tok", [128, 2048], BF16)
        c_sb = lsb("c_sb", [128, 48])
        e_sb = lsb("e_sb", [128, 48])
        f_sb = lsb("f_sb", [128, 48])
        dA_sb = lsb("dA_sb", [128, 48])
        nbeta = lsb("nbeta", [128, 16])
        ST = lsb("ST", [128, 8, 256])
        STb = lsb("STb", [128, 8, 256], BF16)
        GS = lsb("GS", [128, 16, 128])
        GSb = lsb("GSb", [128, 16, 128], BF16)
        IB = [[lsb("ibb%d_%d" % (s_, n_), [128, 4, 128], BF16) for n_ in range(5)]
              + [lsb("ibm%d" % s_, [128, 6, 4, 128], BF16)] for s_ in range(2)]
        attnT = lsb("attnT", [128, 16, 128], BF16)
        T2T = lsb("T2T", [128, 16, 128], BF16)
        ke = lsb("ke", [128, 16, 128], BF16)
        kf = lsb("kf", [128, 16, 128], BF16)
        nwT = lsb("nwT", [128, 16, 128], BF16)
        KKm = lsb("KKm", [128, 8, 128])
        QKm = lsb("QKm", [128, 8, 128])
        Am = [lsb("Am%d" % i, [128, 4, 128]) for i in range(2)]
        DT = [lsb("DT%d" % i, [128, 4, 128]) for i in range(4)]
        MT = [lsb("MT%d" % i, [128, 4, 128], BF16) for i in range(2)]
        CBTm = [lsb("CBTm%d" % i, [128, 128]) for i in range(2)]
        xdt = [lsb("xdt%d" % i, [128, 256], BF16) for i in range(2)]
        xw = [lsb("xw%d" % i, [128, 256], BF16) for i in range(2)]
        xsD = [lsb("xsD%d" % i, [128, 256]) for i in range(2)]
        yacc = [lsb("yacc%d" % i, [128, 256]) for i in range(2)]
        yz = [lsb("yz%d" % i, [128, 256]) for i in range(2)]
        ssq = [lsb("ssq%d" % i, [128, 4]) for i in range(2)]
        ssqs = [lsb("ssqs%d" % i, [128, 4]) for i in range(2)]
        vnew = [lsb("vnew%d" % i, [128, 4, 128], BF16) for i in range(2)]
        osb = [lsb("osb%d" % i, [128, 4, 128]) for i in range(2)]
        on = [lsb("on%d" % i, [128, 4, 128]) for i in range(2)]
        banks = [ps.enter_context(nc.psum_tensor("pb%d" % i, [128, 512], F32)) for i in range(8)]
        bctr = [0]

        def bank():
            i = bctr[0] % 8
            bctr[0] += 1
            return banks[i], "pb%d" % i
        rr = {"am": 0, "ama": 0, "g": 0, "v": 0}

        S.op("pool", lambda e: e.memset(ST[:], 0.0), writes=["ST"])
        S.op("pool", lambda e: e.memset(STb[:], 0.0), writes=["STb"])
        S.op("pool", lambda e: e.memset(GS[:], 0.0), writes=["GS"])
        S.op("pool", lambda e: e.memset(GSb[:], 0.0), writes=["GSb"])

        XBCv = XBC_T.rearrange("(b p) t -> p b t", p=128)
        QKVv = QKV_T.rearrange("(b p) t -> p b t", p=128)

        def loads(tt):
            i = tt % 2
            S.dma(smt[i][:], SM[tt * 128:(tt + 1) * 128, :], reads=["SM"], writes=["p3sm%d" % i])
            S.dma(xbct[i][:], XBCv[:, :, tt * 128:(tt + 1) * 128], reads=["xbc_T"], writes=["p3xbc%d" % i])
            S.dma(qkvt[i][:], QKVv[:, :, tt * 128:(tt + 1) * 128], reads=["qkv_T"], writes=["p3qkv%d" % i])

        def load_z(tt):
            S.dma(zt[0][:], Z[tt * 128:(tt + 1) * 128, :], reads=["Z"], writes=["p3z0"])

        def bc(ap2, n, w):
            return ap2.unsqueeze(2).to_broadcast([128, n, w])

        def v3(ap2, h=4):
            return ap2.rearrange("p (h w) -> p h w", h=h)

        loads(0)
        load_z(0)
        for tt in range(NT):
            i = tt % 2
            sm, smk = smt[i], "p3sm%d" % i
            xbc, xbk = xbct[i], "p3xbc%d" % i
            qkv, qkk = qkvt[i], "p3qkv%d" % i
            z, zk = zt[0], "p3z0"
            y, yk = ytile, "p3y"
            if tt + 1 < NT:
                loads(tt + 1)
            bD, bDk = bank()
            S.op("pe", lambda e, bD=bD, sm=sm: e.matmul(bD[:, 0:48], Umat, sm[:, 48:96], start=True, stop=True),
                 reads=[smk, "cst"], writes=[bDk])
            S.op("pe", lambda e, bD=bD, sm=sm: e.matmul(bD[:, 64:112], ones, sm[:, 48:96], start=True, stop=True),
                 reads=[smk, "cst"], writes=[bDk])
            S.op("act", lambda e, bD=bD: e.activation(out=c_sb[:], in_=bD[:, 0:48], func=AF.Identity), writes=[bDk, "c_sb"])
            S.op("act", lambda e, bD=bD: e.activation(out=e_sb[:], in_=bD[:, 0:48], func=AF.Exp), writes=[bDk, "e_sb"])
            S.op("act", lambda e, bD=bD: e.activation(out=dA_sb[:], in_=bD[:, 64:112], func=AF.Exp), writes=[bDk, "dA_sb"])
            S.op("act", lambda e, bD=bD: e.activation(out=f_sb[:], in_=bD[:, 64:112], func=AF.Identity), writes=[bDk, "f_sb"])
            S.op("dve", lambda e: e.tensor_tensor(out=f_sb[:], in0=f_sb[:], in1=c_sb[:], op=ALU.subtract),
                 reads=["c_sb", "f_sb"], writes=["f_sb"])
            S.op("act", lambda e: e.activation(out=f_sb[:], in_=f_sb[:], func=AF.Exp), reads=["f_sb"], writes=["f_sb"])
            S.op("pool", lambda e, sm=sm: e.tensor_scalar(out=nbeta[:], in0=sm[:, 32:48], scalar1=-1.0, scalar2=None, op0=ALU.mult),
                 reads=[smk], writes=["nbeta"])
            jobs = [(xbc, xbk, 0, xs_tok, "xs_tok", 2), (xbc, xbk, 16, b_tok, "b_tok", 1),
                    (qkv, qkk, 8, k_tok, "k_tok", 1), (qkv, qkk, 16, v_tok, "v_tok", 2)]
            nev = 0
            for (src, srck, b0, dst, dstk, nq) in jobs:
                for q8 in range(nq):
                    bk_, bkk = bank()
                    bb = bk_[:].bitcast(BF16)
                    for a in range(8):
                        blk = b0 + q8 * 8 + a
                        S.op("pe", lambda e, bb=bb, a=a, src=src, blk=blk: e.transpose(
                            bb[:, a * 128:(a + 1) * 128], src[:, blk, :], identb), reads=[srck, "cstb"], writes=[bkk])
                    if nev % 2 == 0:
                        S.op("act", lambda e, bb=bb, dst=dst, q8=q8: e.activation(
                            out=dst[:, q8 * 1024:(q8 + 1) * 1024], in_=bb, func=AF.Identity), writes=[bkk, dstk])
                    else:
                        S.op("dve", lambda e, bb=bb, dst=dst, q8=q8: e.tensor_copy(
                            out=dst[:, q8 * 1024:(q8 + 1) * 1024], in_=bb), writes=[bkk, dstk])
                    nev += 1

            for half in range(2):
                bk_, bkk = bank()
                for j in range(4):
                    hq = half * 4 + j
                    S.op("pe", lambda e, bk_=bk_, j=j, hq=hq, qkv=qkv: e.matmul(
                        bk_[:, j * 128:(j + 1) * 128], qkv[:, 8 + hq, :], qkv[:, 8 + hq, :], start=True, stop=True),
                        reads=[qkk], writes=[bkk])
                S.op("dve", lambda e, bk_=bk_, half=half: e.tensor_tensor(
                    out=KKm[:, half * 4:(half + 1) * 4, :], in0=v3(bk_[:]), in1=SUm.unsqueeze(1).to_broadcast([128, 4, 128]), op=ALU.mult),
                    reads=["cst"], writes=[bkk, "KKm"])
                bq_, bqk = bank()
                for j in range(4):
                    hq = half * 4 + j
                    S.op("pe", lambda e, bq_=bq_, j=j, hq=hq, qkv=qkv: e.matmul(
                        bq_[:, j * 128:(j + 1) * 128], qkv[:, 8 + hq, :], qkv[:, hq, :], start=True, stop=True),
                        reads=[qkk], writes=[bqk])
                S.op("dve", lambda e, bq_=bq_, half=half: e.tensor_tensor(
                    out=QKm[:, half * 4:(half + 1) * 4, :], in0=v3(bq_[:]), in1=Umat.unsqueeze(1).to_broadcast([128, 4, 128]), op=ALU.mult),
                    reads=["cst"], writes=[bqk, "QKm"])
            def fl(t):
                return t[:].rearrange("p h w -> p (h w)")

            def bcm(plane):
                return cst[:, plane, :].unsqueeze(1).to_broadcast([128, 4, 128])

            def build_DT(cols, r, sm, smk):
                ra = rr["ama"] % 2
                rr["ama"] += 1
                S.op("dve", lambda e: e.tensor_tensor(out=Am[ra][:], in0=bcm(C_GT), in1=bc(sm[:, cols:cols + 4], 4, 128), op=ALU.mult),
                     reads=[smk, "cst"], writes=["Am%d" % ra])
                sg, sgk = bank()
                for j in range(4):
                    S.op("pe", lambda e, sg=sg, j=j: e.matmul(sg[:, j * 128:(j + 1) * 128], Am[ra][:, j, :], Umat, start=True, stop=True),
                         reads=["Am%d" % ra, "cst"], writes=[sgk])
                S.op("act", lambda e, sg=sg: e.activation(out=fl(DT[r]), in_=sg[:], func=AF.Exp), writes=[sgk, "DT%d" % r])

            def gdn_quad(q, s_, sm=sm, smk=smk, qkv=qkv, qkk=qkk, z=z, zk=zk, y=y, yk=yk):
                Xb, XTb, Dvb, DvTb, Eb, XMb = IB[s_]
                kX, kXT, kDvb, kDvTb, kEb, kXMb = [("ib", s_, n_) for n_ in range(6)]
                qs = slice(q * 4, (q + 1) * 4)
                r = rr["am"] % 4
                rr["am"] += 1
                build_DT(80 + q * 4, r, sm, smk)
                yield
                for j in range(4):
                    hv = q * 4 + j
                    hq = hv // 2
                    S.op("dve", lambda e, j=j, hv=hv, hq=hq: e.scalar_tensor_tensor(
                        out=Xb[:, j, :], in0=KKm[:, hq, :], scalar=nbeta[:, hv:hv + 1], in1=DT[r][:, j, :], op0=ALU.mult, op1=ALU.mult),
                        reads=["KKm", "nbeta", "DT%d" % r], writes=[kX])
                S.op("dve", lambda e: e.tensor_tensor(
                    out=attnT[:, qs, :].rearrange("p (a b) w -> p a b w", a=2),
                    in0=QKm[:, 2 * q:2 * q + 2, :].unsqueeze(2).to_broadcast([128, 2, 2, 128]),
                    in1=DT[r][:].rearrange("p (a b) w -> p a b w", a=2), op=ALU.mult),
                    reads=["QKm", "DT%d" % r], writes=[("attnT", q)])
                yield

                def mm4(lhs, lk, rhs, rk):
                    b_, bk = bank()
                    for j in range(4):
                        S.op("pe", lambda e, b_=b_, j=j: e.matmul(b_[:, j * 128:(j + 1) * 128], lhs[:, j, :], rhs[:, j, :], start=True, stop=True),
                             reads=[lk, rk], writes=[bk])
                    return b_, bk

                def acc_dve(b_, bk, dst, dk, out=None, ok=None):
                    o_ = fl(dst) if out is None else out
                    S.op("dve", lambda e: e.tensor_tensor(out=o_, in0=fl(dst), in1=b_[:], op=ALU.add),
                         reads=[dk], writes=[bk, dk if ok is None else ok])

                tb_, tbk = bank()
                tbb = tb_[:].bitcast(BF16)
                for j in range(4):
                    S.op("pe", lambda e, j=j: e.transpose(tbb[:, j * 128:(j + 1) * 128], Xb[:, j, :], identb),
                         reads=[kX, "cstb"], writes=[tbk])
                S.op("act", lambda e: e.activation(out=fl(XTb), in_=tbb[:, 0:512], func=AF.Identity), writes=[tbk, kXT])
                S.op("dve", lambda e: e.tensor_tensor(out=Dvb[:], in0=Xb[:], in1=bcm(C_BD8), op=ALU.mult), reads=[kX, "cst"], writes=[kDvb])
                S.op("dve", lambda e: e.tensor_tensor(out=Dvb[:], in0=Dvb[:], in1=bcm(C_ID), op=ALU.add), reads=[kDvb, "cst"], writes=[kDvb])
                yield
                S.op("dve", lambda e: e.tensor_tensor(out=DvTb[:], in0=XTb[:], in1=bcm(C_BD8), op=ALU.mult), reads=[kXT, "cst"], writes=[kDvTb])
                S.op("dve", lambda e: e.tensor_tensor(out=DvTb[:], in0=DvTb[:], in1=bcm(C_ID), op=ALU.add), reads=[kDvTb, "cst"], writes=[kDvTb])
                S.op("dve", lambda e: e.tensor_tensor(
                    out=XMb[:], in0=XTb[:].unsqueeze(1).to_broadcast([128, 6, 4, 128]),
                    in1=cst[:, C_CMT:C_CMT + 6, :].unsqueeze(2).to_broadcast([128, 6, 4, 128]), op=ALU.mult),
                    reads=[kXT, "cst"], writes=[kXMb])
                yield
                for n, b in enumerate((2, 4, 8, 16, 32, 64)):
                    b_, bk = mm4(XMb[:, n, :, :], kXMb, Dvb, kDvb)
                    S.op("act", lambda e, b_=b_: e.activation(out=fl(Eb), in_=b_[:], func=AF.Identity), writes=[bk, kEb])
                    yield
                    b1, b1k = mm4(DvTb, kDvTb, Eb, kEb)
                    if b != 64:
                        b2, b2k = mm4(Eb, kEb, DvTb, kDvTb)
                        acc_dve(b1, b1k, Dvb, kDvb)
                        acc_dve(b2, b2k, DvTb, kDvTb)
                        yield
                    else:
                        acc_dve(b1, b1k, Dvb, kDvb, out=T2T[:, qs, :].rearrange("p h w -> p (h w)"), ok=("T2T", q))
                        yield
                kq = k_tok[:, 2 * q * 128:(2 * q + 2) * 128].rearrange("p (a w) -> p a w", a=2).unsqueeze(2).to_broadcast([128, 2, 2, 128])
                S.op("pool", lambda e: e.tensor_tensor(
                    out=ke[:, qs, :].rearrange("p (a b) w -> p a b w", a=2), in0=kq,
                    in1=e_sb[:, 32 + q * 4:36 + q * 4].rearrange("p (a b) -> p a b", a=2).unsqueeze(3).to_broadcast([128, 2, 2, 128]),
                    op=ALU.mult), reads=["k_tok", "e_sb"], writes=[("ke", q)])
                S.op("pool", lambda e: e.tensor_tensor(
                    out=kf[:, qs, :].rearrange("p (a b) w -> p a b w", a=2), in0=kq,
                    in1=f_sb[:, 32 + q * 4:36 + q * 4].rearrange("p (a b) -> p a b", a=2).unsqueeze(3).to_broadcast([128, 2, 2, 128]),
                    op=ALU.mult), reads=["k_tok", "f_sb"], writes=[("kf", q)])
                wp, wpk = bank()
                for j in range(4):
                    hv = q * 4 + j
                    S.op("pe", lambda e, wp=wp, j=j, hv=hv: e.matmul(wp[:, j * 128:(j + 1) * 128], ke[:, hv, :], T2T[:, hv, :],
                                                                     start=True, stop=True),
                         reads=[("ke", q), ("T2T", q)], writes=[wpk])
                S.op("act", lambda e, wp=wp: e.activation(out=nwT[:, qs, :].rearrange("p h w -> p (h w)"), in_=wp[:],
                                                        func=AF.Identity, scale=-1.0), writes=[wpk, ("nwT", q)])
                yield
                qs = slice(q * 4, (q + 1) * 4)
                vi = rr["v"] % 2
                rr["v"] += 1
                vp, vpk = bank()
                for j in range(4):
                    hv = q * 4 + j
                    S.op("pe", lambda e, vp=vp, j=j, hv=hv: e.matmul(vp[:, j * 128:(j + 1) * 128], T2T[:, hv, :],
                                                                     v_tok[:, hv * 128:(hv + 1) * 128], start=True, stop=False),
                         reads=[("T2T", q), "v_tok"], writes=[vpk])
                    S.op("pe", lambda e, vp=vp, j=j, hv=hv: e.matmul(vp[:, j * 128:(j + 1) * 128], nwT[:, hv, :], GSb[:, hv, :],
                                                                     start=False, stop=True),
                         reads=[("nwT", q), ("GSb", q)], writes=[vpk])
                for j in range(4):
                    hv = q * 4 + j
                    S.op("act", lambda e, vp=vp, j=j, hv=hv, vi=vi: e.activation(
                        out=vnew[vi][:, j, :], in_=vp[:, j * 128:(j + 1) * 128], func=AF.Identity, scale=sm[:, 32 + hv:33 + hv]),
                        reads=[smk], writes=[vpk, "vnew%d" % vi])
                yield
                oi, oik = bank()
                for j in range(4):
                    hv = q * 4 + j
                    hq = hv // 2
                    S.op("pe", lambda e, oi=oi, j=j, hv=hv, hq=hq: e.matmul(oi[:, j * 128:(j + 1) * 128], qkv[:, hq, :], GSb[:, hv, :],
                                                                               start=True, stop=True),
                         reads=[qkk, ("GSb", q)], writes=[oik])
                oa, oak = bank()
                for j in range(4):
                    hv = q * 4 + j
                    S.op("pe", lambda e, oa=oa, j=j, hv=hv, vi=vi: e.matmul(oa[:, j * 128:(j + 1) * 128], attnT[:, hv, :], vnew[vi][:, j, :],
                                                                           start=True, stop=True),
                         reads=[("attnT", q), "vnew%d" % vi], writes=[oak])
                sn, snk = bank()
                for j in range(4):
                    hv = q * 4 + j
                    S.op("pe", lambda e, sn=sn, j=j, hv=hv, vi=vi: e.matmul(sn[:, j * 128:(j + 1) * 128], kf[:, hv, :], vnew[vi][:, j, :],
                                                                           start=True, stop=True),
                         reads=[("kf", q), "vnew%d" % vi], writes=[snk])
                for j in range(4):
                    hv = q * 4 + j
                    S.op("act", lambda e, oi=oi, j=j, hv=hv, vi=vi: e.activation(
                        out=osb[vi][:, j, :], in_=oi[:, j * 128:(j + 1) * 128], func=AF.Identity, scale=e_sb[:, 32 + hv:33 + hv]),
                        reads=["e_sb"], writes=[oik, "osb%d" % vi])
                S.op("dve", lambda e, oa=oa, vi=vi: e.tensor_tensor(
                    out=osb[vi][:].rearrange("p h w -> p (h w)"), in0=osb[vi][:].rearrange("p h w -> p (h w)"), in1=oa[:], op=ALU.add),
                    reads=["osb%d" % vi], writes=[oak, "osb%d" % vi])
                for j in range(4):
                    hv = q * 4 + j
                    S.op("dve", lambda e, sn=sn, j=j, hv=hv: e.scalar_tensor_tensor(
                        out=GS[:, hv, :], in0=GS[:, hv, :], scalar=dA_sb[:, 32 + hv:33 + hv], in1=sn[:, j * 128:(j + 1) * 128],
                        op0=ALU.mult, op1=ALU.add), reads=[("GS", q), "dA_sb"], writes=[snk, ("GS", q)])
                yield
                S.op("act", lambda e, qs=qs: e.activation(out=GSb[:, qs, :], in_=GS[:, qs, :], func=AF.Identity),
                     reads=[("GS", q)], writes=[("GSb", q)])
                for j in range(4):
                    S.op("act", lambda e, j=j, vi=vi: e.activation(out=on[vi][:, j, :], in_=osb[vi][:, j, :], func=AF.Square,
                                                                 accum_out=ssq[vi][:, j:j + 1]),
                         reads=["osb%d" % vi], writes=["on%d" % vi, "ssq%d" % vi])
                S.op("act", lambda e, vi=vi: e.activation(out=ssq[vi][:], in_=ssq[vi][:], func=AF.Sqrt, bias=cst[:, C_EPS, 0:1], scale=1.0 / 128),
                     reads=["ssq%d" % vi, "cst"], writes=["ssq%d" % vi])
                yield
                S.op("dve", lambda e, vi=vi: e.reciprocal(out=ssq[vi][:], in_=ssq[vi][:]), reads=["ssq%d" % vi], writes=["ssq%d" % vi])
                for j in range(4):
                    hv = q * 4 + j
                    S.op("dve", lambda e, j=j, vi=vi: e.scalar_tensor_tensor(
                        out=on[vi][:, j, :], in0=osb[vi][:, j, :], scalar=ssq[vi][:, j:j + 1], in1=rows[:, R_NW2:R_NW2 + 128],
                        op0=ALU.mult, op1=ALU.mult), reads=["osb%d" % vi, "ssq%d" % vi, "rows"], writes=["on%d" % vi])
                S.op("pool", lambda e, vi=vi: e.tensor_tensor(
                    out=y[:, 2048 + q * 512:2048 + (q + 1) * 512], in0=on[vi][:].rearrange("p h w -> p (h w)"),
                    in1=z[:, 2048 + q * 512:2048 + (q + 1) * 512], op=ALU.mult), reads=["on%d" % vi, zk], writes=[yk])
                yield

            def ssd_group(g, sm=sm, smk=smk, xbc=xbc, xbk=xbk, z=z, zk=zk, y=y, yk=yk):
                gi = rr["g"] % 2
                rr["g"] += 1
                r = rr["am"] % 4
                rr["am"] += 1
                b2, b2k = bank()
                S.op("pe", lambda e: e.matmul(b2[:, 0:128], xbc[:, 16 + g, :], xbc[:, 24 + g, :], start=True, stop=True),
                     reads=[xbk], writes=[b2k])
                S.op("dve", lambda e: e.tensor_tensor(out=CBTm[gi][:], in0=b2[:, 0:128], in1=Umat, op=ALU.mult),
                     reads=["cst"], writes=[b2k, "CBTm%d" % gi])
                xs3 = v3(xs_tok[:, g * 256:(g + 1) * 256])
                S.op("dve", lambda e: e.tensor_tensor(
                    out=v3(xdt[gi][:]), in0=xs3, in1=bc(sm[:, g * 4:(g + 1) * 4], 4, 64), op=ALU.mult),
                    reads=["xs_tok", smk], writes=["xdt%d" % gi])
                S.op("dve", lambda e: e.tensor_tensor(
                    out=v3(xw[gi][:]), in0=v3(xdt[gi][:]), in1=bc(f_sb[:, g * 4:(g + 1) * 4], 4, 64), op=ALU.mult),
                    reads=["xdt%d" % gi, "f_sb"], writes=["xw%d" % gi])
                S.op("pool", lambda e: e.tensor_tensor(
                    out=v3(xsD[gi][:]), in0=xs3, in1=bc(rows[:, R_D1 + g * 4:R_D1 + (g + 1) * 4], 4, 64), op=ALU.mult),
                    reads=["xs_tok", "rows"], writes=["xsD%d" % gi])
                S.op("pool", lambda e: e.tensor_tensor(
                    out=v3(ST[:, g, :]), in0=v3(ST[:, g, :]), in1=bc(dA_sb[:, g * 4:(g + 1) * 4], 4, 64), op=ALU.mult),
                    reads=[("ST", g), "dA_sb"], writes=[("ST", g)])
                build_DT(48 + g * 4, r, sm, smk)
                yield
                S.op("dve", lambda e: e.tensor_tensor(out=MT[gi][:], in0=DT[r][:],
                                                      in1=CBTm[gi][:].unsqueeze(1).to_broadcast([128, 4, 128]), op=ALU.mult),
                     reads=["CBTm%d" % gi, "DT%d" % r], writes=["MT%d" % gi])
                yield
                b3, b3k = bank()
                for h in range(4):
                    S.op("pe", lambda e, h=h: e.matmul(b3[:, h * 64:(h + 1) * 64], MT[gi][:, h, :],
                                                       xdt[gi][:, h * 64:(h + 1) * 64], start=True, stop=True),
                         reads=["MT%d" % gi, "xdt%d" % gi], writes=[b3k])
                S.op("pe", lambda e: e.matmul(b3[:, 256:512], xbc[:, 24 + g, :], STb[:, g, :], start=True, stop=True),
                     reads=[xbk, ("STb", g)], writes=[b3k])
                b4, b4k = bank()
                S.op("pe", lambda e: e.matmul(b4[:, 256:512], b_tok[:, g * 128:(g + 1) * 128], xw[gi][:], start=True, stop=True),
                     reads=["b_tok", "xw%d" % gi], writes=[b4k])
                S.op("dve", lambda e: e.tensor_tensor(
                    out=v3(yacc[gi][:]), in0=v3(b3[:, 256:512]), in1=bc(e_sb[:, g * 4:(g + 1) * 4], 4, 64), op=ALU.mult),
                    reads=["e_sb"], writes=[b3k, "yacc%d" % gi])
                S.op("dve", lambda e: e.tensor_tensor(out=yacc[gi][:], in0=yacc[gi][:], in1=b3[:, 0:256], op=ALU.add),
                     reads=["yacc%d" % gi], writes=[b3k, "yacc%d" % gi])
                S.op("dve", lambda e: e.tensor_tensor(out=ST[:, g, :], in0=ST[:, g, :], in1=b4[:, 256:512], op=ALU.add),
                     reads=[("ST", g)], writes=[b4k, ("ST", g)])
                yield
                S.op("pool", lambda e: e.tensor_tensor(out=yacc[gi][:], in0=yacc[gi][:], in1=xsD[gi][:], op=ALU.add),
                     reads=["yacc%d" % gi, "xsD%d" % gi], writes=["yacc%d" % gi])
                S.op("act", lambda e: e.activation(out=STb[:, g, :], in_=ST[:, g, :], func=AF.Identity),
                     reads=[("ST", g)], writes=[("STb", g)])
                yield
                S.op("dve", lambda e: e.tensor_tensor(out=yz[gi][:], in0=yacc[gi][:], in1=z[:, g * 256:(g + 1) * 256], op=ALU.mult),
                     reads=["yacc%d" % gi, zk], writes=["yz%d" % gi])
                yield
                S.op("act", lambda e: e.activation(out=xsD[gi][:], in_=yz[gi][:], func=AF.Square, accum_out=ssqs[gi][:, 0:1]),
                     reads=["yz%d" % gi], writes=["xsD%d" % gi, "ssqs%d" % gi])
                S.op("act", lambda e: e.activation(out=ssqs[gi][:, 0:1], in_=ssqs[gi][:, 0:1], func=AF.Sqrt, bias=cst[:, C_EPS, 0:1], scale=1.0 / 256),
                     reads=["ssqs%d" % gi, "cst"], writes=["ssqs%d" % gi])
                yield
                S.op("dve", lambda e: e.reciprocal(out=ssqs[gi][:, 0:1], in_=ssqs[gi][:, 0:1]), reads=["ssqs%d" % gi], writes=["ssqs%d" % gi])
                S.op("dve", lambda e: e.scalar_tensor_tensor(
                    out=y[:, g * 256:(g + 1) * 256], in0=yz[gi][:], scalar=ssqs[gi][:, 0:1],
                    in1=rows[:, R_NW1 + g * 256:R_NW1 + (g + 1) * 256], op0=ALU.mult, op1=ALU.mult),
                    reads=["yz%d" % gi, "ssqs%d" % gi, "rows"], writes=[yk])
                yield

            pend_g = [gdn_quad(q, q % 2) for q in range(4)]
            pend_s = [ssd_group(g) for g in range(8)]
            live = []
            slots = {"g": 0, "s": 0}

            def refill():
                while slots["g"] < 2 and pend_g:
                    live.append(("g", pend_g.pop(0)))
                    slots["g"] += 1
                while slots["s"] < 2 and pend_s:
                    live.append(("s", pend_s.pop(0)))
                    slots["s"] += 1
            refill()
            while live:
                for item in list(live):
                    kind, g_ = item
                    try:
                        next(g_)
                    except StopIteration:
                        live.remove(item)
                        slots[kind] -= 1
                refill()

            S.dma(Y[tt * 128:(tt + 1) * 128, :], y[:], reads=[yk], writes=["Y"])
            if tt + 1 < NT:
                load_z(tt + 1)
        S.barrier()


def load_weight_bf16(nc, S, dst, dstk, src_v, nk, ncols, stg, stgk):
    cnt = 0
    kc = 2
    for k0 in range(0, nk, kc):
        for c0 in range(0, ncols, 512):
            s_ = stg[cnt % 2]
            sk = stgk[cnt % 2]
            cnt += 1
            S.dma(s_[:], src_v[:, k0:k0 + kc, c0:c0 + 512], writes=[sk])
            if cnt % 2 == 0:
                S.op("act", lambda e, s_=s_, k0=k0, c0=c0: e.activation(out=dst[:, k0:k0 + kc, c0:c0 + 512], in_=s_[:], func=AF.Identity),
                     reads=[sk], writes=[dstk])
            else:
                S.op("dve", lambda e, s_=s_, k0=k0, c0=c0: e.tensor_copy(out=dst[:, k0:k0 + kc, c0:c0 + 512], in_=s_[:]),
                     reads=[sk], writes=[dstk])


def phase4a(nc, S, T, Y, G_T, XT, X2T, w_su, w_gu, w_out, cst, cstb, pv):
    BW = 256
    NBW = T // BW
    NA = BW // 128
    identb = cstb[:, C_ID, :]
    onesD = cst[:, C_ONED, :]
    with ExitStack() as ps:
        lsb = lambda name, shape, dt=F32: ps.enter_context(nc.sbuf_tensor(name, shape, dt))
        Wsu = lsb("Wsu", [128, 16, D], BF16)
        Wgu = lsb("Wgu", [128, 16, D], BF16)
        Wo = lsb("Wo", [128, 8, D], BF16)
        stg = [lsb("stg%d" % i, [128, 2, 512]) for i in range(2)]
        ytoks = [lsb("ytok%d" % i, [128, NA, 4096], BF16) for i in range(2)]
        yT = lsb("yT", [128, 32, BW], BF16)
        gTs = [lsb("gT%d" % i, [128, 16, BW], BF16) for i in range(2)]
        xTs = [lsb("xT4_%d" % i, [128, 8, BW]) for i in range(2)]
        t1 = [lsb("m_t1_%d" % i, [128, BW]) for i in range(2)]
        t2 = [lsb("m_t2_%d" % i, [128, BW]) for i in range(2)]
        mT = lsb("mT", [128, 8, BW], BF16)
        m2s = lsb("m2s", [128, 8, BW])
        sqm = lsb("sqm", [128, 8, BW])
        rstd = lsb("rstd4", [128, BW])
        pu = [ps.enter_context(nc.psum_tensor("pu%d" % i, [128, 512], F32)) for i in range(4)]
        pt = [ps.enter_context(nc.psum_tensor("ptb%d" % i, [128, 1024], BF16)) for i in range(2)]
        pss = ps.enter_context(nc.psum_tensor("pss4", [128, 512], F32))
        load_weight_bf16(nc, S, Wsu, "Wsu", w_su.rearrange("(k p) f -> p k f", p=128), 16, D, stg, ["stg0", "stg1"])
        load_weight_bf16(nc, S, Wgu, "Wgu", w_gu.rearrange("(k p) f -> p k f", p=128), 16, D, stg, ["stg0", "stg1"])
        load_weight_bf16(nc, S, Wo, "Wo", w_out.rearrange("(k p) f -> p k f", p=128), 8, D, stg, ["stg0", "stg1"])
        GTv = G_T.rearrange("(b p) t -> p b t", p=128)
        XTv = XT.rearrange("(k p) t -> p k t", p=128)
        X2Tv = X2T.rearrange("(k p) t -> p k t", p=128)
        pc = 0
        def loads4(nb):
            t0 = nb * BW
            i = nb % 2
            S.dma(ytoks[i][:], Y[t0:t0 + BW, :].rearrange("(a p) f -> p a f", p=128), reads=["Y"], writes=["ytok%d" % i])
            S.dma(gTs[i][:], GTv[:, :, t0:t0 + BW], reads=["G_T"], writes=["gT%d" % i])
            S.dma(xTs[i][:], XTv[:, :, t0:t0 + BW], reads=["XT"], writes=["xT4_%d" % i])
        loads4(0)
        for nb in range(NBW):
            t0 = nb * BW
            ytok, gT, xT = ytoks[nb % 2], gTs[nb % 2], xTs[nb % 2]
            ytk, gTk, xTk = "ytok%d" % (nb % 2), "gT%d" % (nb % 2), "xT4_%d" % (nb % 2)
            if nb + 1 < NBW:
                loads4(nb + 1)
            for cb in range(32):
                p_ = pt[cb % 2]
                pk = "ptb%d" % (cb % 2)
                for a in range(NA):
                    S.op("pe", lambda e, p_=p_, a=a, cb=cb, ytok=ytok: e.transpose(p_[:, a * 128:(a + 1) * 128],
                                                                      ytok[:, a, cb * 128:(cb + 1) * 128], identb),
                         reads=[ytk, "cstb"], writes=[pk])
                if cb % 2 == 0:
                    S.op("act", lambda e, p_=p_, cb=cb: e.activation(out=yT[:, cb, :], in_=p_[:, 0:BW], func=AF.Identity),
                         reads=[pk], writes=["yT"])
                else:
                    S.op("dve", lambda e, p_=p_, cb=cb: e.tensor_copy(out=yT[:, cb, :], in_=p_[:, 0:BW]), reads=[pk], writes=["yT"])
            for blk in range(8):
                p1 = pu[pc % 4]
                p1k = "pu%d" % (pc % 4)
                pc += 1
                p2 = pu[pc % 4]
                p2k = "pu%d" % (pc % 4)
                pc += 1
                for k in range(16):
                    S.op("pe", lambda e, p1=p1, k=k, blk=blk: e.matmul(p1[:, 0:BW], Wsu[:, k, blk * 128:(blk + 1) * 128], yT[:, k, :],
                                                                       start=(k == 0), stop=(k == 15)),
                         reads=["Wsu", "yT"], writes=[p1k])
                for k in range(16):
                    S.op("pe", lambda e, p2=p2, k=k, blk=blk: e.matmul(p2[:, 0:BW], Wgu[:, k, blk * 128:(blk + 1) * 128], yT[:, 16 + k, :],
                                                                       start=(k == 0), stop=(k == 15)),
                         reads=["Wgu", "yT"], writes=[p2k])
                a1 = t1[blk % 2]
                a1k = "m_t1_%d" % (blk % 2)
                a2 = t2[blk % 2]
                a2k = "m_t2_%d" % (blk % 2)
                S.op("dve", lambda e, p1=p1, a1=a1, blk=blk, gT=gT: e.tensor_tensor(out=a1[:], in0=p1[:, 0:BW], in1=gT[:, blk, :], op=ALU.mult),
                     reads=[p1k, gTk], writes=[a1k])
                S.op("dve", lambda e, p2=p2, a2=a2, blk=blk, gT=gT: e.tensor_tensor(out=a2[:], in0=p2[:, 0:BW], in1=gT[:, 8 + blk, :], op=ALU.mult),
                     reads=[p2k, gTk], writes=[a2k])
                S.op("dve", lambda e, a1=a1, a2=a2, blk=blk: e.tensor_tensor(out=mT[:, blk, :], in0=a1[:], in1=a2[:], op=ALU.add),
                     reads=[a1k, a2k], writes=["mT"])
            for blk in range(8):
                p1 = pu[pc % 4]
                p1k = "pu%d" % (pc % 4)
                pc += 1
                for k in range(8):
                    S.op("pe", lambda e, p1=p1, k=k, blk=blk: e.matmul(p1[:, 0:BW], Wo[:, k, blk * 128:(blk + 1) * 128], mT[:, k, :],
                                                                       start=(k == 0), stop=(k == 7)),
                         reads=["Wo", "mT"], writes=[p1k])
                S.op("act", lambda e, p1=p1, blk=blk: e.activation(out=m2s[:, blk, :], in_=p1[:, 0:BW], func=AF.Identity),
                     reads=[p1k], writes=["m2s"])
                S.op("act", lambda e, p1=p1, blk=blk: e.activation(out=sqm[:, blk, :], in_=p1[:, 0:BW], func=AF.Square),
                     reads=[p1k], writes=["sqm"])
            for k in range(8):
                S.op("pe", lambda e, k=k: e.matmul(pss[:, 0:BW], onesD, sqm[:, k, :], start=(k == 0), stop=(k == 7)),
                     reads=["sqm", "cst"], writes=["pss4"])
            rsqrt_to(S, cst, rstd[:], "rstd4", pss[:, 0:BW], "pss4")
            for blk in range(8):
                a1 = t1[blk % 2]
                a1k = "m_t1_%d" % (blk % 2)
                S.op("dve", lambda e, a1=a1, blk=blk: e.tensor_tensor(out=a1[:], in0=m2s[:, blk, :], in1=rstd[:], op=ALU.mult),
                     reads=["m2s", "rstd4"], writes=[a1k])
                S.op("dve", lambda e, a1=a1, blk=blk, xT=xT: e.scalar_tensor_tensor(
                    out=xT[:, blk, :], in0=a1[:], scalar=pv[:, 2, blk:blk + 1], in1=xT[:, blk, :], op0=ALU.mult, op1=ALU.add),
                    reads=[a1k, "pv", xTk], writes=[xTk])
            S.dma(X2Tv[:, :, t0:t0 + BW], xT[:], reads=[xTk], writes=["X2T"])
        S.barrier()


def phase4b(nc, S, T, X2T, out, w_up, w_dn, cst, pv, normT):
    BW = 256
    NBW = T // BW
    NA = BW // 128
    ident = cst[:, C_ID, :]
    onesD = cst[:, C_ONED, :]
    with ExitStack() as ps:
        lsb = lambda name, shape, dt=F32: ps.enter_context(nc.sbuf_tensor(name, shape, dt))
        Wup = lsb("Wup", [128, 8, 4096], BF16)
        Wdn = lsb("Wdn", [128, 32, D], BF16)
        stg = [lsb("stgb%d" % i, [128, 2, 512]) for i in range(2)]
        xTs = [lsb("x5T%d" % i, [128, 8, BW]) for i in range(2)]
        sq = lsb("sq5", [128, 8, BW])
        rstd = lsb("rstd5", [128, BW])
        tmp = [lsb("tmp5_%d" % i, [128, BW]) for i in range(2)]
        h2T = lsb("h2T", [128, 8, BW], BF16)
        rl = [lsb("rl%d" % i, [128, BW]) for i in range(2)]
        actT = lsb("actT", [128, 32, BW], BF16)
        dns = lsb("dns", [128, 8, BW])
        otok = [lsb("otok%d" % i, [128, 512]) for i in range(2)]
        pu = [ps.enter_context(nc.psum_tensor("p5u%d" % i, [128, 512], F32)) for i in range(4)]
        pss = ps.enter_context(nc.psum_tensor("pss5", [128, 512], F32))
        po = [ps.enter_context(nc.psum_tensor("p5o%d" % i, [128, 512], F32)) for i in range(2)]
        load_weight_bf16(nc, S, Wup, "Wup", w_up.rearrange("(k p) f -> p k f", p=128), 8, 4096, stg, ["stgb0", "stgb1"])
        load_weight_bf16(nc, S, Wdn, "Wdn", w_dn.rearrange("(k p) f -> p k f", p=128), 32, D, stg, ["stgb0", "stgb1"])
        X2Tv = X2T.rearrange("(k p) t -> p k t", p=128)
        pc = 0
        oc = 0
        S.dma(xTs[0][:], X2Tv[:, :, 0:BW], reads=["X2T"], writes=["x5T0"])
        for nb in range(NBW):
            t0 = nb * BW
            xT = xTs[nb % 2]
            xk = "x5T%d" % (nb % 2)
            if nb + 1 < NBW:
                S.dma(xTs[(nb + 1) % 2][:], X2Tv[:, :, t0 + BW:t0 + 2 * BW], reads=["X2T"], writes=["x5T%d" % ((nb + 1) % 2)])
            normT(xT, xk, sq, "sq5", rstd, "rstd5", pss, "pss5", tmp, ["tmp5_0", "tmp5_1"],
                  lambda k: h2T[:, k, :], "h2T", 3, 4, BW)
            for blk in range(32):
                p1 = pu[pc % 4]
                p1k = "p5u%d" % (pc % 4)
                pc += 1
                for k in range(8):
                    S.op("pe", lambda e, p1=p1, k=k, blk=blk: e.matmul(p1[:, 0:BW], Wup[:, k, blk * 128:(blk + 1) * 128], h2T[:, k, :],
                                                                       start=(k == 0), stop=(k == 7)),
                         reads=["Wup", "h2T"], writes=[p1k])
                r_ = rl[blk % 2]
                rk = "rl%d" % (blk % 2)
                S.op("act", lambda e, p1=p1, r_=r_: e.activation(out=r_[:], in_=p1[:, 0:BW], func=AF.Relu), reads=[p1k], writes=[rk])
                eng = "pool" if blk % 4 == 0 else "dve"
                S.op(eng, lambda e, r_=r_, blk=blk: e.tensor_tensor(out=actT[:, blk, :], in0=r_[:], in1=r_[:], op=ALU.mult),
                     reads=[rk], writes=["actT"])
            for blk in range(8):
                p1 = pu[pc % 4]
                p1k = "p5u%d" % (pc % 4)
                pc += 1
                for k in range(32):
                    S.op("pe", lambda e, p1=p1, k=k, blk=blk: e.matmul(p1[:, 0:BW], Wdn[:, k, blk * 128:(blk + 1) * 128], actT[:, k, :],
                                                                       start=(k == 0), stop=(k == 31)),
                         reads=["Wdn", "actT"], writes=[p1k])
                S.op("act", lambda e, p1=p1, blk=blk: e.activation(out=dns[:, blk, :], in_=p1[:, 0:BW], func=AF.Identity),
                     reads=[p1k], writes=["dns"])
                S.op("act", lambda e, p1=p1, blk=blk: e.activation(out=sq[:, blk, :], in_=p1[:, 0:BW], func=AF.Square),
                     reads=[p1k], writes=["sq5"])
            for k in range(8):
                S.op("pe", lambda e, k=k: e.matmul(pss[:, 0:BW], onesD, sq[:, k, :], start=(k == 0), stop=(k == 7)),
                     reads=["sq5", "cst"], writes=["pss5"])
            rsqrt_to(S, cst, rstd[:], "rstd5", pss[:, 0:BW], "pss5")
            for blk in range(8):
                t_ = tmp[blk % 2]
                tk = "tmp5_%d" % (blk % 2)
                S.op("dve", lambda e, t_=t_, blk=blk: e.tensor_tensor(out=t_[:], in0=dns[:, blk, :], in1=rstd[:], op=ALU.mult),
                     reads=["dns", "rstd5"], writes=[tk])
                S.op("dve", lambda e, t_=t_, blk=blk, xT=xT: e.scalar_tensor_tensor(
                    out=dns[:, blk, :], in0=t_[:], scalar=pv[:, 5, blk:blk + 1], in1=xT[:, blk, :], op0=ALU.mult, op1=ALU.add),
                    reads=[tk, "pv", xk, "dns"], writes=["dns"])
            for a in range(NA):
                for half in range(2):
                    ot = otok[oc % 2]
                    otk = "otok%d" % (oc % 2)
                    oc += 1
                    p_ = po[half]
                    pk = "p5o%d" % half
                    for b4 in range(4):
                        blk = half * 4 + b4
                        S.op("pe", lambda e, p_=p_, b4=b4, blk=blk, a=a: e.transpose(
                            p_[:, b4 * 128:(b4 + 1) * 128], dns[:, blk, a * 128:(a + 1) * 128], ident),
                            reads=["dns", "cst"], writes=[pk])
                    if half == 0:
                        S.op("act", lambda e, p_=p_, ot=ot: e.activation(out=ot[:], in_=p_[:], func=AF.Identity),
                             reads=[pk], writes=[otk])
                    else:
                        S.op("dve", lambda e, p_=p_, ot=ot: e.tensor_copy(out=ot[:], in_=p_[:]), reads=[pk], writes=[otk])
                    S.dma(out[t0 + a * 128:t0 + (a + 1) * 128, half * 512:(half + 1) * 512], ot[:], reads=[otk], writes=["out"])
        S.barrier()


def host_inputs(inputs, b, T):
    f = lambda a: np.ascontiguousarray(np.asarray(a, dtype=np.float32))
    col = lambda v: f(np.asarray(v).reshape(-1, 128).T)
    nw = np.stack([col(inputs["norm_mix_pre"][0]), col(inputs["norm_mix_post"][0]),
                   col(inputs["norm_mlp_pre"][0]), col(inputs["norm_mlp_post"][0])], axis=1)
    cws = np.concatenate([np.asarray(inputs["ssm_conv_w"][0]), np.asarray(inputs["ssm_conv_b"])], axis=0)
    cws = cws.reshape(5, 32, 128).transpose(2, 1, 0)
    cwg = np.asarray(inputs["gdn_conv_w"][0]).reshape(4, 32, 128).transpose(2, 1, 0)
    rowv = np.concatenate([np.asarray(inputs["ssm_dt_bias"][0]), np.asarray(inputs["ssm_A_log"][0]), np.asarray(inputs["ssm_D"][0]),
                           np.asarray(inputs["gdn_dt_bias"][0]), np.asarray(inputs["gdn_A_log"][0]),
                           np.asarray(inputs["ssm_norm_w"][0]), np.asarray(inputs["gdn_norm_w"][0])])[None, :]
    return {
        "x": f(np.asarray(inputs["x"])[b, :T]),
        "c_col": col(np.asarray(inputs["c"])[b]),
        "w_ada": f(inputs["w_ada"][0]),
        "b_ada_col": col(inputs["b_ada"][0]),
        "nw_col": f(nw),
        "w_in": f(inputs["w_in"][0]),
        "cw_ssm": f(cws),
        "cw_gdn": f(cwg),
        "rowv": f(rowv),
        "w_su": f(inputs["w_ssm_up"][0]),
        "w_gu": f(inputs["w_gdn_up"][0]),
        "w_out": f(inputs["w_out"][0]),
        "w_up": f(inputs["w_mlp_up"][0]),
        "w_dn": f(inputs["w_mlp_down"][0]),
        "consts": make_consts(),
    }


def kernel(**inputs):
    T = 4096
    nc, S = build(T)
    shared = None
    in_maps = []
    for b in range(8):
        m = host_inputs(inputs, b, T)
        if shared is None:
            shared = m
        else:
            for k in m:
                if k not in ("x", "c_col"):
                    m[k] = shared[k]
        in_maps.append(m)
    res = run_bass_kernel_spmd(nc, in_maps, core_ids=list(range(8)))
    return np.stack([np.asarray(r["out"], dtype=np.float32) for r in res.results], axis=0)
```
